# Optimizing a Trainium2 kernel written in Bass

```python
import math
import jax, jax.numpy as jnp
from jax import lax
import numpy as np

D_MODEL = 1024
BATCH = 16
SEQ = 2048
DEPTH = 2

GRID_W = 64
CTX_LEN = 256
CONV_DIM = 512
CONV_W = 3
SSM_DIM = 512
SSM_GROUP = 16
SSM_GROUPS = SSM_DIM // SSM_GROUP
SSM_STATE = 64
DT_MIN = 1e-3
DT_MAX = 1e-1
N_HEADS = 8
N_KV_HEADS = 2
HEAD_DIM = 128
GQA_GROUP = N_HEADS // N_KV_HEADS
ATTN_DIM = N_HEADS * HEAD_DIM
KV_DIM = N_KV_HEADS * HEAD_DIM
Q_BLOCK = 128
ROPE_THETA = 10000.0
ATTN_SCALE = HEAD_DIM ** -0.5
N_BRANCH = 3
D_FF = 2816
LN_EPS = 1e-6
RMS_EPS = 1e-6
DEEPNORM_ALPHA = (2 * DEPTH) ** 0.25
DEEPNORM_BETA = (8 * DEPTH) ** -0.25
COL_SIZES = (KV_DIM, KV_DIM, SSM_DIM, ATTN_DIM, CONV_DIM, CONV_DIM, CONV_DIM, N_BRANCH * D_MODEL)
N_CTX_KEEP = 3
IN_COLS = sum(COL_SIZES)

kernel_name = 'hybrid_conv_s5_gqa_dit_block'


def _layernorm(t):
    tf = t.astype(jnp.float32)
    mu = jnp.mean(tf, -1, keepdims=True)
    var = jnp.mean(jnp.square(tf - mu), -1, keepdims=True)
    return ((tf - mu) * lax.rsqrt(var + LN_EPS)).astype(t.dtype)


def _rmsnorm(t, g):
    tf = t.astype(jnp.float32)
    return (tf * lax.rsqrt(jnp.mean(tf * tf, -1, keepdims=True) + RMS_EPS)).astype(t.dtype) * g


def _modulate(t, shift, scale):
    return _layernorm(t) * (1.0 + scale) + shift


def _split_cols(z, n_groups):
    parts, off = [], 0
    for size in COL_SIZES[:n_groups]:
        parts.append(z[..., off:off + size])
        off += size
    return parts


def _dwconv(t, w):
    pad = CONV_W // 2
    n = t.shape[1]
    tp = jnp.pad(t, ((0, 0), (pad, pad), (0, 0)))
    out = tp[:, 0:n] * w[0]
    for k in range(1, CONV_W):
        out = out + tp[:, k:k + n] * w[k]
    return out


def _heads(t, n):
    return t.reshape(t.shape[:-1] + (n, HEAD_DIM))


def _axial_rope(rows):
    half = HEAD_DIM // 2
    inv_freq = 1.0 / (ROPE_THETA ** (jnp.arange(0, half, 2, dtype=jnp.float32) / half))
    row = jnp.repeat(jnp.arange(rows, dtype=jnp.float32), GRID_W)
    col = jnp.tile(jnp.arange(GRID_W, dtype=jnp.float32), rows)
    ang = jnp.concatenate([row[:, None] * inv_freq, col[:, None] * inv_freq], -1)
    return jnp.cos(ang), jnp.sin(ang)


def _apply_rope(t, cos, sin):
    tf = t.astype(jnp.float32).reshape(t.shape[:-1] + (HEAD_DIM // 2, 2))
    te, to = tf[..., 0], tf[..., 1]
    cs, sn = cos[:, None, :], sin[:, None, :]
    out = jnp.stack([te * cs - to * sn, te * sn + to * cs], -1)
    return out.reshape(t.shape).astype(t.dtype)


def _attend(q, k, v):
    b, tq = q.shape[:2]
    qg = q.reshape(b, tq, N_KV_HEADS, GQA_GROUP, HEAD_DIM)
    s = jnp.einsum('bqkgd,bskd->bkgqs', qg, k).astype(jnp.float32) * ATTN_SCALE
    p = jax.nn.softmax(s, axis=-1).astype(v.dtype)
    o = jnp.einsum('bkgqs,bskd->bqkgd', p, v)
    return o.reshape(b, tq, ATTN_DIM)


def _block_attention(q, k, v):
    b, n = q.shape[:2]
    qb = q.reshape(b, n // Q_BLOCK, Q_BLOCK, N_HEADS, HEAD_DIM).swapaxes(0, 1)
    o = lax.map(lambda blk: _attend(blk, k, v), qb)
    return o.swapaxes(0, 1).reshape(b, n, ATTN_DIM)


def _zoh(lam_re, lam_im, log_dt, b_re, b_im):
    lam = lax.complex(lam_re.astype(jnp.float32), lam_im.astype(jnp.float32))
    dt = jnp.exp(log_dt.astype(jnp.float32))[:, None]
    lam_bar = jnp.exp(lam * dt)
    bmat = lax.complex(b_re.astype(jnp.float32), b_im.astype(jnp.float32))
    b_bar = ((lam_bar - 1.0) / lam)[..., None] * bmat
    return lam_bar, b_bar


def _ssm_drive(u, b_bar):
    b, n = u.shape[:2]
    ug = u.astype(jnp.float32).reshape(b, n, SSM_GROUPS, SSM_GROUP).astype(jnp.complex64)
    return jnp.einsum('btgh,gph->tbgp', ug, b_bar)


def _diag_scan(lam_bar, bu, reverse):
    a = jnp.broadcast_to(lam_bar, (bu.shape[0], 1) + lam_bar.shape)

    def combine(left, right):
        a_l, b_l = left
        a_r, b_r = right
        return a_r * a_l, a_r * b_l + b_r

    _, s = lax.associative_scan(combine, (a, bu), reverse=reverse, axis=0)
    return s


def _ssm_readout(s, c_mat):
    n, b = s.shape[:2]
    return jnp.real(jnp.einsum('tbgp,ghp->btgh', s, c_mat)).reshape(b, n, SSM_DIM)


def _s5_mixer(u_lat, u_ctx, lam_re, lam_im, log_dt, b_re, b_im, c_re, c_im, ssm_d, need_ctx_out):
    y_lat = ssm_d * u_lat.astype(jnp.float32)
    y_ctx = ssm_d * u_ctx.astype(jnp.float32) if need_ctx_out else None
    for d, reverse in enumerate((False, True)):
        lam_bar, b_bar = _zoh(lam_re[d], lam_im[d], log_dt[d], b_re[d], b_im[d])
        c_mat = lax.complex(c_re[d].astype(jnp.float32), c_im[d].astype(jnp.float32))
        s_ctx = _diag_scan(lam_bar, _ssm_drive(u_ctx, b_bar), reverse)
        s_end = s_ctx[0] if reverse else s_ctx[-1]
        first = -1 if reverse else 0
        bu_lat = _ssm_drive(u_lat, b_bar).at[first].add(lam_bar * s_end)
        s_lat = _diag_scan(lam_bar, bu_lat, reverse)
        y_lat = y_lat + _ssm_readout(s_lat, c_mat)
        if need_ctx_out:
            y_ctx = y_ctx + _ssm_readout(s_ctx, c_mat)
    return y_lat.astype(u_lat.dtype), (y_ctx.astype(u_ctx.dtype) if need_ctx_out else None)


def _glu(y, w_glu):
    a, g = jnp.split(jax.nn.gelu(y) @ w_glu, 2, axis=-1)
    return a * jax.nn.sigmoid(g)


def _merge(gate_logits, y_conv, y_ssm, y_attn):
    g_conv, g_ssm, g_attn = jnp.split(jax.nn.sigmoid(gate_logits), N_BRANCH, axis=-1)
    return g_conv * y_conv + g_ssm * y_ssm + g_attn * y_attn


def _token_mixer(h_lat, h_ctx, cos, sin, w_in, conv_w, w_conv_out,
                 lam_re, lam_im, log_dt, b_re, b_im, c_re, c_im, ssm_d, w_glu,
                 q_norm_g, k_norm_g, w_attn_out, w_o, need_ctx_out):
    n_ctx_groups = len(COL_SIZES) if need_ctx_out else N_CTX_KEEP
    z_lat = h_lat @ w_in
    z_ctx = h_ctx @ w_in[:, :sum(COL_SIZES[:n_ctx_groups])]
    k_l, v_l, u_l, q_l, ax_l, bg_l, cg_l, gt_l = _split_cols(z_lat, len(COL_SIZES))
    ctx_parts = _split_cols(z_ctx, n_ctx_groups)
    k_c, v_c, u_c = ctx_parts[:N_CTX_KEEP]

    k_c = _rmsnorm(_heads(k_c, N_KV_HEADS), k_norm_g)
    v_c = _heads(v_c, N_KV_HEADS)
    k_l = _apply_rope(_rmsnorm(_heads(k_l, N_KV_HEADS), k_norm_g), cos, sin)
    q_l = _apply_rope(_rmsnorm(_heads(q_l, N_HEADS), q_norm_g), cos, sin)
    k_all = jnp.concatenate([k_c, k_l], axis=1)
    v_all = jnp.concatenate([v_c, _heads(v_l, N_KV_HEADS)], axis=1)
    attn_lat = _block_attention(q_l, k_all, v_all) @ w_attn_out

    s5_lat, s5_ctx = _s5_mixer(u_l, u_c, lam_re, lam_im, log_dt, b_re, b_im, c_re, c_im, ssm_d, need_ctx_out)
    ssm_lat = _glu(s5_lat, w_glu)

    conv_lat = (bg_l * _dwconv(cg_l * ax_l, conv_w)) @ w_conv_out

    out_lat = _merge(gt_l, conv_lat, ssm_lat, attn_lat) @ w_o
    if not need_ctx_out:
        return out_lat, None

    q_c, ax_c, bg_c, cg_c, gt_c = ctx_parts[N_CTX_KEEP:]
    q_c = _rmsnorm(_heads(q_c, N_HEADS), q_norm_g)
    attn_ctx = _attend(q_c, k_c, v_c) @ w_attn_out
    ssm_ctx = _glu(s5_ctx, w_glu)
    conv_ctx = (bg_c * _dwconv(cg_c * ax_c, conv_w)) @ w_conv_out
    out_ctx = _merge(gt_c, conv_ctx, ssm_ctx, attn_ctx) @ w_o
    return out_lat, out_ctx


def _conv_ffn(h, w_up, conv_w, conv_b, w_down):
    u, v = jnp.split(h @ w_up, 2, axis=-1)
    return (jax.nn.gelu(_dwconv(u, conv_w) + conv_b) * v) @ w_down


def _post_norm(res, update, g, b):
    return _layernorm(DEEPNORM_ALPHA * res + update) * g + b


def setup_inputs(seed: int = 0) -> dict:
    key = jax.random.key(seed)
    keys = iter(jax.random.split(key, 40))

    def nrm(shape, std):
        return std * jax.random.normal(next(keys), shape, dtype=jnp.float32)

    L, D = DEPTH, D_MODEL
    G, P, H = SSM_GROUPS, SSM_STATE, SSM_GROUP
    lam_re = -0.5 + nrm((L, 2, G, P), 0.01)
    lam_im = jnp.pi * jnp.arange(P, dtype=jnp.float32) + nrm((L, 2, G, P), 0.01)
    log_dt = jax.random.uniform(next(keys), (L, 2, G), jnp.float32, math.log(DT_MIN), math.log(DT_MAX))
    return {
        'x': nrm((BATCH, SEQ, D), 1.0),
        'c': nrm((BATCH, D), 1.0),
        'ctx': nrm((BATCH, CTX_LEN, D), 1.0),
        'c_ctx': nrm((D,), 1.0),
        'w_mod': nrm((L, D, 6 * D), 0.5 * D ** -0.5),
        'b_mod': nrm((L, 6 * D), 0.02),
        'w_in': nrm((L, D, IN_COLS), D ** -0.5),
        'conv_w': nrm((L, CONV_W, CONV_DIM), CONV_W ** -0.5),
        'w_conv_out': nrm((L, CONV_DIM, D), CONV_DIM ** -0.5),
        'ssm_lam_re': lam_re,
        'ssm_lam_im': lam_im,
        'ssm_log_dt': log_dt,
        'ssm_b_re': nrm((L, 2, G, P, H), (2 * H) ** -0.5),
        'ssm_b_im': nrm((L, 2, G, P, H), (2 * H) ** -0.5),
        'ssm_c_re': nrm((L, 2, G, H, P), 0.5),
        'ssm_c_im': nrm((L, 2, G, H, P), 0.5),
        'ssm_d': nrm((L, SSM_DIM), 1.0),
        'w_glu': nrm((L, SSM_DIM, 2 * D), SSM_DIM ** -0.5),
        'q_norm_g': 1.0 + nrm((L, HEAD_DIM), 0.02),
        'k_norm_g': 1.0 + nrm((L, HEAD_DIM), 0.02),
        'w_attn_out': nrm((L, ATTN_DIM, D), ATTN_DIM ** -0.5),
        'w_o': nrm((L, D, D), DEEPNORM_BETA * D ** -0.5),
        'ln1_g': 1.0 + nrm((L, D), 0.02),
        'ln1_b': nrm((L, D), 0.02),
        'ffn_w_up': nrm((L, D, 2 * D_FF), D ** -0.5),
        'ffn_conv_w': nrm((L, CONV_W, D_FF), CONV_W ** -0.5),
        'ffn_conv_b': nrm((L, D_FF), 0.02),
        'ffn_w_down': nrm((L, D_FF, D), DEEPNORM_BETA * D_FF ** -0.5),
        'ln2_g': 1.0 + nrm((L, D), 0.02),
        'ln2_b': nrm((L, D), 0.02),
    }


def reference(x, c, ctx, c_ctx, w_mod, b_mod, w_in, conv_w, w_conv_out,
              ssm_lam_re, ssm_lam_im, ssm_log_dt, ssm_b_re, ssm_b_im, ssm_c_re, ssm_c_im, ssm_d, w_glu,
              q_norm_g, k_norm_g, w_attn_out, w_o, ln1_g, ln1_b,
              ffn_w_up, ffn_conv_w, ffn_conv_b, ffn_w_down, ln2_g, ln2_b):
    rows = x.shape[1] // GRID_W
    cos, sin = _axial_rope(rows)
    sc = jax.nn.silu(c)
    sc_ctx = jax.nn.silu(c_ctx)
    for i in range(DEPTH):
        last = i == DEPTH - 1
        mod_lat = (sc @ w_mod[i] + b_mod[i])[:, None, :]
        mod_ctx = sc_ctx @ w_mod[i] + b_mod[i]
        sh1, sc1, g1, sh2, sc2, g2 = jnp.split(mod_lat, 6, axis=-1)
        csh1, csc1, cg1, csh2, csc2, cg2 = jnp.split(mod_ctx, 6, axis=-1)

        m_lat, m_ctx = _token_mixer(
            _modulate(x, sh1, sc1), _modulate(ctx, csh1, csc1), cos, sin,
            w_in[i], conv_w[i], w_conv_out[i],
            ssm_lam_re[i], ssm_lam_im[i], ssm_log_dt[i], ssm_b_re[i], ssm_b_im[i],
            ssm_c_re[i], ssm_c_im[i], ssm_d[i], w_glu[i],
            q_norm_g[i], k_norm_g[i], w_attn_out[i], w_o[i], not last)
        x = _post_norm(x, g1 * m_lat, ln1_g[i], ln1_b[i])
        f_lat = _conv_ffn(_modulate(x, sh2, sc2), ffn_w_up[i], ffn_conv_w[i], ffn_conv_b[i], ffn_w_down[i])
        x = _post_norm(x, g2 * f_lat, ln2_g[i], ln2_b[i])

        if not last:
            ctx = _post_norm(ctx, cg1 * m_ctx, ln1_g[i], ln1_b[i])
            f_ctx = _conv_ffn(_modulate(ctx, csh2, csc2), ffn_w_up[i], ffn_conv_w[i], ffn_conv_b[i], ffn_w_down[i])
            ctx = _post_norm(ctx, cg2 * f_ctx, ln2_g[i], ln2_b[i])
    return x
```

```python
import numpy as np
import ml_dtypes
from contextlib import ExitStack
import concourse.bass as bass
import concourse.mybir as mybir
from concourse.bass_utils import run_bass_kernel_spmd

F32 = mybir.dt.float32
BF16 = mybir.dt.bfloat16
I32 = mybir.dt.int32
AF = mybir.ActivationFunctionType
ALU = mybir.AluOpType

ENGS = ["tensor", "vector", "scalar", "gpsimd", "sync"]

D = 1024
TL = 2048
TC = 256
TT = TC + TL
NBC = 2
DEPTH = 2
IN_COLS = 6656
DFF = 2816
NCH = TT // 8
EPS = 1e-6
ALPHA = float((2 * DEPTH) ** 0.25)
ATTN_SCALE = float(128 ** -0.5)
TWO_PI = float(2 * np.pi)
PI = float(np.pi)


class Buf:
    def __init__(self, name, t):
        self.name = name
        self.t = t
        self.w = {}
        self.r = {}
        self.dkey = None

    def __getitem__(self, idx):
        return self.t[idx]


class Prog:
    def __init__(self, nc, n_dsem=56):
        self.nc = nc
        self.base = ExitStack()
        self.scopes = [ExitStack()]
        self.scope_bufs = [[]]
        self.q = {e: [] for e in ENGS}
        self.sem = {}
        self.cnt = {}
        self.seen = {e: {} for e in ENGS}
        for e in ENGS:
            self.sem[e] = self.base.enter_context(nc.semaphore("s_" + e))
            self.cnt[e] = 0
        self.free_dsem = []
        for i in range(n_dsem):
            k = ("d", i)
            self.sem[k] = self.base.enter_context(nc.semaphore("d%d" % i))
            self.cnt[k] = 0
            self.free_dsem.append(k)
        self.uid = 0

    def push(self):
        self.scopes.append(ExitStack())
        self.scope_bufs.append([])

    def pop(self):
        self.barrier()
        for b in self.scope_bufs.pop():
            if b.dkey is not None:
                self.free_dsem.append(b.dkey)
                b.dkey = None
        self.scopes.pop().close()

    def sb(self, name, shape, dtype):
        self.uid += 1
        t = self.scopes[-1].enter_context(self.nc.sbuf_tensor("%s_%d" % (name, self.uid), list(shape), dtype))
        b = Buf(name, t)
        self.scope_bufs[-1].append(b)
        return b

    def ps(self, name, shape, dtype=F32):
        t = self.base.enter_context(self.nc.psum_tensor(name, list(shape), dtype))
        return Buf(name, t)

    def dram(self, name, shape, dtype, kind="Internal"):
        t = self.nc.dram_tensor(name, list(shape), dtype, kind=kind).ap()
        return Buf(name, t)

    def _dkey(self, b):
        if b.dkey is None:
            b.dkey = self.free_dsem.pop()
        return b.dkey

    def _deps(self, eng, reads, writes, strict=False):
        deps = {}

        def add(d):
            for k, v in d.items():
                if k == eng and eng == "tensor":
                    continue
                if deps.get(k, 0) < v:
                    deps[k] = v

        for r in reads:
            add(r.w)
        for w in writes:
            add(w.w)
            add(w.r)
        out = []
        for k, v in deps.items():
            if self.seen[eng].get(k, 0) >= v:
                continue
            self.seen[eng][k] = v
            out.append((self.sem[k], v))
        return out

    def _commit(self, ev, reads, writes):
        k, v = ev
        for r in reads:
            if r.r.get(k, 0) < v:
                r.r[k] = v
        for w in writes:
            if w.w.get(k, 0) < v:
                w.w[k] = v
            w.r = {}

    def op(self, eng, fn, reads=(), writes=(), inc=True):
        waits = self._deps(eng, reads, writes)
        if inc:
            self.cnt[eng] += 1
            ev = (eng, self.cnt[eng])
        else:
            ev = (eng, self.cnt[eng] + 1)
        sem = self.sem[eng]

        def emit(e, waits=waits, fn=fn, inc=inc, sem=sem):
            for s, v in waits:
                e.wait_ge(s, v)
            ins = fn(e)
            if inc:
                ins.then_inc(sem, 1)

        self.q[eng].append(emit)
        self._commit(ev, reads, writes)

    def dma(self, eng, out, in_, reads=(), writes=(), semb=None, **kw):
        waits = self._deps(eng, reads, writes, strict=True)
        if semb is None:
            for b in list(writes) + list(reads):
                if not isinstance(b, DBuf):
                    semb = b
                    break
        key = self._dkey(semb)
        self.cnt[key] += 16
        ev = (key, self.cnt[key])
        sem = self.sem[key]

        def emit(e, waits=waits, out=out, in_=in_, sem=sem, kw=kw):
            for s, v in waits:
                e.wait_ge(s, v)
            e.dma_start(out=out, in_=in_, **kw).then_inc(sem, 16)

        self.q[eng].append(emit)
        self._commit(ev, reads, writes)

    def barrier(self):
        tot = dict(self.cnt)
        for e in ENGS:
            waits = []
            for k, v in tot.items():
                if k == e or v == 0:
                    continue
                if self.seen[e].get(k, 0) >= v:
                    continue
                self.seen[e][k] = v
                waits.append((self.sem[k], v))

            def emit(en, waits=waits):
                for s, v in waits:
                    en.wait_ge(s, v)

            self.q[e].append(emit)

    def finish(self):
        self.barrier()
        nc = self.nc
        q = self.q
        with nc.Block() as block:
            @block.tensor
            def _(e):
                for f in q["tensor"]:
                    f(e)

            @block.vector
            def _(e):
                for f in q["vector"]:
                    f(e)

            @block.scalar
            def _(e):
                for f in q["scalar"]:
                    f(e)

            @block.gpsimd
            def _(e):
                for f in q["gpsimd"]:
                    f(e)

            @block.sync
            def _(e):
                for f in q["sync"]:
                    f(e)
        while self.scopes:
            self.scopes.pop().close()
        self.base.close()


class DBuf(Buf):
    pass


def TS(out, in0, s1, s2, op0, op1=None):
    if op1 is None:
        return lambda e: e.tensor_scalar(out=out, in0=in0, scalar1=s1, scalar2=None, op0=op0)
    return lambda e: e.tensor_scalar(out=out, in0=in0, scalar1=s1, scalar2=s2, op0=op0, op1=op1)


def TTo(out, in0, in1, op):
    return lambda e: e.tensor_tensor(out=out, in0=in0, in1=in1, op=op)


def STT(out, in0, scalar, in1, op0, op1):
    return lambda e: e.scalar_tensor_tensor(out=out, in0=in0, scalar=scalar, in1=in1, op0=op0, op1=op1)


def ACT(out, in_, func, bias=None, scale=None, accum_out=None):
    kw = {}
    if bias is not None:
        kw["bias"] = bias
    if scale is not None:
        kw["scale"] = scale
    if accum_out is not None:
        kw["accum_out"] = accum_out
    return lambda e: e.activation(out=out, in_=in_, func=func, **kw)


def CP(out, in_):
    return lambda e: e.tensor_copy(out=out, in_=in_)


def MM(out, lhsT, rhs, start, stop):
    return lambda e: e.matmul(out, lhsT=lhsT, rhs=rhs, start=start, stop=stop)


def TR(out, in_, ident):
    return lambda e: e.transpose(out=out, in_=in_, identity=ident)


class K:
    def __init__(self, dbg=(), stop_after=None):
        self.dbg = set(dbg)
        self.stop_after = stop_after
        nc = self.nc = bass.Bass("TRN2", target_bir_lowering=False)
        P = self.P = Prog(nc)
        self.inputs = {}
        self.psb = [P.ps("psb%d" % i, [128, 512], F32) for i in range(8)]
        self.rr = {}

    def din(self, name, shape, dt=F32):
        b = DBuf(name, self.nc.dram_tensor(name, list(shape), dt, kind="ExternalInput").ap())
        self.inputs[name] = b
        return b

    def dscr(self, name, shape, dt):
        kind = "ExternalOutput" if name in self.dbg else "Internal"
        return DBuf(name, self.nc.dram_tensor(name, list(shape), dt, kind=kind).ap())

    def bank(self, group, banks):
        i = self.rr.get(group, 0)
        self.rr[group] = i + 1
        return self.psb[banks[i % len(banks)]]

    def build(self):
        nc, P = self.nc, self.P
        L = DEPTH
        I = self.I = {}
        I["x"] = self.din("x", [NBC, TL, D])
        I["c"] = self.din("c", [NBC, D])
        I["ctx"] = self.din("ctx", [NBC, TC, D])
        I["c_ctx"] = self.din("c_ctx", [1, D])
        for name, shape in [
            ("w_mod", [L, D, 6 * D]), ("b_mod", [L, 6 * D]), ("w_in", [L, D, IN_COLS]), ("conv_w", [L, 3, 512]),
            ("w_conv_out", [L, 512, D]), ("ssm_lam_re", [L, 2, 32, 64]), ("ssm_lam_im", [L, 2, 32, 64]),
            ("ssm_log_dt", [L, 2, 32]), ("ssm_b_re", [L, 2, 32, 64, 16]), ("ssm_b_im", [L, 2, 32, 64, 16]),
            ("ssm_c_re", [L, 2, 32, 16, 64]), ("ssm_c_im", [L, 2, 32, 16, 64]), ("ssm_d", [L, 512]),
            ("w_glu", [L, 512, 2 * D]), ("q_norm_g", [L, 128]), ("k_norm_g", [L, 128]), ("w_attn_out", [L, D, D]),
            ("w_o", [L, D, D]), ("ln1_g", [L, D]), ("ln1_b", [L, D]), ("ffn_w_up", [L, D, 2 * DFF]),
            ("ffn_conv_w", [L, 3, DFF]), ("ffn_conv_b", [L, DFF]), ("ffn_w_down", [L, DFF, D]),
            ("ln2_g", [L, D]), ("ln2_b", [L, D]),
        ]:
            I[name] = self.din(name, shape)
        I["k_identf"] = self.din("k_identf", [128, 128])
        I["k_identb"] = self.din("k_identb", [128, 128], BF16)
        I["k_onesb"] = self.din("k_onesb", [128, 128], BF16)
        I["k_cos"] = self.din("k_cos", [128, 16, 64])
        I["k_sin"] = self.din("k_sin", [128, 16, 64])
        I["k_maskf"] = self.din("k_maskf", [128, 128])
        I["k_maskr"] = self.din("k_maskr", [128, 128])
        self.out = DBuf("out", nc.dram_tensor("out", [NBC, TL, D], F32, kind="ExternalOutput").ap())

        S = self.S = {}
        for l in range(L):
            S["wb_in%d" % l] = self.dscr("wb_in%d" % l, [D, IN_COLS], BF16)
            S["wb_co%d" % l] = self.dscr("wb_co%d" % l, [512, D], BF16)
            S["wb_glu%d" % l] = self.dscr("wb_glu%d" % l, [512, 2 * D], BF16)
            S["wb_ao%d" % l] = self.dscr("wb_ao%d" % l, [D, D], BF16)
            S["wb_o%d" % l] = self.dscr("wb_o%d" % l, [D, D], BF16)
            S["wb_up%d" % l] = self.dscr("wb_up%d" % l, [D, 2 * DFF], BF16)
            S["wb_dn%d" % l] = self.dscr("wb_dn%d" % l, [DFF, D], BF16)
            S["modr%d" % l] = self.dscr("modr%d" % l, [3, 6 * D], F32)
        S["resA"] = self.dscr("resA", [NBC, TT, D], F32)
        S["resB"] = self.dscr("resB", [NBC, TT, D], F32)
        S["KT"] = self.dscr("KT", [NBC, 2, 128, TT], BF16)
        S["V"] = self.dscr("V", [NBC, TT, 256], BF16)
        S["U"] = self.dscr("U", [NBC, TT, 512], BF16)
        S["QT"] = self.dscr("QT", [NBC, 8, 128, TT], BF16)
        S["AXT"] = self.dscr("AXT", [NBC, 512, TT], BF16)
        S["BGT"] = self.dscr("BGT", [NBC, 512, TT], BF16)
        S["CGT"] = self.dscr("CGT", [NBC, 512, TT], BF16)
        S["GT"] = self.dscr("GT", [NBC, 3 * D, TT], BF16)
        S["AOT"] = self.dscr("AOT", [NBC, D, TT], BF16)
        S["SSMT"] = self.dscr("SSMT", [NBC, D, TT], BF16)
        S["YS"] = self.dscr("YS", [NBC, TT, 512], F32)

        C = self.C = {}
        C["identf"] = P.sb("identf", [128, 128], F32)
        C["identb"] = P.sb("identb", [128, 128], BF16)
        C["onesb"] = P.sb("onesb", [128, 128], BF16)
        C["eps"] = P.sb("eps", [128, 1], F32)
        for nm in ["identf", "identb", "onesb"]:
            P.dma("sync", C[nm][:], I["k_" + nm].t, reads=[I["k_" + nm]], writes=[C[nm]])
        P.op("vector", lambda e: e.memset(C["eps"][:], EPS), writes=[C["eps"]])

        try:
            self.weight_prep()
            if self.stop_after == "prep":
                return self.finish()
            for l in range(L):
                self.layer(l)
                if self.stop_after == "layer%d" % l:
                    break
        except StopIteration:
            pass
        return self.finish()

    def finish(self):
        self.P.finish()
        return self.nc

    def check_stop(self, tag):
        if self.stop_after == tag:
            raise StopIteration

    def weight_prep(self):
        P, I, S = self.P, self.I, self.S
        P.push()
        stf = [P.sb("wpf%d" % i, [128, 2048], F32) for i in range(3)]
        stb = [P.sb("wpb%d" % i, [128, 2048], BF16) for i in range(3)]
        n = 0
        engs = ["gpsimd", "vector", "scalar"]
        for l in range(DEPTH):
            for src, dst, Kd, N in [("w_in", "wb_in", D, IN_COLS), ("w_conv_out", "wb_co", 512, D), ("w_glu", "wb_glu", 512, 2 * D),
                                    ("w_attn_out", "wb_ao", D, D), ("w_o", "wb_o", D, D), ("ffn_w_up", "wb_up", D, 2 * DFF),
                                    ("ffn_w_down", "wb_dn", DFF, D)]:
                sa = I[src].t[l]
                db = S[dst + str(l)]
                for kt in range(Kd // 128):
                    for c0 in range(0, N, 2048):
                        w = min(2048, N - c0)
                        f, b = stf[n % 3], stb[n % 3]
                        eng = engs[n % 3]
                        P.dma("sync", f[:, :w], sa[kt * 128:(kt + 1) * 128, c0:c0 + w], reads=[I[src]], writes=[f])
                        if eng == "scalar":
                            P.op(eng, ACT(b[:, :w], f[:, :w], AF.Copy), reads=[f], writes=[b])
                        else:
                            P.op(eng, CP(b[:, :w], f[:, :w]), reads=[f], writes=[b])
                        P.dma("gpsimd", db.t[kt * 128:(kt + 1) * 128, c0:c0 + w], b[:, :w], reads=[b], writes=[db])
                        n += 1
        P.pop()

    def layer(self, l):
        P = self.P
        self.l = l
        self.last = (l == DEPTH - 1)
        P.push()
        self.modT = P.sb("modT", [128, 6, 8, 3], F32)
        self.phase0()
        self.check_stop("p0_%d" % l)
        self.phaseA()
        self.check_stop("pA_%d" % l)
        self.phaseB()
        self.check_stop("pB_%d" % l)
        self.phaseC()
        self.check_stop("pC_%d" % l)
        self.phaseD()
        self.check_stop("pD_%d" % l)
        self.phaseF()
        self.check_stop("pF_%d" % l)
        P.pop()

    def res_in(self, b, seg):
        if self.l == 0:
            return (self.I["ctx"], self.I["ctx"].t[b]) if seg == "ctx" else (self.I["x"], self.I["x"].t[b])
        r = self.S["resB"]
        return (r, r.t[b, 0:TC, :]) if seg == "ctx" else (r, r.t[b, TC:TT, :])

    def phase0(self):
        P, I, S, C, l = self.P, self.I, self.S, self.C, self.l
        P.push()
        scT = P.sb("scT", [128, 8, 3], F32)
        bm3 = P.sb("bm3", [3, 6 * D], F32)
        modrows = P.sb("modrows", [3, 6 * D], F32)
        wst = [P.sb("wmst%d" % i, [128, 8, 512], F32) for i in range(2)]
        for r in range(3):
            src = I["c"].t[r, :] if r < 2 else I["c_ctx"].t[0, :]
            P.dma("sync", scT[:, :, r], src.rearrange("(k p) -> p k", p=128), reads=[I["c"]], writes=[scT],
                  allow_slow_non_contiguous=True)
        P.op("scalar", ACT(scT[:], scT[:], AF.Silu), reads=[scT], writes=[scT])
        P.dma("sync", bm3[0:3, :], I["b_mod"].t[l, :].partition_broadcast(3), reads=[I["b_mod"]], writes=[bm3])
        wm = I["w_mod"].t[l].rearrange("(k p) n -> p k n", p=128)
        for blk in range(12):
            w = wst[blk % 2]
            P.dma("sync", w[:], wm[:, :, blk * 512:(blk + 1) * 512], reads=[I["w_mod"]], writes=[w])
            ps = self.bank("p0", [0, 1])
            for k in range(8):
                P.op("tensor", MM(ps[0:3, 0:512], scT[:, k, :], w[:, k, :], k == 0, k == 7), reads=[scT, w], writes=[ps], inc=(k == 7))
            P.op("vector", TTo(modrows[0:3, blk * 512:(blk + 1) * 512], ps[0:3, 0:512], bm3[0:3, blk * 512:(blk + 1) * 512], ALU.add),
                 reads=[ps, bm3], writes=[modrows])
        P.dma("gpsimd", S["modr%d" % l].t, modrows[0:3, :], reads=[modrows], writes=[S["modr%d" % l]])
        ps = self.bank("p0", [0, 1])
        for t in range(48):
            P.op("tensor", TR(ps[:, t * 3:(t + 1) * 3], modrows[0:3, t * 128:(t + 1) * 128], C["identf"][0:3, 0:3]),
                 reads=[modrows, C["identf"]], writes=[ps], inc=(t == 47))
        mt = self.modT
        P.op("vector", CP(mt[:].rearrange("p a k r -> p (a k r)"), ps[:, 0:144]), reads=[ps], writes=[mt])
        for sec in (1, 4):
            P.op("vector", TS(mt[:, sec], mt[:, sec], 1.0, None, ALU.add), reads=[mt], writes=[mt])
        P.pop()

    def ln_tile(self, xt, out_ap, out_buf, sm, src_ap=None, act_extra_reads=()):
        P, C = self.P, self.C
        st, mv, rs, nb = sm
        src = xt[:] if src_ap is None else src_ap
        P.op("vector", lambda e: e.bn_stats(out=st[:, 0:6], in_=src[:, 0:512]), reads=[xt], writes=[st])
        P.op("vector", lambda e: e.bn_stats(out=st[:, 6:12], in_=src[:, 512:1024]), reads=[xt], writes=[st])
        P.op("vector", lambda e: e.bn_aggr(out=mv[:, 0:2], in_=st[:, 0:12]), reads=[st], writes=[mv])
        P.op("scalar", ACT(rs[:, 0:1], mv[:, 1:2], AF.Sqrt, bias=C["eps"][:, 0:1], scale=1.0), reads=[mv, C["eps"]], writes=[rs])
        P.op("vector", lambda e: e.reciprocal(out=rs[:, 0:1], in_=rs[:, 0:1]), reads=[rs], writes=[rs])
        P.op("vector", TS(nb[:, 0:1], mv[:, 0:1], rs[:, 0:1], -1.0, ALU.mult, ALU.mult), reads=[mv, rs], writes=[nb])
        P.op("scalar", ACT(out_ap, src, AF.Identity, bias=nb[:, 0:1], scale=rs[:, 0:1]), reads=[xt, rs, nb], writes=[out_buf])

    def ln_small(self, tag, n=2):
        P = self.P
        return [(P.sb(tag + "st%d" % i, [128, 12], F32), P.sb(tag + "mv%d" % i, [128, 2], F32),
                 P.sb(tag + "rs%d" % i, [128, 1], F32), P.sb(tag + "nb%d" % i, [128, 1], F32)) for i in range(n)]

    def phaseA(self):
        P, I, S, C, l = self.P, self.I, self.S, self.C, self.l
        P.push()
        W = P.sb("Win", [128, 8, IN_COLS], BF16)
        wsrc = S["wb_in%d" % l]
        for k in range(8):
            P.dma("sync", W[:, k, :], wsrc.t[k * 128:(k + 1) * 128, :], reads=[wsrc], writes=[W])
        cos = P.sb("cos", [128, 16, 64], F32)
        sin = P.sb("sin", [128, 16, 64], F32)
        P.dma("sync", cos[:], I["k_cos"].t, reads=[I["k_cos"]], writes=[cos])
        P.dma("sync", sin[:], I["k_sin"].t, reads=[I["k_sin"]], writes=[sin])
        gq = P.sb("gq", [128, 128], F32)
        gk = P.sb("gk", [128, 128], F32)
        P.dma("sync", gq[:], I["q_norm_g"].t[l, :].partition_broadcast(128), reads=[I["q_norm_g"]], writes=[gq])
        P.dma("sync", gk[:], I["k_norm_g"].t[l, :].partition_broadcast(128), reads=[I["k_norm_g"]], writes=[gk])
        xt = [P.sb("xt%d" % i, [128, D], F32) for i in range(2)]
        xn = [P.sb("xn%d" % i, [128, D], BF16) for i in range(2)]
        sm = self.ln_small("a")
        hT = [P.sb("hT%d" % i, [128, 8, 512], BF16) for i in range(2)]
        junk = P.sb("junk", [128, 128], F32)
        ss = [P.sb("ss%d" % i, [128, 4], F32) for i in range(2)]
        xq = [P.sb("xq%d" % i, [128, 4, 128], F32) for i in range(2)]
        rt = [P.sb("rt%d" % i, [128, 4, 64], F32) for i in range(4)]
        qn = [P.sb("qn%d" % i, [128, 4, 128], BF16) for i in range(2)]
        qTs = P.sb("qTs", [128, 8, 512], BF16)
        kTs = P.sb("kTs", [128, 2, 512], BF16)
        vst = [P.sb("vst%d" % i, [128, 256], BF16) for i in range(2)]
        ust = [P.sb("ust%d" % i, [128, 512], BF16) for i in range(2)]
        fst = [P.sb("fst%d" % i, [128, 512], BF16) for i in range(3)]
        cnt = {"x": 0, "q": 0, "v": 0, "u": 0, "f": 0, "ev": 0}

        def normrope(ps, c0, nh, g, rope, ltile, dst_bf):
            i = cnt["q"]
            cnt["q"] += 1
            s_, x_ = ss[i % 2], xq[i % 2]
            for h in range(nh):
                P.op("scalar", ACT(junk[:], ps[:, c0 + h * 128:c0 + (h + 1) * 128], AF.Square, accum_out=s_[:, h:h + 1]),
                     reads=[ps], writes=[junk, s_])
            P.op("scalar", ACT(s_[:, 0:nh], s_[:, 0:nh], AF.Sqrt, bias=C["eps"][:, 0:1], scale=1.0 / 128), reads=[s_, C["eps"]], writes=[s_])
            P.op("vector", lambda e: e.reciprocal(out=s_[:, 0:nh], in_=s_[:, 0:nh]), reads=[s_], writes=[s_])
            for h in range(nh):
                P.op("vector", STT(x_[:, h, :], ps[:, c0 + h * 128:c0 + (h + 1) * 128], s_[:, h:h + 1], g[:], ALU.mult, ALU.mult),
                     reads=[ps, s_, g], writes=[x_])
            if not rope:
                P.op("gpsimd", CP(dst_bf[:, 0:nh, :], x_[:, 0:nh, :]), reads=[x_], writes=[dst_bf])
                return
            xe = x_[:, 0:nh, 0:128:2]
            xo = x_[:, 0:nh, 1:128:2]
            cb = cos[:, ltile:ltile + 1, :].to_broadcast([128, nh, 64])
            sb_ = sin[:, ltile:ltile + 1, :].to_broadcast([128, nh, 64])
            t1, t2, t3, t4 = [r[:, 0:nh, :] for r in rt]
            eng = "gpsimd"
            P.op(eng, TTo(t1, xe, cb, ALU.mult), reads=[x_, cos], writes=[rt[0]])
            P.op(eng, TTo(t2, xo, sb_, ALU.mult), reads=[x_, sin], writes=[rt[1]])
            P.op(eng, TTo(dst_bf[:, 0:nh, 0:128:2], t1, t2, ALU.subtract), reads=[rt[0], rt[1]], writes=[dst_bf])
            P.op(eng, TTo(t3, xe, sb_, ALU.mult), reads=[x_, sin], writes=[rt[2]])
            P.op(eng, TTo(t4, xo, cb, ALU.mult), reads=[x_, cos], writes=[rt[3]])
            P.op(eng, TTo(dst_bf[:, 0:nh, 1:128:2], t3, t4, ALU.add), reads=[rt[2], rt[3]], writes=[dst_bf])

        for b in range(NBC):
            for seg in (["ctx", "lat"]):
                nst = 1 if seg == "ctx" else 4
                ntok = 256 if seg == "ctx" else 512
                r = 2 if seg == "ctx" else b
                rbuf, rap = self.res_in(b, seg)
                full = (seg == "lat") or (not self.last)
                for st_i in range(nst):
                    t0 = st_i * ntok
                    g0 = t0 + (0 if seg == "ctx" else TC)
                    nt = ntok // 128
                    h = hT[st_i % 2]
                    for ti in range(nt):
                        i = cnt["x"]
                        cnt["x"] += 1
                        x_, n_ = xt[i % 2], xn[i % 2]
                        P.dma("sync", x_[:], rap[t0 + ti * 128:t0 + (ti + 1) * 128, :], reads=[rbuf], writes=[x_])
                        self.ln_tile(x_, n_[:], n_, sm[i % 2])
                        ps = self.bank("atr", [0, 1])
                        pb = ps[:].bitcast(BF16).rearrange("p (k t) -> p k t", k=8)
                        for k in range(8):
                            P.op("tensor", TR(pb[:, k, :], n_[:, k * 128:(k + 1) * 128], C["identb"][:]), reads=[n_, C["identb"]], writes=[ps],
                                 inc=(k == 7))
                        for k in range(8):
                            o = h[:, k, ti * 128:(ti + 1) * 128]
                            if k % 2 == 0:
                                P.op("vector", TS(o, pb[:, k, :], self.modT[:, 1, k, r:r + 1], self.modT[:, 0, k, r:r + 1], ALU.mult, ALU.add),
                                     reads=[ps, self.modT], writes=[h])
                            else:
                                P.op("scalar", ACT(o, pb[:, k, :], AF.Identity, bias=self.modT[:, 0, k, r:r + 1], scale=self.modT[:, 1, k, r:r + 1]),
                                     reads=[ps, self.modT], writes=[h])
                    for ti in range(nt):
                        ltile = (t0 // 128) + ti
                        for blk in range(4 if full else 2):
                            ps = self.bank("amm", [2, 3, 4, 5])
                            for k in range(8):
                                P.op("tensor", MM(ps[:, 0:512], h[:, k, ti * 128:(ti + 1) * 128], W[:, k, blk * 512:(blk + 1) * 512], k == 0, k == 7),
                                     reads=[h, W], writes=[ps], inc=(k == 7))
                            if blk == 0:
                                kn = qn[cnt["q"] % 2]
                                normrope(ps, 0, 2, gk, seg == "lat", ltile, kn)
                                pt = self.bank("atq", [6, 7])
                                ptb = pt[:].bitcast(BF16).rearrange("p (k t) -> p k t", k=8)
                                for hh in range(2):
                                    P.op("tensor", TR(ptb[:, hh, :], kn[:, hh, :], C["identb"][:]), reads=[kn, C["identb"]], writes=[pt], inc=(hh == 1))
                                P.op("vector", CP(kTs[:, :, ti * 128:(ti + 1) * 128], ptb[:, 0:2, :]), reads=[pt], writes=[kTs])
                                v_ = vst[cnt["v"] % 2]
                                cnt["v"] += 1
                                P.op("scalar", ACT(v_[:], ps[:, 256:512], AF.Copy), reads=[ps], writes=[v_])
                                P.dma("gpsimd", S["V"].t[b, g0 + ti * 128:g0 + (ti + 1) * 128, :], v_[:], reads=[v_], writes=[S["V"]])
                            elif blk == 1:
                                u_ = ust[cnt["u"] % 2]
                                cnt["u"] += 1
                                P.op("scalar", ACT(u_[:], ps[:, 0:512], AF.Copy), reads=[ps], writes=[u_])
                                P.dma("gpsimd", S["U"].t[b, g0 + ti * 128:g0 + (ti + 1) * 128, :], u_[:], reads=[u_], writes=[S["U"]])
                            else:
                                q_ = qn[cnt["q"] % 2]
                                normrope(ps, 0, 4, gq, seg == "lat", ltile, q_)
                                pt = self.bank("atq", [6, 7])
                                ptb = pt[:].bitcast(BF16).rearrange("p (k t) -> p k t", k=8)
                                for hh in range(4):
                                    P.op("tensor", TR(ptb[:, hh, :], q_[:, hh, :], C["identb"][:]), reads=[q_, C["identb"]], writes=[pt], inc=(hh == 3))
                                h0 = (blk - 2) * 4
                                P.op("vector", CP(qTs[:, h0:h0 + 4, ti * 128:(ti + 1) * 128], ptb[:, 0:4, :]), reads=[pt], writes=[qTs])
                    for kv in range(2):
                        P.dma("gpsimd", S["KT"].t[b, kv, :, g0:g0 + ntok], kTs[:, kv, 0:ntok], reads=[kTs], writes=[S["KT"]])
                    if full:
                        for hh in range(8):
                            P.dma("gpsimd", S["QT"].t[b, hh, :, g0:g0 + ntok], qTs[:, hh, 0:ntok], reads=[qTs], writes=[S["QT"]])
                        for ct in range(36):
                            ps = self.bank("amm", [2, 3, 4, 5])
                            c0 = 2048 + ct * 128
                            for k in range(8):
                                P.op("tensor", MM(ps[:, 0:ntok], W[:, k, c0:c0 + 128], h[:, k, 0:ntok], k == 0, k == 7), reads=[h, W], writes=[ps],
                                     inc=(k == 7))
                            f_ = fst[cnt["f"] % 3]
                            cnt["f"] += 1
                            if ct < 12:
                                dst = S[["AXT", "BGT", "CGT"][ct // 4]]
                                row0 = (ct % 4) * 128
                                if ct % 2 == 0:
                                    P.op("vector", CP(f_[:, 0:ntok], ps[:, 0:ntok]), reads=[ps], writes=[f_])
                                else:
                                    P.op("scalar", ACT(f_[:, 0:ntok], ps[:, 0:ntok], AF.Copy), reads=[ps], writes=[f_])
                            else:
                                dst = S["GT"]
                                row0 = (ct - 12) * 128
                                P.op("scalar", ACT(f_[:, 0:ntok], ps[:, 0:ntok], AF.Sigmoid), reads=[ps], writes=[f_])
                            P.dma("gpsimd", dst.t[b, row0:row0 + 128, g0:g0 + ntok], f_[:, 0:ntok], reads=[f_], writes=[dst])
        P.pop()


    def phaseB(self):
        P, I, S, C, l = self.P, self.I, self.S, self.C, self.l
        P.push()
        KTs = P.sb("KTs", [128, 2, TT], BF16)
        Vs = P.sb("Vs", [128, 18, 256], BF16)
        qT = [P.sb("qTb%d" % i, [128, 512], BF16) for i in range(2)]
        pT = [P.sb("pT%d" % i, [128, 512], BF16) for i in range(4)]
        rec = [P.sb("rec%d" % i, [128, 512], F32) for i in range(2)]
        oT = [P.sb("oT%d" % i, [128, 512], BF16) for i in range(2)]
        n = 0
        for b in range(NBC):
            for kv in range(2):
                P.dma("sync", KTs[:, kv, :], S["KT"].t[b, kv], reads=[S["KT"]], writes=[KTs])
            P.dma("sync", Vs[:], S["V"].t[b].rearrange("(t p) c -> p t c", p=128), reads=[S["V"]], writes=[Vs])
            blocks = [("lat", qb) for qb in range(4)] + ([] if self.last else [("ctx", 0)])
            for h in range(8):
                kv = h // 4
                for seg, qb in blocks:
                    if seg == "lat":
                        q0, nq, kts = TC + qb * 512, 512, list(range(18))
                    else:
                        q0, nq, kts = 0, 256, [0, 1]
                    q_ = qT[n % 2]
                    P.dma("sync", q_[:, 0:nq], S["QT"].t[b, h, :, q0:q0 + nq], reads=[S["QT"]], writes=[q_])
                    po = self.bank("bo", [4, 5])
                    pz = self.bank("bs", [6, 7])
                    for j, kt in enumerate(kts):
                        ps = self.bank("bqk", [0, 1, 2, 3])
                        P.op("tensor", MM(ps[:, 0:nq], KTs[:, kv, kt * 128:(kt + 1) * 128], q_[:, 0:nq], True, True), reads=[KTs, q_], writes=[ps])
                        p_ = pT[j % 4]
                        P.op("scalar", ACT(p_[:, 0:nq], ps[:, 0:nq], AF.Exp, scale=ATTN_SCALE), reads=[ps], writes=[p_])
                        last = (j == len(kts) - 1)
                        P.op("tensor", MM(po[:, 0:nq], Vs[:, kt, kv * 128:(kv + 1) * 128], p_[:, 0:nq], j == 0, last), reads=[Vs, p_], writes=[po], inc=last)
                        P.op("tensor", MM(pz[:, 0:nq], C["onesb"][:], p_[:, 0:nq], j == 0, last), reads=[C["onesb"], p_], writes=[pz], inc=last)
                    r_, o_ = rec[n % 2], oT[n % 2]
                    P.op("vector", lambda e, r_=r_, pz=pz, nq=nq: e.reciprocal(out=r_[:, 0:nq], in_=pz[:, 0:nq]), reads=[pz], writes=[r_])
                    P.op("vector", TTo(o_[:, 0:nq], po[:, 0:nq], r_[:, 0:nq], ALU.mult), reads=[po, r_], writes=[o_])
                    P.dma("gpsimd", S["AOT"].t[b, h * 128:(h + 1) * 128, q0:q0 + nq], o_[:, 0:nq], reads=[o_], writes=[S["AOT"]])
                    n += 1
        P.pop()


    def phaseC(self):
        P = self.P
        P.push()
        Bst = P.sb("Bst", [128, 32, 2, 128], BF16)
        CL = P.sb("CL", [128, 32, 2, 128], BF16)
        Toep = P.sb("Toep", [128, 32, 128], BF16)
        A8 = P.sb("A8", [128, 32], F32)
        B8 = P.sb("B8", [128, 32], F32)
        self.ssm_consts(Bst, CL, Toep, A8, B8)
        self.check_stop("pC0_%d" % self.l)
        for b in range(NBC):
            self.ssm_batch(b, Bst, CL, Toep, A8, B8)
            self.check_stop("c3a")
        P.pop()

    def ssm_consts(self, Bst, CL, Toep, A8, B8):
        P, I, S, C, l = self.P, self.I, self.S, self.C, self.l
        P.push()
        V_ = "vector"

        def T(name, shape=(128, 32), dt=F32):
            return P.sb(name, list(shape), dt)

        lre, lim, ldt = T("lre"), T("lim"), T("ldt")
        for d in range(2):
            hs = slice(64 * d, 64 * d + 64)
            P.dma("sync", lre[hs, :], I["ssm_lam_re"].t[l, d].rearrange("g p -> p g"), reads=[I["ssm_lam_re"]], writes=[lre], allow_slow_non_contiguous=True)
            P.dma("sync", lim[hs, :], I["ssm_lam_im"].t[l, d].rearrange("g p -> p g"), reads=[I["ssm_lam_im"]], writes=[lim], allow_slow_non_contiguous=True)
            P.dma("sync", ldt[hs, :], I["ssm_log_dt"].t[l, d, :].partition_broadcast(64), reads=[I["ssm_log_dt"]], writes=[ldt])
        dt, tmp, mag, th, kf, kf2, r, m, sn, cs, r2 = [T(n) for n in ["dt", "tmp", "mag", "th", "kf", "kf2", "r", "m", "sn", "cs", "r2"]]
        ki = T("ki", dt=I32)
        P.op("scalar", ACT(dt[:], ldt[:], AF.Exp), reads=[ldt], writes=[dt])
        P.op(V_, TTo(tmp[:], lre[:], dt[:], ALU.mult), reads=[lre, dt], writes=[tmp])
        P.op("scalar", ACT(mag[:], tmp[:], AF.Exp), reads=[tmp], writes=[mag])
        P.op(V_, TTo(th[:], lim[:], dt[:], ALU.mult), reads=[lim, dt], writes=[th])
        P.op(V_, TS(kf[:], th[:], 1.0 / TWO_PI, None, ALU.mult), reads=[th], writes=[kf])
        P.op(V_, CP(ki[:], kf[:]), reads=[kf], writes=[ki])
        P.op(V_, CP(kf2[:], ki[:]), reads=[ki], writes=[kf2])
        P.op(V_, STT(r[:], kf2[:], -TWO_PI, th[:], ALU.mult, ALU.add), reads=[kf2, th], writes=[r])

        def wrap(x):
            P.op(V_, TS(m[:], x[:], PI, TWO_PI, ALU.is_gt, ALU.mult), reads=[x], writes=[m])
            P.op(V_, TTo(x[:], x[:], m[:], ALU.subtract), reads=[x, m], writes=[x])
            P.op(V_, TS(m[:], x[:], -PI, TWO_PI, ALU.is_lt, ALU.mult), reads=[x], writes=[m])
            P.op(V_, TTo(x[:], x[:], m[:], ALU.add), reads=[x, m], writes=[x])

        wrap(r)
        P.op("scalar", ACT(sn[:], r[:], AF.Sin), reads=[r], writes=[sn])
        P.op(V_, TS(r2[:], r[:], PI / 2, None, ALU.add), reads=[r], writes=[r2])
        wrap(r2)
        P.op("scalar", ACT(cs[:], r2[:], AF.Sin), reads=[r2], writes=[cs])
        lbr, lbi = T("lbr"), T("lbi")
        P.op(V_, TTo(lbr[:], mag[:], cs[:], ALU.mult), reads=[mag, cs], writes=[lbr])
        P.op(V_, TTo(lbi[:], mag[:], sn[:], ALU.mult), reads=[mag, sn], writes=[lbi])
        nr, den, t1, t2, fr, fi = [T(n) for n in ["nr", "den", "t1", "t2", "fr", "fi"]]
        P.op(V_, TS(nr[:], lbr[:], -1.0, None, ALU.add), reads=[lbr], writes=[nr])
        P.op(V_, TTo(t1[:], lre[:], lre[:], ALU.mult), reads=[lre], writes=[t1])
        P.op(V_, TTo(t2[:], lim[:], lim[:], ALU.mult), reads=[lim], writes=[t2])
        P.op(V_, TTo(den[:], t1[:], t2[:], ALU.add), reads=[t1, t2], writes=[den])
        P.op(V_, lambda e: e.reciprocal(out=den[:], in_=den[:]), reads=[den], writes=[den])
        P.op(V_, TTo(t1[:], nr[:], lre[:], ALU.mult), reads=[nr, lre], writes=[t1])
        P.op(V_, TTo(t2[:], lbi[:], lim[:], ALU.mult), reads=[lbi, lim], writes=[t2])
        P.op(V_, TTo(fr[:], t1[:], t2[:], ALU.add), reads=[t1, t2], writes=[fr])
        P.op(V_, TTo(fr[:], fr[:], den[:], ALU.mult), reads=[fr, den], writes=[fr])
        P.op(V_, TTo(t1[:], lbi[:], lre[:], ALU.mult), reads=[lbi, lre], writes=[t1])
        P.op(V_, TTo(t2[:], nr[:], lim[:], ALU.mult), reads=[nr, lim], writes=[t2])
        P.op(V_, TTo(fi[:], t1[:], t2[:], ALU.subtract), reads=[t1, t2], writes=[fi])
        P.op(V_, TTo(fi[:], fi[:], den[:], ALU.mult), reads=[fi, den], writes=[fi])
        ir, ii = T("ir"), T("ii")
        P.op(V_, TTo(t1[:], lbr[:], lbr[:], ALU.mult), reads=[lbr], writes=[t1])
        P.op(V_, TTo(t2[:], lbi[:], lbi[:], ALU.mult), reads=[lbi], writes=[t2])
        P.op(V_, TTo(den[:], t1[:], t2[:], ALU.add), reads=[t1, t2], writes=[den])
        P.op(V_, lambda e: e.reciprocal(out=den[:], in_=den[:]), reads=[den], writes=[den])
        P.op(V_, TTo(ir[:], lbr[:], den[:], ALU.mult), reads=[lbr, den], writes=[ir])
        P.op(V_, STT(ii[:], lbi[:], -1.0, den[:], ALU.mult, ALU.mult), reads=[lbi, den], writes=[ii])
        self.check_stop("c0a")
        LPr, LPi = T("LPr", (128, 9, 32)), T("LPi", (128, 9, 32))
        LNr, LNi = T("LNr", (128, 8, 32)), T("LNi", (128, 8, 32))
        for (Xr, Xi, br, bi, n) in [(LPr, LPi, lbr, lbi, 9), (LNr, LNi, ir, ii, 8)]:
            P.op(V_, lambda e, Xr=Xr: e.memset(Xr[:, 0, :], 1.0), writes=[Xr])
            P.op(V_, lambda e, Xi=Xi: e.memset(Xi[:, 0, :], 0.0), writes=[Xi])
            for k in range(1, n):
                P.op(V_, TTo(t1[:], Xr[:, k - 1, :], br[:], ALU.mult), reads=[Xr, br], writes=[t1])
                P.op(V_, TTo(t2[:], Xi[:, k - 1, :], bi[:], ALU.mult), reads=[Xi, bi], writes=[t2])
                P.op(V_, TTo(Xr[:, k, :], t1[:], t2[:], ALU.subtract), reads=[t1, t2], writes=[Xr])
                P.op(V_, TTo(t1[:], Xr[:, k - 1, :], bi[:], ALU.mult), reads=[Xr, bi], writes=[t1])
                P.op(V_, TTo(t2[:], Xi[:, k - 1, :], br[:], ALU.mult), reads=[Xi, br], writes=[t2])
                P.op(V_, TTo(Xi[:, k, :], t1[:], t2[:], ALU.add), reads=[t1, t2], writes=[Xi])
        P.op(V_, CP(A8[:], LPr[:, 8, :]), reads=[LPr], writes=[A8])
        P.op(V_, CP(B8[:], LPi[:, 8, :]), reads=[LPi], writes=[B8])
        G3 = (128, 32, 16)
        Bre, Bim, Bbr, Bbi, CTr, CTi = [T(n, G3) for n in ["Bre", "Bim", "Bbr", "Bbi", "CTr", "CTi"]]
        for d in range(2):
            hs = slice(64 * d, 64 * d + 64)
            P.dma("sync", Bre[hs], I["ssm_b_re"].t[l, d].rearrange("g p h -> p g h"), reads=[I["ssm_b_re"]], writes=[Bre], allow_slow_non_contiguous=True)
            P.dma("sync", Bim[hs], I["ssm_b_im"].t[l, d].rearrange("g p h -> p g h"), reads=[I["ssm_b_im"]], writes=[Bim], allow_slow_non_contiguous=True)
        craw = [T("craw%d" % i, (128, 2, 64)) for i in range(2)]
        n = 0
        for (src, dst) in [("ssm_c_re", CTr), ("ssm_c_im", CTi)]:
            for gb in range(4):
                cr_ = craw[n % 2]
                n += 1
                for d in range(2):
                    P.dma("sync", cr_[:, d, :], I[src].t[l, d, gb * 8:(gb + 1) * 8].rearrange("g h p -> (g h) p"), reads=[I[src]], writes=[cr_])
                ps = self.bank("c0", [0, 1])
                P.op("tensor", TR(ps[:, 0:128], cr_[:].rearrange("q d p -> q (d p)"), C["identf"][:]), reads=[cr_, C["identf"]], writes=[ps])
                P.op(V_, CP(dst[:, gb * 8:(gb + 1) * 8, :].rearrange("q g h -> q (g h)"), ps[:, 0:128]), reads=[ps], writes=[dst])

        self.check_stop("c0b")

        def bc(x_ap):
            return x_ap.unsqueeze(2).to_broadcast([128, 32, 16])

        u1, u2, u3, u4 = [T(n, G3) for n in ["u1", "u2", "u3", "u4"]]
        Rr = [T("Rr%d" % i, G3) for i in range(2)]
        Ri = [T("Ri%d" % i, G3) for i in range(2)]
        fcnt = [0]

        def cprod(lr, li, Xr, Xi, lbufs, neg_im=False):
            i = fcnt[0]
            fcnt[0] += 1
            rr, ri = Rr[i % 2], Ri[i % 2]
            P.op(V_, TTo(u1[:], Xr[:], bc(lr), ALU.mult), reads=[Xr] + lbufs, writes=[u1])
            P.op("gpsimd", TTo(u2[:], Xi[:], bc(li), ALU.mult), reads=[Xi] + lbufs, writes=[u2])
            P.op(V_, TTo(u3[:], Xi[:], bc(lr), ALU.mult), reads=[Xi] + lbufs, writes=[u3])
            P.op("gpsimd", TTo(u4[:], Xr[:], bc(li), ALU.mult), reads=[Xr] + lbufs, writes=[u4])
            P.op(V_, TTo(rr[:], u1[:], u2[:], ALU.subtract), reads=[u1, u2], writes=[rr])
            if neg_im:
                P.op("gpsimd", STT_pool(ri[:], u3[:], u4[:]), reads=[u3, u4], writes=[ri])
            else:
                P.op("gpsimd", TTo(ri[:], u3[:], u4[:], ALU.add), reads=[u3, u4], writes=[ri])
            return rr, ri

        def STT_pool(out, a, b_):
            def f(e):
                e.tensor_scalar(out=a, in0=a, scalar1=-1.0, scalar2=None, op0=ALU.mult)
                return e.tensor_tensor(out=out, in0=a, in1=b_, op=ALU.subtract)
            return f

        rr, ri = cprod(fr[:], fi[:], Bre, Bim, [fr, fi])
        P.op(V_, CP(Bbr[:], rr[:]), reads=[rr], writes=[Bbr])
        P.op(V_, CP(Bbi[:], ri[:]), reads=[ri], writes=[Bbi])

        G4 = (128, 32, 8, 16)
        BSr, BSi, Xr_, Xi_, PCr, PCi = [T(n, G4) for n in ["BSr", "BSi", "Xr_", "Xi_", "PCr", "PCi"]]
        F_, R_ = slice(0, 64), slice(64, 128)
        for Pc in (PCr, PCi):
            P.op("gpsimd", lambda e, Pc=Pc: e.memset(Pc[R_], 0.0), writes=[Pc])
        CLv = CL[:].rearrange("q g r (i h) -> q g r i h", i=8)
        cpe = ["gpsimd", "scalar"]
        cc = [0]

        def cpy(dst_ap, dst_buf, src_ap, src_buf):
            e = cpe[cc[0] % 2]
            cc[0] += 1
            if e == "scalar":
                P.op(e, ACT(dst_ap, src_ap, AF.Copy), reads=[src_buf], writes=[dst_buf])
            else:
                P.op(e, CP(dst_ap, src_ap), reads=[src_buf], writes=[dst_buf])

        for k in range(8):
            rr, ri = cprod(LPr[:, k, :], LPi[:, k, :], Bbr, Bbi, [LPr, LPi])
            cpy(BSr[F_, :, 7 - k, :], BSr, rr[F_], rr)
            cpy(BSi[F_, :, 7 - k, :], BSi, ri[F_], ri)
            cpy(BSr[R_, :, k, :], BSr, rr[R_], rr)
            cpy(BSi[R_, :, k, :], BSi, ri[R_], ri)
        for k in range(8):
            rr, ri = cprod(LNr[:, k, :], LNi[:, k, :], Bbr, Bbi, [LNr, LNi])
            cpy(Xr_[F_, :, k, :], Xr_, rr[F_], rr)
            cpy(Xi_[F_, :, k, :], Xi_, ri[F_], ri)
            rr, ri = cprod(LNr[:, k, :], LNi[:, k, :], CTr, CTi, [LNr, LNi], neg_im=True)
            cpy(Xr_[R_, :, k, :], Xr_, rr[R_], rr)
            cpy(Xi_[R_, :, k, :], Xi_, ri[R_], ri)
        for k in range(9):
            rr, ri = cprod(LPr[:, k, :], LPi[:, k, :], CTr, CTi, [LPr, LPi], neg_im=True)
            if k <= 7:
                cpy(PCr[F_, :, k, :], PCr, rr[F_], rr)
                cpy(PCi[F_, :, k, :], PCi, ri[F_], ri)
            if k >= 1:
                cpy(CLv[F_, :, 0, k - 1, :], CL, rr[F_], rr)
                cpy(CLv[F_, :, 1, k - 1, :], CL, ri[F_], ri)
                cpy(CLv[R_, :, 0, 8 - k, :], CL, rr[R_], rr)
                cpy(CLv[R_, :, 1, 8 - k, :], CL, ri[R_], ri)
        self.check_stop("c0c")
        for g2 in range(16):
            ps = self.bank("c0", [0, 1])
            for gg in range(2):
                g = g2 * 2 + gg
                for ri_, Bs in enumerate((BSr, BSi)):
                    sl = (gg * 2 + ri_) * 128
                    P.op("tensor", TR(ps[:, sl:sl + 128], Bs[:, g].rearrange("q j h -> q (j h)"), C["identf"][:]), reads=[Bs, C["identf"]], writes=[ps],
                         inc=(gg == 1 and ri_ == 1))
            P.op(V_ if g2 % 2 else "scalar", (CP if g2 % 2 else (lambda o, i_: ACT(o, i_, AF.Copy)))(
                Bst[:, g2 * 2:g2 * 2 + 2].rearrange("q g r c -> q (g r c)"), ps[:, 0:512]), reads=[ps], writes=[Bst])
        self.check_stop("c0d")
        for Bs in (BSr, BSi):
            P.op("gpsimd", lambda e, Bs=Bs: e.memset(Bs[F_], 0.0), reads=[Bs], writes=[Bs])
        maskf, maskr, Dcol = T("maskf", (128, 128)), T("maskr", (128, 128)), T("Dcol", (128, 32))
        P.dma("sync", maskf[:], I["k_maskf"].t, reads=[I["k_maskf"]], writes=[maskf])
        P.dma("sync", maskr[:], I["k_maskr"].t, reads=[I["k_maskr"]], writes=[maskr])
        for j in range(8):
            P.dma("sync", Dcol[j * 16:(j + 1) * 16, :], I["ssm_d"].t[l, :].rearrange("(g h) -> h g", h=16), reads=[I["ssm_d"]], writes=[Dcol],
                  allow_slow_non_contiguous=True)
        tf = [T("tf%d" % i, (128, 128)) for i in range(2)]
        tr = [T("tr%d" % i, (128, 128)) for i in range(2)]
        for g in range(32):
            pf = self.bank("c0", [0, 1])
            pr = self.bank("c1", [2, 3])
            fl = lambda X: X[:, g].rearrange("q j h -> q (j h)")
            P.op("tensor", MM(pf[:, 0:128], fl(Xr_), fl(PCr), True, False), reads=[Xr_, PCr], writes=[pf], inc=False)
            P.op("tensor", MM(pf[:, 0:128], fl(Xi_), fl(PCi), False, True), reads=[Xi_, PCi], writes=[pf])
            P.op("tensor", MM(pr[:, 0:128], fl(BSr), fl(Xr_), True, False), reads=[BSr, Xr_], writes=[pr], inc=False)
            P.op("tensor", MM(pr[:, 0:128], fl(BSi), fl(Xi_), False, True), reads=[BSi, Xi_], writes=[pr])
            a_, b_ = tf[g % 2], tr[g % 2]
            P.op(V_, TTo(a_[:], pf[:, 0:128], maskf[:], ALU.mult), reads=[pf, maskf], writes=[a_])
            P.op(V_, TTo(b_[:], pr[:, 0:128], maskr[:], ALU.mult), reads=[pr, maskr], writes=[b_])
            P.op("gpsimd", TTo(a_[:], a_[:], b_[:], ALU.add), reads=[a_, b_], writes=[a_])
            P.op(V_, STT(Toep[:, g, :], C["identf"][:], Dcol[:, g:g + 1], a_[:], ALU.mult, ALU.add), reads=[C["identf"], Dcol, a_], writes=[Toep])
        P.pop()

    def ssm_batch(self, b, Bst, CL, Toep, A8, B8):
        P, I, S, C, l = self.P, self.I, self.S, self.C, self.l
        P.push()
        Ust = P.sb("Ust", [128, 32, NCH], BF16)
        ZSr = P.sb("ZSr", [128, 32, NCH], BF16)
        ZSi = P.sb("ZSi", [128, 32, NCH], BF16)
        Sr = P.sb("Sr", [128, 32, 290], BF16)
        Si = P.sb("Si", [128, 32, 290], BF16)
        F_, R_ = slice(0, 64), slice(64, 128)
        P.push()
        utm = [P.sb("utm%d" % i, [96, 4096], BF16) for i in range(2)]
        utm2 = [P.sb("utm2%d" % i, [96, 4096], BF16) for i in range(2)]
        for tile in range(3):
            u_ = utm[tile % 2]
            P.dma("sync", u_[0:96, :], S["U"].t[b, tile * 768:(tile + 1) * 768, :].rearrange("(c j) ch -> c (j ch)", j=8), reads=[S["U"]], writes=[u_])
            u2 = utm2[tile % 2]
            P.op("gpsimd", CP(u2[0:96, :].rearrange("c (g j h) -> c g j h", g=32, j=8), u_[0:96, :].rearrange("c (j g h) -> c g j h", j=8, g=32)),
                 reads=[u_], writes=[u2])
            uv = u2[0:96, :].rearrange("c (g x) -> c g x", g=32)
            for g8 in range(4):
                ps = self.bank("cs", [4, 5])
                pb = ps[:].bitcast(BF16).rearrange("p (k t) -> p k t", k=8)
                for gg in range(8):
                    P.op("tensor", TR(pb[:, gg, 0:96], uv[:, g8 * 8 + gg, :], C["identb"][0:96, 0:96]), reads=[u2, C["identb"]], writes=[ps], inc=(gg == 7))
                P.op("vector" if g8 % 2 else "scalar",
                     (CP if g8 % 2 else (lambda o, i_: ACT(o, i_, AF.Copy)))(Ust[:, g8 * 8:(g8 + 1) * 8, tile * 96:(tile + 1) * 96], pb[:, :, 0:96]),
                     reads=[ps], writes=[Ust])
        P.pop()
        self.check_stop("c1a")
        for g in range(32):
            for ri_, Z in enumerate((ZSr, ZSi)):
                ps = self.bank("cz", [0, 1, 2, 3])
                P.op("tensor", MM(ps[:, 0:NCH], Bst[:, g, ri_, :], Ust[:, g, :], True, True), reads=[Bst, Ust], writes=[ps])
                if ri_ == 0:
                    P.op("vector", CP(Z[:, g, :], ps[:, 0:NCH]), reads=[ps], writes=[Z])
                else:
                    P.op("scalar", ACT(Z[:, g, :], ps[:, 0:NCH], AF.Copy), reads=[ps], writes=[Z])
        self.check_stop("c1b")
        NR = 4
        Rg = [P.sb("Rg%d" % i, [128, 32], F32) for i in range(NR)]
        Ig = [P.sb("Ig%d" % i, [128, 32], F32) for i in range(NR)]
        tt = [[P.sb("sc%d_%d" % (i, j), [128, 32], F32) for j in range(6)] for i in range(2)]
        P.op("vector", lambda e: e.memset(Rg[0][:], 0.0), writes=[Rg[0]])
        P.op("vector", lambda e: e.memset(Ig[0][:], 0.0), writes=[Ig[0]])
        for Sx in (Sr, Si):
            P.op("gpsimd", lambda e, Sx=Sx: e.memset(Sx[F_, :, 1:2], 0.0), writes=[Sx])
            P.op("gpsimd", lambda e, Sx=Sx: e.memset(Sx[R_, :, 32:33], 0.0), writes=[Sx])
        order_r = list(range(31, -1, -1)) + list(range(287, 31, -1))
        V_ = "vector"
        for n in range(NCH):
            cf, cr = n, order_r[n]
            Rp, Ip, Rn, In = Rg[n % NR], Ig[n % NR], Rg[(n + 1) % NR], Ig[(n + 1) % NR]
            t1, t2, t3, t4, t5, t6 = tt[n % 2]
            P.op(V_, TTo(t1[:], A8[:], Rp[:], ALU.mult), reads=[A8, Rp], writes=[t1])
            P.op(V_, TTo(t2[:], B8[:], Ip[:], ALU.mult), reads=[B8, Ip], writes=[t2])
            P.op(V_, TTo(t4[:], B8[:], Rp[:], ALU.mult), reads=[B8, Rp], writes=[t4])
            P.op(V_, TTo(t5[:], A8[:], Ip[:], ALU.mult), reads=[A8, Ip], writes=[t5])
            P.op(V_, TTo(t3[:], t1[:], t2[:], ALU.subtract), reads=[t1, t2], writes=[t3])
            P.op(V_, TTo(t6[:], t4[:], t5[:], ALU.add), reads=[t4, t5], writes=[t6])
            P.op(V_, TTo(Rn[F_], t3[F_], ZSr[F_, :, cf], ALU.add), reads=[t3, ZSr], writes=[Rn])
            P.op(V_, TTo(Rn[R_], t3[R_], ZSr[R_, :, cr], ALU.add), reads=[t3, ZSr], writes=[Rn])
            P.op(V_, TTo(In[F_], t6[F_], ZSi[F_, :, cf], ALU.add), reads=[t6, ZSi], writes=[In])
            P.op(V_, TTo(In[R_], t6[R_], ZSi[R_, :, cr], ALU.add), reads=[t6, ZSi], writes=[In])
            sf = cf + 2
            for Sx, Xn in ((Sr, Rn), (Si, In)):
                P.op("gpsimd", CP(Sx[F_, :, sf], Xn[F_]), reads=[Xn], writes=[Sx])
                if cr != 32:
                    P.op("gpsimd", CP(Sx[R_, :, cr], Xn[R_]), reads=[Xn], writes=[Sx])
                if cr == 0:
                    P.op("gpsimd", CP(Sx[R_, :, 288], Xn[R_]), reads=[Xn], writes=[Sx])
        self.check_stop("c1c")
        P.push()
        Wg = P.sb("Wg", [128, 4, 2 * D], BF16)
        P.dma("sync", Wg[:], S["wb_glu%d" % l].t.rearrange("(t p) n -> p t n", p=128), reads=[S["wb_glu%d" % l]], writes=[Wg])
        ygl = P.sb("ygl", [128, 8, 512], BF16)
        yT = P.sb("yT", [128, 4, 1024], BF16)
        yTv = yT[:].rearrange("p t (c i) -> p t c i", i=8)
        sig = [P.sb("sig%d" % i, [128, 512], F32) for i in range(2)]
        gst = [P.sb("gst%d" % i, [128, 1024], BF16) for i in range(2)]
        ydb = P.sb("ydb", [128, 8, 512], F32) if "YS" in self.dbg else None
        blocks = ([] if self.last else [(0, 32, 1)]) + [(32, 128, 2), (160, 128, 2)]
        nsg = 0
        for (c0, M, roff) in blocks:
            for g4 in range(8):
                ps = self.bank("cr", [0, 1, 2, 3])
                for gg in range(4):
                    g = g4 * 4 + gg
                    o = ps[0:M, gg * 128:(gg + 1) * 128]
                    P.op("tensor", MM(o, Ust[:, g, c0:c0 + M], Toep[:, g, :], True, False), reads=[Ust, Toep], writes=[ps], inc=False)
                    P.op("tensor", MM(o, Sr[:, g, c0 + 1:c0 + 1 + M], CL[:, g, 0, :], False, False), reads=[Sr, CL], writes=[ps], inc=False)
                    P.op("tensor", MM(o, Si[:, g, c0 + 1:c0 + 1 + M], CL[:, g, 1, :], False, True), reads=[Si, CL], writes=[ps], inc=(gg == 3))
                pv = ps[0:M, :].rearrange("c (g i h) -> c i g h", g=4, i=8)
                yo = ygl[0:M, :, g4 * 64:(g4 + 1) * 64].rearrange("c i (g h) -> c i g h", g=4)
                P.op("scalar", ACT(yo, pv, AF.Gelu_apprx_tanh), reads=[ps], writes=[ygl])
                if ydb is not None:
                    P.op("vector", CP(ydb[0:M, :, g4 * 64:(g4 + 1) * 64].rearrange("c i (g h) -> c i g h", g=4), pv), reads=[ps], writes=[ydb])
            self.check_stop("c2a")
            if ydb is not None:
                P.dma("gpsimd", S["YS"].t[b, c0 * 8:(c0 + M) * 8, :].rearrange("(c i) ch -> c i ch", i=8), ydb[0:M], reads=[ydb], writes=[S["YS"]])
            self.check_stop("c2b")
            for i in range(8):
                ps = self.bank("cs", [4, 5])
                pb = ps[:].bitcast(BF16).rearrange("p (k t) -> p k t", k=8)
                for t in range(4):
                    P.op("tensor", TR(pb[:, t, 0:M], ygl[0:M, i, t * 128:(t + 1) * 128], C["identb"][0:M, 0:M]), reads=[ygl, C["identb"]], writes=[ps], inc=(t == 3))
                P.op("vector", CP(yTv[:, :, 0:M, i], pb[:, 0:4, 0:M]), reads=[ps], writes=[yT])
            self.check_stop("c2c")
            ntok = M * 8
            for ct in range(8):
                g_ = gst[ct % 2]
                for n0 in range(0, ntok, 512):
                    nn = min(512, ntok - n0)
                    pa = self.bank("cg", [6, 7, 0, 1, 2, 3])
                    pg = self.bank("cg", [6, 7, 0, 1, 2, 3])
                    for t in range(4):
                        P.op("tensor", MM(pa[:, 0:nn], Wg[:, t, ct * 128:(ct + 1) * 128], yT[:, t, n0:n0 + nn], t == 0, t == 3), reads=[Wg, yT], writes=[pa], inc=(t == 3))
                    for t in range(4):
                        P.op("tensor", MM(pg[:, 0:nn], Wg[:, t, D + ct * 128:D + (ct + 1) * 128], yT[:, t, n0:n0 + nn], t == 0, t == 3), reads=[Wg, yT], writes=[pg], inc=(t == 3))
                    sg = sig[nsg % 2]
                    nsg += 1
                    P.op("scalar", ACT(sg[:, 0:nn], pg[:, 0:nn], AF.Sigmoid), reads=[pg], writes=[sg])
                    P.op("vector", TTo(g_[:, n0:n0 + nn], pa[:, 0:nn], sg[:, 0:nn], ALU.mult), reads=[pa, sg], writes=[g_])
                P.dma("gpsimd", S["SSMT"].t[b, ct * 128:(ct + 1) * 128, c0 * 8:c0 * 8 + ntok], g_[:, 0:ntok], reads=[g_], writes=[S["SSMT"]])
            self.check_stop("c2d")
            if c0 == 32:
                self.check_stop("c2e")
        P.pop()
        P.pop()


    def postnorm(self, pss, xt, gbc, lng, lnb, upd, yv, yn, sm, dst_buf, dst_ap):
        P = self.P
        for half in range(2):
            P.op("vector", TTo(upd[:, half * 512:(half + 1) * 512], pss[half][:, 0:512], gbc[:, half * 512:(half + 1) * 512], ALU.mult),
                 reads=[pss[half], gbc], writes=[upd])
        P.op("vector", STT(yv[:], xt[:], ALPHA, upd[:], ALU.mult, ALU.add), reads=[xt, upd], writes=[yv])
        self.ln_tile(yv, yn[:], yn, sm)
        P.op("gpsimd", TTo(yn[:], yn[:], lng[:], ALU.mult), reads=[yn, lng], writes=[yn])
        P.op("gpsimd", TTo(yn[:], yn[:], lnb[:], ALU.add), reads=[yn, lnb], writes=[yn])
        P.dma("gpsimd", dst_ap, yn[:], reads=[yn], writes=[dst_buf])

    def segs(self):
        out = []
        for b in range(NBC):
            if not self.last:
                out.append((b, "ctx", 2, 0, TC))
            out.append((b, "lat", b, TC, TL))
        return out

    def phaseD(self):
        P, I, S, C, l = self.P, self.I, self.S, self.C, self.l
        P.push()
        Wco = P.sb("Wco", [128, 4, D], BF16)
        Wao = P.sb("Wao", [128, 8, D], BF16)
        Wo = P.sb("Wo", [128, 8, D], BF16)
        for Wt, nm in ((Wco, "wb_co"), (Wao, "wb_ao"), (Wo, "wb_o")):
            src = S[nm + str(l)]
            P.dma("sync", Wt[:], src.t.rearrange("(t p) n -> p t n", p=128), reads=[src], writes=[Wt])
        cw = P.sb("cw", [128, 4, 3], F32)
        for k in range(3):
            P.dma("sync", cw[:, :, k], I["conv_w"].t[l, k, :].rearrange("(t p) -> p t", p=128), reads=[I["conv_w"]], writes=[cw], allow_slow_non_contiguous=True)
        lng = P.sb("lng", [128, D], F32)
        lnb = P.sb("lnb", [128, D], F32)
        P.dma("sync", lng[:], I["ln1_g"].t[l, :].partition_broadcast(128), reads=[I["ln1_g"]], writes=[lng])
        P.dma("sync", lnb[:], I["ln1_b"].t[l, :].partition_broadcast(128), reads=[I["ln1_b"]], writes=[lnb])
        gbc = P.sb("gbc", [128, D], F32)
        axh = P.sb("axh", [128, 4, 514], BF16)
        cgh = P.sb("cgh", [128, 4, 514], BF16)
        bgs = P.sb("bgs", [128, 4, 512], BF16)
        prod = P.sb("prod", [128, 4, 514], F32)
        acc = P.sb("acc", [128, 4, 512], F32)
        convT = P.sb("convT", [128, 4, 512], BF16)
        aoT = P.sb("aoT", [128, 8, 512], BF16)
        gts = [P.sb("gts%d" % i, [128, 3, 512], BF16) for i in range(2)]
        ssmT = [P.sb("ssmT%d" % i, [128, 512], BF16) for i in range(2)]
        m1 = [P.sb("m1_%d" % i, [128, 512], F32) for i in range(2)]
        m2 = [P.sb("m2_%d" % i, [128, 512], F32) for i in range(2)]
        m3 = [P.sb("m3_%d" % i, [128, 512], F32) for i in range(2)]
        mgT = P.sb("mgT", [128, 8, 512], BF16)
        xt = [P.sb("dxt%d" % i, [128, D], F32) for i in range(2)]
        upd = [P.sb("dupd%d" % i, [128, D], F32) for i in range(2)]
        yv = [P.sb("dyv%d" % i, [128, D], F32) for i in range(2)]
        yn = [P.sb("dyn%d" % i, [128, D], F32) for i in range(2)]
        sm = self.ln_small("d")
        nx = 0
        nd = 0
        for (b, seg, r, gofs, slen) in self.segs():
            P.dma("sync", gbc[:], S["modr%d" % l].t[r, 2 * D:3 * D].partition_broadcast(128), reads=[S["modr%d" % l]], writes=[gbc])
            rbuf, rap = self.res_in(b, seg)
            n = min(512, slen)
            for t0 in range(0, slen, n):
                g0 = gofs + t0
                lo = 1 if t0 == 0 else 0
                hi = n + 1 if t0 + n == slen else n + 2
                for (hb, nm) in ((axh, "AXT"), (cgh, "CGT")):
                    if lo == 1:
                        P.op("gpsimd", lambda e, hb=hb: e.memset(hb[:, :, 0:1], 0.0), writes=[hb])
                    if hi == n + 1:
                        P.op("gpsimd", lambda e, hb=hb, n=n: e.memset(hb[:, :, n + 1:n + 2], 0.0), writes=[hb])
                    P.dma("sync", hb[:, :, lo:hi], S[nm].t[b].rearrange("(t p) c -> p t c", p=128)[:, :, g0 - 1 + lo:g0 - 1 + hi], reads=[S[nm]], writes=[hb])
                P.dma("sync", bgs[:, :, 0:n], S["BGT"].t[b].rearrange("(t p) c -> p t c", p=128)[:, :, g0:g0 + n], reads=[S["BGT"]], writes=[bgs])
                P.dma("sync", aoT[:, :, 0:n], S["AOT"].t[b].rearrange("(t p) c -> p t c", p=128)[:, :, g0:g0 + n], reads=[S["AOT"]], writes=[aoT])
                P.op("gpsimd", TTo(prod[:, :, 0:n + 2], cgh[:, :, 0:n + 2], axh[:, :, 0:n + 2], ALU.mult), reads=[cgh, axh], writes=[prod])
                for t in range(4):
                    P.op("vector", TS(acc[:, t, 0:n], prod[:, t, 0:n], cw[:, t, 0:1], None, ALU.mult), reads=[prod, cw], writes=[acc])
                    P.op("vector", STT(acc[:, t, 0:n], prod[:, t, 1:n + 1], cw[:, t, 1:2], acc[:, t, 0:n], ALU.mult, ALU.add), reads=[prod, cw, acc], writes=[acc])
                    P.op("vector", STT(acc[:, t, 0:n], prod[:, t, 2:n + 2], cw[:, t, 2:3], acc[:, t, 0:n], ALU.mult, ALU.add), reads=[prod, cw, acc], writes=[acc])
                P.op("gpsimd", TTo(convT[:, :, 0:n], acc[:, :, 0:n], bgs[:, :, 0:n], ALU.mult), reads=[acc, bgs], writes=[convT])
                for dt in range(8):
                    gt_, sm_ = gts[nd % 2], ssmT[nd % 2]
                    a1, a2, a3 = m1[nd % 2], m2[nd % 2], m3[nd % 2]
                    nd += 1
                    for s3 in range(3):
                        P.dma("sync", gt_[:, s3, 0:n], S["GT"].t[b, s3 * D + dt * 128:s3 * D + (dt + 1) * 128, g0:g0 + n], reads=[S["GT"]], writes=[gt_])
                    P.dma("sync", sm_[:, 0:n], S["SSMT"].t[b, dt * 128:(dt + 1) * 128, g0:g0 + n], reads=[S["SSMT"]], writes=[sm_])
                    pc = self.bank("dc", [0, 1])
                    pa = self.bank("da", [2, 3])
                    for t in range(4):
                        P.op("tensor", MM(pc[:, 0:n], Wco[:, t, dt * 128:(dt + 1) * 128], convT[:, t, 0:n], t == 0, t == 3), reads=[Wco, convT], writes=[pc], inc=(t == 3))
                    for k in range(8):
                        P.op("tensor", MM(pa[:, 0:n], Wao[:, k, dt * 128:(dt + 1) * 128], aoT[:, k, 0:n], k == 0, k == 7), reads=[Wao, aoT], writes=[pa], inc=(k == 7))
                    P.op("vector", TTo(a1[:, 0:n], pc[:, 0:n], gt_[:, 0, 0:n], ALU.mult), reads=[pc, gt_], writes=[a1])
                    P.op("vector", TTo(a2[:, 0:n], pa[:, 0:n], gt_[:, 2, 0:n], ALU.mult), reads=[pa, gt_], writes=[a2])
                    P.op("gpsimd", TTo(a3[:, 0:n], sm_[:, 0:n], gt_[:, 1, 0:n], ALU.mult), reads=[sm_, gt_], writes=[a3])
                    P.op("gpsimd", TTo(a1[:, 0:n], a1[:, 0:n], a2[:, 0:n], ALU.add), reads=[a1, a2], writes=[a1])
                    P.op("gpsimd", TTo(mgT[:, dt, 0:n], a1[:, 0:n], a3[:, 0:n], ALU.add), reads=[a1, a3], writes=[mgT])
                for ti in range(n // 128):
                    i = nx
                    nx += 1
                    x_ = xt[i % 2]
                    P.dma("sync", x_[:], rap[t0 + ti * 128:t0 + (ti + 1) * 128, :], reads=[rbuf], writes=[x_])
                    pss = []
                    for half in range(2):
                        ps = self.bank("do", [4, 5, 6, 7])
                        for k in range(8):
                            P.op("tensor", MM(ps[:, 0:512], mgT[:, k, ti * 128:(ti + 1) * 128], Wo[:, k, half * 512:(half + 1) * 512], k == 0, k == 7),
                                 reads=[mgT, Wo], writes=[ps], inc=(k == 7))
                        pss.append(ps)
                    row = g0 + ti * 128
                    self.postnorm(pss, x_, gbc, lng, lnb, upd[i % 2], yv[i % 2], yn[i % 2], sm[i % 2], S["resA"], S["resA"].t[b, row:row + 128, :])
        P.pop()

    def phaseF(self):
        P, I, S, C, l = self.P, self.I, self.S, self.C, self.l
        P.push()
        lng = P.sb("lng2", [128, D], F32)
        lnb = P.sb("lnb2", [128, D], F32)
        P.dma("sync", lng[:], I["ln2_g"].t[l, :].partition_broadcast(128), reads=[I["ln2_g"]], writes=[lng])
        P.dma("sync", lnb[:], I["ln2_b"].t[l, :].partition_broadcast(128), reads=[I["ln2_b"]], writes=[lnb])
        cwf = P.sb("cwf", [128, 22, 3], F32)
        cbf = P.sb("cbf", [128, 22], F32)
        for k in range(3):
            P.dma("sync", cwf[:, :, k], I["ffn_conv_w"].t[l, k, :].rearrange("(t p) -> p t", p=128), reads=[I["ffn_conv_w"]], writes=[cwf], allow_slow_non_contiguous=True)
        P.dma("sync", cbf[:], I["ffn_conv_b"].t[l, :].rearrange("(t p) -> p t", p=128), reads=[I["ffn_conv_b"]], writes=[cbf], allow_slow_non_contiguous=True)
        hff = P.sb("hff", [128, 22, TL], BF16)
        gbc = P.sb("gbc2", [128, D], F32)
        wup = S["wb_up%d" % l].t.rearrange("(k p) n -> p k n", p=128)
        for (b, seg, r, gofs, slen) in self.segs():
            ntile = slen // 128
            P.push()
            hT2 = P.sb("hT2", [128, 8, slen], BF16)
            xt = [P.sb("fxt%d" % i, [128, D], F32) for i in range(2)]
            xn = [P.sb("fxn%d" % i, [128, D], BF16) for i in range(2)]
            sm = self.ln_small("f")
            for ti in range(ntile):
                x_, n_ = xt[ti % 2], xn[ti % 2]
                row = gofs + ti * 128
                P.dma("sync", x_[:], S["resA"].t[b, row:row + 128, :], reads=[S["resA"]], writes=[x_])
                self.ln_tile(x_, n_[:], n_, sm[ti % 2])
                ps = self.bank("ftr", [0, 1])
                pb = ps[:].bitcast(BF16).rearrange("p (k t) -> p k t", k=8)
                for k in range(8):
                    P.op("tensor", TR(pb[:, k, :], n_[:, k * 128:(k + 1) * 128], C["identb"][:]), reads=[n_, C["identb"]], writes=[ps], inc=(k == 7))
                for k in range(8):
                    o = hT2[:, k, ti * 128:(ti + 1) * 128]
                    if k % 2 == 0:
                        P.op("vector", TS(o, pb[:, k, :], self.modT[:, 4, k, r:r + 1], self.modT[:, 3, k, r:r + 1], ALU.mult, ALU.add), reads=[ps, self.modT], writes=[hT2])
                    else:
                        P.op("scalar", ACT(o, pb[:, k, :], AF.Identity, bias=self.modT[:, 3, k, r:r + 1], scale=self.modT[:, 4, k, r:r + 1]), reads=[ps, self.modT], writes=[hT2])
            wu = [P.sb("wu%d" % i, [128, 8, 128], BF16) for i in range(2)]
            wv = [P.sb("wv%d" % i, [128, 8, 128], BF16) for i in range(2)]
            ucp = [P.sb("ucp%d" % i, [128, slen + 2], F32) for i in range(2)]
            acc = P.sb("facc", [128, slen], F32)
            ge = [P.sb("fge%d" % i, [128, slen], BF16) for i in range(2)]
            for u_ in ucp:
                P.op("gpsimd", lambda e, u_=u_: e.memset(u_[:, 0:1], 0.0), writes=[u_])
                P.op("gpsimd", lambda e, u_=u_, slen=slen: e.memset(u_[:, slen + 1:slen + 2], 0.0), writes=[u_])
            nbs = [(n0, min(512, slen - n0)) for n0 in range(0, slen, 512)]
            for j in range(22):
                wu_, wv_, u_, g_ = wu[j % 2], wv[j % 2], ucp[j % 2], ge[j % 2]
                P.dma("sync", wu_[:], wup[:, :, j * 128:(j + 1) * 128], reads=[S["wb_up%d" % l]], writes=[wu_])
                P.dma("sync", wv_[:], wup[:, :, DFF + j * 128:DFF + (j + 1) * 128], reads=[S["wb_up%d" % l]], writes=[wv_])
                for (n0, nn) in nbs:
                    ps = self.bank("fu", [0, 1, 2, 3])
                    for k in range(8):
                        P.op("tensor", MM(ps[:, 0:nn], wu_[:, k, :], hT2[:, k, n0:n0 + nn], k == 0, k == 7), reads=[wu_, hT2], writes=[ps], inc=(k == 7))
                    P.op("scalar", ACT(u_[:, 1 + n0:1 + n0 + nn], ps[:, 0:nn], AF.Copy), reads=[ps], writes=[u_])
                P.op("vector", TS(acc[:], u_[:, 0:slen], cwf[:, j, 0:1], None, ALU.mult), reads=[u_, cwf], writes=[acc])
                P.op("vector", STT(acc[:], u_[:, 1:slen + 1], cwf[:, j, 1:2], acc[:], ALU.mult, ALU.add), reads=[u_, cwf, acc], writes=[acc])
                P.op("vector", STT(acc[:], u_[:, 2:slen + 2], cwf[:, j, 2:3], acc[:], ALU.mult, ALU.add), reads=[u_, cwf, acc], writes=[acc])
                P.op("scalar", ACT(g_[:], acc[:], AF.Gelu_apprx_tanh, bias=cbf[:, j:j + 1], scale=1.0), reads=[acc, cbf], writes=[g_])
                for (n0, nn) in nbs:
                    ps = self.bank("fv", [4, 5, 6, 7])
                    for k in range(8):
                        P.op("tensor", MM(ps[:, 0:nn], wv_[:, k, :], hT2[:, k, n0:n0 + nn], k == 0, k == 7), reads=[wv_, hT2], writes=[ps], inc=(k == 7))
                    P.op("vector", TTo(hff[:, j, n0:n0 + nn], ps[:, 0:nn], g_[:, n0:n0 + nn], ALU.mult), reads=[ps, g_], writes=[hff])
            P.pop()
            P.push()
            Wd = P.sb("Wd", [128, 22, D], BF16)
            P.dma("sync", Wd[:], S["wb_dn%d" % l].t.rearrange("(t p) n -> p t n", p=128), reads=[S["wb_dn%d" % l]], writes=[Wd])
            P.dma("sync", gbc[:], S["modr%d" % l].t[r, 5 * D:6 * D].partition_broadcast(128), reads=[S["modr%d" % l]], writes=[gbc])
            xt = [P.sb("gxt%d" % i, [128, D], F32) for i in range(2)]
            upd = [P.sb("gupd%d" % i, [128, D], F32) for i in range(2)]
            yv = [P.sb("gyv%d" % i, [128, D], F32) for i in range(2)]
            yn = [P.sb("gyn%d" % i, [128, D], F32) for i in range(2)]
            sm = self.ln_small("g")
            for ti in range(ntile):
                x_ = xt[ti % 2]
                row = gofs + ti * 128
                P.dma("sync", x_[:], S["resA"].t[b, row:row + 128, :], reads=[S["resA"]], writes=[x_])
                pss = []
                for half in range(2):
                    ps = self.bank("fd", [0, 1, 2, 3])
                    for j in range(22):
                        P.op("tensor", MM(ps[:, 0:512], hff[:, j, ti * 128:(ti + 1) * 128], Wd[:, j, half * 512:(half + 1) * 512], j == 0, j == 21),
                             reads=[hff, Wd], writes=[ps], inc=(j == 21))
                    pss.append(ps)
                if self.last:
                    dbuf, dap = self.out, self.out.t[b, ti * 128:(ti + 1) * 128, :]
                else:
                    dbuf, dap = S["resB"], S["resB"].t[b, row:row + 128, :]
                self.postnorm(pss, x_, gbc, lng, lnb, upd[ti % 2], yv[ti % 2], yn[ti % 2], sm[ti % 2], dbuf, dap)
            P.pop()
        P.pop()


def _rope_tables():
    half = 64
    inv_freq = (1.0 / (np.float32(10000.0) ** (np.arange(0, half, 2, dtype=np.float32) / np.float32(half)))).astype(np.float32)
    rows = TL // 64
    row = np.repeat(np.arange(rows, dtype=np.float32), 64)
    col = np.tile(np.arange(64, dtype=np.float32), rows)
    ang = np.concatenate([row[:, None] * inv_freq, col[:, None] * inv_freq], -1).astype(np.float32)
    cos = np.cos(ang).astype(np.float32).reshape(16, 128, 64).transpose(1, 0, 2)
    sin = np.sin(ang).astype(np.float32).reshape(16, 128, 64).transpose(1, 0, 2)
    return np.ascontiguousarray(cos), np.ascontiguousarray(sin)


def _host_consts():
    cos, sin = _rope_tables()
    jj = np.arange(128) // 16
    maskf = (jj[None, :] >= jj[:, None]).astype(np.float32)
    maskr = (jj[None, :] <= jj[:, None]).astype(np.float32)
    return {
        "k_identf": np.eye(128, dtype=np.float32),
        "k_identb": np.eye(128, dtype=np.float32).astype(ml_dtypes.bfloat16),
        "k_onesb": np.ones((128, 128), dtype=np.float32).astype(ml_dtypes.bfloat16),
        "k_cos": cos, "k_sin": sin, "k_maskf": maskf, "k_maskr": maskr,
    }


_NC_CACHE = {}


def _get_nc():
    if "nc" not in _NC_CACHE:
        _NC_CACHE["nc"] = K().build()
    return _NC_CACHE["nc"]


def make_in_maps(inputs, cores):
    consts = _host_consts()
    maps = []
    for i in cores:
        m = {}
        for k, v in inputs.items():
            v = np.asarray(v)
            if k in ("x", "c", "ctx"):
                m[k] = np.ascontiguousarray(v[NBC * i:NBC * (i + 1)])
            elif k == "c_ctx":
                m[k] = np.ascontiguousarray(v.reshape(1, D))
            else:
                m[k] = v
        m.update(consts)
        maps.append(m)
    return maps


def kernel(**inputs):
    nc = _get_nc()
    maps = make_in_maps(inputs, range(8))
    res = run_bass_kernel_spmd(nc, maps, core_ids=list(range(8)))
    return np.concatenate([np.asarray(r["out"]) for r in res.results], axis=0).astype(np.float32)
```

```python
import numpy as np
import ml_dtypes
from contextlib import ExitStack
import concourse.bass as bass
import concourse.mybir as mybir
from concourse.bass_utils import run_bass_kernel_spmd

F32 = mybir.dt.float32
BF16 = mybir.dt.bfloat16
I32 = mybir.dt.int32
AF = mybir.ActivationFunctionType
ALU = mybir.AluOpType

ENGS = ["tensor", "vector", "scalar", "gpsimd", "sync"]

D = 1024
TL = 2048
TC = 256
TT = TC + TL
NBC = 2
DEPTH = 2
IN_COLS = 6656
DFF = 2816
NCH = TT // 8
EPS = 1e-6
ALPHA = float((2 * DEPTH) ** 0.25)
ATTN_SCALE = float(128 ** -0.5)
TWO_PI = float(2 * np.pi)
PI = float(np.pi)


class Buf:
    def __init__(self, name, t):
        self.name = name
        self.t = t
        self.w = {}
        self.r = {}
        self.dkey = {}

    def __getitem__(self, idx):
        return self.t[idx]


class Prog:
    def __init__(self, nc, n_dsem=56):
        self.nc = nc
        self.base = ExitStack()
        self.scopes = [ExitStack()]
        self.scope_bufs = [[]]
        self.q = {e: [] for e in ENGS}
        self.sem = {}
        self.cnt = {}
        self.seen = {e: {} for e in ENGS}
        for e in ENGS:
            self.sem[e] = self.base.enter_context(nc.semaphore("s_" + e))
            self.cnt[e] = 0
        self.free_dsem = {"sync": [], "gpsimd": []}
        for qn in ("sync", "gpsimd"):
            for i in range(n_dsem // 2):
                k = ("d" + qn, i)
                self.sem[k] = self.base.enter_context(nc.semaphore("d%s%d" % (qn[0], i)))
                self.cnt[k] = 0
                self.free_dsem[qn].append(k)
        self.uid = 0

    def push(self):
        self.scopes.append(ExitStack())
        self.scope_bufs.append([])

    def pop(self):
        self.barrier()
        for b in self.scope_bufs.pop():
            for qn, k in b.dkey.items():
                self.free_dsem[qn].append(k)
            b.dkey = {}
        self.scopes.pop().close()

    def sb(self, name, shape, dtype):
        self.uid += 1
        t = self.scopes[-1].enter_context(self.nc.sbuf_tensor("%s_%d" % (name, self.uid), list(shape), dtype))
        b = Buf(name, t)
        self.scope_bufs[-1].append(b)
        return b

    def ps(self, name, shape, dtype=F32):
        t = self.base.enter_context(self.nc.psum_tensor(name, list(shape), dtype))
        return Buf(name, t)

    def dram(self, name, shape, dtype, kind="Internal"):
        t = self.nc.dram_tensor(name, list(shape), dtype, kind=kind).ap()
        return Buf(name, t)

    def _dkey(self, b, eng):
        if eng not in b.dkey:
            b.dkey[eng] = self.free_dsem[eng].pop()
        return b.dkey[eng]

    def _deps(self, eng, reads, writes, strict=False):
        deps = {}

        def add(d):
            for k, v in d.items():
                if k == eng and eng == "tensor":
                    continue
                if deps.get(k, 0) < v:
                    deps[k] = v

        for r in reads:
            add(r.w)
        for w in writes:
            add(w.w)
            add(w.r)
        out = []
        for k, v in deps.items():
            if self.seen[eng].get(k, 0) >= v:
                continue
            self.seen[eng][k] = v
            out.append((self.sem[k], v))
        return out

    def _commit(self, ev, reads, writes):
        k, v = ev
        for r in reads:
            if r.r.get(k, 0) < v:
                r.r[k] = v
        for w in writes:
            if w.w.get(k, 0) < v:
                w.w[k] = v
            w.r = {}

    def op(self, eng, fn, reads=(), writes=(), inc=True):
        waits = self._deps(eng, reads, writes)
        if inc:
            self.cnt[eng] += 1
            ev = (eng, self.cnt[eng])
        else:
            ev = (eng, self.cnt[eng] + 1)
        sem = self.sem[eng]

        def emit(e, waits=waits, fn=fn, inc=inc, sem=sem):
            for s, v in waits:
                e.wait_ge(s, v)
            ins = fn(e)
            if inc:
                ins.then_inc(sem, 1)

        self.q[eng].append(emit)
        self._commit(ev, reads, writes)

    def dma(self, eng, out, in_, reads=(), writes=(), semb=None, **kw):
        waits = self._deps(eng, reads, writes, strict=True)
        if semb is None:
            for b in list(writes) + list(reads):
                if not isinstance(b, DBuf):
                    semb = b
                    break
        key = self._dkey(semb, eng)
        self.cnt[key] += 16
        ev = (key, self.cnt[key])
        sem = self.sem[key]

        def emit(e, waits=waits, out=out, in_=in_, sem=sem, kw=kw):
            for s, v in waits:
                e.wait_ge(s, v)
            e.dma_start(out=out, in_=in_, **kw).then_inc(sem, 16)

        self.q[eng].append(emit)
        self._commit(ev, reads, writes)

    def barrier(self):
        tot = dict(self.cnt)
        for e in ENGS:
            waits = []
            for k, v in tot.items():
                if k == e or v == 0:
                    continue
                if self.seen[e].get(k, 0) >= v:
                    continue
                self.seen[e][k] = v
                waits.append((self.sem[k], v))

            def emit(en, waits=waits):
                for s, v in waits:
                    en.wait_ge(s, v)

            self.q[e].append(emit)

    def finish(self):
        self.barrier()
        nc = self.nc
        q = self.q
        with nc.Block() as block:
            @block.tensor
            def _(e):
                for f in q["tensor"]:
                    f(e)

            @block.vector
            def _(e):
                for f in q["vector"]:
                    f(e)

            @block.scalar
            def _(e):
                for f in q["scalar"]:
                    f(e)

            @block.gpsimd
            def _(e):
                for f in q["gpsimd"]:
                    f(e)

            @block.sync
            def _(e):
                for f in q["sync"]:
                    f(e)
        while self.scopes:
            self.scopes.pop().close()
        self.base.close()


class DBuf(Buf):
    pass


def TS(out, in0, s1, s2, op0, op1=None):
    if op1 is None:
        return lambda e: e.tensor_scalar(out=out, in0=in0, scalar1=s1, scalar2=None, op0=op0)
    return lambda e: e.tensor_scalar(out=out, in0=in0, scalar1=s1, scalar2=s2, op0=op0, op1=op1)


def TTo(out, in0, in1, op):
    return lambda e: e.tensor_tensor(out=out, in0=in0, in1=in1, op=op)


def STT(out, in0, scalar, in1, op0, op1):
    return lambda e: e.scalar_tensor_tensor(out=out, in0=in0, scalar=scalar, in1=in1, op0=op0, op1=op1)


def ACT(out, in_, func, bias=None, scale=None, accum_out=None):
    kw = {}
    if bias is not None:
        kw["bias"] = bias
    if scale is not None:
        kw["scale"] = scale
    if accum_out is not None:
        kw["accum_out"] = accum_out
    return lambda e: e.activation(out=out, in_=in_, func=func, **kw)


def CP(out, in_):
    return lambda e: e.tensor_copy(out=out, in_=in_)


def MM(out, lhsT, rhs, start, stop):
    return lambda e: e.matmul(out, lhsT=lhsT, rhs=rhs, start=start, stop=stop)


def TR(out, in_, ident):
    return lambda e: e.transpose(out=out, in_=in_, identity=ident)


class K:
    def __init__(self, dbg=(), stop_after=None):
        self.dbg = set(dbg)
        self.stop_after = stop_after
        nc = self.nc = bass.Bass("TRN2", target_bir_lowering=False)
        P = self.P = Prog(nc)
        self.inputs = {}
        self.psb = [P.ps("psb%d" % i, [128, 512], F32) for i in range(8)]
        self.rr = {}

    def din(self, name, shape, dt=F32):
        b = DBuf(name, self.nc.dram_tensor(name, list(shape), dt, kind="ExternalInput").ap())
        self.inputs[name] = b
        return b

    def dscr(self, name, shape, dt):
        kind = "ExternalOutput" if name in self.dbg else "Internal"
        return DBuf(name, self.nc.dram_tensor(name, list(shape), dt, kind=kind).ap())

    def bank(self, group, banks):
        i = self.rr.get(group, 0)
        self.rr[group] = i + 1
        return self.psb[banks[i % len(banks)]]

    def build(self):
        nc, P = self.nc, self.P
        L = DEPTH
        I = self.I = {}
        I["x"] = self.din("x", [NBC, TL, D])
        I["c"] = self.din("c", [NBC, D])
        I["ctx"] = self.din("ctx", [NBC, TC, D])
        I["c_ctx"] = self.din("c_ctx", [1, D])
        for name, shape in [
            ("w_mod", [L, D, 6 * D]), ("b_mod", [L, 6 * D]), ("w_in", [L, D, IN_COLS]), ("conv_w", [L, 3, 512]),
            ("w_conv_out", [L, 512, D]), ("ssm_lam_re", [L, 2, 32, 64]), ("ssm_lam_im", [L, 2, 32, 64]),
            ("ssm_log_dt", [L, 2, 32]), ("ssm_b_re", [L, 2, 32, 64, 16]), ("ssm_b_im", [L, 2, 32, 64, 16]),
            ("ssm_c_re", [L, 2, 32, 16, 64]), ("ssm_c_im", [L, 2, 32, 16, 64]), ("ssm_d", [L, 512]),
            ("w_glu", [L, 512, 2 * D]), ("q_norm_g", [L, 128]), ("k_norm_g", [L, 128]), ("w_attn_out", [L, D, D]),
            ("w_o", [L, D, D]), ("ln1_g", [L, D]), ("ln1_b", [L, D]), ("ffn_w_up", [L, D, 2 * DFF]),
            ("ffn_conv_w", [L, 3, DFF]), ("ffn_conv_b", [L, DFF]), ("ffn_w_down", [L, DFF, D]),
            ("ln2_g", [L, D]), ("ln2_b", [L, D]),
        ]:
            I[name] = self.din(name, shape)
        I["k_identf"] = self.din("k_identf", [128, 128])
        I["k_identb"] = self.din("k_identb", [128, 128], BF16)
        I["k_onesb"] = self.din("k_onesb", [128, 128], BF16)
        I["k_cos"] = self.din("k_cos", [128, 16, 64])
        I["k_sin"] = self.din("k_sin", [128, 16, 64])
        I["k_maskf"] = self.din("k_maskf", [128, 128])
        I["k_maskr"] = self.din("k_maskr", [128, 128])
        self.out = DBuf("out", nc.dram_tensor("out", [NBC, TL, D], F32, kind="ExternalOutput").ap())

        S = self.S = {}
        for l in range(L):
            S["wb_in%d" % l] = self.dscr("wb_in%d" % l, [D, IN_COLS], BF16)
            S["wb_co%d" % l] = self.dscr("wb_co%d" % l, [512, D], BF16)
            S["wb_glu%d" % l] = self.dscr("wb_glu%d" % l, [512, 2 * D], BF16)
            S["wb_ao%d" % l] = self.dscr("wb_ao%d" % l, [D, D], BF16)
            S["wb_o%d" % l] = self.dscr("wb_o%d" % l, [D, D], BF16)
            S["wb_up%d" % l] = self.dscr("wb_up%d" % l, [D, 2 * DFF], BF16)
            S["wb_dn%d" % l] = self.dscr("wb_dn%d" % l, [DFF, D], BF16)
            S["modr%d" % l] = self.dscr("modr%d" % l, [3, 6 * D], F32)
        S["resA"] = self.dscr("resA", [NBC, TT, D], F32)
        S["resB"] = self.dscr("resB", [NBC, TT, D], F32)
        S["KT"] = self.dscr("KT", [NBC, 2, 128, TT], BF16)
        S["V"] = self.dscr("V", [NBC, TT, 256], BF16)
        S["U"] = self.dscr("U", [NBC, TT, 512], BF16)
        S["QT"] = self.dscr("QT", [NBC, 8, 128, TT], BF16)
        S["AXT"] = self.dscr("AXT", [NBC, 512, TT], BF16)
        S["BGT"] = self.dscr("BGT", [NBC, 512, TT], BF16)
        S["CGT"] = self.dscr("CGT", [NBC, 512, TT], BF16)
        S["GT"] = self.dscr("GT", [NBC, 3 * D, TT], BF16)
        S["AOT"] = self.dscr("AOT", [NBC, D, TT], BF16)
        S["SSMT"] = self.dscr("SSMT", [NBC, D, TT], BF16)
        S["YS"] = self.dscr("YS", [NBC, TT, 512], F32)

        C = self.C = {}
        C["identf"] = P.sb("identf", [128, 128], F32)
        C["identb"] = P.sb("identb", [128, 128], BF16)
        C["onesb"] = P.sb("onesb", [128, 128], BF16)
        C["eps"] = P.sb("eps", [128, 1], F32)
        for nm in ["identf", "identb", "onesb"]:
            P.dma("sync", C[nm][:], I["k_" + nm].t, reads=[I["k_" + nm]], writes=[C[nm]])
        P.op("vector", lambda e: e.memset(C["eps"][:], EPS), writes=[C["eps"]])

        try:
            self.weight_prep()
            if self.stop_after == "prep":
                return self.finish()
            for l in range(L):
                self.layer(l)
                if self.stop_after == "layer%d" % l:
                    break
        except StopIteration:
            pass
        return self.finish()

    def finish(self):
        self.P.finish()
        return self.nc

    def check_stop(self, tag):
        if self.stop_after == tag:
            raise StopIteration

    def weight_prep(self):
        P, I, S = self.P, self.I, self.S
        P.push()
        stf = [P.sb("wpf%d" % i, [128, 2048], F32) for i in range(3)]
        stb = [P.sb("wpb%d" % i, [128, 2048], BF16) for i in range(3)]
        n = 0
        engs = ["gpsimd", "vector", "scalar"]
        for l in range(DEPTH):
            for src, dst, Kd, N in [("w_in", "wb_in", D, IN_COLS), ("w_conv_out", "wb_co", 512, D), ("w_glu", "wb_glu", 512, 2 * D),
                                    ("w_attn_out", "wb_ao", D, D), ("w_o", "wb_o", D, D), ("ffn_w_up", "wb_up", D, 2 * DFF),
                                    ("ffn_w_down", "wb_dn", DFF, D)]:
                sa = I[src].t[l]
                db = S[dst + str(l)]
                for kt in range(Kd // 128):
                    for c0 in range(0, N, 2048):
                        w = min(2048, N - c0)
                        f, b = stf[n % 3], stb[n % 3]
                        eng = engs[n % 3]
                        P.dma("sync", f[:, :w], sa[kt * 128:(kt + 1) * 128, c0:c0 + w], reads=[I[src]], writes=[f])
                        if eng == "scalar":
                            P.op(eng, ACT(b[:, :w], f[:, :w], AF.Copy), reads=[f], writes=[b])
                        else:
                            P.op(eng, CP(b[:, :w], f[:, :w]), reads=[f], writes=[b])
                        P.dma("gpsimd", db.t[kt * 128:(kt + 1) * 128, c0:c0 + w], b[:, :w], reads=[b], writes=[db])
                        n += 1
        P.pop()

    def layer(self, l):
        P = self.P
        self.l = l
        self.last = (l == DEPTH - 1)
        P.push()
        self.modT = P.sb("modT", [128, 6, 8, 3], F32)
        self.phase0()
        self.check_stop("p0_%d" % l)
        self.phaseA()
        self.check_stop("pA_%d" % l)
        self.phaseB()
        self.check_stop("pB_%d" % l)
        self.phaseC()
        self.check_stop("pC_%d" % l)
        self.phaseD()
        self.check_stop("pD_%d" % l)
        self.phaseF()
        self.check_stop("pF_%d" % l)
        P.pop()

    def res_in(self, b, seg):
        if self.l == 0:
            return (self.I["ctx"], self.I["ctx"].t[b]) if seg == "ctx" else (self.I["x"], self.I["x"].t[b])
        r = self.S["resB"]
        return (r, r.t[b, 0:TC, :]) if seg == "ctx" else (r, r.t[b, TC:TT, :])

    def phase0(self):
        P, I, S, C, l = self.P, self.I, self.S, self.C, self.l
        P.push()
        scT = P.sb("scT", [128, 8, 3], F32)
        bm3 = P.sb("bm3", [3, 6 * D], F32)
        modrows = P.sb("modrows", [3, 6 * D], F32)
        wst = [P.sb("wmst%d" % i, [128, 8, 512], F32) for i in range(2)]
        for r in range(3):
            src = I["c"].t[r, :] if r < 2 else I["c_ctx"].t[0, :]
            P.dma("sync", scT[:, :, r], src.rearrange("(k p) -> p k", p=128), reads=[I["c"]], writes=[scT],
                  allow_slow_non_contiguous=True)
        P.op("scalar", ACT(scT[:], scT[:], AF.Silu), reads=[scT], writes=[scT])
        P.dma("sync", bm3[0:3, :], I["b_mod"].t[l, :].partition_broadcast(3), reads=[I["b_mod"]], writes=[bm3])
        wm = I["w_mod"].t[l].rearrange("(k p) n -> p k n", p=128)
        for blk in range(12):
            w = wst[blk % 2]
            P.dma("sync", w[:], wm[:, :, blk * 512:(blk + 1) * 512], reads=[I["w_mod"]], writes=[w])
            ps = self.bank("p0", [0, 1])
            for k in range(8):
                P.op("tensor", MM(ps[0:3, 0:512], scT[:, k, :], w[:, k, :], k == 0, k == 7), reads=[scT, w], writes=[ps], inc=(k == 7))
            P.op("vector", TTo(modrows[0:3, blk * 512:(blk + 1) * 512], ps[0:3, 0:512], bm3[0:3, blk * 512:(blk + 1) * 512], ALU.add),
                 reads=[ps, bm3], writes=[modrows])
        P.dma("gpsimd", S["modr%d" % l].t, modrows[0:3, :], reads=[modrows], writes=[S["modr%d" % l]])
        ps = self.bank("p0", [0, 1])
        for t in range(48):
            P.op("tensor", TR(ps[:, t * 3:(t + 1) * 3], modrows[0:3, t * 128:(t + 1) * 128], C["identf"][0:3, 0:3]),
                 reads=[modrows, C["identf"]], writes=[ps], inc=(t == 47))
        mt = self.modT
        P.op("vector", CP(mt[:].rearrange("p a k r -> p (a k r)"), ps[:, 0:144]), reads=[ps], writes=[mt])
        for sec in (1, 4):
            P.op("vector", TS(mt[:, sec], mt[:, sec], 1.0, None, ALU.add), reads=[mt], writes=[mt])
        P.pop()

    def ln_tile(self, xt, out_ap, out_buf, sm, src_ap=None, act_extra_reads=()):
        P, C = self.P, self.C
        st, mv, rs, nb = sm
        src = xt[:] if src_ap is None else src_ap
        P.op("vector", lambda e: e.bn_stats(out=st[:, 0:6], in_=src[:, 0:512]), reads=[xt], writes=[st])
        P.op("vector", lambda e: e.bn_stats(out=st[:, 6:12], in_=src[:, 512:1024]), reads=[xt], writes=[st])
        P.op("vector", lambda e: e.bn_aggr(out=mv[:, 0:2], in_=st[:, 0:12]), reads=[st], writes=[mv])
        P.op("scalar", ACT(rs[:, 0:1], mv[:, 1:2], AF.Sqrt, bias=C["eps"][:, 0:1], scale=1.0), reads=[mv, C["eps"]], writes=[rs])
        P.op("vector", lambda e: e.reciprocal(out=rs[:, 0:1], in_=rs[:, 0:1]), reads=[rs], writes=[rs])
        P.op("vector", TS(nb[:, 0:1], mv[:, 0:1], rs[:, 0:1], -1.0, ALU.mult, ALU.mult), reads=[mv, rs], writes=[nb])
        P.op("scalar", ACT(out_ap, src, AF.Identity, bias=nb[:, 0:1], scale=rs[:, 0:1]), reads=[xt, rs, nb], writes=[out_buf])

    def ln_small(self, tag, n=2):
        P = self.P
        return [(P.sb(tag + "st%d" % i, [128, 12], F32), P.sb(tag + "mv%d" % i, [128, 2], F32),
                 P.sb(tag + "rs%d" % i, [128, 1], F32), P.sb(tag + "nb%d" % i, [128, 1], F32)) for i in range(n)]

    def phaseA(self):
        P, I, S, C, l = self.P, self.I, self.S, self.C, self.l
        P.push()
        W = P.sb("Win", [128, 8, IN_COLS], BF16)
        wsrc = S["wb_in%d" % l]
        for k in range(8):
            P.dma("sync", W[:, k, :], wsrc.t[k * 128:(k + 1) * 128, :], reads=[wsrc], writes=[W])
        cos = P.sb("cos", [128, 16, 64], F32)
        sin = P.sb("sin", [128, 16, 64], F32)
        P.dma("sync", cos[:], I["k_cos"].t, reads=[I["k_cos"]], writes=[cos])
        P.dma("sync", sin[:], I["k_sin"].t, reads=[I["k_sin"]], writes=[sin])
        gq = P.sb("gq", [128, 128], F32)
        gk = P.sb("gk", [128, 128], F32)
        P.dma("sync", gq[:], I["q_norm_g"].t[l, :].partition_broadcast(128), reads=[I["q_norm_g"]], writes=[gq])
        P.dma("sync", gk[:], I["k_norm_g"].t[l, :].partition_broadcast(128), reads=[I["k_norm_g"]], writes=[gk])
        xt = [P.sb("xt%d" % i, [128, D], F32) for i in range(2)]
        xn = [P.sb("xn%d" % i, [128, D], BF16) for i in range(2)]
        sm = self.ln_small("a")
        hT = [P.sb("hT%d" % i, [128, 8, 512], BF16) for i in range(2)]
        junk = P.sb("junk", [128, 128], F32)
        ss = [P.sb("ss%d" % i, [128, 4], F32) for i in range(2)]
        xq = [P.sb("xq%d" % i, [128, 4, 128], F32) for i in range(2)]
        rt = [P.sb("rt%d" % i, [128, 4, 64], F32) for i in range(4)]
        qn = [P.sb("qn%d" % i, [128, 4, 128], BF16) for i in range(2)]
        qTs = P.sb("qTs", [128, 8, 512], BF16)
        kTs = P.sb("kTs", [128, 2, 512], BF16)
        vst = [P.sb("vst%d" % i, [128, 256], BF16) for i in range(2)]
        ust = [P.sb("ust%d" % i, [128, 512], BF16) for i in range(2)]
        fst = [P.sb("fst%d" % i, [128, 512], BF16) for i in range(3)]
        cnt = {"x": 0, "q": 0, "v": 0, "u": 0, "f": 0, "ev": 0}

        def normrope(ps, c0, nh, g, rope, ltile, dst_bf):
            i = cnt["q"]
            cnt["q"] += 1
            s_, x_ = ss[i % 2], xq[i % 2]
            for h in range(nh):
                P.op("scalar", ACT(junk[:], ps[:, c0 + h * 128:c0 + (h + 1) * 128], AF.Square, accum_out=s_[:, h:h + 1]),
                     reads=[ps], writes=[junk, s_])
            P.op("scalar", ACT(s_[:, 0:nh], s_[:, 0:nh], AF.Sqrt, bias=C["eps"][:, 0:1], scale=1.0 / 128), reads=[s_, C["eps"]], writes=[s_])
            P.op("vector", lambda e: e.reciprocal(out=s_[:, 0:nh], in_=s_[:, 0:nh]), reads=[s_], writes=[s_])
            for h in range(nh):
                P.op("vector", STT(x_[:, h, :], ps[:, c0 + h * 128:c0 + (h + 1) * 128], s_[:, h:h + 1], g[:], ALU.mult, ALU.mult),
                     reads=[ps, s_, g], writes=[x_])
            if not rope:
                P.op("gpsimd", CP(dst_bf[:, 0:nh, :], x_[:, 0:nh, :]), reads=[x_], writes=[dst_bf])
                return
            xe = x_[:, 0:nh, 0:128:2]
            xo = x_[:, 0:nh, 1:128:2]
            cb = cos[:, ltile:ltile + 1, :].to_broadcast([128, nh, 64])
            sb_ = sin[:, ltile:ltile + 1, :].to_broadcast([128, nh, 64])
            t1, t2, t3, t4 = [r[:, 0:nh, :] for r in rt]
            eng = "gpsimd"
            P.op(eng, TTo(t1, xe, cb, ALU.mult), reads=[x_, cos], writes=[rt[0]])
            P.op(eng, TTo(t2, xo, sb_, ALU.mult), reads=[x_, sin], writes=[rt[1]])
            P.op(eng, TTo(dst_bf[:, 0:nh, 0:128:2], t1, t2, ALU.subtract), reads=[rt[0], rt[1]], writes=[dst_bf])
            P.op(eng, TTo(t3, xe, sb_, ALU.mult), reads=[x_, sin], writes=[rt[2]])
            P.op(eng, TTo(t4, xo, cb, ALU.mult), reads=[x_, cos], writes=[rt[3]])
            P.op(eng, TTo(dst_bf[:, 0:nh, 1:128:2], t3, t4, ALU.add), reads=[rt[2], rt[3]], writes=[dst_bf])

        for b in range(NBC):
            for seg in (["ctx", "lat"]):
                nst = 1 if seg == "ctx" else 4
                ntok = 256 if seg == "ctx" else 512
                r = 2 if seg == "ctx" else b
                rbuf, rap = self.res_in(b, seg)
                full = (seg == "lat") or (not self.last)
                for st_i in range(nst):
                    t0 = st_i * ntok
                    g0 = t0 + (0 if seg == "ctx" else TC)
                    nt = ntok // 128
                    h = hT[st_i % 2]
                    for ti in range(nt):
                        i = cnt["x"]
                        cnt["x"] += 1
                        x_, n_ = xt[i % 2], xn[i % 2]
                        P.dma("sync", x_[:], rap[t0 + ti * 128:t0 + (ti + 1) * 128, :], reads=[rbuf], writes=[x_])
                        self.ln_tile(x_, n_[:], n_, sm[i % 2])
                        ps = self.bank("atr", [0, 1])
                        pb = ps[:].bitcast(BF16).rearrange("p (k t) -> p k t", k=8)
                        for k in range(8):
                            P.op("tensor", TR(pb[:, k, :], n_[:, k * 128:(k + 1) * 128], C["identb"][:]), reads=[n_, C["identb"]], writes=[ps],
                                 inc=(k == 7))
                        for k in range(8):
                            o = h[:, k, ti * 128:(ti + 1) * 128]
                            if k % 2 == 0:
                                P.op("vector", TS(o, pb[:, k, :], self.modT[:, 1, k, r:r + 1], self.modT[:, 0, k, r:r + 1], ALU.mult, ALU.add),
                                     reads=[ps, self.modT], writes=[h])
                            else:
                                P.op("scalar", ACT(o, pb[:, k, :], AF.Identity, bias=self.modT[:, 0, k, r:r + 1], scale=self.modT[:, 1, k, r:r + 1]),
                                     reads=[ps, self.modT], writes=[h])
                    for ti in range(nt):
                        ltile = (t0 // 128) + ti
                        for blk in range(4 if full else 2):
                            ps = self.bank("amm", [2, 3, 4, 5])
                            for k in range(8):
                                P.op("tensor", MM(ps[:, 0:512], h[:, k, ti * 128:(ti + 1) * 128], W[:, k, blk * 512:(blk + 1) * 512], k == 0, k == 7),
                                     reads=[h, W], writes=[ps], inc=(k == 7))
                            if blk == 0:
                                kn = qn[cnt["q"] % 2]
                                normrope(ps, 0, 2, gk, seg == "lat", ltile, kn)
                                pt = self.bank("atq", [6, 7])
                                ptb = pt[:].bitcast(BF16).rearrange("p (k t) -> p k t", k=8)
                                for hh in range(2):
                                    P.op("tensor", TR(ptb[:, hh, :], kn[:, hh, :], C["identb"][:]), reads=[kn, C["identb"]], writes=[pt], inc=(hh == 1))
                                P.op("vector", CP(kTs[:, :, ti * 128:(ti + 1) * 128], ptb[:, 0:2, :]), reads=[pt], writes=[kTs])
                                v_ = vst[cnt["v"] % 2]
                                cnt["v"] += 1
                                P.op("scalar", ACT(v_[:], ps[:, 256:512], AF.Copy), reads=[ps], writes=[v_])
                                P.dma("gpsimd", S["V"].t[b, g0 + ti * 128:g0 + (ti + 1) * 128, :], v_[:], reads=[v_], writes=[S["V"]])
                            elif blk == 1:
                                u_ = ust[cnt["u"] % 2]
                                cnt["u"] += 1
                                P.op("scalar", ACT(u_[:], ps[:, 0:512], AF.Copy), reads=[ps], writes=[u_])
                                P.dma("gpsimd", S["U"].t[b, g0 + ti * 128:g0 + (ti + 1) * 128, :], u_[:], reads=[u_], writes=[S["U"]])
                            else:
                                q_ = qn[cnt["q"] % 2]
                                normrope(ps, 0, 4, gq, seg == "lat", ltile, q_)
                                pt = self.bank("atq", [6, 7])
                                ptb = pt[:].bitcast(BF16).rearrange("p (k t) -> p k t", k=8)
                                for hh in range(4):
                                    P.op("tensor", TR(ptb[:, hh, :], q_[:, hh, :], C["identb"][:]), reads=[q_, C["identb"]], writes=[pt], inc=(hh == 3))
                                h0 = (blk - 2) * 4
                                P.op("vector", CP(qTs[:, h0:h0 + 4, ti * 128:(ti + 1) * 128], ptb[:, 0:4, :]), reads=[pt], writes=[qTs])
                    for kv in range(2):
                        P.dma("gpsimd", S["KT"].t[b, kv, :, g0:g0 + ntok], kTs[:, kv, 0:ntok], reads=[kTs], writes=[S["KT"]])
                    if full:
                        for hh in range(8):
                            P.dma("gpsimd", S["QT"].t[b, hh, :, g0:g0 + ntok], qTs[:, hh, 0:ntok], reads=[qTs], writes=[S["QT"]])
                        for ct in range(36):
                            ps = self.bank("amm", [2, 3, 4, 5])
                            c0 = 2048 + ct * 128
                            for k in range(8):
                                P.op("tensor", MM(ps[:, 0:ntok], W[:, k, c0:c0 + 128], h[:, k, 0:ntok], k == 0, k == 7), reads=[h, W], writes=[ps],
                                     inc=(k == 7))
                            f_ = fst[cnt["f"] % 3]
                            cnt["f"] += 1
                            if ct < 12:
                                dst = S[["AXT", "BGT", "CGT"][ct // 4]]
                                row0 = (ct % 4) * 128
                                if ct % 2 == 0:
                                    P.op("vector", CP(f_[:, 0:ntok], ps[:, 0:ntok]), reads=[ps], writes=[f_])
                                else:
                                    P.op("scalar", ACT(f_[:, 0:ntok], ps[:, 0:ntok], AF.Copy), reads=[ps], writes=[f_])
                            else:
                                dst = S["GT"]
                                row0 = (ct - 12) * 128
                                P.op("scalar", ACT(f_[:, 0:ntok], ps[:, 0:ntok], AF.Sigmoid), reads=[ps], writes=[f_])
                            P.dma("gpsimd", dst.t[b, row0:row0 + 128, g0:g0 + ntok], f_[:, 0:ntok], reads=[f_], writes=[dst])
        P.pop()


    def phaseB(self):
        P, I, S, C, l = self.P, self.I, self.S, self.C, self.l
        P.push()
        KTs = P.sb("KTs", [128, 2, TT], BF16)
        Vs = P.sb("Vs", [128, 18, 256], BF16)
        qT = [P.sb("qTb%d" % i, [128, 512], BF16) for i in range(2)]
        pT = [P.sb("pT%d" % i, [128, 512], BF16) for i in range(4)]
        rec = [P.sb("rec%d" % i, [128, 512], F32) for i in range(2)]
        oT = [P.sb("oT%d" % i, [128, 512], BF16) for i in range(2)]
        n = 0
        for b in range(NBC):
            for kv in range(2):
                P.dma("sync", KTs[:, kv, :], S["KT"].t[b, kv], reads=[S["KT"]], writes=[KTs])
            P.dma("sync", Vs[:], S["V"].t[b].rearrange("(t p) c -> p t c", p=128), reads=[S["V"]], writes=[Vs])
            blocks = [("lat", qb) for qb in range(4)] + ([] if self.last else [("ctx", 0)])
            for h in range(8):
                kv = h // 4
                for seg, qb in blocks:
                    if seg == "lat":
                        q0, nq, kts = TC + qb * 512, 512, list(range(18))
                    else:
                        q0, nq, kts = 0, 256, [0, 1]
                    q_ = qT[n % 2]
                    P.dma("sync", q_[:, 0:nq], S["QT"].t[b, h, :, q0:q0 + nq], reads=[S["QT"]], writes=[q_])
                    po = self.bank("bo", [4, 5])
                    pz = self.bank("bs", [6, 7])
                    def qk(j):
                        kt = kts[j]
                        ps = self.bank("bqk", [0, 1, 2, 3])
                        P.op("tensor", MM(ps[:, 0:nq], KTs[:, kv, kt * 128:(kt + 1) * 128], q_[:, 0:nq], True, True), reads=[KTs, q_], writes=[ps])
                        P.op("scalar", ACT(pT[j % 4][:, 0:nq], ps[:, 0:nq], AF.Exp, scale=ATTN_SCALE), reads=[ps], writes=[pT[j % 4]])

                    LOOK = 2
                    for j in range(min(LOOK, len(kts))):
                        qk(j)
                    for j, kt in enumerate(kts):
                        if j + LOOK < len(kts):
                            qk(j + LOOK)
                        p_ = pT[j % 4]
                        last = (j == len(kts) - 1)
                        P.op("tensor", MM(po[:, 0:nq], Vs[:, kt, kv * 128:(kv + 1) * 128], p_[:, 0:nq], j == 0, last), reads=[Vs, p_], writes=[po], inc=last)
                        P.op("tensor", MM(pz[:, 0:nq], C["onesb"][:], p_[:, 0:nq], j == 0, last), reads=[C["onesb"], p_], writes=[pz], inc=last)
                    r_, o_ = rec[n % 2], oT[n % 2]
                    P.op("vector", lambda e, r_=r_, pz=pz, nq=nq: e.reciprocal(out=r_[:, 0:nq], in_=pz[:, 0:nq]), reads=[pz], writes=[r_])
                    P.op("vector", TTo(o_[:, 0:nq], po[:, 0:nq], r_[:, 0:nq], ALU.mult), reads=[po, r_], writes=[o_])
                    P.dma("gpsimd", S["AOT"].t[b, h * 128:(h + 1) * 128, q0:q0 + nq], o_[:, 0:nq], reads=[o_], writes=[S["AOT"]])
                    n += 1
        P.pop()


    def phaseC(self):
        P = self.P
        P.push()
        Bst = P.sb("Bst", [128, 32, 2, 128], BF16)
        CL = P.sb("CL", [128, 32, 2, 128], BF16)
        Toep = P.sb("Toep", [128, 32, 128], BF16)
        A8 = P.sb("A8", [128, 32], F32)
        B8 = P.sb("B8", [128, 32], F32)
        self.ssm_consts(Bst, CL, Toep, A8, B8)
        self.check_stop("pC0_%d" % self.l)
        for b in range(NBC):
            self.ssm_batch(b, Bst, CL, Toep, A8, B8)
            self.check_stop("c3a")
        P.pop()

    def ssm_consts(self, Bst, CL, Toep, A8, B8):
        P, I, S, C, l = self.P, self.I, self.S, self.C, self.l
        P.push()
        V_ = "vector"

        def T(name, shape=(128, 32), dt=F32):
            return P.sb(name, list(shape), dt)

        lre, lim, ldt = T("lre"), T("lim"), T("ldt")
        for d in range(2):
            hs = slice(64 * d, 64 * d + 64)
            P.dma("sync", lre[hs, :], I["ssm_lam_re"].t[l, d].rearrange("g p -> p g"), reads=[I["ssm_lam_re"]], writes=[lre], allow_slow_non_contiguous=True)
            P.dma("sync", lim[hs, :], I["ssm_lam_im"].t[l, d].rearrange("g p -> p g"), reads=[I["ssm_lam_im"]], writes=[lim], allow_slow_non_contiguous=True)
            P.dma("sync", ldt[hs, :], I["ssm_log_dt"].t[l, d, :].partition_broadcast(64), reads=[I["ssm_log_dt"]], writes=[ldt])
        dt, tmp, mag, th, kf, kf2, r, m, sn, cs, r2 = [T(n) for n in ["dt", "tmp", "mag", "th", "kf", "kf2", "r", "m", "sn", "cs", "r2"]]
        ki = T("ki", dt=I32)
        P.op("scalar", ACT(dt[:], ldt[:], AF.Exp), reads=[ldt], writes=[dt])
        P.op(V_, TTo(tmp[:], lre[:], dt[:], ALU.mult), reads=[lre, dt], writes=[tmp])
        P.op("scalar", ACT(mag[:], tmp[:], AF.Exp), reads=[tmp], writes=[mag])
        P.op(V_, TTo(th[:], lim[:], dt[:], ALU.mult), reads=[lim, dt], writes=[th])
        P.op(V_, TS(kf[:], th[:], 1.0 / TWO_PI, None, ALU.mult), reads=[th], writes=[kf])
        P.op(V_, CP(ki[:], kf[:]), reads=[kf], writes=[ki])
        P.op(V_, CP(kf2[:], ki[:]), reads=[ki], writes=[kf2])
        P.op(V_, STT(r[:], kf2[:], -TWO_PI, th[:], ALU.mult, ALU.add), reads=[kf2, th], writes=[r])

        def wrap(x):
            P.op(V_, TS(m[:], x[:], PI, TWO_PI, ALU.is_gt, ALU.mult), reads=[x], writes=[m])
            P.op(V_, TTo(x[:], x[:], m[:], ALU.subtract), reads=[x, m], writes=[x])
            P.op(V_, TS(m[:], x[:], -PI, TWO_PI, ALU.is_lt, ALU.mult), reads=[x], writes=[m])
            P.op(V_, TTo(x[:], x[:], m[:], ALU.add), reads=[x, m], writes=[x])

        wrap(r)
        P.op("scalar", ACT(sn[:], r[:], AF.Sin), reads=[r], writes=[sn])
        P.op(V_, TS(r2[:], r[:], PI / 2, None, ALU.add), reads=[r], writes=[r2])
        wrap(r2)
        P.op("scalar", ACT(cs[:], r2[:], AF.Sin), reads=[r2], writes=[cs])
        lbr, lbi = T("lbr"), T("lbi")
        P.op(V_, TTo(lbr[:], mag[:], cs[:], ALU.mult), reads=[mag, cs], writes=[lbr])
        P.op(V_, TTo(lbi[:], mag[:], sn[:], ALU.mult), reads=[mag, sn], writes=[lbi])
        nr, den, t1, t2, fr, fi = [T(n) for n in ["nr", "den", "t1", "t2", "fr", "fi"]]
        P.op(V_, TS(nr[:], lbr[:], -1.0, None, ALU.add), reads=[lbr], writes=[nr])
        P.op(V_, TTo(t1[:], lre[:], lre[:], ALU.mult), reads=[lre], writes=[t1])
        P.op(V_, TTo(t2[:], lim[:], lim[:], ALU.mult), reads=[lim], writes=[t2])
        P.op(V_, TTo(den[:], t1[:], t2[:], ALU.add), reads=[t1, t2], writes=[den])
        P.op(V_, lambda e: e.reciprocal(out=den[:], in_=den[:]), reads=[den], writes=[den])
        P.op(V_, TTo(t1[:], nr[:], lre[:], ALU.mult), reads=[nr, lre], writes=[t1])
        P.op(V_, TTo(t2[:], lbi[:], lim[:], ALU.mult), reads=[lbi, lim], writes=[t2])
        P.op(V_, TTo(fr[:], t1[:], t2[:], ALU.add), reads=[t1, t2], writes=[fr])
        P.op(V_, TTo(fr[:], fr[:], den[:], ALU.mult), reads=[fr, den], writes=[fr])
        P.op(V_, TTo(t1[:], lbi[:], lre[:], ALU.mult), reads=[lbi, lre], writes=[t1])
        P.op(V_, TTo(t2[:], nr[:], lim[:], ALU.mult), reads=[nr, lim], writes=[t2])
        P.op(V_, TTo(fi[:], t1[:], t2[:], ALU.subtract), reads=[t1, t2], writes=[fi])
        P.op(V_, TTo(fi[:], fi[:], den[:], ALU.mult), reads=[fi, den], writes=[fi])
        ir, ii = T("ir"), T("ii")
        P.op(V_, TTo(t1[:], lbr[:], lbr[:], ALU.mult), reads=[lbr], writes=[t1])
        P.op(V_, TTo(t2[:], lbi[:], lbi[:], ALU.mult), reads=[lbi], writes=[t2])
        P.op(V_, TTo(den[:], t1[:], t2[:], ALU.add), reads=[t1, t2], writes=[den])
        P.op(V_, lambda e: e.reciprocal(out=den[:], in_=den[:]), reads=[den], writes=[den])
        P.op(V_, TTo(ir[:], lbr[:], den[:], ALU.mult), reads=[lbr, den], writes=[ir])
        P.op(V_, STT(ii[:], lbi[:], -1.0, den[:], ALU.mult, ALU.mult), reads=[lbi, den], writes=[ii])
        self.check_stop("c0a")
        LPr, LPi = T("LPr", (128, 9, 32)), T("LPi", (128, 9, 32))
        LNr, LNi = T("LNr", (128, 8, 32)), T("LNi", (128, 8, 32))
        for (Xr, Xi, br, bi, n) in [(LPr, LPi, lbr, lbi, 9), (LNr, LNi, ir, ii, 8)]:
            P.op(V_, lambda e, Xr=Xr: e.memset(Xr[:, 0, :], 1.0), writes=[Xr])
            P.op(V_, lambda e, Xi=Xi: e.memset(Xi[:, 0, :], 0.0), writes=[Xi])
            for k in range(1, n):
                P.op(V_, TTo(t1[:], Xr[:, k - 1, :], br[:], ALU.mult), reads=[Xr, br], writes=[t1])
                P.op(V_, TTo(t2[:], Xi[:, k - 1, :], bi[:], ALU.mult), reads=[Xi, bi], writes=[t2])
                P.op(V_, TTo(Xr[:, k, :], t1[:], t2[:], ALU.subtract), reads=[t1, t2], writes=[Xr])
                P.op(V_, TTo(t1[:], Xr[:, k - 1, :], bi[:], ALU.mult), reads=[Xr, bi], writes=[t1])
                P.op(V_, TTo(t2[:], Xi[:, k - 1, :], br[:], ALU.mult), reads=[Xi, br], writes=[t2])
                P.op(V_, TTo(Xi[:, k, :], t1[:], t2[:], ALU.add), reads=[t1, t2], writes=[Xi])
        P.op(V_, CP(A8[:], LPr[:, 8, :]), reads=[LPr], writes=[A8])
        P.op(V_, CP(B8[:], LPi[:, 8, :]), reads=[LPi], writes=[B8])
        G3 = (128, 32, 16)
        Bre, Bim, Bbr, Bbi, CTr, CTi = [T(n, G3) for n in ["Bre", "Bim", "Bbr", "Bbi", "CTr", "CTi"]]
        for d in range(2):
            hs = slice(64 * d, 64 * d + 64)
            P.dma("sync", Bre[hs], I["ssm_b_re"].t[l, d].rearrange("g p h -> p g h"), reads=[I["ssm_b_re"]], writes=[Bre], allow_slow_non_contiguous=True)
            P.dma("sync", Bim[hs], I["ssm_b_im"].t[l, d].rearrange("g p h -> p g h"), reads=[I["ssm_b_im"]], writes=[Bim], allow_slow_non_contiguous=True)
        craw = [T("craw%d" % i, (128, 2, 64)) for i in range(2)]
        n = 0
        for (src, dst) in [("ssm_c_re", CTr), ("ssm_c_im", CTi)]:
            for gb in range(4):
                cr_ = craw[n % 2]
                n += 1
                for d in range(2):
                    P.dma("sync", cr_[:, d, :], I[src].t[l, d, gb * 8:(gb + 1) * 8].rearrange("g h p -> (g h) p"), reads=[I[src]], writes=[cr_])
                ps = self.bank("c0", [0, 1])
                P.op("tensor", TR(ps[:, 0:128], cr_[:].rearrange("q d p -> q (d p)"), C["identf"][:]), reads=[cr_, C["identf"]], writes=[ps])
                P.op(V_, CP(dst[:, gb * 8:(gb + 1) * 8, :].rearrange("q g h -> q (g h)"), ps[:, 0:128]), reads=[ps], writes=[dst])

        self.check_stop("c0b")

        def bc(x_ap):
            return x_ap.unsqueeze(2).to_broadcast([128, 32, 16])

        u1, u2, u3, u4 = [T(n, G3) for n in ["u1", "u2", "u3", "u4"]]
        Rr = [T("Rr%d" % i, G3) for i in range(2)]
        Ri = [T("Ri%d" % i, G3) for i in range(2)]
        fcnt = [0]

        def cprod(lr, li, Xr, Xi, lbufs, neg_im=False):
            i = fcnt[0]
            fcnt[0] += 1
            rr, ri = Rr[i % 2], Ri[i % 2]
            P.op(V_, TTo(u1[:], Xr[:], bc(lr), ALU.mult), reads=[Xr] + lbufs, writes=[u1])
            P.op("gpsimd", TTo(u2[:], Xi[:], bc(li), ALU.mult), reads=[Xi] + lbufs, writes=[u2])
            P.op(V_, TTo(u3[:], Xi[:], bc(lr), ALU.mult), reads=[Xi] + lbufs, writes=[u3])
            P.op("gpsimd", TTo(u4[:], Xr[:], bc(li), ALU.mult), reads=[Xr] + lbufs, writes=[u4])
            P.op(V_, TTo(rr[:], u1[:], u2[:], ALU.subtract), reads=[u1, u2], writes=[rr])
            if neg_im:
                P.op("gpsimd", STT_pool(ri[:], u3[:], u4[:]), reads=[u3, u4], writes=[ri])
            else:
                P.op("gpsimd", TTo(ri[:], u3[:], u4[:], ALU.add), reads=[u3, u4], writes=[ri])
            return rr, ri

        def STT_pool(out, a, b_):
            def f(e):
                e.tensor_scalar(out=a, in0=a, scalar1=-1.0, scalar2=None, op0=ALU.mult)
                return e.tensor_tensor(out=out, in0=a, in1=b_, op=ALU.subtract)
            return f

        rr, ri = cprod(fr[:], fi[:], Bre, Bim, [fr, fi])
        P.op(V_, CP(Bbr[:], rr[:]), reads=[rr], writes=[Bbr])
        P.op(V_, CP(Bbi[:], ri[:]), reads=[ri], writes=[Bbi])

        G4 = (128, 32, 8, 16)
        BSr, BSi, Xr_, Xi_, PCr, PCi = [T(n, G4) for n in ["BSr", "BSi", "Xr_", "Xi_", "PCr", "PCi"]]
        F_, R_ = slice(0, 64), slice(64, 128)
        for Pc in (PCr, PCi):
            P.op("gpsimd", lambda e, Pc=Pc: e.memset(Pc[R_], 0.0), writes=[Pc])
        CLv = CL[:].rearrange("q g r (i h) -> q g r i h", i=8)
        cpe = ["gpsimd", "scalar"]
        cc = [0]

        def cpy(dst_ap, dst_buf, src_ap, src_buf):
            e = cpe[cc[0] % 2]
            cc[0] += 1
            if e == "scalar":
                P.op(e, ACT(dst_ap, src_ap, AF.Copy), reads=[src_buf], writes=[dst_buf])
            else:
                P.op(e, CP(dst_ap, src_ap), reads=[src_buf], writes=[dst_buf])

        for k in range(8):
            rr, ri = cprod(LPr[:, k, :], LPi[:, k, :], Bbr, Bbi, [LPr, LPi])
            cpy(BSr[F_, :, 7 - k, :], BSr, rr[F_], rr)
            cpy(BSi[F_, :, 7 - k, :], BSi, ri[F_], ri)
            cpy(BSr[R_, :, k, :], BSr, rr[R_], rr)
            cpy(BSi[R_, :, k, :], BSi, ri[R_], ri)
        for k in range(8):
            rr, ri = cprod(LNr[:, k, :], LNi[:, k, :], Bbr, Bbi, [LNr, LNi])
            cpy(Xr_[F_, :, k, :], Xr_, rr[F_], rr)
            cpy(Xi_[F_, :, k, :], Xi_, ri[F_], ri)
            rr, ri = cprod(LNr[:, k, :], LNi[:, k, :], CTr, CTi, [LNr, LNi], neg_im=True)
            cpy(Xr_[R_, :, k, :], Xr_, rr[R_], rr)
            cpy(Xi_[R_, :, k, :], Xi_, ri[R_], ri)
        for k in range(9):
            rr, ri = cprod(LPr[:, k, :], LPi[:, k, :], CTr, CTi, [LPr, LPi], neg_im=True)
            if k <= 7:
                cpy(PCr[F_, :, k, :], PCr, rr[F_], rr)
                cpy(PCi[F_, :, k, :], PCi, ri[F_], ri)
            if k >= 1:
                cpy(CLv[F_, :, 0, k - 1, :], CL, rr[F_], rr)
                cpy(CLv[F_, :, 1, k - 1, :], CL, ri[F_], ri)
                cpy(CLv[R_, :, 0, 8 - k, :], CL, rr[R_], rr)
                cpy(CLv[R_, :, 1, 8 - k, :], CL, ri[R_], ri)
        self.check_stop("c0c")
        for g2 in range(16):
            ps = self.bank("c0", [0, 1])
            for gg in range(2):
                g = g2 * 2 + gg
                for ri_, Bs in enumerate((BSr, BSi)):
                    sl = (gg * 2 + ri_) * 128
                    P.op("tensor", TR(ps[:, sl:sl + 128], Bs[:, g].rearrange("q j h -> q (j h)"), C["identf"][:]), reads=[Bs, C["identf"]], writes=[ps],
                         inc=(gg == 1 and ri_ == 1))
            P.op(V_ if g2 % 2 else "scalar", (CP if g2 % 2 else (lambda o, i_: ACT(o, i_, AF.Copy)))(
                Bst[:, g2 * 2:g2 * 2 + 2].rearrange("q g r c -> q (g r c)"), ps[:, 0:512]), reads=[ps], writes=[Bst])
        self.check_stop("c0d")
        for Bs in (BSr, BSi):
            P.op("gpsimd", lambda e, Bs=Bs: e.memset(Bs[F_], 0.0), reads=[Bs], writes=[Bs])
        maskf, maskr, Dcol = T("maskf", (128, 128)), T("maskr", (128, 128)), T("Dcol", (128, 32))
        P.dma("sync", maskf[:], I["k_maskf"].t, reads=[I["k_maskf"]], writes=[maskf])
        P.dma("sync", maskr[:], I["k_maskr"].t, reads=[I["k_maskr"]], writes=[maskr])
        for j in range(8):
            P.dma("sync", Dcol[j * 16:(j + 1) * 16, :], I["ssm_d"].t[l, :].rearrange("(g h) -> h g", h=16), reads=[I["ssm_d"]], writes=[Dcol],
                  allow_slow_non_contiguous=True)
        tf = [T("tf%d" % i, (128, 128)) for i in range(2)]
        tr = [T("tr%d" % i, (128, 128)) for i in range(2)]
        for g in range(32):
            pf = self.bank("c0", [0, 1])
            pr = self.bank("c1", [2, 3])
            fl = lambda X: X[:, g].rearrange("q j h -> q (j h)")
            P.op("tensor", MM(pf[:, 0:128], fl(Xr_), fl(PCr), True, False), reads=[Xr_, PCr], writes=[pf], inc=False)
            P.op("tensor", MM(pf[:, 0:128], fl(Xi_), fl(PCi), False, True), reads=[Xi_, PCi], writes=[pf])
            P.op("tensor", MM(pr[:, 0:128], fl(BSr), fl(Xr_), True, False), reads=[BSr, Xr_], writes=[pr], inc=False)
            P.op("tensor", MM(pr[:, 0:128], fl(BSi), fl(Xi_), False, True), reads=[BSi, Xi_], writes=[pr])
            a_, b_ = tf[g % 2], tr[g % 2]
            P.op(V_, TTo(a_[:], pf[:, 0:128], maskf[:], ALU.mult), reads=[pf, maskf], writes=[a_])
            P.op(V_, TTo(b_[:], pr[:, 0:128], maskr[:], ALU.mult), reads=[pr, maskr], writes=[b_])
            P.op("gpsimd", TTo(a_[:], a_[:], b_[:], ALU.add), reads=[a_, b_], writes=[a_])
            P.op(V_, STT(Toep[:, g, :], C["identf"][:], Dcol[:, g:g + 1], a_[:], ALU.mult, ALU.add), reads=[C["identf"], Dcol, a_], writes=[Toep])
        P.pop()

    def ssm_batch(self, b, Bst, CL, Toep, A8, B8):
        P, I, S, C, l = self.P, self.I, self.S, self.C, self.l
        P.push()
        Ust = P.sb("Ust", [128, 32, NCH], BF16)
        ZSr = P.sb("ZSr", [128, 32, NCH], BF16)
        ZSi = P.sb("ZSi", [128, 32, NCH], BF16)
        Sr = P.sb("Sr", [128, 32, 290], BF16)
        Si = P.sb("Si", [128, 32, 290], BF16)
        F_, R_ = slice(0, 64), slice(64, 128)
        P.push()
        utm = [P.sb("utm%d" % i, [96, 4096], BF16) for i in range(2)]
        utm2 = [P.sb("utm2%d" % i, [96, 4096], BF16) for i in range(2)]
        for tile in range(3):
            u_ = utm[tile % 2]
            P.dma("sync", u_[0:96, :], S["U"].t[b, tile * 768:(tile + 1) * 768, :].rearrange("(c j) ch -> c (j ch)", j=8), reads=[S["U"]], writes=[u_])
            u2 = utm2[tile % 2]
            P.op("gpsimd", CP(u2[0:96, :].rearrange("c (g j h) -> c g j h", g=32, j=8), u_[0:96, :].rearrange("c (j g h) -> c g j h", j=8, g=32)),
                 reads=[u_], writes=[u2])
            uv = u2[0:96, :].rearrange("c (g x) -> c g x", g=32)
            for g8 in range(4):
                ps = self.bank("cs", [4, 5])
                pb = ps[:].bitcast(BF16).rearrange("p (k t) -> p k t", k=8)
                for gg in range(8):
                    P.op("tensor", TR(pb[:, gg, 0:96], uv[:, g8 * 8 + gg, :], C["identb"][0:96, 0:96]), reads=[u2, C["identb"]], writes=[ps], inc=(gg == 7))
                P.op("vector" if g8 % 2 else "scalar",
                     (CP if g8 % 2 else (lambda o, i_: ACT(o, i_, AF.Copy)))(Ust[:, g8 * 8:(g8 + 1) * 8, tile * 96:(tile + 1) * 96], pb[:, :, 0:96]),
                     reads=[ps], writes=[Ust])
        P.pop()
        self.check_stop("c1a")
        for g in range(32):
            for ri_, Z in enumerate((ZSr, ZSi)):
                ps = self.bank("cz", [0, 1, 2, 3])
                P.op("tensor", MM(ps[:, 0:NCH], Bst[:, g, ri_, :], Ust[:, g, :], True, True), reads=[Bst, Ust], writes=[ps])
                if ri_ == 0:
                    P.op("vector", CP(Z[:, g, :], ps[:, 0:NCH]), reads=[ps], writes=[Z])
                else:
                    P.op("scalar", ACT(Z[:, g, :], ps[:, 0:NCH], AF.Copy), reads=[ps], writes=[Z])
        self.check_stop("c1b")
        NR = 4
        Rg = [P.sb("Rg%d" % i, [128, 32], F32) for i in range(NR)]
        Ig = [P.sb("Ig%d" % i, [128, 32], F32) for i in range(NR)]
        tt = [[P.sb("sc%d_%d" % (i, j), [128, 32], F32) for j in range(6)] for i in range(2)]
        P.op("vector", lambda e: e.memset(Rg[0][:], 0.0), writes=[Rg[0]])
        P.op("vector", lambda e: e.memset(Ig[0][:], 0.0), writes=[Ig[0]])
        for Sx in (Sr, Si):
            P.op("gpsimd", lambda e, Sx=Sx: e.memset(Sx[F_, :, 1:2], 0.0), writes=[Sx])
            P.op("gpsimd", lambda e, Sx=Sx: e.memset(Sx[R_, :, 32:33], 0.0), writes=[Sx])
        order_r = list(range(31, -1, -1)) + list(range(287, 31, -1))
        V_ = "vector"
        for n in range(NCH):
            cf, cr = n, order_r[n]
            Rp, Ip, Rn, In = Rg[n % NR], Ig[n % NR], Rg[(n + 1) % NR], Ig[(n + 1) % NR]
            t1, t2, t3, t4, t5, t6 = tt[n % 2]
            P.op(V_, TTo(t1[:], A8[:], Rp[:], ALU.mult), reads=[A8, Rp], writes=[t1])
            P.op(V_, TTo(t2[:], B8[:], Ip[:], ALU.mult), reads=[B8, Ip], writes=[t2])
            P.op(V_, TTo(t4[:], B8[:], Rp[:], ALU.mult), reads=[B8, Rp], writes=[t4])
            P.op(V_, TTo(t5[:], A8[:], Ip[:], ALU.mult), reads=[A8, Ip], writes=[t5])
            P.op(V_, TTo(t3[:], t1[:], t2[:], ALU.subtract), reads=[t1, t2], writes=[t3])
            P.op(V_, TTo(t6[:], t4[:], t5[:], ALU.add), reads=[t4, t5], writes=[t6])
            P.op(V_, TTo(Rn[F_], t3[F_], ZSr[F_, :, cf], ALU.add), reads=[t3, ZSr], writes=[Rn])
            P.op(V_, TTo(Rn[R_], t3[R_], ZSr[R_, :, cr], ALU.add), reads=[t3, ZSr], writes=[Rn])
            P.op(V_, TTo(In[F_], t6[F_], ZSi[F_, :, cf], ALU.add), reads=[t6, ZSi], writes=[In])
            P.op(V_, TTo(In[R_], t6[R_], ZSi[R_, :, cr], ALU.add), reads=[t6, ZSi], writes=[In])
            sf = cf + 2
            for Sx, Xn in ((Sr, Rn), (Si, In)):
                P.op("gpsimd", CP(Sx[F_, :, sf], Xn[F_]), reads=[Xn], writes=[Sx])
                if cr != 32:
                    P.op("gpsimd", CP(Sx[R_, :, cr], Xn[R_]), reads=[Xn], writes=[Sx])
                if cr == 0:
                    P.op("gpsimd", CP(Sx[R_, :, 288], Xn[R_]), reads=[Xn], writes=[Sx])
        self.check_stop("c1c")
        P.push()
        Wg = P.sb("Wg", [128, 4, 2 * D], BF16)
        P.dma("sync", Wg[:], S["wb_glu%d" % l].t.rearrange("(t p) n -> p t n", p=128), reads=[S["wb_glu%d" % l]], writes=[Wg])
        ygl = P.sb("ygl", [128, 8, 512], BF16)
        yT = P.sb("yT", [128, 4, 1024], BF16)
        yTv = yT[:].rearrange("p t (c i) -> p t c i", i=8)
        sig = [P.sb("sig%d" % i, [128, 512], F32) for i in range(2)]
        gst = [P.sb("gst%d" % i, [128, 1024], BF16) for i in range(2)]
        ydb = P.sb("ydb", [128, 8, 512], F32) if "YS" in self.dbg else None
        blocks = ([] if self.last else [(0, 32, 1)]) + [(32, 128, 2), (160, 128, 2)]
        nsg = 0
        for (c0, M, roff) in blocks:
            for g4 in range(8):
                ps = self.bank("cr", [0, 1, 2, 3])
                for gg in range(4):
                    g = g4 * 4 + gg
                    o = ps[0:M, gg * 128:(gg + 1) * 128]
                    P.op("tensor", MM(o, Ust[:, g, c0:c0 + M], Toep[:, g, :], True, False), reads=[Ust, Toep], writes=[ps], inc=False)
                    P.op("tensor", MM(o, Sr[:, g, c0 + 1:c0 + 1 + M], CL[:, g, 0, :], False, False), reads=[Sr, CL], writes=[ps], inc=False)
                    P.op("tensor", MM(o, Si[:, g, c0 + 1:c0 + 1 + M], CL[:, g, 1, :], False, True), reads=[Si, CL], writes=[ps], inc=(gg == 3))
                pv = ps[0:M, :].rearrange("c (g i h) -> c i g h", g=4, i=8)
                yo = ygl[0:M, :, g4 * 64:(g4 + 1) * 64].rearrange("c i (g h) -> c i g h", g=4)
                P.op("scalar", ACT(yo, pv, AF.Gelu_apprx_tanh), reads=[ps], writes=[ygl])
                if ydb is not None:
                    P.op("vector", CP(ydb[0:M, :, g4 * 64:(g4 + 1) * 64].rearrange("c i (g h) -> c i g h", g=4), pv), reads=[ps], writes=[ydb])
            self.check_stop("c2a")
            if ydb is not None:
                P.dma("gpsimd", S["YS"].t[b, c0 * 8:(c0 + M) * 8, :].rearrange("(c i) ch -> c i ch", i=8), ydb[0:M], reads=[ydb], writes=[S["YS"]])
            self.check_stop("c2b")
            for i in range(8):
                ps = self.bank("cs", [4, 5])
                pb = ps[:].bitcast(BF16).rearrange("p (k t) -> p k t", k=8)
                for t in range(4):
                    P.op("tensor", TR(pb[:, t, 0:M], ygl[0:M, i, t * 128:(t + 1) * 128], C["identb"][0:M, 0:M]), reads=[ygl, C["identb"]], writes=[ps], inc=(t == 3))
                P.op("vector", CP(yTv[:, :, 0:M, i], pb[:, 0:4, 0:M]), reads=[ps], writes=[yT])
            self.check_stop("c2c")
            ntok = M * 8
            for ct in range(8):
                g_ = gst[ct % 2]
                for n0 in range(0, ntok, 512):
                    nn = min(512, ntok - n0)
                    pa = self.bank("cg", [6, 7, 0, 1, 2, 3])
                    pg = self.bank("cg", [6, 7, 0, 1, 2, 3])
                    for t in range(4):
                        P.op("tensor", MM(pa[:, 0:nn], Wg[:, t, ct * 128:(ct + 1) * 128], yT[:, t, n0:n0 + nn], t == 0, t == 3), reads=[Wg, yT], writes=[pa], inc=(t == 3))
                    for t in range(4):
                        P.op("tensor", MM(pg[:, 0:nn], Wg[:, t, D + ct * 128:D + (ct + 1) * 128], yT[:, t, n0:n0 + nn], t == 0, t == 3), reads=[Wg, yT], writes=[pg], inc=(t == 3))
                    sg = sig[nsg % 2]
                    nsg += 1
                    P.op("scalar", ACT(sg[:, 0:nn], pg[:, 0:nn], AF.Sigmoid), reads=[pg], writes=[sg])
                    P.op("vector", TTo(g_[:, n0:n0 + nn], pa[:, 0:nn], sg[:, 0:nn], ALU.mult), reads=[pa, sg], writes=[g_])
                P.dma("gpsimd", S["SSMT"].t[b, ct * 128:(ct + 1) * 128, c0 * 8:c0 * 8 + ntok], g_[:, 0:ntok], reads=[g_], writes=[S["SSMT"]])
            self.check_stop("c2d")
            if c0 == 32:
                self.check_stop("c2e")
        P.pop()
        P.pop()


    def postnorm(self, pss, xt, gbc, lng, lnb, upd, yv, yn, sm, dst_buf, dst_ap):
        P = self.P
        for half in range(2):
            P.op("vector", TTo(upd[:, half * 512:(half + 1) * 512], pss[half][:, 0:512], gbc[:, half * 512:(half + 1) * 512], ALU.mult),
                 reads=[pss[half], gbc], writes=[upd])
        P.op("vector", STT(yv[:], xt[:], ALPHA, upd[:], ALU.mult, ALU.add), reads=[xt, upd], writes=[yv])
        self.ln_tile(yv, yn[:], yn, sm)
        P.op("gpsimd", TTo(yn[:], yn[:], lng[:], ALU.mult), reads=[yn, lng], writes=[yn])
        P.op("gpsimd", TTo(yn[:], yn[:], lnb[:], ALU.add), reads=[yn, lnb], writes=[yn])
        P.dma("gpsimd", dst_ap, yn[:], reads=[yn], writes=[dst_buf])

    def segs(self):
        out = []
        for b in range(NBC):
            if not self.last:
                out.append((b, "ctx", 2, 0, TC))
            out.append((b, "lat", b, TC, TL))
        return out

    def phaseD(self):
        P, I, S, C, l = self.P, self.I, self.S, self.C, self.l
        P.push()
        Wco = P.sb("Wco", [128, 4, D], BF16)
        Wao = P.sb("Wao", [128, 8, D], BF16)
        Wo = P.sb("Wo", [128, 8, D], BF16)
        for Wt, nm in ((Wco, "wb_co"), (Wao, "wb_ao"), (Wo, "wb_o")):
            src = S[nm + str(l)]
            P.dma("sync", Wt[:], src.t.rearrange("(t p) n -> p t n", p=128), reads=[src], writes=[Wt])
        cw = P.sb("cw", [128, 4, 3], F32)
        for k in range(3):
            P.dma("sync", cw[:, :, k], I["conv_w"].t[l, k, :].rearrange("(t p) -> p t", p=128), reads=[I["conv_w"]], writes=[cw], allow_slow_non_contiguous=True)
        lng = P.sb("lng", [128, D], F32)
        lnb = P.sb("lnb", [128, D], F32)
        P.dma("sync", lng[:], I["ln1_g"].t[l, :].partition_broadcast(128), reads=[I["ln1_g"]], writes=[lng])
        P.dma("sync", lnb[:], I["ln1_b"].t[l, :].partition_broadcast(128), reads=[I["ln1_b"]], writes=[lnb])
        gbc = P.sb("gbc", [128, D], F32)
        axh = P.sb("axh", [128, 4, 514], BF16)
        cgh = P.sb("cgh", [128, 4, 514], BF16)
        bgs = P.sb("bgs", [128, 4, 512], BF16)
        prod = P.sb("prod", [128, 4, 514], F32)
        acc = P.sb("acc", [128, 4, 512], F32)
        convT = P.sb("convT", [128, 4, 512], BF16)
        aoT = P.sb("aoT", [128, 8, 512], BF16)
        gts = [P.sb("gts%d" % i, [128, 3, 512], BF16) for i in range(2)]
        ssmT = [P.sb("ssmT%d" % i, [128, 512], BF16) for i in range(2)]
        m1 = [P.sb("m1_%d" % i, [128, 512], F32) for i in range(2)]
        m2 = [P.sb("m2_%d" % i, [128, 512], F32) for i in range(2)]
        m3 = [P.sb("m3_%d" % i, [128, 512], F32) for i in range(2)]
        mgT = P.sb("mgT", [128, 8, 512], BF16)
        xt = [P.sb("dxt%d" % i, [128, D], F32) for i in range(2)]
        upd = [P.sb("dupd%d" % i, [128, D], F32) for i in range(2)]
        yv = [P.sb("dyv%d" % i, [128, D], F32) for i in range(2)]
        yn = [P.sb("dyn%d" % i, [128, D], F32) for i in range(2)]
        sm = self.ln_small("d")
        nx = 0
        nd = 0
        for (b, seg, r, gofs, slen) in self.segs():
            P.dma("sync", gbc[:], S["modr%d" % l].t[r, 2 * D:3 * D].partition_broadcast(128), reads=[S["modr%d" % l]], writes=[gbc])
            rbuf, rap = self.res_in(b, seg)
            n = min(512, slen)
            for t0 in range(0, slen, n):
                g0 = gofs + t0
                lo = 1 if t0 == 0 else 0
                hi = n + 1 if t0 + n == slen else n + 2
                for (hb, nm) in ((axh, "AXT"), (cgh, "CGT")):
                    if lo == 1:
                        P.op("gpsimd", lambda e, hb=hb: e.memset(hb[:, :, 0:1], 0.0), writes=[hb])
                    if hi == n + 1:
                        P.op("gpsimd", lambda e, hb=hb, n=n: e.memset(hb[:, :, n + 1:n + 2], 0.0), writes=[hb])
                    P.dma("sync", hb[:, :, lo:hi], S[nm].t[b].rearrange("(t p) c -> p t c", p=128)[:, :, g0 - 1 + lo:g0 - 1 + hi], reads=[S[nm]], writes=[hb])
                P.dma("sync", bgs[:, :, 0:n], S["BGT"].t[b].rearrange("(t p) c -> p t c", p=128)[:, :, g0:g0 + n], reads=[S["BGT"]], writes=[bgs])
                P.dma("sync", aoT[:, :, 0:n], S["AOT"].t[b].rearrange("(t p) c -> p t c", p=128)[:, :, g0:g0 + n], reads=[S["AOT"]], writes=[aoT])
                P.op("gpsimd", TTo(prod[:, :, 0:n + 2], cgh[:, :, 0:n + 2], axh[:, :, 0:n + 2], ALU.mult), reads=[cgh, axh], writes=[prod])
                for t in range(4):
                    P.op("vector", TS(acc[:, t, 0:n], prod[:, t, 0:n], cw[:, t, 0:1], None, ALU.mult), reads=[prod, cw], writes=[acc])
                    P.op("vector", STT(acc[:, t, 0:n], prod[:, t, 1:n + 1], cw[:, t, 1:2], acc[:, t, 0:n], ALU.mult, ALU.add), reads=[prod, cw, acc], writes=[acc])
                    P.op("vector", STT(acc[:, t, 0:n], prod[:, t, 2:n + 2], cw[:, t, 2:3], acc[:, t, 0:n], ALU.mult, ALU.add), reads=[prod, cw, acc], writes=[acc])
                P.op("gpsimd", TTo(convT[:, :, 0:n], acc[:, :, 0:n], bgs[:, :, 0:n], ALU.mult), reads=[acc, bgs], writes=[convT])
                for dt in range(8):
                    gt_, sm_ = gts[nd % 2], ssmT[nd % 2]
                    a1, a2, a3 = m1[nd % 2], m2[nd % 2], m3[nd % 2]
                    nd += 1
                    for s3 in range(3):
                        P.dma("sync", gt_[:, s3, 0:n], S["GT"].t[b, s3 * D + dt * 128:s3 * D + (dt + 1) * 128, g0:g0 + n], reads=[S["GT"]], writes=[gt_])
                    P.dma("sync", sm_[:, 0:n], S["SSMT"].t[b, dt * 128:(dt + 1) * 128, g0:g0 + n], reads=[S["SSMT"]], writes=[sm_])
                    pc = self.bank("dc", [0, 1])
                    pa = self.bank("da", [2, 3])
                    for t in range(4):
                        P.op("tensor", MM(pc[:, 0:n], Wco[:, t, dt * 128:(dt + 1) * 128], convT[:, t, 0:n], t == 0, t == 3), reads=[Wco, convT], writes=[pc], inc=(t == 3))
                    for k in range(8):
                        P.op("tensor", MM(pa[:, 0:n], Wao[:, k, dt * 128:(dt + 1) * 128], aoT[:, k, 0:n], k == 0, k == 7), reads=[Wao, aoT], writes=[pa], inc=(k == 7))
                    P.op("vector", TTo(a1[:, 0:n], pc[:, 0:n], gt_[:, 0, 0:n], ALU.mult), reads=[pc, gt_], writes=[a1])
                    P.op("vector", TTo(a2[:, 0:n], pa[:, 0:n], gt_[:, 2, 0:n], ALU.mult), reads=[pa, gt_], writes=[a2])
                    P.op("gpsimd", TTo(a3[:, 0:n], sm_[:, 0:n], gt_[:, 1, 0:n], ALU.mult), reads=[sm_, gt_], writes=[a3])
                    P.op("gpsimd", TTo(a1[:, 0:n], a1[:, 0:n], a2[:, 0:n], ALU.add), reads=[a1, a2], writes=[a1])
                    P.op("gpsimd", TTo(mgT[:, dt, 0:n], a1[:, 0:n], a3[:, 0:n], ALU.add), reads=[a1, a3], writes=[mgT])
                for ti in range(n // 128):
                    i = nx
                    nx += 1
                    x_ = xt[i % 2]
                    P.dma("sync", x_[:], rap[t0 + ti * 128:t0 + (ti + 1) * 128, :], reads=[rbuf], writes=[x_])
                    pss = []
                    for half in range(2):
                        ps = self.bank("do", [4, 5, 6, 7])
                        for k in range(8):
                            P.op("tensor", MM(ps[:, 0:512], mgT[:, k, ti * 128:(ti + 1) * 128], Wo[:, k, half * 512:(half + 1) * 512], k == 0, k == 7),
                                 reads=[mgT, Wo], writes=[ps], inc=(k == 7))
                        pss.append(ps)
                    row = g0 + ti * 128
                    self.postnorm(pss, x_, gbc, lng, lnb, upd[i % 2], yv[i % 2], yn[i % 2], sm[i % 2], S["resA"], S["resA"].t[b, row:row + 128, :])
        P.pop()

    def phaseF(self):
        P, I, S, C, l = self.P, self.I, self.S, self.C, self.l
        P.push()
        lng = P.sb("lng2", [128, D], F32)
        lnb = P.sb("lnb2", [128, D], F32)
        P.dma("sync", lng[:], I["ln2_g"].t[l, :].partition_broadcast(128), reads=[I["ln2_g"]], writes=[lng])
        P.dma("sync", lnb[:], I["ln2_b"].t[l, :].partition_broadcast(128), reads=[I["ln2_b"]], writes=[lnb])
        cwf = P.sb("cwf", [128, 22, 3], F32)
        cbf = P.sb("cbf", [128, 22], F32)
        for k in range(3):
            P.dma("sync", cwf[:, :, k], I["ffn_conv_w"].t[l, k, :].rearrange("(t p) -> p t", p=128), reads=[I["ffn_conv_w"]], writes=[cwf], allow_slow_non_contiguous=True)
        P.dma("sync", cbf[:], I["ffn_conv_b"].t[l, :].rearrange("(t p) -> p t", p=128), reads=[I["ffn_conv_b"]], writes=[cbf], allow_slow_non_contiguous=True)
        hff = P.sb("hff", [128, 22, TL], BF16)
        gbc = P.sb("gbc2", [128, D], F32)
        wup = S["wb_up%d" % l].t.rearrange("(k p) n -> p k n", p=128)
        for (b, seg, r, gofs, slen) in self.segs():
            ntile = slen // 128
            P.push()
            hT2 = P.sb("hT2", [128, 8, slen], BF16)
            xt = [P.sb("fxt%d" % i, [128, D], F32) for i in range(2)]
            xn = [P.sb("fxn%d" % i, [128, D], BF16) for i in range(2)]
            sm = self.ln_small("f")
            for ti in range(ntile):
                x_, n_ = xt[ti % 2], xn[ti % 2]
                row = gofs + ti * 128
                P.dma("sync", x_[:], S["resA"].t[b, row:row + 128, :], reads=[S["resA"]], writes=[x_])
                self.ln_tile(x_, n_[:], n_, sm[ti % 2])
                ps = self.bank("ftr", [0, 1])
                pb = ps[:].bitcast(BF16).rearrange("p (k t) -> p k t", k=8)
                for k in range(8):
                    P.op("tensor", TR(pb[:, k, :], n_[:, k * 128:(k + 1) * 128], C["identb"][:]), reads=[n_, C["identb"]], writes=[ps], inc=(k == 7))
                for k in range(8):
                    o = hT2[:, k, ti * 128:(ti + 1) * 128]
                    if k % 2 == 0:
                        P.op("vector", TS(o, pb[:, k, :], self.modT[:, 4, k, r:r + 1], self.modT[:, 3, k, r:r + 1], ALU.mult, ALU.add), reads=[ps, self.modT], writes=[hT2])
                    else:
                        P.op("scalar", ACT(o, pb[:, k, :], AF.Identity, bias=self.modT[:, 3, k, r:r + 1], scale=self.modT[:, 4, k, r:r + 1]), reads=[ps, self.modT], writes=[hT2])
            wu = [P.sb("wu%d" % i, [128, 8, 128], BF16) for i in range(2)]
            wv = [P.sb("wv%d" % i, [128, 8, 128], BF16) for i in range(2)]
            ucp = [P.sb("ucp%d" % i, [128, slen + 2], F32) for i in range(2)]
            acc = P.sb("facc", [128, slen], F32)
            ge = [P.sb("fge%d" % i, [128, slen], BF16) for i in range(2)]
            for u_ in ucp:
                P.op("gpsimd", lambda e, u_=u_: e.memset(u_[:, 0:1], 0.0), writes=[u_])
                P.op("gpsimd", lambda e, u_=u_, slen=slen: e.memset(u_[:, slen + 1:slen + 2], 0.0), writes=[u_])
            nbs = [(n0, min(512, slen - n0)) for n0 in range(0, slen, 512)]
            for j in range(22):
                wu_, wv_, u_, g_ = wu[j % 2], wv[j % 2], ucp[j % 2], ge[j % 2]
                P.dma("sync", wu_[:], wup[:, :, j * 128:(j + 1) * 128], reads=[S["wb_up%d" % l]], writes=[wu_])
                P.dma("sync", wv_[:], wup[:, :, DFF + j * 128:DFF + (j + 1) * 128], reads=[S["wb_up%d" % l]], writes=[wv_])
                for (n0, nn) in nbs:
                    ps = self.bank("fu", [0, 1, 2, 3])
                    for k in range(8):
                        P.op("tensor", MM(ps[:, 0:nn], wu_[:, k, :], hT2[:, k, n0:n0 + nn], k == 0, k == 7), reads=[wu_, hT2], writes=[ps], inc=(k == 7))
                    P.op("scalar", ACT(u_[:, 1 + n0:1 + n0 + nn], ps[:, 0:nn], AF.Copy), reads=[ps], writes=[u_])
                P.op("vector", TS(acc[:], u_[:, 0:slen], cwf[:, j, 0:1], None, ALU.mult), reads=[u_, cwf], writes=[acc])
                P.op("vector", STT(acc[:], u_[:, 1:slen + 1], cwf[:, j, 1:2], acc[:], ALU.mult, ALU.add), reads=[u_, cwf, acc], writes=[acc])
                P.op("vector", STT(acc[:], u_[:, 2:slen + 2], cwf[:, j, 2:3], acc[:], ALU.mult, ALU.add), reads=[u_, cwf, acc], writes=[acc])
                P.op("scalar", ACT(g_[:], acc[:], AF.Gelu_apprx_tanh, bias=cbf[:, j:j + 1], scale=1.0), reads=[acc, cbf], writes=[g_])
                for (n0, nn) in nbs:
                    ps = self.bank("fv", [4, 5, 6, 7])
                    for k in range(8):
                        P.op("tensor", MM(ps[:, 0:nn], wv_[:, k, :], hT2[:, k, n0:n0 + nn], k == 0, k == 7), reads=[wv_, hT2], writes=[ps], inc=(k == 7))
                    P.op("vector", TTo(hff[:, j, n0:n0 + nn], ps[:, 0:nn], g_[:, n0:n0 + nn], ALU.mult), reads=[ps, g_], writes=[hff])
            P.pop()
            P.push()
            Wd = P.sb("Wd", [128, 22, D], BF16)
            P.dma("sync", Wd[:], S["wb_dn%d" % l].t.rearrange("(t p) n -> p t n", p=128), reads=[S["wb_dn%d" % l]], writes=[Wd])
            P.dma("sync", gbc[:], S["modr%d" % l].t[r, 5 * D:6 * D].partition_broadcast(128), reads=[S["modr%d" % l]], writes=[gbc])
            xt = [P.sb("gxt%d" % i, [128, D], F32) for i in range(2)]
            upd = [P.sb("gupd%d" % i, [128, D], F32) for i in range(2)]
            yv = [P.sb("gyv%d" % i, [128, D], F32) for i in range(2)]
            yn = [P.sb("gyn%d" % i, [128, D], F32) for i in range(2)]
            sm = self.ln_small("g")
            for ti in range(ntile):
                x_ = xt[ti % 2]
                row = gofs + ti * 128
                P.dma("sync", x_[:], S["resA"].t[b, row:row + 128, :], reads=[S["resA"]], writes=[x_])
                pss = []
                for half in range(2):
                    ps = self.bank("fd", [0, 1, 2, 3])
                    for j in range(22):
                        P.op("tensor", MM(ps[:, 0:512], hff[:, j, ti * 128:(ti + 1) * 128], Wd[:, j, half * 512:(half + 1) * 512], j == 0, j == 21),
                             reads=[hff, Wd], writes=[ps], inc=(j == 21))
                    pss.append(ps)
                if self.last:
                    dbuf, dap = self.out, self.out.t[b, ti * 128:(ti + 1) * 128, :]
                else:
                    dbuf, dap = S["resB"], S["resB"].t[b, row:row + 128, :]
                self.postnorm(pss, x_, gbc, lng, lnb, upd[ti % 2], yv[ti % 2], yn[ti % 2], sm[ti % 2], dbuf, dap)
            P.pop()
        P.pop()


def _rope_tables():
    half = 64
    inv_freq = (1.0 / (np.float32(10000.0) ** (np.arange(0, half, 2, dtype=np.float32) / np.float32(half)))).astype(np.float32)
    rows = TL // 64
    row = np.repeat(np.arange(rows, dtype=np.float32), 64)
    col = np.tile(np.arange(64, dtype=np.float32), rows)
    ang = np.concatenate([row[:, None] * inv_freq, col[:, None] * inv_freq], -1).astype(np.float32)
    cos = np.cos(ang).astype(np.float32).reshape(16, 128, 64).transpose(1, 0, 2)
    sin = np.sin(ang).astype(np.float32).reshape(16, 128, 64).transpose(1, 0, 2)
    return np.ascontiguousarray(cos), np.ascontiguousarray(sin)


def _host_consts():
    cos, sin = _rope_tables()
    jj = np.arange(128) // 16
    maskf = (jj[None, :] >= jj[:, None]).astype(np.float32)
    maskr = (jj[None, :] <= jj[:, None]).astype(np.float32)
    return {
        "k_identf": np.eye(128, dtype=np.float32),
        "k_identb": np.eye(128, dtype=np.float32).astype(ml_dtypes.bfloat16),
        "k_onesb": np.ones((128, 128), dtype=np.float32).astype(ml_dtypes.bfloat16),
        "k_cos": cos, "k_sin": sin, "k_maskf": maskf, "k_maskr": maskr,
    }


_NC_CACHE = {}


def _get_nc():
    if "nc" not in _NC_CACHE:
        _NC_CACHE["nc"] = K().build()
    return _NC_CACHE["nc"]


def make_in_maps(inputs, cores):
    consts = _host_consts()
    maps = []
    for i in cores:
        m = {}
        for k, v in inputs.items():
            v = np.asarray(v)
            if k in ("x", "c", "ctx"):
                m[k] = np.ascontiguousarray(v[NBC * i:NBC * (i + 1)])
            elif k == "c_ctx":
                m[k] = np.ascontiguousarray(v.reshape(1, D))
            else:
                m[k] = v
        m.update(consts)
        maps.append(m)
    return maps


def kernel(**inputs):
    nc = _get_nc()
    maps = make_in_maps(inputs, range(8))
    res = run_bass_kernel_spmd(nc, maps, core_ids=list(range(8)))
    return np.concatenate([np.asarray(r["out"]) for r in res.results], axis=0).astype(np.float32)
```

```python
import numpy as np
import ml_dtypes
from contextlib import ExitStack
import concourse.bass as bass
import concourse.mybir as mybir
from concourse.bass_utils import run_bass_kernel_spmd

F32 = mybir.dt.float32
BF16 = mybir.dt.bfloat16
I32 = mybir.dt.int32
AF = mybir.ActivationFunctionType
ALU = mybir.AluOpType

ENGS = ["tensor", "vector", "scalar", "gpsimd", "sync"]

D = 1024
TL = 2048
TC = 256
TT = TC + TL
NBC = 2
DEPTH = 2
IN_COLS = 6656
DFF = 2816
NCH = TT // 8
EPS = 1e-6
ALPHA = float((2 * DEPTH) ** 0.25)
ATTN_SCALE = float(128 ** -0.5)
TWO_PI = float(2 * np.pi)
PI = float(np.pi)


class Buf:
    def __init__(self, name, t):
        self.name = name
        self.t = t
        self.w = {}
        self.r = {}
        self.dkey = {}

    def __getitem__(self, idx):
        return self.t[idx]


class Prog:
    def __init__(self, nc, n_dsem=56):
        self.nc = nc
        self.base = ExitStack()
        self.scopes = [ExitStack()]
        self.scope_bufs = [[]]
        self.q = {e: [] for e in ENGS}
        self.sem = {}
        self.cnt = {}
        self.seen = {e: {} for e in ENGS}
        for e in ENGS:
            self.sem[e] = self.base.enter_context(nc.semaphore("s_" + e))
            self.cnt[e] = 0
        self.free_dsem = {"sync": [], "gpsimd": [], "scalar": []}
        for qn, nq_ in (("sync", 36), ("gpsimd", 24), ("scalar", 24)):
            for i in range(nq_):
                k = ("d" + qn, i)
                self.sem[k] = self.base.enter_context(nc.semaphore("d%s%d" % (qn[:2], i)))
                self.cnt[k] = 0
                self.free_dsem[qn].append(k)
        self.uid = 0

    def push(self):
        self.scopes.append(ExitStack())
        self.scope_bufs.append([])

    def pop(self):
        self.barrier()
        for b in self.scope_bufs.pop():
            for qn, k in b.dkey.items():
                self.free_dsem[qn].append(k)
            b.dkey = {}
        self.scopes.pop().close()

    def sb(self, name, shape, dtype):
        self.uid += 1
        t = self.scopes[-1].enter_context(self.nc.sbuf_tensor("%s_%d" % (name, self.uid), list(shape), dtype))
        b = Buf(name, t)
        self.scope_bufs[-1].append(b)
        return b

    def ps(self, name, shape, dtype=F32):
        t = self.base.enter_context(self.nc.psum_tensor(name, list(shape), dtype))
        return Buf(name, t)

    def dram(self, name, shape, dtype, kind="Internal"):
        t = self.nc.dram_tensor(name, list(shape), dtype, kind=kind).ap()
        return Buf(name, t)

    def _dkey(self, b, eng):
        if eng not in b.dkey:
            b.dkey[eng] = self.free_dsem[eng].pop()
        return b.dkey[eng]

    def _deps(self, eng, reads, writes, strict=False, nowaw=False):
        deps = {}

        def add(d):
            for k, v in d.items():
                if k == eng and eng == "tensor":
                    continue
                if deps.get(k, 0) < v:
                    deps[k] = v

        for r in reads:
            add(r.w)
        for w in writes:
            if not nowaw:
                add(w.w)
            add(w.r)
        out = []
        for k, v in deps.items():
            if self.seen[eng].get(k, 0) >= v:
                continue
            self.seen[eng][k] = v
            out.append((self.sem[k], v))
        return out

    def _commit(self, ev, reads, writes, nowaw=False):
        k, v = ev
        for r in reads:
            if r.r.get(k, 0) < v:
                r.r[k] = v
        for w in writes:
            if w.w.get(k, 0) < v:
                w.w[k] = v
            if not nowaw:
                w.r = {}

    def op(self, eng, fn, reads=(), writes=(), inc=True, nowaw=False):
        waits = self._deps(eng, reads, writes, nowaw=nowaw)
        if inc:
            self.cnt[eng] += 1
            ev = (eng, self.cnt[eng])
        else:
            ev = (eng, self.cnt[eng] + 1)
        sem = self.sem[eng]

        def emit(e, waits=waits, fn=fn, inc=inc, sem=sem):
            for s, v in waits:
                e.wait_ge(s, v)
            ins = fn(e)
            if inc:
                ins.then_inc(sem, 1)

        self.q[eng].append(emit)
        self._commit(ev, reads, writes, nowaw=nowaw)

    def dma(self, eng, out, in_, reads=(), writes=(), semb=None, **kw):
        waits = self._deps(eng, reads, writes, strict=True)
        if semb is None:
            for b in list(writes) + list(reads):
                if not isinstance(b, DBuf):
                    semb = b
                    break
        key = self._dkey(semb, eng)
        self.cnt[key] += 16
        ev = (key, self.cnt[key])
        sem = self.sem[key]

        def emit(e, waits=waits, out=out, in_=in_, sem=sem, kw=kw):
            for s, v in waits:
                e.wait_ge(s, v)
            e.dma_start(out=out, in_=in_, **kw).then_inc(sem, 16)

        self.q[eng].append(emit)
        self._commit(ev, reads, writes)

    def barrier(self):
        tot = dict(self.cnt)
        for e in ENGS:
            waits = []
            for k, v in tot.items():
                if k == e or v == 0:
                    continue
                if self.seen[e].get(k, 0) >= v:
                    continue
                self.seen[e][k] = v
                waits.append((self.sem[k], v))

            def emit(en, waits=waits):
                for s, v in waits:
                    en.wait_ge(s, v)

            self.q[e].append(emit)

    def finish(self):
        self.barrier()
        nc = self.nc
        q = self.q
        with nc.Block() as block:
            @block.tensor
            def _(e):
                for f in q["tensor"]:
                    f(e)

            @block.vector
            def _(e):
                for f in q["vector"]:
                    f(e)

            @block.scalar
            def _(e):
                for f in q["scalar"]:
                    f(e)

            @block.gpsimd
            def _(e):
                for f in q["gpsimd"]:
                    f(e)

            @block.sync
            def _(e):
                for f in q["sync"]:
                    f(e)
        while self.scopes:
            self.scopes.pop().close()
        self.base.close()


class DBuf(Buf):
    pass


def TS(out, in0, s1, s2, op0, op1=None):
    if op1 is None:
        return lambda e: e.tensor_scalar(out=out, in0=in0, scalar1=s1, scalar2=None, op0=op0)
    return lambda e: e.tensor_scalar(out=out, in0=in0, scalar1=s1, scalar2=s2, op0=op0, op1=op1)


def TTo(out, in0, in1, op):
    return lambda e: e.tensor_tensor(out=out, in0=in0, in1=in1, op=op)


def STT(out, in0, scalar, in1, op0, op1):
    return lambda e: e.scalar_tensor_tensor(out=out, in0=in0, scalar=scalar, in1=in1, op0=op0, op1=op1)


def ACT(out, in_, func, bias=None, scale=None, accum_out=None):
    kw = {}
    if bias is not None:
        kw["bias"] = bias
    if scale is not None:
        kw["scale"] = scale
    if accum_out is not None:
        kw["accum_out"] = accum_out
    return lambda e: e.activation(out=out, in_=in_, func=func, **kw)


def CP(out, in_):
    return lambda e: e.tensor_copy(out=out, in_=in_)


def MM(out, lhsT, rhs, start, stop):
    return lambda e: e.matmul(out, lhsT=lhsT, rhs=rhs, start=start, stop=stop)


def TR(out, in_, ident):
    return lambda e: e.transpose(out=out, in_=in_, identity=ident)


class K:
    def __init__(self, dbg=(), stop_after=None):
        self.dbg = set(dbg)
        self.stop_after = stop_after
        nc = self.nc = bass.Bass("TRN2", target_bir_lowering=False)
        P = self.P = Prog(nc)
        self.inputs = {}
        self.psb = [P.ps("psb%d" % i, [128, 512], F32) for i in range(8)]
        self.rr = {}

    def din(self, name, shape, dt=F32):
        b = DBuf(name, self.nc.dram_tensor(name, list(shape), dt, kind="ExternalInput").ap())
        self.inputs[name] = b
        return b

    def dscr(self, name, shape, dt):
        kind = "ExternalOutput" if name in self.dbg else "Internal"
        return DBuf(name, self.nc.dram_tensor(name, list(shape), dt, kind=kind).ap())

    def bank(self, group, banks):
        i = self.rr.get(group, 0)
        self.rr[group] = i + 1
        return self.psb[banks[i % len(banks)]]

    def build(self):
        nc, P = self.nc, self.P
        L = DEPTH
        I = self.I = {}
        I["x"] = self.din("x", [NBC, TL, D])
        I["c"] = self.din("c", [NBC, D])
        I["ctx"] = self.din("ctx", [NBC, TC, D])
        I["c_ctx"] = self.din("c_ctx", [1, D])
        for name, shape in [
            ("w_mod", [L, D, 6 * D]), ("b_mod", [L, 6 * D]), ("w_in", [L, D, IN_COLS]), ("conv_w", [L, 3, 512]),
            ("w_conv_out", [L, 512, D]), ("ssm_lam_re", [L, 2, 32, 64]), ("ssm_lam_im", [L, 2, 32, 64]),
            ("ssm_log_dt", [L, 2, 32]), ("ssm_b_re", [L, 2, 32, 64, 16]), ("ssm_b_im", [L, 2, 32, 64, 16]),
            ("ssm_c_re", [L, 2, 32, 16, 64]), ("ssm_c_im", [L, 2, 32, 16, 64]), ("ssm_d", [L, 512]),
            ("w_glu", [L, 512, 2 * D]), ("q_norm_g", [L, 128]), ("k_norm_g", [L, 128]), ("w_attn_out", [L, D, D]),
            ("w_o", [L, D, D]), ("ln1_g", [L, D]), ("ln1_b", [L, D]), ("ffn_w_up", [L, D, 2 * DFF]),
            ("ffn_conv_w", [L, 3, DFF]), ("ffn_conv_b", [L, DFF]), ("ffn_w_down", [L, DFF, D]),
            ("ln2_g", [L, D]), ("ln2_b", [L, D]),
        ]:
            I[name] = self.din(name, shape)
        I["k_identf"] = self.din("k_identf", [128, 128])
        I["k_identb"] = self.din("k_identb", [128, 128], BF16)
        I["k_onesb"] = self.din("k_onesb", [128, 128], BF16)
        I["k_cos"] = self.din("k_cos", [128, 16, 64])
        I["k_sin"] = self.din("k_sin", [128, 16, 64])
        I["k_maskf"] = self.din("k_maskf", [128, 128])
        I["k_maskr"] = self.din("k_maskr", [128, 128])
        self.out = DBuf("out", nc.dram_tensor("out", [NBC, TL, D], F32, kind="ExternalOutput").ap())

        S = self.S = {}
        for l in range(L):
            S["wb_in%d" % l] = self.dscr("wb_in%d" % l, [D, IN_COLS], BF16)
            S["wb_co%d" % l] = self.dscr("wb_co%d" % l, [512, D], BF16)
            S["wb_glu%d" % l] = self.dscr("wb_glu%d" % l, [512, 2 * D], BF16)
            S["wb_ao%d" % l] = self.dscr("wb_ao%d" % l, [D, D], BF16)
            S["wb_o%d" % l] = self.dscr("wb_o%d" % l, [D, D], BF16)
            S["wb_up%d" % l] = self.dscr("wb_up%d" % l, [D, 2 * DFF], BF16)
            S["wb_dn%d" % l] = self.dscr("wb_dn%d" % l, [DFF, D], BF16)
            S["modr%d" % l] = self.dscr("modr%d" % l, [3, 6 * D], F32)
        S["resA"] = self.dscr("resA", [NBC, TT, D], F32)
        S["resB"] = self.dscr("resB", [NBC, TT, D], F32)
        S["KT"] = self.dscr("KT", [NBC, 2, 128, TT], BF16)
        S["V"] = self.dscr("V", [NBC, TT, 256], BF16)
        S["U"] = self.dscr("U", [NBC, TT, 512], BF16)
        S["QT"] = self.dscr("QT", [NBC, 8, 128, TT], BF16)
        S["AXT"] = self.dscr("AXT", [NBC, 512, TT], BF16)
        S["BGT"] = self.dscr("BGT", [NBC, 512, TT], BF16)
        S["CGT"] = self.dscr("CGT", [NBC, 512, TT], BF16)
        S["GT"] = self.dscr("GT", [NBC, 3 * D, TT], BF16)
        S["AOT"] = self.dscr("AOT", [NBC, D, TT], BF16)
        S["SSMT"] = self.dscr("SSMT", [NBC, D, TT], BF16)
        S["YS"] = self.dscr("YS", [NBC, TT, 512], F32)

        C = self.C = {}
        C["identf"] = P.sb("identf", [128, 128], F32)
        C["identb"] = P.sb("identb", [128, 128], BF16)
        C["onesb"] = P.sb("onesb", [128, 128], BF16)
        C["eps"] = P.sb("eps", [128, 1], F32)
        for nm in ["identf", "identb", "onesb"]:
            P.dma("sync", C[nm][:], I["k_" + nm].t, reads=[I["k_" + nm]], writes=[C[nm]])
        P.op("vector", lambda e: e.memset(C["eps"][:], EPS), writes=[C["eps"]])
        C["mhalf"] = P.sb("mhalf", [128, 16], F32)
        P.op("gpsimd", lambda e: e.memset(C["mhalf"][:], -0.5), writes=[C["mhalf"]])

        try:
            self.weight_prep()
            if self.stop_after == "prep":
                return self.finish()
            for l in range(L):
                self.layer(l)
                if self.stop_after == "layer%d" % l:
                    break
        except StopIteration:
            pass
        return self.finish()

    def finish(self):
        self.P.finish()
        return self.nc

    def check_stop(self, tag):
        if self.stop_after == tag:
            raise StopIteration

    def weight_prep(self):
        P, I, S = self.P, self.I, self.S
        P.push()
        stf = [P.sb("wpf%d" % i, [128, 2048], F32) for i in range(3)]
        stb = [P.sb("wpb%d" % i, [128, 2048], BF16) for i in range(3)]
        n = 0
        engs = ["gpsimd", "vector", "scalar"]
        for l in range(DEPTH):
            for src, dst, Kd, N in [("w_in", "wb_in", D, IN_COLS), ("w_conv_out", "wb_co", 512, D), ("w_glu", "wb_glu", 512, 2 * D),
                                    ("w_attn_out", "wb_ao", D, D), ("w_o", "wb_o", D, D), ("ffn_w_up", "wb_up", D, 2 * DFF),
                                    ("ffn_w_down", "wb_dn", DFF, D)]:
                sa = I[src].t[l]
                db = S[dst + str(l)]
                for kt in range(Kd // 128):
                    for c0 in range(0, N, 2048):
                        w = min(2048, N - c0)
                        f, b = stf[n % 3], stb[n % 3]
                        eng = engs[n % 3]
                        P.dma("sync", f[:, :w], sa[kt * 128:(kt + 1) * 128, c0:c0 + w], reads=[I[src]], writes=[f])
                        if eng == "scalar":
                            P.op(eng, ACT(b[:, :w], f[:, :w], AF.Copy), reads=[f], writes=[b])
                        else:
                            P.op(eng, CP(b[:, :w], f[:, :w]), reads=[f], writes=[b])
                        P.dma("gpsimd", db.t[kt * 128:(kt + 1) * 128, c0:c0 + w], b[:, :w], reads=[b], writes=[db])
                        n += 1
        P.pop()

    def layer(self, l):
        P = self.P
        self.l = l
        self.last = (l == DEPTH - 1)
        P.push()
        self.modT = P.sb("modT", [128, 6, 8, 3], F32)
        self.phase0()
        self.check_stop("p0_%d" % l)
        self.phaseA()
        self.check_stop("pA_%d" % l)
        self.phaseB()
        self.check_stop("pB_%d" % l)
        self.phaseC()
        self.check_stop("pC_%d" % l)
        self.phaseD()
        self.check_stop("pD_%d" % l)
        self.phaseF()
        self.check_stop("pF_%d" % l)
        P.pop()

    def res_in(self, b, seg):
        if self.l == 0:
            return (self.I["ctx"], self.I["ctx"].t[b]) if seg == "ctx" else (self.I["x"], self.I["x"].t[b])
        r = self.S["resB"]
        return (r, r.t[b, 0:TC, :]) if seg == "ctx" else (r, r.t[b, TC:TT, :])

    def phase0(self):
        P, I, S, C, l = self.P, self.I, self.S, self.C, self.l
        P.push()
        scT = P.sb("scT", [128, 8, 3], F32)
        bm3 = P.sb("bm3", [3, 6 * D], F32)
        modrows = P.sb("modrows", [3, 6 * D], F32)
        wst = [P.sb("wmst%d" % i, [128, 8, 512], F32) for i in range(2)]
        for r in range(3):
            src = I["c"].t[r, :] if r < 2 else I["c_ctx"].t[0, :]
            P.dma("sync", scT[:, :, r], src.rearrange("(k p) -> p k", p=128), reads=[I["c"]], writes=[scT],
                  allow_slow_non_contiguous=True)
        P.op("scalar", ACT(scT[:], scT[:], AF.Silu), reads=[scT], writes=[scT])
        P.dma("sync", bm3[0:3, :], I["b_mod"].t[l, :].partition_broadcast(3), reads=[I["b_mod"]], writes=[bm3])
        wm = I["w_mod"].t[l].rearrange("(k p) n -> p k n", p=128)
        for blk in range(12):
            w = wst[blk % 2]
            P.dma("sync", w[:], wm[:, :, blk * 512:(blk + 1) * 512], reads=[I["w_mod"]], writes=[w])
            ps = self.bank("p0", [0, 1])
            for k in range(8):
                P.op("tensor", MM(ps[0:3, 0:512], scT[:, k, :], w[:, k, :], k == 0, k == 7), reads=[scT, w], writes=[ps], inc=(k == 7))
            P.op("vector", TTo(modrows[0:3, blk * 512:(blk + 1) * 512], ps[0:3, 0:512], bm3[0:3, blk * 512:(blk + 1) * 512], ALU.add),
                 reads=[ps, bm3], writes=[modrows])
        P.dma("gpsimd", S["modr%d" % l].t, modrows[0:3, :], reads=[modrows], writes=[S["modr%d" % l]])
        ps = self.bank("p0", [0, 1])
        for t in range(48):
            P.op("tensor", TR(ps[:, t * 3:(t + 1) * 3], modrows[0:3, t * 128:(t + 1) * 128], C["identf"][0:3, 0:3]),
                 reads=[modrows, C["identf"]], writes=[ps], inc=(t == 47))
        mt = self.modT
        P.op("vector", CP(mt[:].rearrange("p a k r -> p (a k r)"), ps[:, 0:144]), reads=[ps], writes=[mt])
        for sec in (1, 4):
            P.op("vector", TS(mt[:, sec], mt[:, sec], 1.0, None, ALU.add), reads=[mt], writes=[mt])
        P.pop()

    def ln_tile(self, xt, out_ap, out_buf, sm, src_ap=None, act_extra_reads=()):
        P, C = self.P, self.C
        st, mv, rs, nb = sm
        src = xt[:] if src_ap is None else src_ap
        P.op("vector", lambda e: e.bn_stats(out=st[:, 0:6], in_=src[:, 0:512]), reads=[xt], writes=[st])
        P.op("vector", lambda e: e.bn_stats(out=st[:, 6:12], in_=src[:, 512:1024]), reads=[xt], writes=[st])
        P.op("vector", lambda e: e.bn_aggr(out=mv[:, 0:2], in_=st[:, 0:12]), reads=[st], writes=[mv])
        P.op("gpsimd", TS(rs[:, 0:1], mv[:, 1:2], EPS, None, ALU.add), reads=[mv], writes=[rs])
        P.op("gpsimd", TTo(rs[:, 0:1], rs[:, 0:1], C["mhalf"][:, 0:1], ALU.pow), reads=[rs, C["mhalf"]], writes=[rs])
        P.op("vector", TS(nb[:, 0:1], mv[:, 0:1], rs[:, 0:1], -1.0, ALU.mult, ALU.mult), reads=[mv, rs], writes=[nb])
        P.op("scalar", ACT(out_ap, src, AF.Identity, bias=nb[:, 0:1], scale=rs[:, 0:1]), reads=[xt, rs, nb], writes=[out_buf])

    def ln_small(self, tag, n=2):
        P = self.P
        return [(P.sb(tag + "st%d" % i, [128, 12], F32), P.sb(tag + "mv%d" % i, [128, 2], F32),
                 P.sb(tag + "rs%d" % i, [128, 1], F32), P.sb(tag + "nb%d" % i, [128, 1], F32)) for i in range(n)]

    def phaseA(self):
        P, I, S, C, l = self.P, self.I, self.S, self.C, self.l
        P.push()
        W = P.sb("Win", [128, 8, IN_COLS], BF16)
        wsrc = S["wb_in%d" % l]
        for k in range(8):
            P.dma("sync", W[:, k, :], wsrc.t[k * 128:(k + 1) * 128, :], reads=[wsrc], writes=[W])
        cos = P.sb("cos", [128, 16, 64], F32)
        sin = P.sb("sin", [128, 16, 64], F32)
        P.dma("sync", cos[:], I["k_cos"].t, reads=[I["k_cos"]], writes=[cos])
        P.dma("sync", sin[:], I["k_sin"].t, reads=[I["k_sin"]], writes=[sin])
        gqk = P.sb("gqk", [128, 10, 128], F32)
        P.dma("sync", gqk[:, 0:2, :], I["k_norm_g"].t[l:l + 1, :].partition_broadcast(128).to_broadcast([128, 2, 128]) if False else
              I["k_norm_g"].t[l, :].partition_broadcast(128).unsqueeze(1).to_broadcast([128, 2, 128]), reads=[I["k_norm_g"]], writes=[gqk])
        P.dma("sync", gqk[:, 2:10, :], I["q_norm_g"].t[l, :].partition_broadcast(128).unsqueeze(1).to_broadcast([128, 8, 128]), reads=[I["q_norm_g"]], writes=[gqk])
        xt = [P.sb("xt%d" % i, [128, D], F32) for i in range(2)]
        xn = [P.sb("xn%d" % i, [128, D], BF16) for i in range(2)]
        sm = self.ln_small("a")
        hT = [P.sb("hT%d" % i, [128, 8, 512], BF16) for i in range(2)]
        xraw = [P.sb("xraw%d" % i, [128, 10, 128], F32) for i in range(2)]
        sq = P.sb("sq", [128, 10, 128], F32)
        ss = [P.sb("ss%d" % i, [128, 10], F32) for i in range(2)]
        rt = [P.sb("rt%d" % i, [128, 10, 64], F32) for i in range(4)]
        qn = [P.sb("qn%d" % i, [128, 10, 128], BF16) for i in range(2)]
        qTs = P.sb("qTs", [128, 8, 512], BF16)
        kTs = P.sb("kTs", [128, 2, 512], BF16)
        vst = [P.sb("vst%d" % i, [128, 256], BF16) for i in range(2)]
        ust = [P.sb("ust%d" % i, [128, 512], BF16) for i in range(2)]
        fst = [P.sb("fst%d" % i, [128, 512], BF16) for i in range(3)]
        cnt = {"x": 0, "t": 0, "f": 0}

        sts = []
        for b in range(NBC):
            for seg in ("ctx", "lat"):
                ntok = 256 if seg == "ctx" else 512
                for st_i in range(1 if seg == "ctx" else 4):
                    sts.append((b, seg, st_i, ntok))

        def ln_tile_emit(si, ti):
            b, seg, st_i, ntok = sts[si]
            r = 2 if seg == "ctx" else b
            rbuf, rap = self.res_in(b, seg)
            t0 = st_i * ntok
            h = hT[si % 2]
            i = cnt["x"]
            cnt["x"] += 1
            x_, n_ = xt[i % 2], xn[i % 2]
            P.dma("sync", x_[:], rap[t0 + ti * 128:t0 + (ti + 1) * 128, :], reads=[rbuf], writes=[x_])
            self.ln_tile(x_, n_[:], n_, sm[i % 2])
            ps = self.bank("atr", [0, 1])
            pb = ps[:].bitcast(BF16).rearrange("p (k t) -> p k t", k=8)
            for k in range(8):
                P.op("tensor", TR(pb[:, k, :], n_[:, k * 128:(k + 1) * 128], C["identb"][:]), reads=[n_, C["identb"]], writes=[ps], inc=(k == 7))
            for k in range(8):
                o = h[:, k, ti * 128:(ti + 1) * 128]
                rd = [ps, self.modT] if k in (0, 7) else []
                wr = [h] if k in (0, 7) else []
                if i % 2 == 0:
                    P.op("vector", TS(o, pb[:, k, :], self.modT[:, 1, k, r:r + 1], self.modT[:, 0, k, r:r + 1], ALU.mult, ALU.add), reads=rd, writes=wr)
                else:
                    P.op("scalar", ACT(o, pb[:, k, :], AF.Identity, bias=self.modT[:, 0, k, r:r + 1], scale=self.modT[:, 1, k, r:r + 1]), reads=rd, writes=wr)

        def tokmajor(si):
            b, seg, st_i, ntok = sts[si]
            full = (seg == "lat") or (not self.last)
            t0 = st_i * ntok
            g0 = t0 + (0 if seg == "ctx" else TC)
            h = hT[si % 2]
            rope = (seg == "lat")
            for ti in range(ntok // 128):
                ltile = (t0 // 128) + ti
                it = cnt["t"]
                cnt["t"] += 1
                xr, s_, q_ = xraw[it % 2], ss[it % 2], qn[it % 2]
                nh = 10 if full else 2
                for blk in range(4 if full else 2):
                    ps = self.bank("amm", [2, 3, 4, 5])
                    for k in range(8):
                        P.op("tensor", MM(ps[:, 0:512], h[:, k, ti * 128:(ti + 1) * 128], W[:, k, blk * 512:(blk + 1) * 512], k == 0, k == 7),
                             reads=[h, W], writes=[ps], inc=(k == 7))
                    if blk == 0:
                        P.op("scalar", ACT(xr[:, 0:2, :].rearrange("p h d -> p (h d)"), ps[:, 0:256], AF.Copy), reads=[ps], writes=[xr])
                        v_ = vst[it % 2]
                        P.op("scalar", ACT(v_[:], ps[:, 256:512], AF.Copy), reads=[ps], writes=[v_])
                        P.dma("scalar", S["V"].t[b, g0 + ti * 128:g0 + (ti + 1) * 128, :], v_[:], reads=[v_], writes=[S["V"]])
                    elif blk == 1:
                        u_ = ust[it % 2]
                        P.op("scalar", ACT(u_[:], ps[:, 0:512], AF.Copy), reads=[ps], writes=[u_])
                        P.dma("scalar", S["U"].t[b, g0 + ti * 128:g0 + (ti + 1) * 128, :], u_[:], reads=[u_], writes=[S["U"]])
                    else:
                        h0 = 2 + (blk - 2) * 4
                        dst = xr[:, h0:h0 + 4, :].rearrange("p h d -> p (h d)")
                        P.op("scalar", ACT(dst, ps[:, 0:512], AF.Copy), reads=[ps], writes=[xr])
                P.op("scalar", ACT(sq[:, 0:nh, :], xr[:, 0:nh, :], AF.Square), reads=[xr], writes=[sq])
                P.op("vector", lambda e, s_=s_, nh=nh: e.tensor_reduce(out=s_[:, 0:nh], in_=sq[:, 0:nh, :], axis=mybir.AxisListType.X, op=ALU.add),
                     reads=[sq], writes=[s_])
                P.op("gpsimd", TS(s_[:, 0:nh], s_[:, 0:nh], 1.0 / 128, EPS, ALU.mult, ALU.add), reads=[s_], writes=[s_])
                P.op("gpsimd", TTo(s_[:, 0:nh], s_[:, 0:nh], C["mhalf"][:, 0:nh], ALU.pow), reads=[s_, C["mhalf"]], writes=[s_])
                for hh in range(nh):
                    P.op("vector", STT(xr[:, hh, :], xr[:, hh, :], s_[:, hh:hh + 1], gqk[:, hh, :], ALU.mult, ALU.mult),
                         reads=([xr, s_, gqk] if hh in (0, nh - 1) else []), writes=([xr] if hh in (0, nh - 1) else []))
                if not rope:
                    P.op("gpsimd", CP(q_[:, 0:nh, :], xr[:, 0:nh, :]), reads=[xr], writes=[q_])
                else:
                    xe = xr[:, 0:nh, 0:128:2]
                    xo = xr[:, 0:nh, 1:128:2]
                    cb = cos[:, ltile:ltile + 1, :].to_broadcast([128, nh, 64])
                    sb_ = sin[:, ltile:ltile + 1, :].to_broadcast([128, nh, 64])
                    t1, t2, t3, t4 = [r_[:, 0:nh, :] for r_ in rt]
                    P.op("vector", TTo(t1, xe, cb, ALU.mult), reads=[xr, cos], writes=[rt[0]])
                    P.op("gpsimd", TTo(t3, xe, sb_, ALU.mult), reads=[xr, sin], writes=[rt[2]])
                    P.op("vector", TTo(t2, xo, sb_, ALU.mult), reads=[xr, sin], writes=[rt[1]])
                    P.op("gpsimd", TTo(t4, xo, cb, ALU.mult), reads=[xr, cos], writes=[rt[3]])
                    P.op("vector", TTo(q_[:, 0:nh, 0:128:2], t1, t2, ALU.subtract), reads=[rt[0], rt[1]], writes=[q_])
                    P.op("gpsimd", TTo(q_[:, 0:nh, 1:128:2], t3, t4, ALU.add), reads=[rt[2], rt[3]], writes=[q_])
                pt = self.bank("atq", [6, 7])
                ptb = pt[:].bitcast(BF16).rearrange("p (k t) -> p k t", k=8)
                for hh in range(2):
                    P.op("tensor", TR(ptb[:, hh, :], q_[:, hh, :], C["identb"][:]), reads=[q_, C["identb"]], writes=[pt], inc=(hh == 1))
                P.op("vector", CP(kTs[:, :, ti * 128:(ti + 1) * 128], ptb[:, 0:2, :]), reads=[pt], writes=[kTs])
                if full:
                    pt = self.bank("atq", [6, 7])
                    ptb = pt[:].bitcast(BF16).rearrange("p (k t) -> p k t", k=8)
                    for hh in range(8):
                        P.op("tensor", TR(ptb[:, hh, :], q_[:, 2 + hh, :], C["identb"][:]), reads=[q_, C["identb"]], writes=[pt], inc=(hh == 7))
                    P.op("scalar", ACT(qTs[:, :, ti * 128:(ti + 1) * 128], ptb[:, :, :], AF.Copy), reads=[pt], writes=[qTs])
            for kv in range(2):
                P.dma("sync", S["KT"].t[b, kv, :, g0:g0 + ntok], kTs[:, kv, 0:ntok], reads=[kTs], writes=[S["KT"]])
            if full:
                for hh in range(8):
                    P.dma("sync", S["QT"].t[b, hh, :, g0:g0 + ntok], qTs[:, hh, 0:ntok], reads=[qTs], writes=[S["QT"]])

        def featmajor(si, between):
            b, seg, st_i, ntok = sts[si]
            full = (seg == "lat") or (not self.last)
            g0 = st_i * ntok + (0 if seg == "ctx" else TC)
            h = hT[si % 2]
            if not full:
                for f in between:
                    f()
                return
            for ct in range(36):
                ps = self.bank("amm", [2, 3, 4, 5])
                c0 = 2048 + ct * 128
                for k in range(8):
                    P.op("tensor", MM(ps[:, 0:ntok], W[:, k, c0:c0 + 128], h[:, k, 0:ntok], k == 0, k == 7), reads=[h, W], writes=[ps], inc=(k == 7))
                f_ = fst[cnt["f"] % 3]
                cnt["f"] += 1
                if ct < 12:
                    dst = S[["AXT", "BGT", "CGT"][ct // 4]]
                    row0 = (ct % 4) * 128
                    P.op("scalar", ACT(f_[:, 0:ntok], ps[:, 0:ntok], AF.Copy), reads=[ps], writes=[f_])
                else:
                    dst = S["GT"]
                    row0 = (ct - 12) * 128
                    P.op("scalar", ACT(f_[:, 0:ntok], ps[:, 0:ntok], AF.Sigmoid), reads=[ps], writes=[f_])
                P.dma("scalar", dst.t[b, row0:row0 + 128, g0:g0 + ntok], f_[:, 0:ntok], reads=[f_], writes=[dst])
                if ct % 9 == 8 and between:
                    between.pop(0)()
            for f in between:
                f()

        for ti in range(sts[0][3] // 128):
            ln_tile_emit(0, ti)
        for si in range(len(sts)):
            tokmajor(si)
            between = []
            if si + 1 < len(sts):
                between = [(lambda si=si, ti=ti: ln_tile_emit(si + 1, ti)) for ti in range(sts[si + 1][3] // 128)]
            featmajor(si, between)
        P.pop()

    def phaseB(self):
        P, I, S, C, l = self.P, self.I, self.S, self.C, self.l
        P.push()
        KTs = P.sb("KTs", [128, 2, TT], BF16)
        Vs = P.sb("Vs", [128, 18, 256], BF16)
        qT = [P.sb("qTb%d" % i, [128, 512], BF16) for i in range(2)]
        pT = [P.sb("pT%d" % i, [128, 512], BF16) for i in range(4)]
        rec = [P.sb("rec%d" % i, [128, 512], F32) for i in range(2)]
        oT = [P.sb("oT%d" % i, [128, 512], BF16) for i in range(2)]
        n = 0
        for b in range(NBC):
            for kv in range(2):
                P.dma("sync", KTs[:, kv, :], S["KT"].t[b, kv], reads=[S["KT"]], writes=[KTs])
            P.dma("sync", Vs[:], S["V"].t[b].rearrange("(t p) c -> p t c", p=128), reads=[S["V"]], writes=[Vs])
            blocks = [("lat", qb) for qb in range(4)] + ([] if self.last else [("ctx", 0)])
            for h in range(8):
                kv = h // 4
                for seg, qb in blocks:
                    if seg == "lat":
                        q0, nq, kts = TC + qb * 512, 512, list(range(18))
                    else:
                        q0, nq, kts = 0, 256, [0, 1]
                    q_ = qT[n % 2]
                    P.dma("sync", q_[:, 0:nq], S["QT"].t[b, h, :, q0:q0 + nq], reads=[S["QT"]], writes=[q_])
                    po = self.bank("bo", [4, 5])
                    pz = self.bank("bs", [6, 7])
                    def qk(j):
                        kt = kts[j]
                        ps = self.bank("bqk", [0, 1, 2, 3])
                        P.op("tensor", MM(ps[:, 0:nq], KTs[:, kv, kt * 128:(kt + 1) * 128], q_[:, 0:nq], True, True), reads=[KTs, q_], writes=[ps])
                        P.op("scalar", ACT(pT[j % 4][:, 0:nq], ps[:, 0:nq], AF.Exp, scale=ATTN_SCALE), reads=[ps], writes=[pT[j % 4]])

                    LOOK = 2
                    for j in range(min(LOOK, len(kts))):
                        qk(j)
                    for j, kt in enumerate(kts):
                        if j + LOOK < len(kts):
                            qk(j + LOOK)
                        p_ = pT[j % 4]
                        last = (j == len(kts) - 1)
                        P.op("tensor", MM(po[:, 0:nq], Vs[:, kt, kv * 128:(kv + 1) * 128], p_[:, 0:nq], j == 0, last), reads=[Vs, p_], writes=[po], inc=last)
                        P.op("tensor", MM(pz[:, 0:nq], C["onesb"][:], p_[:, 0:nq], j == 0, last), reads=[C["onesb"], p_], writes=[pz], inc=last)
                    r_, o_ = rec[n % 2], oT[n % 2]
                    P.op("vector", lambda e, r_=r_, pz=pz, nq=nq: e.reciprocal(out=r_[:, 0:nq], in_=pz[:, 0:nq]), reads=[pz], writes=[r_])
                    P.op("vector", TTo(o_[:, 0:nq], po[:, 0:nq], r_[:, 0:nq], ALU.mult), reads=[po, r_], writes=[o_])
                    P.dma("gpsimd", S["AOT"].t[b, h * 128:(h + 1) * 128, q0:q0 + nq], o_[:, 0:nq], reads=[o_], writes=[S["AOT"]])
                    n += 1
        P.pop()


    def phaseC(self):
        P = self.P
        P.push()
        Bst = P.sb("Bst", [128, 32, 2, 128], BF16)
        CL = P.sb("CL", [128, 32, 2, 128], BF16)
        Toep = P.sb("Toep", [128, 32, 128], BF16)
        A8 = P.sb("A8", [128, 32], F32)
        B8 = P.sb("B8", [128, 32], F32)
        self.ssm_consts(Bst, CL, Toep, A8, B8)
        self.check_stop("pC0_%d" % self.l)
        for b in range(NBC):
            self.ssm_batch(b, Bst, CL, Toep, A8, B8)
            self.check_stop("c3a")
        P.pop()

    def ssm_consts(self, Bst, CL, Toep, A8, B8):
        P, I, S, C, l = self.P, self.I, self.S, self.C, self.l
        P.push()
        V_ = "vector"

        def T(name, shape=(128, 32), dt=F32):
            return P.sb(name, list(shape), dt)

        lre, lim, ldt = T("lre"), T("lim"), T("ldt")
        for d in range(2):
            hs = slice(64 * d, 64 * d + 64)
            P.dma("sync", lre[hs, :], I["ssm_lam_re"].t[l, d].rearrange("g p -> p g"), reads=[I["ssm_lam_re"]], writes=[lre], allow_slow_non_contiguous=True)
            P.dma("sync", lim[hs, :], I["ssm_lam_im"].t[l, d].rearrange("g p -> p g"), reads=[I["ssm_lam_im"]], writes=[lim], allow_slow_non_contiguous=True)
            P.dma("sync", ldt[hs, :], I["ssm_log_dt"].t[l, d, :].partition_broadcast(64), reads=[I["ssm_log_dt"]], writes=[ldt])
        dt, tmp, mag, th, kf, kf2, r, m, sn, cs, r2 = [T(n) for n in ["dt", "tmp", "mag", "th", "kf", "kf2", "r", "m", "sn", "cs", "r2"]]
        ki = T("ki", dt=I32)
        P.op("scalar", ACT(dt[:], ldt[:], AF.Exp), reads=[ldt], writes=[dt])
        P.op(V_, TTo(tmp[:], lre[:], dt[:], ALU.mult), reads=[lre, dt], writes=[tmp])
        P.op("scalar", ACT(mag[:], tmp[:], AF.Exp), reads=[tmp], writes=[mag])
        P.op(V_, TTo(th[:], lim[:], dt[:], ALU.mult), reads=[lim, dt], writes=[th])
        P.op(V_, TS(kf[:], th[:], 1.0 / TWO_PI, None, ALU.mult), reads=[th], writes=[kf])
        P.op(V_, CP(ki[:], kf[:]), reads=[kf], writes=[ki])
        P.op(V_, CP(kf2[:], ki[:]), reads=[ki], writes=[kf2])
        P.op(V_, STT(r[:], kf2[:], -TWO_PI, th[:], ALU.mult, ALU.add), reads=[kf2, th], writes=[r])

        def wrap(x):
            P.op(V_, TS(m[:], x[:], PI, TWO_PI, ALU.is_gt, ALU.mult), reads=[x], writes=[m])
            P.op(V_, TTo(x[:], x[:], m[:], ALU.subtract), reads=[x, m], writes=[x])
            P.op(V_, TS(m[:], x[:], -PI, TWO_PI, ALU.is_lt, ALU.mult), reads=[x], writes=[m])
            P.op(V_, TTo(x[:], x[:], m[:], ALU.add), reads=[x, m], writes=[x])

        wrap(r)
        P.op("scalar", ACT(sn[:], r[:], AF.Sin), reads=[r], writes=[sn])
        P.op(V_, TS(r2[:], r[:], PI / 2, None, ALU.add), reads=[r], writes=[r2])
        wrap(r2)
        P.op("scalar", ACT(cs[:], r2[:], AF.Sin), reads=[r2], writes=[cs])
        lbr, lbi = T("lbr"), T("lbi")
        P.op(V_, TTo(lbr[:], mag[:], cs[:], ALU.mult), reads=[mag, cs], writes=[lbr])
        P.op(V_, TTo(lbi[:], mag[:], sn[:], ALU.mult), reads=[mag, sn], writes=[lbi])
        nr, den, t1, t2, fr, fi = [T(n) for n in ["nr", "den", "t1", "t2", "fr", "fi"]]
        P.op(V_, TS(nr[:], lbr[:], -1.0, None, ALU.add), reads=[lbr], writes=[nr])
        P.op(V_, TTo(t1[:], lre[:], lre[:], ALU.mult), reads=[lre], writes=[t1])
        P.op(V_, TTo(t2[:], lim[:], lim[:], ALU.mult), reads=[lim], writes=[t2])
        P.op(V_, TTo(den[:], t1[:], t2[:], ALU.add), reads=[t1, t2], writes=[den])
        P.op(V_, lambda e: e.reciprocal(out=den[:], in_=den[:]), reads=[den], writes=[den])
        P.op(V_, TTo(t1[:], nr[:], lre[:], ALU.mult), reads=[nr, lre], writes=[t1])
        P.op(V_, TTo(t2[:], lbi[:], lim[:], ALU.mult), reads=[lbi, lim], writes=[t2])
        P.op(V_, TTo(fr[:], t1[:], t2[:], ALU.add), reads=[t1, t2], writes=[fr])
        P.op(V_, TTo(fr[:], fr[:], den[:], ALU.mult), reads=[fr, den], writes=[fr])
        P.op(V_, TTo(t1[:], lbi[:], lre[:], ALU.mult), reads=[lbi, lre], writes=[t1])
        P.op(V_, TTo(t2[:], nr[:], lim[:], ALU.mult), reads=[nr, lim], writes=[t2])
        P.op(V_, TTo(fi[:], t1[:], t2[:], ALU.subtract), reads=[t1, t2], writes=[fi])
        P.op(V_, TTo(fi[:], fi[:], den[:], ALU.mult), reads=[fi, den], writes=[fi])
        ir, ii = T("ir"), T("ii")
        P.op(V_, TTo(t1[:], lbr[:], lbr[:], ALU.mult), reads=[lbr], writes=[t1])
        P.op(V_, TTo(t2[:], lbi[:], lbi[:], ALU.mult), reads=[lbi], writes=[t2])
        P.op(V_, TTo(den[:], t1[:], t2[:], ALU.add), reads=[t1, t2], writes=[den])
        P.op(V_, lambda e: e.reciprocal(out=den[:], in_=den[:]), reads=[den], writes=[den])
        P.op(V_, TTo(ir[:], lbr[:], den[:], ALU.mult), reads=[lbr, den], writes=[ir])
        P.op(V_, STT(ii[:], lbi[:], -1.0, den[:], ALU.mult, ALU.mult), reads=[lbi, den], writes=[ii])
        self.check_stop("c0a")
        LPr, LPi = T("LPr", (128, 9, 32)), T("LPi", (128, 9, 32))
        LNr, LNi = T("LNr", (128, 8, 32)), T("LNi", (128, 8, 32))
        for (Xr, Xi, br, bi, n) in [(LPr, LPi, lbr, lbi, 9), (LNr, LNi, ir, ii, 8)]:
            P.op(V_, lambda e, Xr=Xr: e.memset(Xr[:, 0, :], 1.0), writes=[Xr])
            P.op(V_, lambda e, Xi=Xi: e.memset(Xi[:, 0, :], 0.0), writes=[Xi])
            for k in range(1, n):
                P.op(V_, TTo(t1[:], Xr[:, k - 1, :], br[:], ALU.mult), reads=[Xr, br], writes=[t1])
                P.op(V_, TTo(t2[:], Xi[:, k - 1, :], bi[:], ALU.mult), reads=[Xi, bi], writes=[t2])
                P.op(V_, TTo(Xr[:, k, :], t1[:], t2[:], ALU.subtract), reads=[t1, t2], writes=[Xr])
                P.op(V_, TTo(t1[:], Xr[:, k - 1, :], bi[:], ALU.mult), reads=[Xr, bi], writes=[t1])
                P.op(V_, TTo(t2[:], Xi[:, k - 1, :], br[:], ALU.mult), reads=[Xi, br], writes=[t2])
                P.op(V_, TTo(Xi[:, k, :], t1[:], t2[:], ALU.add), reads=[t1, t2], writes=[Xi])
        P.op(V_, CP(A8[:], LPr[:, 8, :]), reads=[LPr], writes=[A8])
        P.op(V_, CP(B8[:], LPi[:, 8, :]), reads=[LPi], writes=[B8])
        G3 = (128, 32, 16)
        Bre, Bim, Bbr, Bbi, CTr, CTi = [T(n, G3) for n in ["Bre", "Bim", "Bbr", "Bbi", "CTr", "CTi"]]
        for d in range(2):
            hs = slice(64 * d, 64 * d + 64)
            P.dma("sync", Bre[hs], I["ssm_b_re"].t[l, d].rearrange("g p h -> p g h"), reads=[I["ssm_b_re"]], writes=[Bre], allow_slow_non_contiguous=True)
            P.dma("sync", Bim[hs], I["ssm_b_im"].t[l, d].rearrange("g p h -> p g h"), reads=[I["ssm_b_im"]], writes=[Bim], allow_slow_non_contiguous=True)
        craw = [T("craw%d" % i, (128, 2, 64)) for i in range(2)]
        n = 0
        for (src, dst) in [("ssm_c_re", CTr), ("ssm_c_im", CTi)]:
            for gb in range(4):
                cr_ = craw[n % 2]
                n += 1
                for d in range(2):
                    P.dma("sync", cr_[:, d, :], I[src].t[l, d, gb * 8:(gb + 1) * 8].rearrange("g h p -> (g h) p"), reads=[I[src]], writes=[cr_])
                ps = self.bank("c0", [0, 1])
                P.op("tensor", TR(ps[:, 0:128], cr_[:].rearrange("q d p -> q (d p)"), C["identf"][:]), reads=[cr_, C["identf"]], writes=[ps])
                P.op(V_, CP(dst[:, gb * 8:(gb + 1) * 8, :].rearrange("q g h -> q (g h)"), ps[:, 0:128]), reads=[ps], writes=[dst])

        self.check_stop("c0b")

        def bc(x_ap):
            return x_ap.unsqueeze(2).to_broadcast([128, 32, 16])

        u1, u2, u3, u4 = [T(n, G3) for n in ["u1", "u2", "u3", "u4"]]
        Rr = [T("Rr%d" % i, G3) for i in range(2)]
        Ri = [T("Ri%d" % i, G3) for i in range(2)]
        fcnt = [0]

        def cprod(lr, li, Xr, Xi, lbufs, neg_im=False):
            i = fcnt[0]
            fcnt[0] += 1
            rr, ri = Rr[i % 2], Ri[i % 2]
            P.op(V_, TTo(u1[:], Xr[:], bc(lr), ALU.mult), reads=[Xr] + lbufs, writes=[u1])
            P.op("gpsimd", TTo(u2[:], Xi[:], bc(li), ALU.mult), reads=[Xi] + lbufs, writes=[u2])
            P.op(V_, TTo(u3[:], Xi[:], bc(lr), ALU.mult), reads=[Xi] + lbufs, writes=[u3])
            P.op("gpsimd", TTo(u4[:], Xr[:], bc(li), ALU.mult), reads=[Xr] + lbufs, writes=[u4])
            P.op(V_, TTo(rr[:], u1[:], u2[:], ALU.subtract), reads=[u1, u2], writes=[rr])
            if neg_im:
                P.op("gpsimd", STT_pool(ri[:], u3[:], u4[:]), reads=[u3, u4], writes=[ri])
            else:
                P.op("gpsimd", TTo(ri[:], u3[:], u4[:], ALU.add), reads=[u3, u4], writes=[ri])
            return rr, ri

        def STT_pool(out, a, b_):
            def f(e):
                e.tensor_scalar(out=a, in0=a, scalar1=-1.0, scalar2=None, op0=ALU.mult)
                return e.tensor_tensor(out=out, in0=a, in1=b_, op=ALU.subtract)
            return f

        rr, ri = cprod(fr[:], fi[:], Bre, Bim, [fr, fi])
        P.op(V_, CP(Bbr[:], rr[:]), reads=[rr], writes=[Bbr])
        P.op(V_, CP(Bbi[:], ri[:]), reads=[ri], writes=[Bbi])

        G4 = (128, 32, 8, 16)
        BSr, BSi, Xr_, Xi_, PCr, PCi = [T(n, G4) for n in ["BSr", "BSi", "Xr_", "Xi_", "PCr", "PCi"]]
        F_, R_ = slice(0, 64), slice(64, 128)
        for Pc in (PCr, PCi):
            P.op("gpsimd", lambda e, Pc=Pc: e.memset(Pc[R_], 0.0), writes=[Pc])
        CLv = CL[:].rearrange("q g r (i h) -> q g r i h", i=8)
        cpe = ["gpsimd", "scalar"]
        cc = [0]

        def cpy(dst_ap, dst_buf, src_ap, src_buf):
            e = cpe[cc[0] % 2]
            cc[0] += 1
            if e == "scalar":
                P.op(e, ACT(dst_ap, src_ap, AF.Copy), reads=[src_buf], writes=[dst_buf])
            else:
                P.op(e, CP(dst_ap, src_ap), reads=[src_buf], writes=[dst_buf])

        for k in range(8):
            rr, ri = cprod(LPr[:, k, :], LPi[:, k, :], Bbr, Bbi, [LPr, LPi])
            cpy(BSr[F_, :, 7 - k, :], BSr, rr[F_], rr)
            cpy(BSi[F_, :, 7 - k, :], BSi, ri[F_], ri)
            cpy(BSr[R_, :, k, :], BSr, rr[R_], rr)
            cpy(BSi[R_, :, k, :], BSi, ri[R_], ri)
        for k in range(8):
            rr, ri = cprod(LNr[:, k, :], LNi[:, k, :], Bbr, Bbi, [LNr, LNi])
            cpy(Xr_[F_, :, k, :], Xr_, rr[F_], rr)
            cpy(Xi_[F_, :, k, :], Xi_, ri[F_], ri)
            rr, ri = cprod(LNr[:, k, :], LNi[:, k, :], CTr, CTi, [LNr, LNi], neg_im=True)
            cpy(Xr_[R_, :, k, :], Xr_, rr[R_], rr)
            cpy(Xi_[R_, :, k, :], Xi_, ri[R_], ri)
        for k in range(9):
            rr, ri = cprod(LPr[:, k, :], LPi[:, k, :], CTr, CTi, [LPr, LPi], neg_im=True)
            if k <= 7:
                cpy(PCr[F_, :, k, :], PCr, rr[F_], rr)
                cpy(PCi[F_, :, k, :], PCi, ri[F_], ri)
            if k >= 1:
                cpy(CLv[F_, :, 0, k - 1, :], CL, rr[F_], rr)
                cpy(CLv[F_, :, 1, k - 1, :], CL, ri[F_], ri)
                cpy(CLv[R_, :, 0, 8 - k, :], CL, rr[R_], rr)
                cpy(CLv[R_, :, 1, 8 - k, :], CL, ri[R_], ri)
        self.check_stop("c0c")
        for g2 in range(16):
            ps = self.bank("c0", [0, 1])
            for gg in range(2):
                g = g2 * 2 + gg
                for ri_, Bs in enumerate((BSr, BSi)):
                    sl = (gg * 2 + ri_) * 128
                    P.op("tensor", TR(ps[:, sl:sl + 128], Bs[:, g].rearrange("q j h -> q (j h)"), C["identf"][:]), reads=[Bs, C["identf"]], writes=[ps],
                         inc=(gg == 1 and ri_ == 1))
            P.op(V_ if g2 % 2 else "scalar", (CP if g2 % 2 else (lambda o, i_: ACT(o, i_, AF.Copy)))(
                Bst[:, g2 * 2:g2 * 2 + 2].rearrange("q g r c -> q (g r c)"), ps[:, 0:512]), reads=[ps], writes=[Bst])
        self.check_stop("c0d")
        for Bs in (BSr, BSi):
            P.op("gpsimd", lambda e, Bs=Bs: e.memset(Bs[F_], 0.0), reads=[Bs], writes=[Bs])
        maskf, maskr, Dcol = T("maskf", (128, 128)), T("maskr", (128, 128)), T("Dcol", (128, 32))
        P.dma("sync", maskf[:], I["k_maskf"].t, reads=[I["k_maskf"]], writes=[maskf])
        P.dma("sync", maskr[:], I["k_maskr"].t, reads=[I["k_maskr"]], writes=[maskr])
        for j in range(8):
            P.dma("sync", Dcol[j * 16:(j + 1) * 16, :], I["ssm_d"].t[l, :].rearrange("(g h) -> h g", h=16), reads=[I["ssm_d"]], writes=[Dcol],
                  allow_slow_non_contiguous=True)
        tf = [T("tf%d" % i, (128, 128)) for i in range(2)]
        tr = [T("tr%d" % i, (128, 128)) for i in range(2)]
        for g in range(32):
            pf = self.bank("c0", [0, 1])
            pr = self.bank("c1", [2, 3])
            fl = lambda X: X[:, g].rearrange("q j h -> q (j h)")
            P.op("tensor", MM(pf[:, 0:128], fl(Xr_), fl(PCr), True, False), reads=[Xr_, PCr], writes=[pf], inc=False)
            P.op("tensor", MM(pf[:, 0:128], fl(Xi_), fl(PCi), False, True), reads=[Xi_, PCi], writes=[pf])
            P.op("tensor", MM(pr[:, 0:128], fl(BSr), fl(Xr_), True, False), reads=[BSr, Xr_], writes=[pr], inc=False)
            P.op("tensor", MM(pr[:, 0:128], fl(BSi), fl(Xi_), False, True), reads=[BSi, Xi_], writes=[pr])
            a_, b_ = tf[g % 2], tr[g % 2]
            P.op(V_, TTo(a_[:], pf[:, 0:128], maskf[:], ALU.mult), reads=[pf, maskf], writes=[a_])
            P.op(V_, TTo(b_[:], pr[:, 0:128], maskr[:], ALU.mult), reads=[pr, maskr], writes=[b_])
            P.op("gpsimd", TTo(a_[:], a_[:], b_[:], ALU.add), reads=[a_, b_], writes=[a_])
            P.op(V_, STT(Toep[:, g, :], C["identf"][:], Dcol[:, g:g + 1], a_[:], ALU.mult, ALU.add), reads=[C["identf"], Dcol, a_], writes=[Toep])
        P.pop()

    def ssm_batch(self, b, Bst, CL, Toep, A8, B8):
        P, I, S, C, l = self.P, self.I, self.S, self.C, self.l
        P.push()
        Ust = P.sb("Ust", [128, 32, NCH], BF16)
        ZSr = P.sb("ZSr", [128, 32, NCH], BF16)
        ZSi = P.sb("ZSi", [128, 32, NCH], BF16)
        Sr = P.sb("Sr", [128, 32, 290], BF16)
        Si = P.sb("Si", [128, 32, 290], BF16)
        F_, R_ = slice(0, 64), slice(64, 128)
        P.push()
        utm = [P.sb("utm%d" % i, [96, 4096], BF16) for i in range(2)]
        utm2 = [P.sb("utm2%d" % i, [96, 4096], BF16) for i in range(2)]
        for tile in range(3):
            u_ = utm[tile % 2]
            P.dma("sync", u_[0:96, :], S["U"].t[b, tile * 768:(tile + 1) * 768, :].rearrange("(c j) ch -> c (j ch)", j=8), reads=[S["U"]], writes=[u_])
            u2 = utm2[tile % 2]
            P.op("gpsimd", CP(u2[0:96, :].rearrange("c (g j h) -> c g j h", g=32, j=8), u_[0:96, :].rearrange("c (j g h) -> c g j h", j=8, g=32)),
                 reads=[u_], writes=[u2])
            uv = u2[0:96, :].rearrange("c (g x) -> c g x", g=32)
            for g8 in range(4):
                ps = self.bank("cs", [4, 5])
                pb = ps[:].bitcast(BF16).rearrange("p (k t) -> p k t", k=8)
                for gg in range(8):
                    P.op("tensor", TR(pb[:, gg, 0:96], uv[:, g8 * 8 + gg, :], C["identb"][0:96, 0:96]), reads=[u2, C["identb"]], writes=[ps], inc=(gg == 7))
                P.op("vector" if g8 % 2 else "scalar",
                     (CP if g8 % 2 else (lambda o, i_: ACT(o, i_, AF.Copy)))(Ust[:, g8 * 8:(g8 + 1) * 8, tile * 96:(tile + 1) * 96], pb[:, :, 0:96]),
                     reads=[ps], writes=[Ust])
        P.pop()
        self.check_stop("c1a")
        for g in range(32):
            for ri_, Z in enumerate((ZSr, ZSi)):
                ps = self.bank("cz", [0, 1, 2, 3])
                P.op("tensor", MM(ps[:, 0:NCH], Bst[:, g, ri_, :], Ust[:, g, :], True, True), reads=[Bst, Ust], writes=[ps])
                if ri_ == 0:
                    P.op("vector", CP(Z[:, g, :], ps[:, 0:NCH]), reads=[ps], writes=[Z])
                else:
                    P.op("scalar", ACT(Z[:, g, :], ps[:, 0:NCH], AF.Copy), reads=[ps], writes=[Z])
        self.check_stop("c1b")
        NR = 4
        Rg = [P.sb("Rg%d" % i, [128, 32], F32) for i in range(NR)]
        Ig = [P.sb("Ig%d" % i, [128, 32], F32) for i in range(NR)]
        tt = [[P.sb("sc%d_%d" % (i, j), [128, 32], F32) for j in range(6)] for i in range(2)]
        P.op("vector", lambda e: e.memset(Rg[0][:], 0.0), writes=[Rg[0]])
        P.op("vector", lambda e: e.memset(Ig[0][:], 0.0), writes=[Ig[0]])
        for Sx in (Sr, Si):
            P.op("gpsimd", lambda e, Sx=Sx: e.memset(Sx[F_, :, 1:2], 0.0), writes=[Sx])
            P.op("gpsimd", lambda e, Sx=Sx: e.memset(Sx[R_, :, 32:33], 0.0), writes=[Sx])
        order_r = list(range(31, -1, -1)) + list(range(287, 31, -1))
        V_ = "vector"
        for n in range(NCH):
            cf, cr = n, order_r[n]
            Rp, Ip, Rn, In = Rg[n % NR], Ig[n % NR], Rg[(n + 1) % NR], Ig[(n + 1) % NR]
            t1, t2, t3, t4, t5, t6 = tt[n % 2]
            P.op(V_, TTo(t1[:], A8[:], Rp[:], ALU.mult), reads=[A8, Rp], writes=[t1])
            P.op(V_, TTo(t2[:], B8[:], Ip[:], ALU.mult), reads=[B8, Ip], writes=[t2])
            P.op(V_, TTo(t4[:], B8[:], Rp[:], ALU.mult), reads=[B8, Rp], writes=[t4])
            P.op(V_, TTo(t5[:], A8[:], Ip[:], ALU.mult), reads=[A8, Ip], writes=[t5])
            P.op(V_, TTo(t3[:], t1[:], t2[:], ALU.subtract), reads=[t1, t2], writes=[t3])
            P.op(V_, TTo(t6[:], t4[:], t5[:], ALU.add), reads=[t4, t5], writes=[t6])
            P.op(V_, TTo(Rn[F_], t3[F_], ZSr[F_, :, cf], ALU.add), reads=[t3, ZSr], writes=[Rn])
            P.op(V_, TTo(Rn[R_], t3[R_], ZSr[R_, :, cr], ALU.add), reads=[t3, ZSr], writes=[Rn])
            P.op(V_, TTo(In[F_], t6[F_], ZSi[F_, :, cf], ALU.add), reads=[t6, ZSi], writes=[In])
            P.op(V_, TTo(In[R_], t6[R_], ZSi[R_, :, cr], ALU.add), reads=[t6, ZSi], writes=[In])
            sf = cf + 2
            for Sx, Xn in ((Sr, Rn), (Si, In)):
                P.op("gpsimd", CP(Sx[F_, :, sf], Xn[F_]), reads=[Xn], writes=[Sx])
                if cr != 32:
                    P.op("gpsimd", CP(Sx[R_, :, cr], Xn[R_]), reads=[Xn], writes=[Sx])
                if cr == 0:
                    P.op("gpsimd", CP(Sx[R_, :, 288], Xn[R_]), reads=[Xn], writes=[Sx])
        self.check_stop("c1c")
        P.push()
        Wg = P.sb("Wg", [128, 4, 2 * D], BF16)
        P.dma("sync", Wg[:], S["wb_glu%d" % l].t.rearrange("(t p) n -> p t n", p=128), reads=[S["wb_glu%d" % l]], writes=[Wg])
        ygl = P.sb("ygl", [128, 8, 512], BF16)
        yT = P.sb("yT", [128, 4, 1024], BF16)
        yTv = yT[:].rearrange("p t (c i) -> p t c i", i=8)
        sig = [P.sb("sig%d" % i, [128, 512], F32) for i in range(2)]
        gst = [P.sb("gst%d" % i, [128, 1024], BF16) for i in range(2)]
        ydb = P.sb("ydb", [128, 8, 512], F32) if "YS" in self.dbg else None
        blocks = ([] if self.last else [(0, 32, 1)]) + [(32, 128, 2), (160, 128, 2)]
        nsg = 0
        for (c0, M, roff) in blocks:
            for g4 in range(8):
                ps = self.bank("cr", [0, 1, 2, 3])
                for gg in range(4):
                    g = g4 * 4 + gg
                    o = ps[0:M, gg * 128:(gg + 1) * 128]
                    P.op("tensor", MM(o, Ust[:, g, c0:c0 + M], Toep[:, g, :], True, False), reads=[Ust, Toep], writes=[ps], inc=False)
                    P.op("tensor", MM(o, Sr[:, g, c0 + 1:c0 + 1 + M], CL[:, g, 0, :], False, False), reads=[Sr, CL], writes=[ps], inc=False)
                    P.op("tensor", MM(o, Si[:, g, c0 + 1:c0 + 1 + M], CL[:, g, 1, :], False, True), reads=[Si, CL], writes=[ps], inc=(gg == 3))
                pv = ps[0:M, :].rearrange("c (g i h) -> c i g h", g=4, i=8)
                yo = ygl[0:M, :, g4 * 64:(g4 + 1) * 64].rearrange("c i (g h) -> c i g h", g=4)
                P.op("scalar", ACT(yo, pv, AF.Gelu_apprx_tanh), reads=[ps], writes=[ygl])
                if ydb is not None:
                    P.op("vector", CP(ydb[0:M, :, g4 * 64:(g4 + 1) * 64].rearrange("c i (g h) -> c i g h", g=4), pv), reads=[ps], writes=[ydb])
            self.check_stop("c2a")
            if ydb is not None:
                P.dma("gpsimd", S["YS"].t[b, c0 * 8:(c0 + M) * 8, :].rearrange("(c i) ch -> c i ch", i=8), ydb[0:M], reads=[ydb], writes=[S["YS"]])
            self.check_stop("c2b")
            for i in range(8):
                ps = self.bank("cs", [4, 5])
                pb = ps[:].bitcast(BF16).rearrange("p (k t) -> p k t", k=8)
                for t in range(4):
                    P.op("tensor", TR(pb[:, t, 0:M], ygl[0:M, i, t * 128:(t + 1) * 128], C["identb"][0:M, 0:M]), reads=[ygl, C["identb"]], writes=[ps], inc=(t == 3))
                P.op("vector", CP(yTv[:, :, 0:M, i], pb[:, 0:4, 0:M]), reads=[ps], writes=[yT])
            self.check_stop("c2c")
            ntok = M * 8
            for ct in range(8):
                g_ = gst[ct % 2]
                for n0 in range(0, ntok, 512):
                    nn = min(512, ntok - n0)
                    pa = self.bank("cg", [6, 7, 0, 1, 2, 3])
                    pg = self.bank("cg", [6, 7, 0, 1, 2, 3])
                    for t in range(4):
                        P.op("tensor", MM(pa[:, 0:nn], Wg[:, t, ct * 128:(ct + 1) * 128], yT[:, t, n0:n0 + nn], t == 0, t == 3), reads=[Wg, yT], writes=[pa], inc=(t == 3))
                    for t in range(4):
                        P.op("tensor", MM(pg[:, 0:nn], Wg[:, t, D + ct * 128:D + (ct + 1) * 128], yT[:, t, n0:n0 + nn], t == 0, t == 3), reads=[Wg, yT], writes=[pg], inc=(t == 3))
                    sg = sig[nsg % 2]
                    nsg += 1
                    P.op("scalar", ACT(sg[:, 0:nn], pg[:, 0:nn], AF.Sigmoid), reads=[pg], writes=[sg])
                    P.op("vector", TTo(g_[:, n0:n0 + nn], pa[:, 0:nn], sg[:, 0:nn], ALU.mult), reads=[pa, sg], writes=[g_])
                P.dma("gpsimd", S["SSMT"].t[b, ct * 128:(ct + 1) * 128, c0 * 8:c0 * 8 + ntok], g_[:, 0:ntok], reads=[g_], writes=[S["SSMT"]])
            self.check_stop("c2d")
            if c0 == 32:
                self.check_stop("c2e")
        P.pop()
        P.pop()


    def postnorm(self, pss, xt, gbc, lng, lnb, upd, yv, yn, sm, dst_buf, dst_ap):
        P = self.P
        for half in range(2):
            P.op("vector", TTo(upd[:, half * 512:(half + 1) * 512], pss[half][:, 0:512], gbc[:, half * 512:(half + 1) * 512], ALU.mult),
                 reads=[pss[half], gbc], writes=[upd])
        P.op("vector", STT(yv[:], xt[:], ALPHA, upd[:], ALU.mult, ALU.add), reads=[xt, upd], writes=[yv])
        self.ln_tile(yv, yn[:], yn, sm)
        P.op("gpsimd", TTo(yn[:], yn[:], lng[:], ALU.mult), reads=[yn, lng], writes=[yn])
        P.op("gpsimd", TTo(yn[:], yn[:], lnb[:], ALU.add), reads=[yn, lnb], writes=[yn])
        P.dma("gpsimd", dst_ap, yn[:], reads=[yn], writes=[dst_buf])

    def segs(self):
        out = []
        for b in range(NBC):
            if not self.last:
                out.append((b, "ctx", 2, 0, TC))
            out.append((b, "lat", b, TC, TL))
        return out

    def phaseD(self):
        P, I, S, C, l = self.P, self.I, self.S, self.C, self.l
        P.push()
        Wco = P.sb("Wco", [128, 4, D], BF16)
        Wao = P.sb("Wao", [128, 8, D], BF16)
        Wo = P.sb("Wo", [128, 8, D], BF16)
        for Wt, nm in ((Wco, "wb_co"), (Wao, "wb_ao"), (Wo, "wb_o")):
            src = S[nm + str(l)]
            P.dma("sync", Wt[:], src.t.rearrange("(t p) n -> p t n", p=128), reads=[src], writes=[Wt])
        cw = P.sb("cw", [128, 4, 3], F32)
        for k in range(3):
            P.dma("sync", cw[:, :, k], I["conv_w"].t[l, k, :].rearrange("(t p) -> p t", p=128), reads=[I["conv_w"]], writes=[cw], allow_slow_non_contiguous=True)
        lng = P.sb("lng", [128, D], F32)
        lnb = P.sb("lnb", [128, D], F32)
        P.dma("sync", lng[:], I["ln1_g"].t[l, :].partition_broadcast(128), reads=[I["ln1_g"]], writes=[lng])
        P.dma("sync", lnb[:], I["ln1_b"].t[l, :].partition_broadcast(128), reads=[I["ln1_b"]], writes=[lnb])
        gbc = P.sb("gbc", [128, D], F32)
        axh = P.sb("axh", [128, 4, 514], BF16)
        cgh = P.sb("cgh", [128, 4, 514], BF16)
        bgs = P.sb("bgs", [128, 4, 512], BF16)
        prod = P.sb("prod", [128, 4, 514], F32)
        acc = P.sb("acc", [128, 4, 512], F32)
        convT = P.sb("convT", [128, 4, 512], BF16)
        aoT = P.sb("aoT", [128, 8, 512], BF16)
        gts = [P.sb("gts%d" % i, [128, 3, 512], BF16) for i in range(2)]
        ssmT = [P.sb("ssmT%d" % i, [128, 512], BF16) for i in range(2)]
        m1 = [P.sb("m1_%d" % i, [128, 512], F32) for i in range(2)]
        m2 = [P.sb("m2_%d" % i, [128, 512], F32) for i in range(2)]
        m3 = [P.sb("m3_%d" % i, [128, 512], F32) for i in range(2)]
        mgT = P.sb("mgT", [128, 8, 512], BF16)
        xt = [P.sb("dxt%d" % i, [128, D], F32) for i in range(2)]
        upd = [P.sb("dupd%d" % i, [128, D], F32) for i in range(2)]
        yv = [P.sb("dyv%d" % i, [128, D], F32) for i in range(2)]
        yn = [P.sb("dyn%d" % i, [128, D], F32) for i in range(2)]
        sm = self.ln_small("d")
        nx = 0
        nd = 0
        for (b, seg, r, gofs, slen) in self.segs():
            P.dma("sync", gbc[:], S["modr%d" % l].t[r, 2 * D:3 * D].partition_broadcast(128), reads=[S["modr%d" % l]], writes=[gbc])
            rbuf, rap = self.res_in(b, seg)
            n = min(512, slen)
            for t0 in range(0, slen, n):
                g0 = gofs + t0
                lo = 1 if t0 == 0 else 0
                hi = n + 1 if t0 + n == slen else n + 2
                for (hb, nm) in ((axh, "AXT"), (cgh, "CGT")):
                    if lo == 1:
                        P.op("gpsimd", lambda e, hb=hb: e.memset(hb[:, :, 0:1], 0.0), writes=[hb])
                    if hi == n + 1:
                        P.op("gpsimd", lambda e, hb=hb, n=n: e.memset(hb[:, :, n + 1:n + 2], 0.0), writes=[hb])
                    P.dma("sync", hb[:, :, lo:hi], S[nm].t[b].rearrange("(t p) c -> p t c", p=128)[:, :, g0 - 1 + lo:g0 - 1 + hi], reads=[S[nm]], writes=[hb])
                P.dma("sync", bgs[:, :, 0:n], S["BGT"].t[b].rearrange("(t p) c -> p t c", p=128)[:, :, g0:g0 + n], reads=[S["BGT"]], writes=[bgs])
                P.dma("sync", aoT[:, :, 0:n], S["AOT"].t[b].rearrange("(t p) c -> p t c", p=128)[:, :, g0:g0 + n], reads=[S["AOT"]], writes=[aoT])
                P.op("gpsimd", TTo(prod[:, :, 0:n + 2], cgh[:, :, 0:n + 2], axh[:, :, 0:n + 2], ALU.mult), reads=[cgh, axh], writes=[prod])
                for t in range(4):
                    P.op("vector", TS(acc[:, t, 0:n], prod[:, t, 0:n], cw[:, t, 0:1], None, ALU.mult), reads=[prod, cw], writes=[acc])
                    P.op("vector", STT(acc[:, t, 0:n], prod[:, t, 1:n + 1], cw[:, t, 1:2], acc[:, t, 0:n], ALU.mult, ALU.add), reads=[prod, cw, acc], writes=[acc])
                    P.op("vector", STT(acc[:, t, 0:n], prod[:, t, 2:n + 2], cw[:, t, 2:3], acc[:, t, 0:n], ALU.mult, ALU.add), reads=[prod, cw, acc], writes=[acc])
                P.op("gpsimd", TTo(convT[:, :, 0:n], acc[:, :, 0:n], bgs[:, :, 0:n], ALU.mult), reads=[acc, bgs], writes=[convT])
                for dt in range(8):
                    gt_, sm_ = gts[nd % 2], ssmT[nd % 2]
                    a1, a2, a3 = m1[nd % 2], m2[nd % 2], m3[nd % 2]
                    nd += 1
                    for s3 in range(3):
                        P.dma("sync", gt_[:, s3, 0:n], S["GT"].t[b, s3 * D + dt * 128:s3 * D + (dt + 1) * 128, g0:g0 + n], reads=[S["GT"]], writes=[gt_])
                    P.dma("sync", sm_[:, 0:n], S["SSMT"].t[b, dt * 128:(dt + 1) * 128, g0:g0 + n], reads=[S["SSMT"]], writes=[sm_])
                    pc = self.bank("dc", [0, 1])
                    pa = self.bank("da", [2, 3])
                    for t in range(4):
                        P.op("tensor", MM(pc[:, 0:n], Wco[:, t, dt * 128:(dt + 1) * 128], convT[:, t, 0:n], t == 0, t == 3), reads=[Wco, convT], writes=[pc], inc=(t == 3))
                    for k in range(8):
                        P.op("tensor", MM(pa[:, 0:n], Wao[:, k, dt * 128:(dt + 1) * 128], aoT[:, k, 0:n], k == 0, k == 7), reads=[Wao, aoT], writes=[pa], inc=(k == 7))
                    P.op("vector", TTo(a1[:, 0:n], pc[:, 0:n], gt_[:, 0, 0:n], ALU.mult), reads=[pc, gt_], writes=[a1])
                    P.op("vector", TTo(a2[:, 0:n], pa[:, 0:n], gt_[:, 2, 0:n], ALU.mult), reads=[pa, gt_], writes=[a2])
                    P.op("gpsimd", TTo(a3[:, 0:n], sm_[:, 0:n], gt_[:, 1, 0:n], ALU.mult), reads=[sm_, gt_], writes=[a3])
                    P.op("gpsimd", TTo(a1[:, 0:n], a1[:, 0:n], a2[:, 0:n], ALU.add), reads=[a1, a2], writes=[a1])
                    P.op("gpsimd", TTo(mgT[:, dt, 0:n], a1[:, 0:n], a3[:, 0:n], ALU.add), reads=[a1, a3], writes=[mgT])
                for ti in range(n // 128):
                    i = nx
                    nx += 1
                    x_ = xt[i % 2]
                    P.dma("sync", x_[:], rap[t0 + ti * 128:t0 + (ti + 1) * 128, :], reads=[rbuf], writes=[x_])
                    pss = []
                    for half in range(2):
                        ps = self.bank("do", [4, 5, 6, 7])
                        for k in range(8):
                            P.op("tensor", MM(ps[:, 0:512], mgT[:, k, ti * 128:(ti + 1) * 128], Wo[:, k, half * 512:(half + 1) * 512], k == 0, k == 7),
                                 reads=[mgT, Wo], writes=[ps], inc=(k == 7))
                        pss.append(ps)
                    row = g0 + ti * 128
                    self.postnorm(pss, x_, gbc, lng, lnb, upd[i % 2], yv[i % 2], yn[i % 2], sm[i % 2], S["resA"], S["resA"].t[b, row:row + 128, :])
        P.pop()

    def phaseF(self):
        P, I, S, C, l = self.P, self.I, self.S, self.C, self.l
        P.push()
        lng = P.sb("lng2", [128, D], F32)
        lnb = P.sb("lnb2", [128, D], F32)
        P.dma("sync", lng[:], I["ln2_g"].t[l, :].partition_broadcast(128), reads=[I["ln2_g"]], writes=[lng])
        P.dma("sync", lnb[:], I["ln2_b"].t[l, :].partition_broadcast(128), reads=[I["ln2_b"]], writes=[lnb])
        cwf = P.sb("cwf", [128, 22, 3], F32)
        cbf = P.sb("cbf", [128, 22], F32)
        for k in range(3):
            P.dma("sync", cwf[:, :, k], I["ffn_conv_w"].t[l, k, :].rearrange("(t p) -> p t", p=128), reads=[I["ffn_conv_w"]], writes=[cwf], allow_slow_non_contiguous=True)
        P.dma("sync", cbf[:], I["ffn_conv_b"].t[l, :].rearrange("(t p) -> p t", p=128), reads=[I["ffn_conv_b"]], writes=[cbf], allow_slow_non_contiguous=True)
        hff = P.sb("hff", [128, 22, TL], BF16)
        gbc = P.sb("gbc2", [128, D], F32)
        wup = S["wb_up%d" % l].t.rearrange("(k p) n -> p k n", p=128)
        for (b, seg, r, gofs, slen) in self.segs():
            ntile = slen // 128
            P.push()
            hT2 = P.sb("hT2", [128, 8, slen], BF16)
            xt = [P.sb("fxt%d" % i, [128, D], F32) for i in range(2)]
            xn = [P.sb("fxn%d" % i, [128, D], BF16) for i in range(2)]
            sm = self.ln_small("f")
            for ti in range(ntile):
                x_, n_ = xt[ti % 2], xn[ti % 2]
                row = gofs + ti * 128
                P.dma("sync", x_[:], S["resA"].t[b, row:row + 128, :], reads=[S["resA"]], writes=[x_])
                self.ln_tile(x_, n_[:], n_, sm[ti % 2])
                ps = self.bank("ftr", [0, 1])
                pb = ps[:].bitcast(BF16).rearrange("p (k t) -> p k t", k=8)
                for k in range(8):
                    P.op("tensor", TR(pb[:, k, :], n_[:, k * 128:(k + 1) * 128], C["identb"][:]), reads=[n_, C["identb"]], writes=[ps], inc=(k == 7))
                for k in range(8):
                    o = hT2[:, k, ti * 128:(ti + 1) * 128]
                    rd = [ps, self.modT] if k in (0, 7) else []
                    wr = [hT2] if k in (0, 7) else []
                    if ti % 2 == 0:
                        P.op("vector", TS(o, pb[:, k, :], self.modT[:, 4, k, r:r + 1], self.modT[:, 3, k, r:r + 1], ALU.mult, ALU.add), reads=rd, writes=wr)
                    else:
                        P.op("scalar", ACT(o, pb[:, k, :], AF.Identity, bias=self.modT[:, 3, k, r:r + 1], scale=self.modT[:, 4, k, r:r + 1]), reads=rd, writes=wr)
            wu = [P.sb("wu%d" % i, [128, 8, 128], BF16) for i in range(2)]
            wv = [P.sb("wv%d" % i, [128, 8, 128], BF16) for i in range(2)]
            ucp = [P.sb("ucp%d" % i, [128, slen + 2], F32) for i in range(2)]
            acc = P.sb("facc", [128, slen], F32)
            ge = [P.sb("fge%d" % i, [128, slen], BF16) for i in range(2)]
            for u_ in ucp:
                P.op("gpsimd", lambda e, u_=u_: e.memset(u_[:, 0:1], 0.0), writes=[u_])
                P.op("gpsimd", lambda e, u_=u_, slen=slen: e.memset(u_[:, slen + 1:slen + 2], 0.0), writes=[u_])
            nbs = [(n0, min(512, slen - n0)) for n0 in range(0, slen, 512)]
            for j in range(22):
                wu_, wv_, u_, g_ = wu[j % 2], wv[j % 2], ucp[j % 2], ge[j % 2]
                P.dma("sync", wu_[:], wup[:, :, j * 128:(j + 1) * 128], reads=[S["wb_up%d" % l]], writes=[wu_])
                P.dma("sync", wv_[:], wup[:, :, DFF + j * 128:DFF + (j + 1) * 128], reads=[S["wb_up%d" % l]], writes=[wv_])
                for (n0, nn) in nbs:
                    ps = self.bank("fu", [0, 1, 2, 3])
                    for k in range(8):
                        P.op("tensor", MM(ps[:, 0:nn], wu_[:, k, :], hT2[:, k, n0:n0 + nn], k == 0, k == 7), reads=[wu_, hT2], writes=[ps], inc=(k == 7))
                    P.op("scalar", ACT(u_[:, 1 + n0:1 + n0 + nn], ps[:, 0:nn], AF.Copy), reads=[ps], writes=[u_])
                P.op("vector", TS(acc[:], u_[:, 0:slen], cwf[:, j, 0:1], None, ALU.mult), reads=[u_, cwf], writes=[acc])
                P.op("vector", STT(acc[:], u_[:, 1:slen + 1], cwf[:, j, 1:2], acc[:], ALU.mult, ALU.add), reads=[u_, cwf, acc], writes=[acc])
                P.op("vector", STT(acc[:], u_[:, 2:slen + 2], cwf[:, j, 2:3], acc[:], ALU.mult, ALU.add), reads=[u_, cwf, acc], writes=[acc])
                P.op("scalar", ACT(g_[:], acc[:], AF.Gelu_apprx_tanh, bias=cbf[:, j:j + 1], scale=1.0), reads=[acc, cbf], writes=[g_])
                for (n0, nn) in nbs:
                    ps = self.bank("fv", [4, 5, 6, 7])
                    for k in range(8):
                        P.op("tensor", MM(ps[:, 0:nn], wv_[:, k, :], hT2[:, k, n0:n0 + nn], k == 0, k == 7), reads=[wv_, hT2], writes=[ps], inc=(k == 7))
                    P.op("vector", TTo(hff[:, j, n0:n0 + nn], ps[:, 0:nn], g_[:, n0:n0 + nn], ALU.mult), reads=[ps, g_], writes=[hff])
            P.pop()
            P.push()
            Wd = P.sb("Wd", [128, 22, D], BF16)
            P.dma("sync", Wd[:], S["wb_dn%d" % l].t.rearrange("(t p) n -> p t n", p=128), reads=[S["wb_dn%d" % l]], writes=[Wd])
            P.dma("sync", gbc[:], S["modr%d" % l].t[r, 5 * D:6 * D].partition_broadcast(128), reads=[S["modr%d" % l]], writes=[gbc])
            xt = [P.sb("gxt%d" % i, [128, D], F32) for i in range(2)]
            upd = [P.sb("gupd%d" % i, [128, D], F32) for i in range(2)]
            yv = [P.sb("gyv%d" % i, [128, D], F32) for i in range(2)]
            yn = [P.sb("gyn%d" % i, [128, D], F32) for i in range(2)]
            sm = self.ln_small("g")
            for ti in range(ntile):
                x_ = xt[ti % 2]
                row = gofs + ti * 128
                P.dma("sync", x_[:], S["resA"].t[b, row:row + 128, :], reads=[S["resA"]], writes=[x_])
                pss = []
                for half in range(2):
                    ps = self.bank("fd", [0, 1, 2, 3])
                    for j in range(22):
                        P.op("tensor", MM(ps[:, 0:512], hff[:, j, ti * 128:(ti + 1) * 128], Wd[:, j, half * 512:(half + 1) * 512], j == 0, j == 21),
                             reads=[hff, Wd], writes=[ps], inc=(j == 21))
                    pss.append(ps)
                if self.last:
                    dbuf, dap = self.out, self.out.t[b, ti * 128:(ti + 1) * 128, :]
                else:
                    dbuf, dap = S["resB"], S["resB"].t[b, row:row + 128, :]
                self.postnorm(pss, x_, gbc, lng, lnb, upd[ti % 2], yv[ti % 2], yn[ti % 2], sm[ti % 2], dbuf, dap)
            P.pop()
        P.pop()


def _rope_tables():
    half = 64
    inv_freq = (1.0 / (np.float32(10000.0) ** (np.arange(0, half, 2, dtype=np.float32) / np.float32(half)))).astype(np.float32)
    rows = TL // 64
    row = np.repeat(np.arange(rows, dtype=np.float32), 64)
    col = np.tile(np.arange(64, dtype=np.float32), rows)
    ang = np.concatenate([row[:, None] * inv_freq, col[:, None] * inv_freq], -1).astype(np.float32)
    cos = np.cos(ang).astype(np.float32).reshape(16, 128, 64).transpose(1, 0, 2)
    sin = np.sin(ang).astype(np.float32).reshape(16, 128, 64).transpose(1, 0, 2)
    return np.ascontiguousarray(cos), np.ascontiguousarray(sin)


def _host_consts():
    cos, sin = _rope_tables()
    jj = np.arange(128) // 16
    maskf = (jj[None, :] >= jj[:, None]).astype(np.float32)
    maskr = (jj[None, :] <= jj[:, None]).astype(np.float32)
    return {
        "k_identf": np.eye(128, dtype=np.float32),
        "k_identb": np.eye(128, dtype=np.float32).astype(ml_dtypes.bfloat16),
        "k_onesb": np.ones((128, 128), dtype=np.float32).astype(ml_dtypes.bfloat16),
        "k_cos": cos, "k_sin": sin, "k_maskf": maskf, "k_maskr": maskr,
    }


_NC_CACHE = {}


def _get_nc():
    if "nc" not in _NC_CACHE:
        _NC_CACHE["nc"] = K().build()
    return _NC_CACHE["nc"]


def make_in_maps(inputs, cores):
    consts = _host_consts()
    maps = []
    for i in cores:
        m = {}
        for k, v in inputs.items():
            v = np.asarray(v)
            if k in ("x", "c", "ctx"):
                m[k] = np.ascontiguousarray(v[NBC * i:NBC * (i + 1)])
            elif k == "c_ctx":
                m[k] = np.ascontiguousarray(v.reshape(1, D))
            else:
                m[k] = v
        m.update(consts)
        maps.append(m)
    return maps


def kernel(**inputs):
    nc = _get_nc()
    maps = make_in_maps(inputs, range(8))
    res = run_bass_kernel_spmd(nc, maps, core_ids=list(range(8)))
    return np.concatenate([np.asarray(r["out"]) for r in res.results], axis=0).astype(np.float32)
```

```python
import numpy as np
import ml_dtypes
from contextlib import ExitStack
import concourse.bass as bass
import concourse.mybir as mybir
from concourse.bass_utils import run_bass_kernel_spmd

F32 = mybir.dt.float32
BF16 = mybir.dt.bfloat16
I32 = mybir.dt.int32
AF = mybir.ActivationFunctionType
ALU = mybir.AluOpType

ENGS = ["tensor", "vector", "scalar", "gpsimd", "sync"]

D = 1024
TL = 2048
TC = 256
TT = TC + TL
NBC = 2
DEPTH = 2
IN_COLS = 6656
DFF = 2816
NCH = TT // 8
EPS = 1e-6
ALPHA = float((2 * DEPTH) ** 0.25)
ATTN_SCALE = float(128 ** -0.5)
TWO_PI = float(2 * np.pi)
PI = float(np.pi)


class Buf:
    def __init__(self, name, t):
        self.name = name
        self.t = t
        self.w = {}
        self.r = {}
        self.dkey = {}

    def __getitem__(self, idx):
        return self.t[idx]


class Prog:
    def __init__(self, nc, n_dsem=56):
        self.nc = nc
        self.base = ExitStack()
        self.scopes = [ExitStack()]
        self.scope_bufs = [[]]
        self.q = {e: [] for e in ENGS}
        self.sem = {}
        self.cnt = {}
        self.seen = {e: {} for e in ENGS}
        for e in ENGS:
            self.sem[e] = self.base.enter_context(nc.semaphore("s_" + e))
            self.cnt[e] = 0
        self.free_dsem = {"sync": [], "gpsimd": [], "scalar": []}
        for qn, nq_ in (("sync", 36), ("gpsimd", 24), ("scalar", 24)):
            for i in range(nq_):
                k = ("d" + qn, i)
                self.sem[k] = self.base.enter_context(nc.semaphore("d%s%d" % (qn[:2], i)))
                self.cnt[k] = 0
                self.free_dsem[qn].append(k)
        self.uid = 0

    def push(self):
        self.scopes.append(ExitStack())
        self.scope_bufs.append([])

    def pop(self):
        self.barrier()
        for b in self.scope_bufs.pop():
            for qn, k in b.dkey.items():
                self.free_dsem[qn].append(k)
            b.dkey = {}
        self.scopes.pop().close()

    def sb(self, name, shape, dtype):
        self.uid += 1
        t = self.scopes[-1].enter_context(self.nc.sbuf_tensor("%s_%d" % (name, self.uid), list(shape), dtype))
        b = Buf(name, t)
        self.scope_bufs[-1].append(b)
        return b

    def ps(self, name, shape, dtype=F32):
        t = self.base.enter_context(self.nc.psum_tensor(name, list(shape), dtype))
        return Buf(name, t)

    def dram(self, name, shape, dtype, kind="Internal"):
        t = self.nc.dram_tensor(name, list(shape), dtype, kind=kind).ap()
        return Buf(name, t)

    def _dkey(self, b, eng):
        if eng not in b.dkey:
            b.dkey[eng] = self.free_dsem[eng].pop()
        return b.dkey[eng]

    def _deps(self, eng, reads, writes, strict=False, nowaw=False):
        deps = {}

        def add(d):
            for k, v in d.items():
                if k == eng and eng == "tensor":
                    continue
                if deps.get(k, 0) < v:
                    deps[k] = v

        for r in reads:
            add(r.w)
        for w in writes:
            if not nowaw:
                add(w.w)
            add(w.r)
        out = []
        for k, v in deps.items():
            if self.seen[eng].get(k, 0) >= v:
                continue
            self.seen[eng][k] = v
            out.append((self.sem[k], v))
        return out

    def _commit(self, ev, reads, writes, nowaw=False):
        k, v = ev
        for r in reads:
            if r.r.get(k, 0) < v:
                r.r[k] = v
        for w in writes:
            if w.w.get(k, 0) < v:
                w.w[k] = v
            if not nowaw:
                w.r = {}

    def op(self, eng, fn, reads=(), writes=(), inc=True, nowaw=False):
        waits = self._deps(eng, reads, writes, nowaw=nowaw)
        if inc:
            self.cnt[eng] += 1
            ev = (eng, self.cnt[eng])
        else:
            ev = (eng, self.cnt[eng] + 1)
        sem = self.sem[eng]

        def emit(e, waits=waits, fn=fn, inc=inc, sem=sem):
            for s, v in waits:
                e.wait_ge(s, v)
            ins = fn(e)
            if inc:
                ins.then_inc(sem, 1)

        self.q[eng].append(emit)
        self._commit(ev, reads, writes, nowaw=nowaw)

    def dma(self, eng, out, in_, reads=(), writes=(), semb=None, **kw):
        waits = self._deps(eng, reads, writes, strict=True)
        if semb is None:
            for b in list(writes) + list(reads):
                if not isinstance(b, DBuf):
                    semb = b
                    break
        key = self._dkey(semb, eng)
        self.cnt[key] += 16
        ev = (key, self.cnt[key])
        sem = self.sem[key]

        def emit(e, waits=waits, out=out, in_=in_, sem=sem, kw=kw):
            for s, v in waits:
                e.wait_ge(s, v)
            e.dma_start(out=out, in_=in_, **kw).then_inc(sem, 16)

        self.q[eng].append(emit)
        self._commit(ev, reads, writes)

    def barrier(self):
        tot = dict(self.cnt)
        for e in ENGS:
            waits = []
            for k, v in tot.items():
                if k == e or v == 0:
                    continue
                if self.seen[e].get(k, 0) >= v:
                    continue
                self.seen[e][k] = v
                waits.append((self.sem[k], v))

            def emit(en, waits=waits):
                for s, v in waits:
                    en.wait_ge(s, v)

            self.q[e].append(emit)

    def finish(self):
        self.barrier()
        nc = self.nc
        q = self.q
        with nc.Block() as block:
            @block.tensor
            def _(e):
                for f in q["tensor"]:
                    f(e)

            @block.vector
            def _(e):
                for f in q["vector"]:
                    f(e)

            @block.scalar
            def _(e):
                for f in q["scalar"]:
                    f(e)

            @block.gpsimd
            def _(e):
                for f in q["gpsimd"]:
                    f(e)

            @block.sync
            def _(e):
                for f in q["sync"]:
                    f(e)
        while self.scopes:
            self.scopes.pop().close()
        self.base.close()


class DBuf(Buf):
    pass


def TS(out, in0, s1, s2, op0, op1=None):
    if op1 is None:
        return lambda e: e.tensor_scalar(out=out, in0=in0, scalar1=s1, scalar2=None, op0=op0)
    return lambda e: e.tensor_scalar(out=out, in0=in0, scalar1=s1, scalar2=s2, op0=op0, op1=op1)


def TTo(out, in0, in1, op):
    return lambda e: e.tensor_tensor(out=out, in0=in0, in1=in1, op=op)


def STT(out, in0, scalar, in1, op0, op1):
    return lambda e: e.scalar_tensor_tensor(out=out, in0=in0, scalar=scalar, in1=in1, op0=op0, op1=op1)


def ACT(out, in_, func, bias=None, scale=None, accum_out=None):
    kw = {}
    if bias is not None:
        kw["bias"] = bias
    if scale is not None:
        kw["scale"] = scale
    if accum_out is not None:
        kw["accum_out"] = accum_out
    return lambda e: e.activation(out=out, in_=in_, func=func, **kw)


def CP(out, in_):
    return lambda e: e.tensor_copy(out=out, in_=in_)


def MM(out, lhsT, rhs, start, stop):
    return lambda e: e.matmul(out, lhsT=lhsT, rhs=rhs, start=start, stop=stop)


def TR(out, in_, ident):
    return lambda e: e.transpose(out=out, in_=in_, identity=ident)


class K:
    def __init__(self, dbg=(), stop_after=None):
        self.dbg = set(dbg)
        self.stop_after = stop_after
        nc = self.nc = bass.Bass("TRN2", target_bir_lowering=False)
        P = self.P = Prog(nc)
        self.inputs = {}
        self.psb = [P.ps("psb%d" % i, [128, 512], F32) for i in range(8)]
        self.rr = {}

    def din(self, name, shape, dt=F32):
        b = DBuf(name, self.nc.dram_tensor(name, list(shape), dt, kind="ExternalInput").ap())
        self.inputs[name] = b
        return b

    def dscr(self, name, shape, dt):
        kind = "ExternalOutput" if name in self.dbg else "Internal"
        return DBuf(name, self.nc.dram_tensor(name, list(shape), dt, kind=kind).ap())

    def bank(self, group, banks):
        i = self.rr.get(group, 0)
        self.rr[group] = i + 1
        return self.psb[banks[i % len(banks)]]

    def build(self):
        nc, P = self.nc, self.P
        L = DEPTH
        I = self.I = {}
        I["x"] = self.din("x", [NBC, TL, D])
        I["c"] = self.din("c", [NBC, D])
        I["ctx"] = self.din("ctx", [NBC, TC, D])
        I["c_ctx"] = self.din("c_ctx", [1, D])
        for name, shape in [
            ("w_mod", [L, D, 6 * D]), ("b_mod", [L, 6 * D]), ("w_in", [L, D, IN_COLS]), ("conv_w", [L, 3, 512]),
            ("w_conv_out", [L, 512, D]), ("ssm_lam_re", [L, 2, 32, 64]), ("ssm_lam_im", [L, 2, 32, 64]),
            ("ssm_log_dt", [L, 2, 32]), ("ssm_b_re", [L, 2, 32, 64, 16]), ("ssm_b_im", [L, 2, 32, 64, 16]),
            ("ssm_c_re", [L, 2, 32, 16, 64]), ("ssm_c_im", [L, 2, 32, 16, 64]), ("ssm_d", [L, 512]),
            ("w_glu", [L, 512, 2 * D]), ("q_norm_g", [L, 128]), ("k_norm_g", [L, 128]), ("w_attn_out", [L, D, D]),
            ("w_o", [L, D, D]), ("ln1_g", [L, D]), ("ln1_b", [L, D]), ("ffn_w_up", [L, D, 2 * DFF]),
            ("ffn_conv_w", [L, 3, DFF]), ("ffn_conv_b", [L, DFF]), ("ffn_w_down", [L, DFF, D]),
            ("ln2_g", [L, D]), ("ln2_b", [L, D]),
        ]:
            I[name] = self.din(name, shape)
        I["k_identf"] = self.din("k_identf", [128, 128])
        I["k_identb"] = self.din("k_identb", [128, 128], BF16)
        I["k_onesb"] = self.din("k_onesb", [128, 128], BF16)
        I["k_cos"] = self.din("k_cos", [128, 16, 64])
        I["k_sin"] = self.din("k_sin", [128, 16, 64])
        I["k_maskf"] = self.din("k_maskf", [128, 128])
        I["k_maskr"] = self.din("k_maskr", [128, 128])
        self.out = DBuf("out", nc.dram_tensor("out", [NBC, TL, D], F32, kind="ExternalOutput").ap())

        S = self.S = {}
        for l in range(L):
            S["wb_in%d" % l] = self.dscr("wb_in%d" % l, [D, IN_COLS], BF16)
            S["wb_co%d" % l] = self.dscr("wb_co%d" % l, [512, D], BF16)
            S["wb_glu%d" % l] = self.dscr("wb_glu%d" % l, [512, 2 * D], BF16)
            S["wb_ao%d" % l] = self.dscr("wb_ao%d" % l, [D, D], BF16)
            S["wb_o%d" % l] = self.dscr("wb_o%d" % l, [D, D], BF16)
            S["wb_up%d" % l] = self.dscr("wb_up%d" % l, [D, 2 * DFF], BF16)
            S["wb_dn%d" % l] = self.dscr("wb_dn%d" % l, [DFF, D], BF16)
            S["modr%d" % l] = self.dscr("modr%d" % l, [3, 6 * D], F32)
        S["resA"] = self.dscr("resA", [NBC, TT, D], F32)
        S["resB"] = self.dscr("resB", [NBC, TT, D], F32)
        S["KT"] = self.dscr("KT", [NBC, 2, 128, TT], BF16)
        S["V"] = self.dscr("V", [NBC, TT, 256], BF16)
        S["U"] = self.dscr("U", [NBC, TT, 512], BF16)
        S["QT"] = self.dscr("QT", [NBC, 8, 128, TT], BF16)
        S["AXT"] = self.dscr("AXT", [NBC, 512, TT], BF16)
        S["BGT"] = self.dscr("BGT", [NBC, 512, TT], BF16)
        S["CGT"] = self.dscr("CGT", [NBC, 512, TT], BF16)
        S["GT"] = self.dscr("GT", [NBC, 3 * D, TT], BF16)
        S["AOT"] = self.dscr("AOT", [NBC, D, TT], BF16)
        S["SSMT"] = self.dscr("SSMT", [NBC, D, TT], BF16)
        S["YS"] = self.dscr("YS", [NBC, TT, 512], F32)

        C = self.C = {}
        C["identf"] = P.sb("identf", [128, 128], F32)
        C["identb"] = P.sb("identb", [128, 128], BF16)
        C["onesb"] = P.sb("onesb", [128, 128], BF16)
        C["eps"] = P.sb("eps", [128, 1], F32)
        for nm in ["identf", "identb", "onesb"]:
            P.dma("sync", C[nm][:], I["k_" + nm].t, reads=[I["k_" + nm]], writes=[C[nm]])
        P.op("vector", lambda e: e.memset(C["eps"][:], EPS), writes=[C["eps"]])
        C["mhalf"] = P.sb("mhalf", [128, 16], F32)
        P.op("gpsimd", lambda e: e.memset(C["mhalf"][:], -0.5), writes=[C["mhalf"]])

        try:
            self.weight_prep()
            if self.stop_after == "prep":
                return self.finish()
            for l in range(L):
                self.layer(l)
                if self.stop_after == "layer%d" % l:
                    break
        except StopIteration:
            pass
        return self.finish()

    def finish(self):
        self.P.finish()
        return self.nc

    def check_stop(self, tag):
        if self.stop_after == tag:
            raise StopIteration

    def weight_prep(self):
        P, I, S = self.P, self.I, self.S
        self.prep_todo = []
        for l in range(DEPTH):
            for src, dst, Kd, N in [("ffn_w_down", "wb_dn", DFF, D)]:
                for kt in range(Kd // 128):
                    for c0 in range(0, N, 2048):
                        self.prep_todo.append((src, l, dst + str(l), kt, c0, min(2048, N - c0)))
        self.prep_n = 0

    def prep_bufs(self):
        P = self.P
        return ([P.sb("wpf%d" % i, [128, 1024], F32) for i in range(3)], [P.sb("wpb%d" % i, [128, 1024], BF16) for i in range(3)])

    def prep_emit(self, bufs, count, engs):
        P, I, S = self.P, self.I, self.S
        stf, stb = bufs
        for _ in range(count):
            if not self.prep_todo:
                return
            src, l, dst, kt, c0, w = self.prep_todo.pop(0)
            n = self.prep_n
            self.prep_n += 1
            f, b = stf[n % 3], stb[n % 3]
            eng = engs[n % len(engs)]
            P.dma("sync", f[:, :w], I[src].t[l][kt * 128:(kt + 1) * 128, c0:c0 + w], reads=[I[src]], writes=[f])
            if eng == "scalar":
                P.op(eng, ACT(b[:, :w], f[:, :w], AF.Copy), reads=[f], writes=[b])
            else:
                P.op(eng, CP(b[:, :w], f[:, :w]), reads=[f], writes=[b])
            P.dma("gpsimd", S[dst].t[kt * 128:(kt + 1) * 128, c0:c0 + w], b[:, :w], reads=[b], writes=[S[dst]])


    def wload(self, dst, src3, nk, ncols, eng="gpsimd", chunk=1024, src_buf=None):
        P = self.P
        P.push()
        stg = [P.sb("wst%d" % i, [128, chunk], F32) for i in range(3)]
        n = 0
        for k in range(nk):
            for c0 in range(0, ncols, chunk):
                w = min(chunk, ncols - c0)
                f = stg[n % 3]
                n += 1
                P.dma("sync", f[:, 0:w], src3[:, k, c0:c0 + w], reads=[src_buf] if src_buf else [], writes=[f])
                P.op(eng, CP(dst[:, k, c0:c0 + w], f[:, 0:w]), reads=[f], writes=[dst])
        P.pop()

    def layer(self, l):
        P = self.P
        self.l = l
        self.last = (l == DEPTH - 1)
        P.push()
        self.modT = P.sb("modT", [128, 6, 8, 3], F32)
        self.phase0()
        self.check_stop("p0_%d" % l)
        self.phaseA()
        self.check_stop("pA_%d" % l)
        P.push()
        Bst = P.sb("Bst", [128, 32, 2, 128], BF16)
        CL = P.sb("CL", [128, 32, 2, 128], BF16)
        Toep = P.sb("Toep", [128, 32, 128], BF16)
        A8 = P.sb("A8", [128, 32], F32)
        B8 = P.sb("B8", [128, 32], F32)
        self.phaseB(self.ssm_consts(Bst, CL, Toep, A8, B8))
        self.check_stop("pB_%d" % l)
        for b in range(NBC):
            self.ssm_batch(b, Bst, CL, Toep, A8, B8)
        P.pop()
        self.check_stop("pC_%d" % l)
        self.phaseD()
        self.check_stop("pD_%d" % l)
        self.phaseF()
        self.check_stop("pF_%d" % l)
        P.pop()

    def res_in(self, b, seg):
        if self.l == 0:
            return (self.I["ctx"], self.I["ctx"].t[b]) if seg == "ctx" else (self.I["x"], self.I["x"].t[b])
        r = self.S["resB"]
        return (r, r.t[b, 0:TC, :]) if seg == "ctx" else (r, r.t[b, TC:TT, :])

    def phase0(self):
        P, I, S, C, l = self.P, self.I, self.S, self.C, self.l
        P.push()
        scT = P.sb("scT", [128, 8, 3], F32)
        bm3 = P.sb("bm3", [3, 6 * D], F32)
        modrows = P.sb("modrows", [3, 6 * D], F32)
        wst = [P.sb("wmst%d" % i, [128, 8, 512], F32) for i in range(2)]
        for r in range(3):
            src = I["c"].t[r, :] if r < 2 else I["c_ctx"].t[0, :]
            P.dma("sync", scT[:, :, r], src.rearrange("(k p) -> p k", p=128), reads=[I["c"]], writes=[scT],
                  allow_slow_non_contiguous=True)
        P.op("scalar", ACT(scT[:], scT[:], AF.Silu), reads=[scT], writes=[scT])
        P.dma("sync", bm3[0:3, :], I["b_mod"].t[l, :].partition_broadcast(3), reads=[I["b_mod"]], writes=[bm3])
        wm = I["w_mod"].t[l].rearrange("(k p) n -> p k n", p=128)
        for blk in range(12):
            w = wst[blk % 2]
            P.dma("sync", w[:], wm[:, :, blk * 512:(blk + 1) * 512], reads=[I["w_mod"]], writes=[w])
            ps = self.bank("p0", [0, 1])
            for k in range(8):
                P.op("tensor", MM(ps[0:3, 0:512], scT[:, k, :], w[:, k, :], k == 0, k == 7), reads=[scT, w], writes=[ps], inc=(k == 7))
            P.op("vector", TTo(modrows[0:3, blk * 512:(blk + 1) * 512], ps[0:3, 0:512], bm3[0:3, blk * 512:(blk + 1) * 512], ALU.add),
                 reads=[ps, bm3], writes=[modrows])
        P.dma("gpsimd", S["modr%d" % l].t, modrows[0:3, :], reads=[modrows], writes=[S["modr%d" % l]])
        ps = self.bank("p0", [0, 1])
        for t in range(48):
            P.op("tensor", TR(ps[:, t * 3:(t + 1) * 3], modrows[0:3, t * 128:(t + 1) * 128], C["identf"][0:3, 0:3]),
                 reads=[modrows, C["identf"]], writes=[ps], inc=(t == 47))
        mt = self.modT
        P.op("vector", CP(mt[:].rearrange("p a k r -> p (a k r)"), ps[:, 0:144]), reads=[ps], writes=[mt])
        for sec in (1, 4):
            P.op("vector", TS(mt[:, sec], mt[:, sec], 1.0, None, ALU.add), reads=[mt], writes=[mt])
        P.pop()

    def ln_tile(self, xt, out_ap, out_buf, sm, src_ap=None, act_extra_reads=()):
        P, C = self.P, self.C
        st, mv, rs, nb = sm
        src = xt[:] if src_ap is None else src_ap
        P.op("vector", lambda e: e.bn_stats(out=st[:, 0:6], in_=src[:, 0:512]), reads=[xt], writes=[st])
        P.op("vector", lambda e: e.bn_stats(out=st[:, 6:12], in_=src[:, 512:1024]), reads=[xt], writes=[st])
        P.op("vector", lambda e: e.bn_aggr(out=mv[:, 0:2], in_=st[:, 0:12]), reads=[st], writes=[mv])
        P.op("gpsimd", TS(rs[:, 0:1], mv[:, 1:2], EPS, None, ALU.add), reads=[mv], writes=[rs])
        P.op("gpsimd", TTo(rs[:, 0:1], rs[:, 0:1], C["mhalf"][:, 0:1], ALU.pow), reads=[rs, C["mhalf"]], writes=[rs])
        P.op("vector", TS(nb[:, 0:1], mv[:, 0:1], rs[:, 0:1], -1.0, ALU.mult, ALU.mult), reads=[mv, rs], writes=[nb])
        P.op("scalar", ACT(out_ap, src, AF.Identity, bias=nb[:, 0:1], scale=rs[:, 0:1]), reads=[xt, rs, nb], writes=[out_buf])

    def ln_small(self, tag, n=2):
        P = self.P
        return [(P.sb(tag + "st%d" % i, [128, 12], F32), P.sb(tag + "mv%d" % i, [128, 2], F32),
                 P.sb(tag + "rs%d" % i, [128, 1], F32), P.sb(tag + "nb%d" % i, [128, 1], F32)) for i in range(n)]

    def phaseA(self):
        P, I, S, C, l = self.P, self.I, self.S, self.C, self.l
        P.push()
        W = P.sb("Win", [128, 8, IN_COLS], BF16)
        self.wload(W, I["w_in"].t[l].rearrange("(k p) n -> p k n", p=128), 8, IN_COLS, eng="gpsimd", chunk=1664)
        cos = P.sb("cos", [128, 16, 64], F32)
        sin = P.sb("sin", [128, 16, 64], F32)
        P.dma("sync", cos[:], I["k_cos"].t, reads=[I["k_cos"]], writes=[cos])
        P.dma("sync", sin[:], I["k_sin"].t, reads=[I["k_sin"]], writes=[sin])
        gqk = P.sb("gqk", [128, 10, 128], F32)
        P.dma("sync", gqk[:, 0:2, :], I["k_norm_g"].t[l:l + 1, :].partition_broadcast(128).to_broadcast([128, 2, 128]) if False else
              I["k_norm_g"].t[l, :].partition_broadcast(128).unsqueeze(1).to_broadcast([128, 2, 128]), reads=[I["k_norm_g"]], writes=[gqk])
        P.dma("sync", gqk[:, 2:10, :], I["q_norm_g"].t[l, :].partition_broadcast(128).unsqueeze(1).to_broadcast([128, 8, 128]), reads=[I["q_norm_g"]], writes=[gqk])
        xt = [P.sb("xt%d" % i, [128, D], F32) for i in range(2)]
        xn = [P.sb("xn%d" % i, [128, D], BF16) for i in range(2)]
        sm = self.ln_small("a")
        hT = [P.sb("hT%d" % i, [128, 8, 512], BF16) for i in range(2)]
        xraw = [P.sb("xraw%d" % i, [128, 10, 128], F32) for i in range(2)]
        sq = P.sb("sq", [128, 10, 128], F32)
        ss = [P.sb("ss%d" % i, [128, 10], F32) for i in range(2)]
        rt = [P.sb("rt%d" % i, [128, 10, 64], F32) for i in range(4)]
        qn = [P.sb("qn%d" % i, [128, 10, 128], BF16) for i in range(2)]
        qTs = P.sb("qTs", [128, 8, 512], BF16)
        kTs = P.sb("kTs", [128, 2, 512], BF16)
        vst = [P.sb("vst%d" % i, [128, 256], BF16) for i in range(2)]
        ust = [P.sb("ust%d" % i, [128, 512], BF16) for i in range(2)]
        fst = [P.sb("fst%d" % i, [128, 512], BF16) for i in range(3)]
        cnt = {"x": 0, "t": 0, "f": 0}

        sts = []
        for b in range(NBC):
            for seg in ("ctx", "lat"):
                ntok = 256 if seg == "ctx" else 512
                for st_i in range(1 if seg == "ctx" else 4):
                    sts.append((b, seg, st_i, ntok))

        def ln_tile_emit(si, ti):
            b, seg, st_i, ntok = sts[si]
            r = 2 if seg == "ctx" else b
            rbuf, rap = self.res_in(b, seg)
            t0 = st_i * ntok
            h = hT[si % 2]
            i = cnt["x"]
            cnt["x"] += 1
            x_, n_ = xt[i % 2], xn[i % 2]
            P.dma("sync", x_[:], rap[t0 + ti * 128:t0 + (ti + 1) * 128, :], reads=[rbuf], writes=[x_])
            self.ln_tile(x_, n_[:], n_, sm[i % 2])
            ps = self.bank("atr", [0, 1])
            pb = ps[:].bitcast(BF16).rearrange("p (k t) -> p k t", k=8)
            for k in range(8):
                P.op("tensor", TR(pb[:, k, :], n_[:, k * 128:(k + 1) * 128], C["identb"][:]), reads=[n_, C["identb"]], writes=[ps], inc=(k == 7))
            for k in range(8):
                o = h[:, k, ti * 128:(ti + 1) * 128]
                rd = [ps, self.modT] if k in (0, 7) else []
                wr = [h] if k in (0, 7) else []
                if i % 2 == 0:
                    P.op("vector", TS(o, pb[:, k, :], self.modT[:, 1, k, r:r + 1], self.modT[:, 0, k, r:r + 1], ALU.mult, ALU.add), reads=rd, writes=wr)
                else:
                    P.op("scalar", ACT(o, pb[:, k, :], AF.Identity, bias=self.modT[:, 0, k, r:r + 1], scale=self.modT[:, 1, k, r:r + 1]), reads=rd, writes=wr)

        def tokmajor(si):
            b, seg, st_i, ntok = sts[si]
            full = (seg == "lat") or (not self.last)
            t0 = st_i * ntok
            g0 = t0 + (0 if seg == "ctx" else TC)
            h = hT[si % 2]
            rope = (seg == "lat")
            for ti in range(ntok // 128):
                ltile = (t0 // 128) + ti
                it = cnt["t"]
                cnt["t"] += 1
                xr, s_, q_ = xraw[it % 2], ss[it % 2], qn[it % 2]
                nh = 10 if full else 2
                for blk in range(4 if full else 2):
                    ps = self.bank("amm", [2, 3, 4, 5])
                    for k in range(8):
                        P.op("tensor", MM(ps[:, 0:512], h[:, k, ti * 128:(ti + 1) * 128], W[:, k, blk * 512:(blk + 1) * 512], k == 0, k == 7),
                             reads=[h, W], writes=[ps], inc=(k == 7))
                    if blk == 0:
                        P.op("scalar", ACT(xr[:, 0:2, :].rearrange("p h d -> p (h d)"), ps[:, 0:256], AF.Copy), reads=[ps], writes=[xr])
                        v_ = vst[it % 2]
                        P.op("scalar", ACT(v_[:], ps[:, 256:512], AF.Copy), reads=[ps], writes=[v_])
                        P.dma("scalar", S["V"].t[b, g0 + ti * 128:g0 + (ti + 1) * 128, :], v_[:], reads=[v_], writes=[S["V"]])
                    elif blk == 1:
                        u_ = ust[it % 2]
                        P.op("scalar", ACT(u_[:], ps[:, 0:512], AF.Copy), reads=[ps], writes=[u_])
                        P.dma("scalar", S["U"].t[b, g0 + ti * 128:g0 + (ti + 1) * 128, :], u_[:], reads=[u_], writes=[S["U"]])
                    else:
                        h0 = 2 + (blk - 2) * 4
                        dst = xr[:, h0:h0 + 4, :].rearrange("p h d -> p (h d)")
                        P.op("scalar", ACT(dst, ps[:, 0:512], AF.Copy), reads=[ps], writes=[xr])
                P.op("scalar", ACT(sq[:, 0:nh, :], xr[:, 0:nh, :], AF.Square), reads=[xr], writes=[sq])
                P.op("vector", lambda e, s_=s_, nh=nh: e.tensor_reduce(out=s_[:, 0:nh], in_=sq[:, 0:nh, :], axis=mybir.AxisListType.X, op=ALU.add),
                     reads=[sq], writes=[s_])
                P.op("gpsimd", TS(s_[:, 0:nh], s_[:, 0:nh], 1.0 / 128, EPS, ALU.mult, ALU.add), reads=[s_], writes=[s_])
                P.op("gpsimd", TTo(s_[:, 0:nh], s_[:, 0:nh], C["mhalf"][:, 0:nh], ALU.pow), reads=[s_, C["mhalf"]], writes=[s_])
                for hh in range(nh):
                    P.op("vector", STT(xr[:, hh, :], xr[:, hh, :], s_[:, hh:hh + 1], gqk[:, hh, :], ALU.mult, ALU.mult),
                         reads=([xr, s_, gqk] if hh in (0, nh - 1) else []), writes=([xr] if hh in (0, nh - 1) else []))
                if not rope:
                    P.op("gpsimd", CP(q_[:, 0:nh, :], xr[:, 0:nh, :]), reads=[xr], writes=[q_])
                else:
                    xe = xr[:, 0:nh, 0:128:2]
                    xo = xr[:, 0:nh, 1:128:2]
                    cb = cos[:, ltile:ltile + 1, :].to_broadcast([128, nh, 64])
                    sb_ = sin[:, ltile:ltile + 1, :].to_broadcast([128, nh, 64])
                    t1, t2, t3, t4 = [r_[:, 0:nh, :] for r_ in rt]
                    P.op("vector", TTo(t1, xe, cb, ALU.mult), reads=[xr, cos], writes=[rt[0]])
                    P.op("gpsimd", TTo(t3, xe, sb_, ALU.mult), reads=[xr, sin], writes=[rt[2]])
                    P.op("vector", TTo(t2, xo, sb_, ALU.mult), reads=[xr, sin], writes=[rt[1]])
                    P.op("gpsimd", TTo(t4, xo, cb, ALU.mult), reads=[xr, cos], writes=[rt[3]])
                    P.op("vector", TTo(q_[:, 0:nh, 0:128:2], t1, t2, ALU.subtract), reads=[rt[0], rt[1]], writes=[q_])
                    P.op("gpsimd", TTo(q_[:, 0:nh, 1:128:2], t3, t4, ALU.add), reads=[rt[2], rt[3]], writes=[q_])
                pt = self.bank("atq", [6, 7])
                ptb = pt[:].bitcast(BF16).rearrange("p (k t) -> p k t", k=8)
                for hh in range(2):
                    P.op("tensor", TR(ptb[:, hh, :], q_[:, hh, :], C["identb"][:]), reads=[q_, C["identb"]], writes=[pt], inc=(hh == 1))
                P.op("vector", CP(kTs[:, :, ti * 128:(ti + 1) * 128], ptb[:, 0:2, :]), reads=[pt], writes=[kTs])
                if full:
                    pt = self.bank("atq", [6, 7])
                    ptb = pt[:].bitcast(BF16).rearrange("p (k t) -> p k t", k=8)
                    for hh in range(8):
                        P.op("tensor", TR(ptb[:, hh, :], q_[:, 2 + hh, :], C["identb"][:]), reads=[q_, C["identb"]], writes=[pt], inc=(hh == 7))
                    P.op("scalar", ACT(qTs[:, :, ti * 128:(ti + 1) * 128], ptb[:, :, :], AF.Copy), reads=[pt], writes=[qTs])
            for kv in range(2):
                P.dma("sync", S["KT"].t[b, kv, :, g0:g0 + ntok], kTs[:, kv, 0:ntok], reads=[kTs], writes=[S["KT"]])
            if full:
                for hh in range(8):
                    P.dma("sync", S["QT"].t[b, hh, :, g0:g0 + ntok], qTs[:, hh, 0:ntok], reads=[qTs], writes=[S["QT"]])

        def featmajor(si, between):
            b, seg, st_i, ntok = sts[si]
            full = (seg == "lat") or (not self.last)
            g0 = st_i * ntok + (0 if seg == "ctx" else TC)
            h = hT[si % 2]
            if not full:
                for f in between:
                    f()
                return
            for ct in range(36):
                ps = self.bank("amm", [2, 3, 4, 5])
                c0 = 2048 + ct * 128
                for k in range(8):
                    P.op("tensor", MM(ps[:, 0:ntok], W[:, k, c0:c0 + 128], h[:, k, 0:ntok], k == 0, k == 7), reads=[h, W], writes=[ps], inc=(k == 7))
                f_ = fst[cnt["f"] % 3]
                cnt["f"] += 1
                if ct < 12:
                    dst = S[["AXT", "BGT", "CGT"][ct // 4]]
                    row0 = (ct % 4) * 128
                    P.op("scalar", ACT(f_[:, 0:ntok], ps[:, 0:ntok], AF.Copy), reads=[ps], writes=[f_])
                else:
                    dst = S["GT"]
                    row0 = (ct - 12) * 128
                    P.op("scalar", ACT(f_[:, 0:ntok], ps[:, 0:ntok], AF.Sigmoid), reads=[ps], writes=[f_])
                P.dma("scalar", dst.t[b, row0:row0 + 128, g0:g0 + ntok], f_[:, 0:ntok], reads=[f_], writes=[dst])
                if ct % 9 == 8 and between:
                    between.pop(0)()
            for f in between:
                f()

        for ti in range(sts[0][3] // 128):
            ln_tile_emit(0, ti)
        for si in range(len(sts)):
            tokmajor(si)
            between = []
            if si + 1 < len(sts):
                between = [(lambda si=si, ti=ti: ln_tile_emit(si + 1, ti)) for ti in range(sts[si + 1][3] // 128)]
            featmajor(si, between)
        P.pop()

    def phaseB(self, bg=None):
        P, I, S, C, l = self.P, self.I, self.S, self.C, self.l
        P.push()
        KTs = P.sb("KTs", [128, 2, TT], BF16)
        Vs = P.sb("Vs", [128, 18, 256], BF16)
        qT = [P.sb("qTb%d" % i, [128, 512], BF16) for i in range(2)]
        pT = [P.sb("pT%d" % i, [128, 512], BF16) for i in range(4)]
        rec = [P.sb("rec%d" % i, [128, 512], F32) for i in range(2)]
        oT = [P.sb("oT%d" % i, [128, 512], BF16) for i in range(2)]
        n = 0
        pbufs = self.prep_bufs() if self.prep_todo else None
        niter = NBC * 8 * (4 if self.last else 5)
        per_it = -(-len(self.prep_todo) // niter) if self.prep_todo else 0
        for b in range(NBC):
            for kv in range(2):
                P.dma("sync", KTs[:, kv, :], S["KT"].t[b, kv], reads=[S["KT"]], writes=[KTs])
            P.dma("sync", Vs[:], S["V"].t[b].rearrange("(t p) c -> p t c", p=128), reads=[S["V"]], writes=[Vs])
            blocks = [("lat", qb) for qb in range(4)] + ([] if self.last else [("ctx", 0)])
            for h in range(8):
                kv = h // 4
                for seg, qb in blocks:
                    if seg == "lat":
                        q0, nq, kts = TC + qb * 512, 512, list(range(18))
                    else:
                        q0, nq, kts = 0, 256, [0, 1]
                    q_ = qT[n % 2]
                    P.dma("sync", q_[:, 0:nq], S["QT"].t[b, h, :, q0:q0 + nq], reads=[S["QT"]], writes=[q_])
                    po = self.bank("bo", [4, 5])
                    pz = self.bank("bs", [6, 7])
                    def qk(j):
                        kt = kts[j]
                        ps = self.bank("bqk", [0, 1, 2])
                        P.op("tensor", MM(ps[:, 0:nq], KTs[:, kv, kt * 128:(kt + 1) * 128], q_[:, 0:nq], True, True), reads=[KTs, q_], writes=[ps])
                        P.op("scalar", ACT(pT[j % 4][:, 0:nq], ps[:, 0:nq], AF.Exp, scale=ATTN_SCALE), reads=[ps], writes=[pT[j % 4]])

                    LOOK = 2
                    for j in range(min(LOOK, len(kts))):
                        qk(j)
                    for j, kt in enumerate(kts):
                        if j + LOOK < len(kts):
                            qk(j + LOOK)
                        p_ = pT[j % 4]
                        last = (j == len(kts) - 1)
                        P.op("tensor", MM(po[:, 0:nq], Vs[:, kt, kv * 128:(kv + 1) * 128], p_[:, 0:nq], j == 0, last), reads=[Vs, p_], writes=[po], inc=last)
                        P.op("tensor", MM(pz[:, 0:nq], C["onesb"][:], p_[:, 0:nq], j == 0, last), reads=[C["onesb"], p_], writes=[pz], inc=last)
                    r_, o_ = rec[n % 2], oT[n % 2]
                    P.op("vector", lambda e, r_=r_, pz=pz, nq=nq: e.reciprocal(out=r_[:, 0:nq], in_=pz[:, 0:nq]), reads=[pz], writes=[r_])
                    P.op("vector", TTo(o_[:, 0:nq], po[:, 0:nq], r_[:, 0:nq], ALU.mult), reads=[po, r_], writes=[o_])
                    P.dma("gpsimd", S["AOT"].t[b, h * 128:(h + 1) * 128, q0:q0 + nq], o_[:, 0:nq], reads=[o_], writes=[S["AOT"]])
                    n += 1
                    if pbufs is not None:
                        self.prep_emit(pbufs, per_it, ["gpsimd", "vector"])
                    if bg is not None:
                        for _ in range(2 if (self.last and n % 2 == 0) else 1):
                            next(bg, None)
        if pbufs is not None:
            self.prep_emit(pbufs, len(self.prep_todo), ["gpsimd", "vector"])
        if bg is not None:
            for _ in bg:
                pass
        P.pop()


    def ssm_consts(self, Bst, CL, Toep, A8, B8):
        P, I, S, C, l = self.P, self.I, self.S, self.C, self.l
        V_ = "vector"

        def T(name, shape=(128, 32), dt=F32):
            return P.sb(name, list(shape), dt)

        lre, lim, ldt = T("lre"), T("lim"), T("ldt")
        for d in range(2):
            hs = slice(64 * d, 64 * d + 64)
            P.dma("sync", lre[hs, :], I["ssm_lam_re"].t[l, d].rearrange("g p -> p g"), reads=[I["ssm_lam_re"]], writes=[lre], allow_slow_non_contiguous=True)
            P.dma("sync", lim[hs, :], I["ssm_lam_im"].t[l, d].rearrange("g p -> p g"), reads=[I["ssm_lam_im"]], writes=[lim], allow_slow_non_contiguous=True)
            P.dma("sync", ldt[hs, :], I["ssm_log_dt"].t[l, d, :].partition_broadcast(64), reads=[I["ssm_log_dt"]], writes=[ldt])
        dt, tmp, mag, th, kf, kf2, r, m, sn, cs, r2 = [T(n) for n in ["dt", "tmp", "mag", "th", "kf", "kf2", "r", "m", "sn", "cs", "r2"]]
        ki = T("ki", dt=I32)
        P.op("scalar", ACT(dt[:], ldt[:], AF.Exp), reads=[ldt], writes=[dt])
        P.op(V_, TTo(tmp[:], lre[:], dt[:], ALU.mult), reads=[lre, dt], writes=[tmp])
        P.op("scalar", ACT(mag[:], tmp[:], AF.Exp), reads=[tmp], writes=[mag])
        P.op(V_, TTo(th[:], lim[:], dt[:], ALU.mult), reads=[lim, dt], writes=[th])
        P.op(V_, TS(kf[:], th[:], 1.0 / TWO_PI, None, ALU.mult), reads=[th], writes=[kf])
        P.op(V_, CP(ki[:], kf[:]), reads=[kf], writes=[ki])
        P.op(V_, CP(kf2[:], ki[:]), reads=[ki], writes=[kf2])
        P.op(V_, STT(r[:], kf2[:], -TWO_PI, th[:], ALU.mult, ALU.add), reads=[kf2, th], writes=[r])

        def wrap(x):
            P.op(V_, TS(m[:], x[:], PI, TWO_PI, ALU.is_gt, ALU.mult), reads=[x], writes=[m])
            P.op(V_, TTo(x[:], x[:], m[:], ALU.subtract), reads=[x, m], writes=[x])
            P.op(V_, TS(m[:], x[:], -PI, TWO_PI, ALU.is_lt, ALU.mult), reads=[x], writes=[m])
            P.op(V_, TTo(x[:], x[:], m[:], ALU.add), reads=[x, m], writes=[x])

        wrap(r)
        P.op("scalar", ACT(sn[:], r[:], AF.Sin), reads=[r], writes=[sn])
        P.op(V_, TS(r2[:], r[:], PI / 2, None, ALU.add), reads=[r], writes=[r2])
        wrap(r2)
        P.op("scalar", ACT(cs[:], r2[:], AF.Sin), reads=[r2], writes=[cs])
        lbr, lbi = T("lbr"), T("lbi")
        P.op(V_, TTo(lbr[:], mag[:], cs[:], ALU.mult), reads=[mag, cs], writes=[lbr])
        P.op(V_, TTo(lbi[:], mag[:], sn[:], ALU.mult), reads=[mag, sn], writes=[lbi])
        nr, den, t1, t2, fr, fi = [T(n) for n in ["nr", "den", "t1", "t2", "fr", "fi"]]
        P.op(V_, TS(nr[:], lbr[:], -1.0, None, ALU.add), reads=[lbr], writes=[nr])
        P.op(V_, TTo(t1[:], lre[:], lre[:], ALU.mult), reads=[lre], writes=[t1])
        P.op(V_, TTo(t2[:], lim[:], lim[:], ALU.mult), reads=[lim], writes=[t2])
        P.op(V_, TTo(den[:], t1[:], t2[:], ALU.add), reads=[t1, t2], writes=[den])
        P.op(V_, lambda e: e.reciprocal(out=den[:], in_=den[:]), reads=[den], writes=[den])
        P.op(V_, TTo(t1[:], nr[:], lre[:], ALU.mult), reads=[nr, lre], writes=[t1])
        P.op(V_, TTo(t2[:], lbi[:], lim[:], ALU.mult), reads=[lbi, lim], writes=[t2])
        P.op(V_, TTo(fr[:], t1[:], t2[:], ALU.add), reads=[t1, t2], writes=[fr])
        P.op(V_, TTo(fr[:], fr[:], den[:], ALU.mult), reads=[fr, den], writes=[fr])
        P.op(V_, TTo(t1[:], lbi[:], lre[:], ALU.mult), reads=[lbi, lre], writes=[t1])
        P.op(V_, TTo(t2[:], nr[:], lim[:], ALU.mult), reads=[nr, lim], writes=[t2])
        P.op(V_, TTo(fi[:], t1[:], t2[:], ALU.subtract), reads=[t1, t2], writes=[fi])
        P.op(V_, TTo(fi[:], fi[:], den[:], ALU.mult), reads=[fi, den], writes=[fi])
        ir, ii = T("ir"), T("ii")
        P.op(V_, TTo(t1[:], lbr[:], lbr[:], ALU.mult), reads=[lbr], writes=[t1])
        P.op(V_, TTo(t2[:], lbi[:], lbi[:], ALU.mult), reads=[lbi], writes=[t2])
        P.op(V_, TTo(den[:], t1[:], t2[:], ALU.add), reads=[t1, t2], writes=[den])
        P.op(V_, lambda e: e.reciprocal(out=den[:], in_=den[:]), reads=[den], writes=[den])
        P.op(V_, TTo(ir[:], lbr[:], den[:], ALU.mult), reads=[lbr, den], writes=[ir])
        P.op(V_, STT(ii[:], lbi[:], -1.0, den[:], ALU.mult, ALU.mult), reads=[lbi, den], writes=[ii])
        yield
        LPr, LPi = T("LPr", (128, 9, 32)), T("LPi", (128, 9, 32))
        LNr, LNi = T("LNr", (128, 8, 32)), T("LNi", (128, 8, 32))
        for (Xr, Xi, br, bi, n) in [(LPr, LPi, lbr, lbi, 9), (LNr, LNi, ir, ii, 8)]:
            P.op(V_, lambda e, Xr=Xr: e.memset(Xr[:, 0, :], 1.0), writes=[Xr])
            P.op(V_, lambda e, Xi=Xi: e.memset(Xi[:, 0, :], 0.0), writes=[Xi])
            for k in range(1, n):
                P.op(V_, TTo(t1[:], Xr[:, k - 1, :], br[:], ALU.mult), reads=[Xr, br], writes=[t1])
                P.op(V_, TTo(t2[:], Xi[:, k - 1, :], bi[:], ALU.mult), reads=[Xi, bi], writes=[t2])
                P.op(V_, TTo(Xr[:, k, :], t1[:], t2[:], ALU.subtract), reads=[t1, t2], writes=[Xr])
                P.op(V_, TTo(t1[:], Xr[:, k - 1, :], bi[:], ALU.mult), reads=[Xr, bi], writes=[t1])
                P.op(V_, TTo(t2[:], Xi[:, k - 1, :], br[:], ALU.mult), reads=[Xi, br], writes=[t2])
                P.op(V_, TTo(Xi[:, k, :], t1[:], t2[:], ALU.add), reads=[t1, t2], writes=[Xi])
        P.op(V_, CP(A8[:], LPr[:, 8, :]), reads=[LPr], writes=[A8])
        P.op(V_, CP(B8[:], LPi[:, 8, :]), reads=[LPi], writes=[B8])
        yield
        G3 = (128, 32, 16)
        Bre, Bim, Bbr, Bbi, CTr, CTi = [T(n, G3) for n in ["Bre", "Bim", "Bbr", "Bbi", "CTr", "CTi"]]
        for d in range(2):
            hs = slice(64 * d, 64 * d + 64)
            P.dma("sync", Bre[hs], I["ssm_b_re"].t[l, d].rearrange("g p h -> p g h"), reads=[I["ssm_b_re"]], writes=[Bre], allow_slow_non_contiguous=True)
            P.dma("sync", Bim[hs], I["ssm_b_im"].t[l, d].rearrange("g p h -> p g h"), reads=[I["ssm_b_im"]], writes=[Bim], allow_slow_non_contiguous=True)
        craw = [T("craw%d" % i, (128, 2, 64)) for i in range(2)]
        n = 0
        for (src, dst) in [("ssm_c_re", CTr), ("ssm_c_im", CTi)]:
            for gb in range(4):
                cr_ = craw[n % 2]
                n += 1
                for d in range(2):
                    P.dma("sync", cr_[:, d, :], I[src].t[l, d, gb * 8:(gb + 1) * 8].rearrange("g h p -> (g h) p"), reads=[I[src]], writes=[cr_])
                ps = self.bank("c0", [3])
                P.op("tensor", TR(ps[:, 0:128], cr_[:].rearrange("q d p -> q (d p)"), C["identf"][:]), reads=[cr_, C["identf"]], writes=[ps])
                P.op(V_, CP(dst[:, gb * 8:(gb + 1) * 8, :].rearrange("q g h -> q (g h)"), ps[:, 0:128]), reads=[ps], writes=[dst])

        yield

        def bc(x_ap):
            return x_ap.unsqueeze(2).to_broadcast([128, 32, 16])

        u1, u2, u3, u4 = [T(n, G3) for n in ["u1", "u2", "u3", "u4"]]
        Rr = [T("Rr%d" % i, G3) for i in range(2)]
        Ri = [T("Ri%d" % i, G3) for i in range(2)]
        fcnt = [0]

        def cprod(lr, li, Xr, Xi, lbufs, neg_im=False):
            i = fcnt[0]
            fcnt[0] += 1
            rr, ri = Rr[i % 2], Ri[i % 2]
            P.op(V_, TTo(u1[:], Xr[:], bc(lr), ALU.mult), reads=[Xr] + lbufs, writes=[u1])
            P.op("gpsimd", TTo(u2[:], Xi[:], bc(li), ALU.mult), reads=[Xi] + lbufs, writes=[u2])
            P.op(V_, TTo(u3[:], Xi[:], bc(lr), ALU.mult), reads=[Xi] + lbufs, writes=[u3])
            P.op("gpsimd", TTo(u4[:], Xr[:], bc(li), ALU.mult), reads=[Xr] + lbufs, writes=[u4])
            P.op(V_, TTo(rr[:], u1[:], u2[:], ALU.subtract), reads=[u1, u2], writes=[rr])
            if neg_im:
                P.op("gpsimd", TS(u3[:], u3[:], -1.0, None, ALU.mult), reads=[u3], writes=[u3])
                P.op("gpsimd", TTo(ri[:], u3[:], u4[:], ALU.subtract), reads=[u3, u4], writes=[ri])
            else:
                P.op("gpsimd", TTo(ri[:], u3[:], u4[:], ALU.add), reads=[u3, u4], writes=[ri])
            return rr, ri

        def STT_pool(out, a, b_):
            def f(e):
                e.tensor_scalar(out=a, in0=a, scalar1=-1.0, scalar2=None, op0=ALU.mult)
                return e.tensor_tensor(out=out, in0=a, in1=b_, op=ALU.subtract)
            return f

        rr, ri = cprod(fr[:], fi[:], Bre, Bim, [fr, fi])
        P.op(V_, CP(Bbr[:], rr[:]), reads=[rr], writes=[Bbr])
        P.op(V_, CP(Bbi[:], ri[:]), reads=[ri], writes=[Bbi])

        G4 = (128, 32, 8, 16)
        BSr, BSi, Xr_, Xi_, PCr, PCi = [T(n, G4, BF16) for n in ["BSr", "BSi", "Xr_", "Xi_", "PCr", "PCi"]]
        F_, R_ = slice(0, 64), slice(64, 128)
        for Pc in (PCr, PCi):
            P.op("gpsimd", lambda e, Pc=Pc: e.memset(Pc[R_], 0.0), writes=[Pc])
        CLv = CL[:].rearrange("q g r (i h) -> q g r i h", i=8)
        cpe = ["gpsimd", "vector"]
        cc = [0]

        def cpy(dst_ap, dst_buf, src_ap, src_buf):
            e = cpe[cc[0] % 2]
            cc[0] += 1
            if e == "scalar":
                P.op(e, ACT(dst_ap, src_ap, AF.Copy), reads=[src_buf], writes=[dst_buf])
            else:
                P.op(e, CP(dst_ap, src_ap), reads=[src_buf], writes=[dst_buf])

        for k in range(8):
            rr, ri = cprod(LPr[:, k, :], LPi[:, k, :], Bbr, Bbi, [LPr, LPi])
            cpy(BSr[F_, :, 7 - k, :], BSr, rr[F_], rr)
            cpy(BSi[F_, :, 7 - k, :], BSi, ri[F_], ri)
            cpy(BSr[R_, :, k, :], BSr, rr[R_], rr)
            cpy(BSi[R_, :, k, :], BSi, ri[R_], ri)
            yield
        for k in range(8):
            rr, ri = cprod(LNr[:, k, :], LNi[:, k, :], Bbr, Bbi, [LNr, LNi])
            cpy(Xr_[F_, :, k, :], Xr_, rr[F_], rr)
            cpy(Xi_[F_, :, k, :], Xi_, ri[F_], ri)
            yield
            rr, ri = cprod(LNr[:, k, :], LNi[:, k, :], CTr, CTi, [LNr, LNi], neg_im=True)
            cpy(Xr_[R_, :, k, :], Xr_, rr[R_], rr)
            cpy(Xi_[R_, :, k, :], Xi_, ri[R_], ri)
            yield
        for k in range(9):
            rr, ri = cprod(LPr[:, k, :], LPi[:, k, :], CTr, CTi, [LPr, LPi], neg_im=True)
            if k <= 7:
                cpy(PCr[F_, :, k, :], PCr, rr[F_], rr)
                cpy(PCi[F_, :, k, :], PCi, ri[F_], ri)
            if k >= 1:
                cpy(CLv[F_, :, 0, k - 1, :], CL, rr[F_], rr)
                cpy(CLv[F_, :, 1, k - 1, :], CL, ri[F_], ri)
                cpy(CLv[R_, :, 0, 8 - k, :], CL, rr[R_], rr)
                cpy(CLv[R_, :, 1, 8 - k, :], CL, ri[R_], ri)
            yield
        yield
        for g4 in range(8):
            ps = self.bank("c0", [3])
            pb = ps[:].bitcast(BF16).rearrange("p (k t) -> p k t", k=8)
            for gg in range(4):
                g = g4 * 4 + gg
                for ri_, Bs in enumerate((BSr, BSi)):
                    P.op("tensor", TR(pb[:, gg * 2 + ri_, :], Bs[:, g].rearrange("q j h -> q (j h)"), C["identb"][:]), reads=[Bs, C["identb"]], writes=[ps],
                         inc=(gg == 3 and ri_ == 1))
            P.op(V_, CP(Bst[:, g4 * 4:g4 * 4 + 4].rearrange("q g r c -> q (g r) c"), pb[:, :, :]), reads=[ps], writes=[Bst])
            yield
        yield
        for Bs in (BSr, BSi):
            P.op("gpsimd", lambda e, Bs=Bs: e.memset(Bs[F_], 0.0), reads=[Bs], writes=[Bs])
        maskf, maskr, Dcol = T("maskf", (128, 128)), T("maskr", (128, 128)), T("Dcol", (128, 32))
        P.dma("sync", maskf[:], I["k_maskf"].t, reads=[I["k_maskf"]], writes=[maskf])
        P.dma("sync", maskr[:], I["k_maskr"].t, reads=[I["k_maskr"]], writes=[maskr])
        for j in range(8):
            P.dma("sync", Dcol[j * 16:(j + 1) * 16, :], I["ssm_d"].t[l, :].rearrange("(g h) -> h g", h=16), reads=[I["ssm_d"]], writes=[Dcol],
                  allow_slow_non_contiguous=True)
        tf = [T("tf%d" % i, (128, 128)) for i in range(2)]
        tr = [T("tr%d" % i, (128, 128)) for i in range(2)]
        for g in range(32):
            pf = self.bank("c0", [3])
            pr = self.bank("c1", [3])
            fl = lambda X: X[:, g].rearrange("q j h -> q (j h)")
            P.op("tensor", MM(pf[:, 0:128], fl(Xr_), fl(PCr), True, False), reads=[Xr_, PCr], writes=[pf], inc=False)
            P.op("tensor", MM(pf[:, 0:128], fl(Xi_), fl(PCi), False, True), reads=[Xi_, PCi], writes=[pf])
            P.op("tensor", MM(pr[:, 128:256], fl(BSr), fl(Xr_), True, False), reads=[BSr, Xr_], writes=[pr], inc=False)
            P.op("tensor", MM(pr[:, 128:256], fl(BSi), fl(Xi_), False, True), reads=[BSi, Xi_], writes=[pr])
            a_, b_ = tf[g % 2], tr[g % 2]
            P.op(V_, TTo(a_[:], pf[:, 0:128], maskf[:], ALU.mult), reads=[pf, maskf], writes=[a_])
            P.op(V_, TTo(b_[:], pr[:, 128:256], maskr[:], ALU.mult), reads=[pr, maskr], writes=[b_])
            P.op("gpsimd", TTo(a_[:], a_[:], b_[:], ALU.add), reads=[a_, b_], writes=[a_])
            P.op(V_, STT(Toep[:, g, :], C["identf"][:], Dcol[:, g:g + 1], a_[:], ALU.mult, ALU.add), reads=[C["identf"], Dcol, a_], writes=[Toep])
            yield

    def ssm_batch(self, b, Bst, CL, Toep, A8, B8):
        P, I, S, C, l = self.P, self.I, self.S, self.C, self.l
        P.push()
        Ust = P.sb("Ust", [128, 32, NCH], BF16)
        ZSr = P.sb("ZSr", [128, 32, NCH], BF16)
        ZSi = P.sb("ZSi", [128, 32, NCH], BF16)
        Sr = P.sb("Sr", [128, 32, 290], BF16)
        Si = P.sb("Si", [128, 32, 290], BF16)
        F_, R_ = slice(0, 64), slice(64, 128)
        P.push()
        utm = [P.sb("utm%d" % i, [96, 4096], BF16) for i in range(2)]
        utm2 = [P.sb("utm2%d" % i, [96, 4096], BF16) for i in range(2)]
        for tile in range(3):
            u_ = utm[tile % 2]
            P.dma("sync", u_[0:96, :], S["U"].t[b, tile * 768:(tile + 1) * 768, :].rearrange("(c j) ch -> c (j ch)", j=8), reads=[S["U"]], writes=[u_])
            u2 = utm2[tile % 2]
            P.op("gpsimd", CP(u2[0:96, :].rearrange("c (g j h) -> c g j h", g=32, j=8), u_[0:96, :].rearrange("c (j g h) -> c g j h", j=8, g=32)),
                 reads=[u_], writes=[u2])
            uv = u2[0:96, :].rearrange("c (g x) -> c g x", g=32)
            for g8 in range(4):
                ps = self.bank("cs", [4, 5])
                pb = ps[:].bitcast(BF16).rearrange("p (k t) -> p k t", k=8)
                for gg in range(8):
                    P.op("tensor", TR(pb[:, gg, 0:96], uv[:, g8 * 8 + gg, :], C["identb"][0:96, 0:96]), reads=[u2, C["identb"]], writes=[ps], inc=(gg == 7))
                P.op("vector" if g8 % 2 else "scalar",
                     (CP if g8 % 2 else (lambda o, i_: ACT(o, i_, AF.Copy)))(Ust[:, g8 * 8:(g8 + 1) * 8, tile * 96:(tile + 1) * 96], pb[:, :, 0:96]),
                     reads=[ps], writes=[Ust])
        P.pop()
        self.check_stop("c1a")
        for g in range(32):
            for ri_, Z in enumerate((ZSr, ZSi)):
                ps = self.bank("cz", [0, 1, 2, 3])
                P.op("tensor", MM(ps[:, 0:NCH], Bst[:, g, ri_, :], Ust[:, g, :], True, True), reads=[Bst, Ust], writes=[ps])
                if ri_ == 0:
                    P.op("vector", CP(Z[:, g, :], ps[:, 0:NCH]), reads=[ps], writes=[Z])
                else:
                    P.op("scalar", ACT(Z[:, g, :], ps[:, 0:NCH], AF.Copy), reads=[ps], writes=[Z])
        self.check_stop("c1b")
        NR = 4
        Rg = [P.sb("Rg%d" % i, [128, 32], F32) for i in range(NR)]
        Ig = [P.sb("Ig%d" % i, [128, 32], F32) for i in range(NR)]
        tt = [[P.sb("sc%d_%d" % (i, j), [128, 32], F32) for j in range(6)] for i in range(2)]
        P.op("vector", lambda e: e.memset(Rg[0][:], 0.0), writes=[Rg[0]])
        P.op("vector", lambda e: e.memset(Ig[0][:], 0.0), writes=[Ig[0]])
        for Sx in (Sr, Si):
            P.op("gpsimd", lambda e, Sx=Sx: e.memset(Sx[F_, :, 1:2], 0.0), writes=[Sx])
            P.op("gpsimd", lambda e, Sx=Sx: e.memset(Sx[R_, :, 32:33], 0.0), writes=[Sx])
        order_r = list(range(31, -1, -1)) + list(range(287, 31, -1))
        V_ = "vector"
        for n in range(NCH):
            cf, cr = n, order_r[n]
            Rp, Ip, Rn, In = Rg[n % NR], Ig[n % NR], Rg[(n + 1) % NR], Ig[(n + 1) % NR]
            t1, t2, t3, t4, t5, t6 = tt[n % 2]
            P.op(V_, TTo(t1[:], A8[:], Rp[:], ALU.mult), reads=[A8, Rp], writes=[t1])
            P.op(V_, TTo(t2[:], B8[:], Ip[:], ALU.mult), reads=[B8, Ip], writes=[t2])
            P.op(V_, TTo(t4[:], B8[:], Rp[:], ALU.mult), reads=[B8, Rp], writes=[t4])
            P.op(V_, TTo(t5[:], A8[:], Ip[:], ALU.mult), reads=[A8, Ip], writes=[t5])
            P.op(V_, TTo(t3[:], t1[:], t2[:], ALU.subtract), reads=[t1, t2], writes=[t3])
            P.op(V_, TTo(t6[:], t4[:], t5[:], ALU.add), reads=[t4, t5], writes=[t6])
            P.op(V_, TTo(Rn[F_], t3[F_], ZSr[F_, :, cf], ALU.add), reads=[t3, ZSr], writes=[Rn])
            P.op(V_, TTo(Rn[R_], t3[R_], ZSr[R_, :, cr], ALU.add), reads=[t3, ZSr], writes=[Rn])
            P.op(V_, TTo(In[F_], t6[F_], ZSi[F_, :, cf], ALU.add), reads=[t6, ZSi], writes=[In])
            P.op(V_, TTo(In[R_], t6[R_], ZSi[R_, :, cr], ALU.add), reads=[t6, ZSi], writes=[In])
            sf = cf + 2
            for Sx, Xn in ((Sr, Rn), (Si, In)):
                P.op("gpsimd", CP(Sx[F_, :, sf], Xn[F_]), reads=[Xn], writes=[Sx])
                if cr != 32:
                    P.op("gpsimd", CP(Sx[R_, :, cr], Xn[R_]), reads=[Xn], writes=[Sx])
                if cr == 0:
                    P.op("gpsimd", CP(Sx[R_, :, 288], Xn[R_]), reads=[Xn], writes=[Sx])
        self.check_stop("c1c")
        P.push()
        Wg = P.sb("Wg", [128, 4, 2 * D], BF16)
        self.wload(Wg, I["w_glu"].t[l].rearrange("(t p) n -> p t n", p=128), 4, 2 * D, eng="gpsimd")
        ygl = P.sb("ygl", [128, 8, 512], BF16)
        yT = P.sb("yT", [128, 4, 1024], BF16)
        yTv = yT[:].rearrange("p t (c i) -> p t c i", i=8)
        sig = [P.sb("sig%d" % i, [128, 512], F32) for i in range(2)]
        gst = [P.sb("gst%d" % i, [128, 1024], BF16) for i in range(2)]
        ydb = P.sb("ydb", [128, 8, 512], F32) if "YS" in self.dbg else None
        blocks = ([] if self.last else [(0, 32, 1)]) + [(32, 128, 2), (160, 128, 2)]
        nsg = 0
        for (c0, M, roff) in blocks:
            for g4 in range(8):
                ps = self.bank("cr", [0, 1, 2, 3])
                for gg in range(4):
                    g = g4 * 4 + gg
                    o = ps[0:M, gg * 128:(gg + 1) * 128]
                    P.op("tensor", MM(o, Ust[:, g, c0:c0 + M], Toep[:, g, :], True, False), reads=[Ust, Toep], writes=[ps], inc=False)
                    P.op("tensor", MM(o, Sr[:, g, c0 + 1:c0 + 1 + M], CL[:, g, 0, :], False, False), reads=[Sr, CL], writes=[ps], inc=False)
                    P.op("tensor", MM(o, Si[:, g, c0 + 1:c0 + 1 + M], CL[:, g, 1, :], False, True), reads=[Si, CL], writes=[ps], inc=(gg == 3))
                pv = ps[0:M, :].rearrange("c (g i h) -> c i g h", g=4, i=8)
                yo = ygl[0:M, :, g4 * 64:(g4 + 1) * 64].rearrange("c i (g h) -> c i g h", g=4)
                P.op("scalar", ACT(yo, pv, AF.Gelu_apprx_tanh), reads=[ps], writes=[ygl])
                if ydb is not None:
                    P.op("vector", CP(ydb[0:M, :, g4 * 64:(g4 + 1) * 64].rearrange("c i (g h) -> c i g h", g=4), pv), reads=[ps], writes=[ydb])
            self.check_stop("c2a")
            if ydb is not None:
                P.dma("gpsimd", S["YS"].t[b, c0 * 8:(c0 + M) * 8, :].rearrange("(c i) ch -> c i ch", i=8), ydb[0:M], reads=[ydb], writes=[S["YS"]])
            self.check_stop("c2b")
            for i in range(8):
                ps = self.bank("cs", [4, 5])
                pb = ps[:].bitcast(BF16).rearrange("p (k t) -> p k t", k=8)
                for t in range(4):
                    P.op("tensor", TR(pb[:, t, 0:M], ygl[0:M, i, t * 128:(t + 1) * 128], C["identb"][0:M, 0:M]), reads=[ygl, C["identb"]], writes=[ps], inc=(t == 3))
                P.op("vector", CP(yTv[:, :, 0:M, i], pb[:, 0:4, 0:M]), reads=[ps], writes=[yT])
            self.check_stop("c2c")
            ntok = M * 8
            for ct in range(8):
                g_ = gst[ct % 2]
                for n0 in range(0, ntok, 512):
                    nn = min(512, ntok - n0)
                    pa = self.bank("cg", [6, 7, 0, 1, 2, 3])
                    pg = self.bank("cg", [6, 7, 0, 1, 2, 3])
                    for t in range(4):
                        P.op("tensor", MM(pa[:, 0:nn], Wg[:, t, ct * 128:(ct + 1) * 128], yT[:, t, n0:n0 + nn], t == 0, t == 3), reads=[Wg, yT], writes=[pa], inc=(t == 3))
                    for t in range(4):
                        P.op("tensor", MM(pg[:, 0:nn], Wg[:, t, D + ct * 128:D + (ct + 1) * 128], yT[:, t, n0:n0 + nn], t == 0, t == 3), reads=[Wg, yT], writes=[pg], inc=(t == 3))
                    sg = sig[nsg % 2]
                    nsg += 1
                    P.op("scalar", ACT(sg[:, 0:nn], pg[:, 0:nn], AF.Sigmoid), reads=[pg], writes=[sg])
                    P.op("vector", TTo(g_[:, n0:n0 + nn], pa[:, 0:nn], sg[:, 0:nn], ALU.mult), reads=[pa, sg], writes=[g_])
                P.dma("gpsimd", S["SSMT"].t[b, ct * 128:(ct + 1) * 128, c0 * 8:c0 * 8 + ntok], g_[:, 0:ntok], reads=[g_], writes=[S["SSMT"]])
            self.check_stop("c2d")
            if c0 == 32:
                self.check_stop("c2e")
        P.pop()
        P.pop()


    def postnorm(self, pss, xt, gbc, lng, lnb, upd, yv, yn, sm, dst_buf, dst_ap):
        P = self.P
        for half in range(2):
            P.op("vector", TTo(upd[:, half * 512:(half + 1) * 512], pss[half][:, 0:512], gbc[:, half * 512:(half + 1) * 512], ALU.mult),
                 reads=[pss[half], gbc], writes=[upd])
        P.op("vector", STT(yv[:], xt[:], ALPHA, upd[:], ALU.mult, ALU.add), reads=[xt, upd], writes=[yv])
        self.ln_tile(yv, yn[:], yn, sm)
        P.op("gpsimd", TTo(yn[:], yn[:], lng[:], ALU.mult), reads=[yn, lng], writes=[yn])
        P.op("gpsimd", TTo(yn[:], yn[:], lnb[:], ALU.add), reads=[yn, lnb], writes=[yn])
        P.dma("gpsimd", dst_ap, yn[:], reads=[yn], writes=[dst_buf])

    def segs(self):
        out = []
        for b in range(NBC):
            if not self.last:
                out.append((b, "ctx", 2, 0, TC))
            out.append((b, "lat", b, TC, TL))
        return out

    def phaseD(self):
        P, I, S, C, l = self.P, self.I, self.S, self.C, self.l
        P.push()
        Wco = P.sb("Wco", [128, 4, D], BF16)
        Wao = P.sb("Wao", [128, 8, D], BF16)
        Wo = P.sb("Wo", [128, 8, D], BF16)
        for Wt, nm, nk in ((Wco, "w_conv_out", 4), (Wao, "w_attn_out", 8), (Wo, "w_o", 8)):
            self.wload(Wt, I[nm].t[l].rearrange("(t p) n -> p t n", p=128), nk, D, eng="gpsimd")
        cw = P.sb("cw", [128, 4, 3], F32)
        for k in range(3):
            P.dma("sync", cw[:, :, k], I["conv_w"].t[l, k, :].rearrange("(t p) -> p t", p=128), reads=[I["conv_w"]], writes=[cw], allow_slow_non_contiguous=True)
        lng = P.sb("lng", [128, D], F32)
        lnb = P.sb("lnb", [128, D], F32)
        P.dma("sync", lng[:], I["ln1_g"].t[l, :].partition_broadcast(128), reads=[I["ln1_g"]], writes=[lng])
        P.dma("sync", lnb[:], I["ln1_b"].t[l, :].partition_broadcast(128), reads=[I["ln1_b"]], writes=[lnb])
        gbc = P.sb("gbc", [128, D], F32)
        axh = P.sb("axh", [128, 4, 514], BF16)
        cgh = P.sb("cgh", [128, 4, 514], BF16)
        bgs = P.sb("bgs", [128, 4, 512], BF16)
        prod = P.sb("prod", [128, 4, 514], F32)
        acc = P.sb("acc", [128, 4, 512], F32)
        convT = P.sb("convT", [128, 4, 512], BF16)
        aoT = P.sb("aoT", [128, 8, 512], BF16)
        gts = [P.sb("gts%d" % i, [128, 3, 512], BF16) for i in range(2)]
        ssmT = [P.sb("ssmT%d" % i, [128, 512], BF16) for i in range(2)]
        m1 = [P.sb("m1_%d" % i, [128, 512], F32) for i in range(2)]
        m2 = [P.sb("m2_%d" % i, [128, 512], F32) for i in range(2)]
        m3 = [P.sb("m3_%d" % i, [128, 512], F32) for i in range(2)]
        mgT = P.sb("mgT", [128, 8, 512], BF16)
        xt = [P.sb("dxt%d" % i, [128, D], F32) for i in range(2)]
        upd = [P.sb("dupd%d" % i, [128, D], F32) for i in range(2)]
        yv = [P.sb("dyv%d" % i, [128, D], F32) for i in range(2)]
        yn = [P.sb("dyn%d" % i, [128, D], F32) for i in range(2)]
        sm = self.ln_small("d")
        nx = 0
        nd = 0
        for (b, seg, r, gofs, slen) in self.segs():
            P.dma("sync", gbc[:], S["modr%d" % l].t[r, 2 * D:3 * D].partition_broadcast(128), reads=[S["modr%d" % l]], writes=[gbc])
            rbuf, rap = self.res_in(b, seg)
            n = min(512, slen)
            for t0 in range(0, slen, n):
                g0 = gofs + t0
                lo = 1 if t0 == 0 else 0
                hi = n + 1 if t0 + n == slen else n + 2
                for (hb, nm) in ((axh, "AXT"), (cgh, "CGT")):
                    if lo == 1:
                        P.op("gpsimd", lambda e, hb=hb: e.memset(hb[:, :, 0:1], 0.0), writes=[hb])
                    if hi == n + 1:
                        P.op("gpsimd", lambda e, hb=hb, n=n: e.memset(hb[:, :, n + 1:n + 2], 0.0), writes=[hb])
                    P.dma("sync", hb[:, :, lo:hi], S[nm].t[b].rearrange("(t p) c -> p t c", p=128)[:, :, g0 - 1 + lo:g0 - 1 + hi], reads=[S[nm]], writes=[hb])
                P.dma("sync", bgs[:, :, 0:n], S["BGT"].t[b].rearrange("(t p) c -> p t c", p=128)[:, :, g0:g0 + n], reads=[S["BGT"]], writes=[bgs])
                P.dma("sync", aoT[:, :, 0:n], S["AOT"].t[b].rearrange("(t p) c -> p t c", p=128)[:, :, g0:g0 + n], reads=[S["AOT"]], writes=[aoT])
                P.op("gpsimd", TTo(prod[:, :, 0:n + 2], cgh[:, :, 0:n + 2], axh[:, :, 0:n + 2], ALU.mult), reads=[cgh, axh], writes=[prod])
                for t in range(4):
                    P.op("vector", TS(acc[:, t, 0:n], prod[:, t, 0:n], cw[:, t, 0:1], None, ALU.mult), reads=[prod, cw], writes=[acc])
                    P.op("vector", STT(acc[:, t, 0:n], prod[:, t, 1:n + 1], cw[:, t, 1:2], acc[:, t, 0:n], ALU.mult, ALU.add), reads=[prod, cw, acc], writes=[acc])
                    P.op("vector", STT(acc[:, t, 0:n], prod[:, t, 2:n + 2], cw[:, t, 2:3], acc[:, t, 0:n], ALU.mult, ALU.add), reads=[prod, cw, acc], writes=[acc])
                P.op("gpsimd", TTo(convT[:, :, 0:n], acc[:, :, 0:n], bgs[:, :, 0:n], ALU.mult), reads=[acc, bgs], writes=[convT])
                for dt in range(8):
                    gt_, sm_ = gts[nd % 2], ssmT[nd % 2]
                    a1, a2, a3 = m1[nd % 2], m2[nd % 2], m3[nd % 2]
                    nd += 1
                    for s3 in range(3):
                        P.dma("sync", gt_[:, s3, 0:n], S["GT"].t[b, s3 * D + dt * 128:s3 * D + (dt + 1) * 128, g0:g0 + n], reads=[S["GT"]], writes=[gt_])
                    P.dma("sync", sm_[:, 0:n], S["SSMT"].t[b, dt * 128:(dt + 1) * 128, g0:g0 + n], reads=[S["SSMT"]], writes=[sm_])
                    pc = self.bank("dc", [0, 1])
                    pa = self.bank("da", [2, 3])
                    for t in range(4):
                        P.op("tensor", MM(pc[:, 0:n], Wco[:, t, dt * 128:(dt + 1) * 128], convT[:, t, 0:n], t == 0, t == 3), reads=[Wco, convT], writes=[pc], inc=(t == 3))
                    for k in range(8):
                        P.op("tensor", MM(pa[:, 0:n], Wao[:, k, dt * 128:(dt + 1) * 128], aoT[:, k, 0:n], k == 0, k == 7), reads=[Wao, aoT], writes=[pa], inc=(k == 7))
                    P.op("vector", TTo(a1[:, 0:n], pc[:, 0:n], gt_[:, 0, 0:n], ALU.mult), reads=[pc, gt_], writes=[a1])
                    P.op("vector", TTo(a2[:, 0:n], pa[:, 0:n], gt_[:, 2, 0:n], ALU.mult), reads=[pa, gt_], writes=[a2])
                    P.op("gpsimd", TTo(a3[:, 0:n], sm_[:, 0:n], gt_[:, 1, 0:n], ALU.mult), reads=[sm_, gt_], writes=[a3])
                    P.op("gpsimd", TTo(a1[:, 0:n], a1[:, 0:n], a2[:, 0:n], ALU.add), reads=[a1, a2], writes=[a1])
                    P.op("gpsimd", TTo(mgT[:, dt, 0:n], a1[:, 0:n], a3[:, 0:n], ALU.add), reads=[a1, a3], writes=[mgT])
                for ti in range(n // 128):
                    i = nx
                    nx += 1
                    x_ = xt[i % 2]
                    P.dma("sync", x_[:], rap[t0 + ti * 128:t0 + (ti + 1) * 128, :], reads=[rbuf], writes=[x_])
                    pss = []
                    for half in range(2):
                        ps = self.bank("do", [4, 5, 6, 7])
                        for k in range(8):
                            P.op("tensor", MM(ps[:, 0:512], mgT[:, k, ti * 128:(ti + 1) * 128], Wo[:, k, half * 512:(half + 1) * 512], k == 0, k == 7),
                                 reads=[mgT, Wo], writes=[ps], inc=(k == 7))
                        pss.append(ps)
                    row = g0 + ti * 128
                    self.postnorm(pss, x_, gbc, lng, lnb, upd[i % 2], yv[i % 2], yn[i % 2], sm[i % 2], S["resA"], S["resA"].t[b, row:row + 128, :])
        P.pop()

    def phaseF(self):
        P, I, S, C, l = self.P, self.I, self.S, self.C, self.l
        P.push()
        lng = P.sb("lng2", [128, D], F32)
        lnb = P.sb("lnb2", [128, D], F32)
        P.dma("sync", lng[:], I["ln2_g"].t[l, :].partition_broadcast(128), reads=[I["ln2_g"]], writes=[lng])
        P.dma("sync", lnb[:], I["ln2_b"].t[l, :].partition_broadcast(128), reads=[I["ln2_b"]], writes=[lnb])
        cwf = P.sb("cwf", [128, 22, 3], F32)
        cbf = P.sb("cbf", [128, 22], F32)
        for k in range(3):
            P.dma("sync", cwf[:, :, k], I["ffn_conv_w"].t[l, k, :].rearrange("(t p) -> p t", p=128), reads=[I["ffn_conv_w"]], writes=[cwf], allow_slow_non_contiguous=True)
        P.dma("sync", cbf[:], I["ffn_conv_b"].t[l, :].rearrange("(t p) -> p t", p=128), reads=[I["ffn_conv_b"]], writes=[cbf], allow_slow_non_contiguous=True)
        hff = P.sb("hff", [128, 22, TL], BF16)
        gbc = P.sb("gbc2", [128, D], F32)
        wup = I["ffn_w_up"].t[l].rearrange("(k p) n -> p k n", p=128)
        for (b, seg, r, gofs, slen) in self.segs():
            ntile = slen // 128
            P.push()
            hT2 = P.sb("hT2", [128, 8, slen], BF16)
            P.push()
            xt = [P.sb("fxt%d" % i, [128, D], F32) for i in range(2)]
            xn = [P.sb("fxn%d" % i, [128, D], BF16) for i in range(2)]
            sm = self.ln_small("f")
            for ti in range(ntile):
                x_, n_ = xt[ti % 2], xn[ti % 2]
                row = gofs + ti * 128
                P.dma("sync", x_[:], S["resA"].t[b, row:row + 128, :], reads=[S["resA"]], writes=[x_])
                self.ln_tile(x_, n_[:], n_, sm[ti % 2])
                ps = self.bank("ftr", [0, 1])
                pb = ps[:].bitcast(BF16).rearrange("p (k t) -> p k t", k=8)
                for k in range(8):
                    P.op("tensor", TR(pb[:, k, :], n_[:, k * 128:(k + 1) * 128], C["identb"][:]), reads=[n_, C["identb"]], writes=[ps], inc=(k == 7))
                for k in range(8):
                    o = hT2[:, k, ti * 128:(ti + 1) * 128]
                    rd = [ps, self.modT] if k in (0, 7) else []
                    wr = [hT2] if k in (0, 7) else []
                    if ti % 2 == 0:
                        P.op("vector", TS(o, pb[:, k, :], self.modT[:, 4, k, r:r + 1], self.modT[:, 3, k, r:r + 1], ALU.mult, ALU.add), reads=rd, writes=wr)
                    else:
                        P.op("scalar", ACT(o, pb[:, k, :], AF.Identity, bias=self.modT[:, 3, k, r:r + 1], scale=self.modT[:, 4, k, r:r + 1]), reads=rd, writes=wr)
            P.pop()
            wu = [P.sb("wu%d" % i, [128, 8, 128], BF16) for i in range(2)]
            wv = [P.sb("wv%d" % i, [128, 8, 128], BF16) for i in range(2)]
            wuf = [P.sb("wuf%d" % i, [128, 8, 128], F32) for i in range(2)]
            wvf = [P.sb("wvf%d" % i, [128, 8, 128], F32) for i in range(2)]
            ucp = [P.sb("ucp%d" % i, [128, slen + 2], F32) for i in range(2)]
            acc = P.sb("facc", [128, slen], F32)
            ge = [P.sb("fge%d" % i, [128, slen], BF16) for i in range(2)]
            for u_ in ucp:
                P.op("gpsimd", lambda e, u_=u_: e.memset(u_[:, 0:1], 0.0), writes=[u_])
                P.op("gpsimd", lambda e, u_=u_, slen=slen: e.memset(u_[:, slen + 1:slen + 2], 0.0), writes=[u_])
            nbs = [(n0, min(512, slen - n0)) for n0 in range(0, slen, 512)]
            for j in range(22):
                wu_, wv_, u_, g_ = wu[j % 2], wv[j % 2], ucp[j % 2], ge[j % 2]
                P.dma("sync", wuf[j % 2][:], wup[:, :, j * 128:(j + 1) * 128], writes=[wuf[j % 2]])
                P.dma("sync", wvf[j % 2][:], wup[:, :, DFF + j * 128:DFF + (j + 1) * 128], writes=[wvf[j % 2]])
                P.op("gpsimd", CP(wu_[:], wuf[j % 2][:]), reads=[wuf[j % 2]], writes=[wu_])
                P.op("gpsimd", CP(wv_[:], wvf[j % 2][:]), reads=[wvf[j % 2]], writes=[wv_])
                for (n0, nn) in nbs:
                    ps = self.bank("fu", [0, 1, 2, 3])
                    for k in range(8):
                        P.op("tensor", MM(ps[:, 0:nn], wu_[:, k, :], hT2[:, k, n0:n0 + nn], k == 0, k == 7), reads=[wu_, hT2], writes=[ps], inc=(k == 7))
                    P.op("scalar", ACT(u_[:, 1 + n0:1 + n0 + nn], ps[:, 0:nn], AF.Copy), reads=[ps], writes=[u_])
                P.op("vector", TS(acc[:], u_[:, 0:slen], cwf[:, j, 0:1], None, ALU.mult), reads=[u_, cwf], writes=[acc])
                P.op("vector", STT(acc[:], u_[:, 1:slen + 1], cwf[:, j, 1:2], acc[:], ALU.mult, ALU.add), reads=[u_, cwf, acc], writes=[acc])
                P.op("vector", STT(acc[:], u_[:, 2:slen + 2], cwf[:, j, 2:3], acc[:], ALU.mult, ALU.add), reads=[u_, cwf, acc], writes=[acc])
                P.op("scalar", ACT(g_[:], acc[:], AF.Gelu_apprx_tanh, bias=cbf[:, j:j + 1], scale=1.0), reads=[acc, cbf], writes=[g_])
                for (n0, nn) in nbs:
                    ps = self.bank("fv", [4, 5, 6, 7])
                    for k in range(8):
                        P.op("tensor", MM(ps[:, 0:nn], wv_[:, k, :], hT2[:, k, n0:n0 + nn], k == 0, k == 7), reads=[wv_, hT2], writes=[ps], inc=(k == 7))
                    P.op("vector", TTo(hff[:, j, n0:n0 + nn], ps[:, 0:nn], g_[:, n0:n0 + nn], ALU.mult), reads=[ps, g_], writes=[hff])
            P.pop()
            P.push()
            Wd = P.sb("Wd", [128, 22, D], BF16)
            P.dma("sync", Wd[:], S["wb_dn%d" % l].t.rearrange("(t p) n -> p t n", p=128), reads=[S["wb_dn%d" % l]], writes=[Wd])
            P.dma("sync", gbc[:], S["modr%d" % l].t[r, 5 * D:6 * D].partition_broadcast(128), reads=[S["modr%d" % l]], writes=[gbc])
            xt = [P.sb("gxt%d" % i, [128, D], F32) for i in range(2)]
            upd = [P.sb("gupd%d" % i, [128, D], F32) for i in range(2)]
            yv = [P.sb("gyv%d" % i, [128, D], F32) for i in range(2)]
            yn = [P.sb("gyn%d" % i, [128, D], F32) for i in range(2)]
            sm = self.ln_small("g")
            for ti in range(ntile):
                x_ = xt[ti % 2]
                row = gofs + ti * 128
                P.dma("sync", x_[:], S["resA"].t[b, row:row + 128, :], reads=[S["resA"]], writes=[x_])
                pss = []
                for half in range(2):
                    ps = self.bank("fd", [0, 1, 2, 3])
                    for j in range(22):
                        P.op("tensor", MM(ps[:, 0:512], hff[:, j, ti * 128:(ti + 1) * 128], Wd[:, j, half * 512:(half + 1) * 512], j == 0, j == 21),
                             reads=[hff, Wd], writes=[ps], inc=(j == 21))
                    pss.append(ps)
                if self.last:
                    dbuf, dap = self.out, self.out.t[b, ti * 128:(ti + 1) * 128, :]
                else:
                    dbuf, dap = S["resB"], S["resB"].t[b, row:row + 128, :]
                self.postnorm(pss, x_, gbc, lng, lnb, upd[ti % 2], yv[ti % 2], yn[ti % 2], sm[ti % 2], dbuf, dap)
            P.pop()
        P.pop()


def _rope_tables():
    half = 64
    inv_freq = (1.0 / (np.float32(10000.0) ** (np.arange(0, half, 2, dtype=np.float32) / np.float32(half)))).astype(np.float32)
    rows = TL // 64
    row = np.repeat(np.arange(rows, dtype=np.float32), 64)
    col = np.tile(np.arange(64, dtype=np.float32), rows)
    ang = np.concatenate([row[:, None] * inv_freq, col[:, None] * inv_freq], -1).astype(np.float32)
    cos = np.cos(ang).astype(np.float32).reshape(16, 128, 64).transpose(1, 0, 2)
    sin = np.sin(ang).astype(np.float32).reshape(16, 128, 64).transpose(1, 0, 2)
    return np.ascontiguousarray(cos), np.ascontiguousarray(sin)


def _host_consts():
    cos, sin = _rope_tables()
    jj = np.arange(128) // 16
    maskf = (jj[None, :] >= jj[:, None]).astype(np.float32)
    maskr = (jj[None, :] <= jj[:, None]).astype(np.float32)
    return {
        "k_identf": np.eye(128, dtype=np.float32),
        "k_identb": np.eye(128, dtype=np.float32).astype(ml_dtypes.bfloat16),
        "k_onesb": np.ones((128, 128), dtype=np.float32).astype(ml_dtypes.bfloat16),
        "k_cos": cos, "k_sin": sin, "k_maskf": maskf, "k_maskr": maskr,
    }


_NC_CACHE = {}


def _get_nc():
    if "nc" not in _NC_CACHE:
        _NC_CACHE["nc"] = K().build()
    return _NC_CACHE["nc"]


def make_in_maps(inputs, cores):
    consts = _host_consts()
    maps = []
    for i in cores:
        m = {}
        for k, v in inputs.items():
            v = np.asarray(v)
            if k in ("x", "c", "ctx"):
                m[k] = np.ascontiguousarray(v[NBC * i:NBC * (i + 1)])
            elif k == "c_ctx":
                m[k] = np.ascontiguousarray(v.reshape(1, D))
            else:
                m[k] = v
        m.update(consts)
        maps.append(m)
    return maps


def kernel(**inputs):
    nc = _get_nc()
    maps = make_in_maps(inputs, range(8))
    res = run_bass_kernel_spmd(nc, maps, core_ids=list(range(8)))
    return np.concatenate([np.asarray(r["out"]) for r in res.results], axis=0).astype(np.float32)
```

```python
import numpy as np
import ml_dtypes
from contextlib import ExitStack
import concourse.bass as bass
import concourse.mybir as mybir
from concourse.bass_utils import run_bass_kernel_spmd

F32 = mybir.dt.float32
BF16 = mybir.dt.bfloat16
I32 = mybir.dt.int32
AF = mybir.ActivationFunctionType
ALU = mybir.AluOpType

ENGS = ["tensor", "vector", "scalar", "gpsimd", "sync"]

D = 1024
TL = 2048
TC = 256
TT = TC + TL
NBC = 2
DEPTH = 2
IN_COLS = 6656
DFF = 2816
NCH = TT // 8
EPS = 1e-6
ALPHA = float((2 * DEPTH) ** 0.25)
ATTN_SCALE = float(128 ** -0.5)
TWO_PI = float(2 * np.pi)
PI = float(np.pi)


class Buf:
    def __init__(self, name, t):
        self.name = name
        self.t = t
        self.w = {}
        self.r = {}
        self.dkey = {}

    def __getitem__(self, idx):
        return self.t[idx]


class Prog:
    def __init__(self, nc, n_dsem=56):
        self.nc = nc
        self.base = ExitStack()
        self.scopes = [ExitStack()]
        self.scope_bufs = [[]]
        self.q = {e: [] for e in ENGS}
        self.sem = {}
        self.cnt = {}
        self.seen = {e: {} for e in ENGS}
        for e in ENGS:
            self.sem[e] = self.base.enter_context(nc.semaphore("s_" + e))
            self.cnt[e] = 0
        self.free_dsem = {"sync": [], "gpsimd": [], "scalar": []}
        for qn, nq_ in (("sync", 36), ("gpsimd", 24), ("scalar", 24)):
            for i in range(nq_):
                k = ("d" + qn, i)
                self.sem[k] = self.base.enter_context(nc.semaphore("d%s%d" % (qn[:2], i)))
                self.cnt[k] = 0
                self.free_dsem[qn].append(k)
        self.uid = 0

    def push(self):
        self.scopes.append(ExitStack())
        self.scope_bufs.append([])

    def pop(self):
        self.barrier()
        for b in self.scope_bufs.pop():
            for qn, k in b.dkey.items():
                self.free_dsem[qn].append(k)
            b.dkey = {}
        self.scopes.pop().close()

    def sb(self, name, shape, dtype):
        self.uid += 1
        t = self.scopes[-1].enter_context(self.nc.sbuf_tensor("%s_%d" % (name, self.uid), list(shape), dtype))
        b = Buf(name, t)
        self.scope_bufs[-1].append(b)
        return b

    def ps(self, name, shape, dtype=F32):
        t = self.base.enter_context(self.nc.psum_tensor(name, list(shape), dtype))
        return Buf(name, t)

    def dram(self, name, shape, dtype, kind="Internal"):
        t = self.nc.dram_tensor(name, list(shape), dtype, kind=kind).ap()
        return Buf(name, t)

    def _dkey(self, b, eng):
        if eng not in b.dkey:
            b.dkey[eng] = self.free_dsem[eng].pop()
        return b.dkey[eng]

    def _deps(self, eng, reads, writes, strict=False, nowaw=False):
        deps = {}

        def add(d):
            for k, v in d.items():
                if k == eng and eng == "tensor":
                    continue
                if deps.get(k, 0) < v:
                    deps[k] = v

        for r in reads:
            add(r.w)
        for w in writes:
            if not nowaw:
                add(w.w)
            add(w.r)
        out = []
        for k, v in deps.items():
            if self.seen[eng].get(k, 0) >= v:
                continue
            self.seen[eng][k] = v
            out.append((self.sem[k], v))
        return out

    def _commit(self, ev, reads, writes, nowaw=False):
        k, v = ev
        for r in reads:
            if r.r.get(k, 0) < v:
                r.r[k] = v
        for w in writes:
            if w.w.get(k, 0) < v:
                w.w[k] = v
            if not nowaw:
                w.r = {}

    def op(self, eng, fn, reads=(), writes=(), inc=True, nowaw=False):
        waits = self._deps(eng, reads, writes, nowaw=nowaw)
        if inc:
            self.cnt[eng] += 1
            ev = (eng, self.cnt[eng])
        else:
            ev = (eng, self.cnt[eng] + 1)
        sem = self.sem[eng]

        def emit(e, waits=waits, fn=fn, inc=inc, sem=sem):
            for s, v in waits:
                e.wait_ge(s, v)
            ins = fn(e)
            if inc:
                ins.then_inc(sem, 1)

        self.q[eng].append(emit)
        self._commit(ev, reads, writes, nowaw=nowaw)

    def dma(self, eng, out, in_, reads=(), writes=(), semb=None, **kw):
        waits = self._deps(eng, reads, writes, strict=True)
        if semb is None:
            for b in list(writes) + list(reads):
                if not isinstance(b, DBuf):
                    semb = b
                    break
        key = self._dkey(semb, eng)
        self.cnt[key] += 16
        ev = (key, self.cnt[key])
        sem = self.sem[key]

        def emit(e, waits=waits, out=out, in_=in_, sem=sem, kw=kw):
            for s, v in waits:
                e.wait_ge(s, v)
            e.dma_start(out=out, in_=in_, **kw).then_inc(sem, 16)

        self.q[eng].append(emit)
        self._commit(ev, reads, writes)

    def barrier(self):
        tot = dict(self.cnt)
        for e in ENGS:
            waits = []
            for k, v in tot.items():
                if k == e or v == 0:
                    continue
                if self.seen[e].get(k, 0) >= v:
                    continue
                self.seen[e][k] = v
                waits.append((self.sem[k], v))

            def emit(en, waits=waits):
                for s, v in waits:
                    en.wait_ge(s, v)

            self.q[e].append(emit)

    def finish(self):
        self.barrier()
        nc = self.nc
        q = self.q
        with nc.Block() as block:
            @block.tensor
            def _(e):
                for f in q["tensor"]:
                    f(e)

            @block.vector
            def _(e):
                for f in q["vector"]:
                    f(e)

            @block.scalar
            def _(e):
                for f in q["scalar"]:
                    f(e)

            @block.gpsimd
            def _(e):
                for f in q["gpsimd"]:
                    f(e)

            @block.sync
            def _(e):
                for f in q["sync"]:
                    f(e)
        while self.scopes:
            self.scopes.pop().close()
        self.base.close()


class DBuf(Buf):
    pass


def TS(out, in0, s1, s2, op0, op1=None):
    if op1 is None:
        return lambda e: e.tensor_scalar(out=out, in0=in0, scalar1=s1, scalar2=None, op0=op0)
    return lambda e: e.tensor_scalar(out=out, in0=in0, scalar1=s1, scalar2=s2, op0=op0, op1=op1)


def TTo(out, in0, in1, op):
    return lambda e: e.tensor_tensor(out=out, in0=in0, in1=in1, op=op)


def STT(out, in0, scalar, in1, op0, op1):
    return lambda e: e.scalar_tensor_tensor(out=out, in0=in0, scalar=scalar, in1=in1, op0=op0, op1=op1)


def ACT(out, in_, func, bias=None, scale=None, accum_out=None):
    kw = {}
    if bias is not None:
        kw["bias"] = bias
    if scale is not None:
        kw["scale"] = scale
    if accum_out is not None:
        kw["accum_out"] = accum_out
    return lambda e: e.activation(out=out, in_=in_, func=func, **kw)


def CP(out, in_):
    return lambda e: e.tensor_copy(out=out, in_=in_)


def MM(out, lhsT, rhs, start, stop):
    return lambda e: e.matmul(out, lhsT=lhsT, rhs=rhs, start=start, stop=stop)


def TR(out, in_, ident):
    return lambda e: e.transpose(out=out, in_=in_, identity=ident)


class K:
    def __init__(self, dbg=(), stop_after=None):
        self.dbg = set(dbg)
        self.stop_after = stop_after
        nc = self.nc = bass.Bass("TRN2", target_bir_lowering=False)
        P = self.P = Prog(nc)
        self.inputs = {}
        self.psb = [P.ps("psb%d" % i, [128, 512], F32) for i in range(8)]
        self.rr = {}

    def din(self, name, shape, dt=F32):
        b = DBuf(name, self.nc.dram_tensor(name, list(shape), dt, kind="ExternalInput").ap())
        self.inputs[name] = b
        return b

    def dscr(self, name, shape, dt):
        kind = "ExternalOutput" if name in self.dbg else "Internal"
        return DBuf(name, self.nc.dram_tensor(name, list(shape), dt, kind=kind).ap())

    def bank(self, group, banks):
        i = self.rr.get(group, 0)
        self.rr[group] = i + 1
        return self.psb[banks[i % len(banks)]]

    def build(self):
        nc, P = self.nc, self.P
        L = DEPTH
        I = self.I = {}
        I["x"] = self.din("x", [NBC, TL, D])
        I["c"] = self.din("c", [NBC, D])
        I["ctx"] = self.din("ctx", [NBC, TC, D])
        I["c_ctx"] = self.din("c_ctx", [1, D])
        for name, shape in [
            ("w_mod", [L, D, 6 * D]), ("b_mod", [L, 6 * D]), ("w_in", [L, D, IN_COLS]), ("conv_w", [L, 3, 512]),
            ("w_conv_out", [L, 512, D]), ("ssm_lam_re", [L, 2, 32, 64]), ("ssm_lam_im", [L, 2, 32, 64]),
            ("ssm_log_dt", [L, 2, 32]), ("ssm_b_re", [L, 2, 32, 64, 16]), ("ssm_b_im", [L, 2, 32, 64, 16]),
            ("ssm_c_re", [L, 2, 32, 16, 64]), ("ssm_c_im", [L, 2, 32, 16, 64]), ("ssm_d", [L, 512]),
            ("w_glu", [L, 512, 2 * D]), ("q_norm_g", [L, 128]), ("k_norm_g", [L, 128]), ("w_attn_out", [L, D, D]),
            ("w_o", [L, D, D]), ("ln1_g", [L, D]), ("ln1_b", [L, D]), ("ffn_w_up", [L, D, 2 * DFF]),
            ("ffn_conv_w", [L, 3, DFF]), ("ffn_conv_b", [L, DFF]), ("ffn_w_down", [L, DFF, D]),
            ("ln2_g", [L, D]), ("ln2_b", [L, D]),
        ]:
            I[name] = self.din(name, shape)
        I["k_identf"] = self.din("k_identf", [128, 128])
        I["k_identb"] = self.din("k_identb", [128, 128], BF16)
        I["k_onesb"] = self.din("k_onesb", [128, 128], BF16)
        I["k_cos"] = self.din("k_cos", [128, 16, 64])
        I["k_sin"] = self.din("k_sin", [128, 16, 64])
        I["k_maskf"] = self.din("k_maskf", [128, 128])
        I["k_maskr"] = self.din("k_maskr", [128, 128])
        self.out = DBuf("out", nc.dram_tensor("out", [NBC, TL, D], F32, kind="ExternalOutput").ap())

        S = self.S = {}
        for l in range(L):
            S["wb_in%d" % l] = self.dscr("wb_in%d" % l, [D, IN_COLS], BF16)
            S["wb_co%d" % l] = self.dscr("wb_co%d" % l, [512, D], BF16)
            S["wb_glu%d" % l] = self.dscr("wb_glu%d" % l, [512, 2 * D], BF16)
            S["wb_ao%d" % l] = self.dscr("wb_ao%d" % l, [D, D], BF16)
            S["wb_o%d" % l] = self.dscr("wb_o%d" % l, [D, D], BF16)
            S["wb_up%d" % l] = self.dscr("wb_up%d" % l, [D, 2 * DFF], BF16)
            S["wb_dn%d" % l] = self.dscr("wb_dn%d" % l, [DFF, D], BF16)
            S["modr%d" % l] = self.dscr("modr%d" % l, [3, 6 * D], F32)
        S["resA"] = self.dscr("resA", [NBC, TT, D], F32)
        S["resB"] = self.dscr("resB", [NBC, TT, D], F32)
        S["KT"] = self.dscr("KT", [NBC, 2, 128, TT], BF16)
        S["V"] = self.dscr("V", [NBC, TT, 256], BF16)
        S["U"] = self.dscr("U", [NBC, TT, 512], BF16)
        S["QT"] = self.dscr("QT", [NBC, 8, 128, TT], BF16)
        S["AXT"] = self.dscr("AXT", [NBC, 512, TT], BF16)
        S["BGT"] = self.dscr("BGT", [NBC, 512, TT], BF16)
        S["CGT"] = self.dscr("CGT", [NBC, 512, TT], BF16)
        S["GT"] = self.dscr("GT", [NBC, 3 * D, TT], BF16)
        S["AOT"] = self.dscr("AOT", [NBC, D, TT], BF16)
        S["SSMT"] = self.dscr("SSMT", [NBC, D, TT], BF16)
        S["YS"] = self.dscr("YS", [NBC, TT, 512], F32)

        C = self.C = {}
        C["identf"] = P.sb("identf", [128, 128], F32)
        C["identb"] = P.sb("identb", [128, 128], BF16)
        C["onesb"] = P.sb("onesb", [128, 128], BF16)
        C["eps"] = P.sb("eps", [128, 1], F32)
        for nm in ["identf", "identb", "onesb"]:
            P.dma("sync", C[nm][:], I["k_" + nm].t, reads=[I["k_" + nm]], writes=[C[nm]])
        P.op("vector", lambda e: e.memset(C["eps"][:], EPS), writes=[C["eps"]])
        C["mhalf"] = P.sb("mhalf", [128, 16], F32)
        P.op("gpsimd", lambda e: e.memset(C["mhalf"][:], -0.5), writes=[C["mhalf"]])

        try:
            self.weight_prep()
            if self.stop_after == "prep":
                return self.finish()
            for l in range(L):
                self.layer(l)
                if self.stop_after == "layer%d" % l:
                    break
        except StopIteration:
            pass
        return self.finish()

    def finish(self):
        self.P.finish()
        return self.nc

    def check_stop(self, tag):
        if self.stop_after == tag:
            raise StopIteration

    def weight_prep(self):
        P, I, S = self.P, self.I, self.S
        self.prep_todo = []
        for l in range(DEPTH):
            for src, dst, Kd, N in [("ffn_w_down", "wb_dn", DFF, D)]:
                for kt in range(Kd // 128):
                    for c0 in range(0, N, 2048):
                        self.prep_todo.append((src, l, dst + str(l), kt, c0, min(2048, N - c0)))
        self.prep_n = 0

    def prep_bufs(self):
        P = self.P
        return ([P.sb("wpf%d" % i, [128, 1024], F32) for i in range(3)], [P.sb("wpb%d" % i, [128, 1024], BF16) for i in range(3)])

    def prep_emit(self, bufs, count, engs):
        P, I, S = self.P, self.I, self.S
        stf, stb = bufs
        for _ in range(count):
            if not self.prep_todo:
                return
            src, l, dst, kt, c0, w = self.prep_todo.pop(0)
            n = self.prep_n
            self.prep_n += 1
            f, b = stf[n % 3], stb[n % 3]
            eng = engs[n % len(engs)]
            P.dma("sync", f[:, :w], I[src].t[l][kt * 128:(kt + 1) * 128, c0:c0 + w], reads=[I[src]], writes=[f])
            if eng == "scalar":
                P.op(eng, ACT(b[:, :w], f[:, :w], AF.Copy), reads=[f], writes=[b])
            else:
                P.op(eng, CP(b[:, :w], f[:, :w]), reads=[f], writes=[b])
            P.dma("gpsimd", S[dst].t[kt * 128:(kt + 1) * 128, c0:c0 + w], b[:, :w], reads=[b], writes=[S[dst]])


    def wload(self, dst, src3, nk, ncols, eng="gpsimd", chunk=1024, src_buf=None):
        P = self.P
        P.push()
        stg = [P.sb("wst%d" % i, [128, chunk], F32) for i in range(3)]
        n = 0
        for k in range(nk):
            for c0 in range(0, ncols, chunk):
                w = min(chunk, ncols - c0)
                f = stg[n % 3]
                n += 1
                P.dma("sync", f[:, 0:w], src3[:, k, c0:c0 + w], reads=[src_buf] if src_buf else [], writes=[f])
                P.op(eng, CP(dst[:, k, c0:c0 + w], f[:, 0:w]), reads=[f], writes=[dst])
        P.pop()

    def layer(self, l):
        P = self.P
        self.l = l
        self.last = (l == DEPTH - 1)
        P.push()
        self.modT = P.sb("modT", [128, 6, 8, 3], F32)
        self.phase0()
        self.check_stop("p0_%d" % l)
        self.phaseA()
        self.check_stop("pA_%d" % l)
        P.push()
        Bst = P.sb("Bst", [128, 32, 2, 128], BF16)
        CL = P.sb("CL", [128, 32, 2, 128], BF16)
        Toep = P.sb("Toep", [128, 32, 128], BF16)
        A8 = P.sb("A8", [128, 32], F32)
        B8 = P.sb("B8", [128, 32], F32)
        self.phaseB(self.ssm_consts(Bst, CL, Toep, A8, B8))
        self.check_stop("pB_%d" % l)
        for b in range(NBC):
            self.ssm_batch(b, Bst, CL, Toep, A8, B8)
        P.pop()
        self.check_stop("pC_%d" % l)
        self.phaseD()
        self.check_stop("pD_%d" % l)
        self.phaseF()
        self.check_stop("pF_%d" % l)
        P.pop()

    def res_in(self, b, seg):
        if self.l == 0:
            return (self.I["ctx"], self.I["ctx"].t[b]) if seg == "ctx" else (self.I["x"], self.I["x"].t[b])
        r = self.S["resB"]
        return (r, r.t[b, 0:TC, :]) if seg == "ctx" else (r, r.t[b, TC:TT, :])

    def phase0(self):
        P, I, S, C, l = self.P, self.I, self.S, self.C, self.l
        P.push()
        scT = P.sb("scT", [128, 8, 3], F32)
        bm3 = P.sb("bm3", [3, 6 * D], F32)
        modrows = P.sb("modrows", [3, 6 * D], F32)
        wst = [P.sb("wmst%d" % i, [128, 8, 512], F32) for i in range(2)]
        for r in range(3):
            src = I["c"].t[r, :] if r < 2 else I["c_ctx"].t[0, :]
            P.dma("sync", scT[:, :, r], src.rearrange("(k p) -> p k", p=128), reads=[I["c"]], writes=[scT],
                  allow_slow_non_contiguous=True)
        P.op("scalar", ACT(scT[:], scT[:], AF.Silu), reads=[scT], writes=[scT])
        P.dma("sync", bm3[0:3, :], I["b_mod"].t[l, :].partition_broadcast(3), reads=[I["b_mod"]], writes=[bm3])
        wm = I["w_mod"].t[l].rearrange("(k p) n -> p k n", p=128)
        for blk in range(12):
            w = wst[blk % 2]
            P.dma("sync", w[:], wm[:, :, blk * 512:(blk + 1) * 512], reads=[I["w_mod"]], writes=[w])
            ps = self.bank("p0", [0, 1])
            for k in range(8):
                P.op("tensor", MM(ps[0:3, 0:512], scT[:, k, :], w[:, k, :], k == 0, k == 7), reads=[scT, w], writes=[ps], inc=(k == 7))
            P.op("vector", TTo(modrows[0:3, blk * 512:(blk + 1) * 512], ps[0:3, 0:512], bm3[0:3, blk * 512:(blk + 1) * 512], ALU.add),
                 reads=[ps, bm3], writes=[modrows])
        P.dma("gpsimd", S["modr%d" % l].t, modrows[0:3, :], reads=[modrows], writes=[S["modr%d" % l]])
        ps = self.bank("p0", [0, 1])
        for t in range(48):
            P.op("tensor", TR(ps[:, t * 3:(t + 1) * 3], modrows[0:3, t * 128:(t + 1) * 128], C["identf"][0:3, 0:3]),
                 reads=[modrows, C["identf"]], writes=[ps], inc=(t == 47))
        mt = self.modT
        P.op("vector", CP(mt[:].rearrange("p a k r -> p (a k r)"), ps[:, 0:144]), reads=[ps], writes=[mt])
        for sec in (1, 4):
            P.op("vector", TS(mt[:, sec], mt[:, sec], 1.0, None, ALU.add), reads=[mt], writes=[mt])
        P.pop()

    def ln_tile(self, xt, out_ap, out_buf, sm, src_ap=None, act_extra_reads=()):
        P, C = self.P, self.C
        st, mv, rs, nb = sm
        src = xt[:] if src_ap is None else src_ap
        P.op("vector", lambda e: e.bn_stats(out=st[:, 0:6], in_=src[:, 0:512]), reads=[xt], writes=[st])
        P.op("vector", lambda e: e.bn_stats(out=st[:, 6:12], in_=src[:, 512:1024]), reads=[xt], writes=[st])
        P.op("vector", lambda e: e.bn_aggr(out=mv[:, 0:2], in_=st[:, 0:12]), reads=[st], writes=[mv])
        P.op("gpsimd", TS(rs[:, 0:1], mv[:, 1:2], EPS, None, ALU.add), reads=[mv], writes=[rs])
        P.op("gpsimd", TTo(rs[:, 0:1], rs[:, 0:1], C["mhalf"][:, 0:1], ALU.pow), reads=[rs, C["mhalf"]], writes=[rs])
        P.op("vector", TS(nb[:, 0:1], mv[:, 0:1], rs[:, 0:1], -1.0, ALU.mult, ALU.mult), reads=[mv, rs], writes=[nb])
        P.op("scalar", ACT(out_ap, src, AF.Identity, bias=nb[:, 0:1], scale=rs[:, 0:1]), reads=[xt, rs, nb], writes=[out_buf])

    def ln_small(self, tag, n=2):
        P = self.P
        return [(P.sb(tag + "st%d" % i, [128, 12], F32), P.sb(tag + "mv%d" % i, [128, 2], F32),
                 P.sb(tag + "rs%d" % i, [128, 1], F32), P.sb(tag + "nb%d" % i, [128, 1], F32)) for i in range(n)]

    def phaseA(self):
        P, I, S, C, l = self.P, self.I, self.S, self.C, self.l
        P.push()
        W = P.sb("Win", [128, 8, IN_COLS], BF16)
        self.wload(W, I["w_in"].t[l].rearrange("(k p) n -> p k n", p=128), 8, IN_COLS, eng="gpsimd", chunk=1664)
        cos = P.sb("cos", [128, 16, 64], F32)
        sin = P.sb("sin", [128, 16, 64], F32)
        P.dma("sync", cos[:], I["k_cos"].t, reads=[I["k_cos"]], writes=[cos])
        P.dma("sync", sin[:], I["k_sin"].t, reads=[I["k_sin"]], writes=[sin])
        gqk = P.sb("gqk", [128, 10, 128], F32)
        P.dma("sync", gqk[:, 0:2, :], I["k_norm_g"].t[l:l + 1, :].partition_broadcast(128).to_broadcast([128, 2, 128]) if False else
              I["k_norm_g"].t[l, :].partition_broadcast(128).unsqueeze(1).to_broadcast([128, 2, 128]), reads=[I["k_norm_g"]], writes=[gqk])
        P.dma("sync", gqk[:, 2:10, :], I["q_norm_g"].t[l, :].partition_broadcast(128).unsqueeze(1).to_broadcast([128, 8, 128]), reads=[I["q_norm_g"]], writes=[gqk])
        xt = [P.sb("xt%d" % i, [128, D], F32) for i in range(2)]
        xn = [P.sb("xn%d" % i, [128, D], BF16) for i in range(2)]
        sm = self.ln_small("a")
        hT = [P.sb("hT%d" % i, [128, 8, 512], BF16) for i in range(2)]
        xraw = [P.sb("xraw%d" % i, [128, 10, 128], F32) for i in range(2)]
        sq = P.sb("sq", [128, 10, 128], F32)
        ss = [P.sb("ss%d" % i, [128, 10], F32) for i in range(2)]
        rt = [P.sb("rt%d" % i, [128, 10, 64], F32) for i in range(4)]
        qn = [P.sb("qn%d" % i, [128, 10, 128], BF16) for i in range(2)]
        qTs = P.sb("qTs", [128, 8, 512], BF16)
        kTs = P.sb("kTs", [128, 2, 512], BF16)
        vst = [P.sb("vst%d" % i, [128, 256], BF16) for i in range(2)]
        ust = [P.sb("ust%d" % i, [128, 512], BF16) for i in range(2)]
        fst = [P.sb("fst%d" % i, [128, 512], BF16) for i in range(3)]
        cnt = {"x": 0, "t": 0, "f": 0}

        sts = []
        for b in range(NBC):
            for seg in ("ctx", "lat"):
                ntok = 256 if seg == "ctx" else 512
                for st_i in range(1 if seg == "ctx" else 4):
                    sts.append((b, seg, st_i, ntok))

        def ln_tile_emit(si, ti):
            b, seg, st_i, ntok = sts[si]
            r = 2 if seg == "ctx" else b
            rbuf, rap = self.res_in(b, seg)
            t0 = st_i * ntok
            h = hT[si % 2]
            i = cnt["x"]
            cnt["x"] += 1
            x_, n_ = xt[i % 2], xn[i % 2]
            P.dma("sync", x_[:], rap[t0 + ti * 128:t0 + (ti + 1) * 128, :], reads=[rbuf], writes=[x_])
            self.ln_tile(x_, n_[:], n_, sm[i % 2])
            ps = self.bank("atr", [0, 1])
            pb = ps[:].bitcast(BF16).rearrange("p (k t) -> p k t", k=8)
            for k in range(8):
                P.op("tensor", TR(pb[:, k, :], n_[:, k * 128:(k + 1) * 128], C["identb"][:]), reads=[n_, C["identb"]], writes=[ps], inc=(k == 7))
            for k in range(8):
                o = h[:, k, ti * 128:(ti + 1) * 128]
                rd = [ps, self.modT] if k in (0, 7) else []
                wr = [h] if k in (0, 7) else []
                if i % 2 == 0:
                    P.op("vector", TS(o, pb[:, k, :], self.modT[:, 1, k, r:r + 1], self.modT[:, 0, k, r:r + 1], ALU.mult, ALU.add), reads=rd, writes=wr)
                else:
                    P.op("scalar", ACT(o, pb[:, k, :], AF.Identity, bias=self.modT[:, 0, k, r:r + 1], scale=self.modT[:, 1, k, r:r + 1]), reads=rd, writes=wr)

        def tokmajor(si):
            b, seg, st_i, ntok = sts[si]
            full = (seg == "lat") or (not self.last)
            t0 = st_i * ntok
            g0 = t0 + (0 if seg == "ctx" else TC)
            h = hT[si % 2]
            rope = (seg == "lat")
            for ti in range(ntok // 128):
                ltile = (t0 // 128) + ti
                it = cnt["t"]
                cnt["t"] += 1
                xr, s_, q_ = xraw[it % 2], ss[it % 2], qn[it % 2]
                nh = 10 if full else 2
                for blk in range(4 if full else 2):
                    ps = self.bank("amm", [2, 3, 4, 5])
                    for k in range(8):
                        P.op("tensor", MM(ps[:, 0:512], h[:, k, ti * 128:(ti + 1) * 128], W[:, k, blk * 512:(blk + 1) * 512], k == 0, k == 7),
                             reads=[h, W], writes=[ps], inc=(k == 7))
                    if blk == 0:
                        P.op("scalar", ACT(xr[:, 0:2, :].rearrange("p h d -> p (h d)"), ps[:, 0:256], AF.Copy), reads=[ps], writes=[xr])
                        v_ = vst[it % 2]
                        P.op("scalar", ACT(v_[:], ps[:, 256:512], AF.Copy), reads=[ps], writes=[v_])
                        P.dma("scalar", S["V"].t[b, g0 + ti * 128:g0 + (ti + 1) * 128, :], v_[:], reads=[v_], writes=[S["V"]])
                    elif blk == 1:
                        u_ = ust[it % 2]
                        P.op("scalar", ACT(u_[:], ps[:, 0:512], AF.Copy), reads=[ps], writes=[u_])
                        P.dma("scalar", S["U"].t[b, g0 + ti * 128:g0 + (ti + 1) * 128, :], u_[:], reads=[u_], writes=[S["U"]])
                    else:
                        h0 = 2 + (blk - 2) * 4
                        dst = xr[:, h0:h0 + 4, :].rearrange("p h d -> p (h d)")
                        P.op("scalar", ACT(dst, ps[:, 0:512], AF.Copy), reads=[ps], writes=[xr])
                P.op("scalar", ACT(sq[:, 0:nh, :], xr[:, 0:nh, :], AF.Square), reads=[xr], writes=[sq])
                P.op("vector", lambda e, s_=s_, nh=nh: e.tensor_reduce(out=s_[:, 0:nh], in_=sq[:, 0:nh, :], axis=mybir.AxisListType.X, op=ALU.add),
                     reads=[sq], writes=[s_])
                P.op("gpsimd", TS(s_[:, 0:nh], s_[:, 0:nh], 1.0 / 128, EPS, ALU.mult, ALU.add), reads=[s_], writes=[s_])
                P.op("gpsimd", TTo(s_[:, 0:nh], s_[:, 0:nh], C["mhalf"][:, 0:nh], ALU.pow), reads=[s_, C["mhalf"]], writes=[s_])
                for hh in range(nh):
                    P.op("vector", STT(xr[:, hh, :], xr[:, hh, :], s_[:, hh:hh + 1], gqk[:, hh, :], ALU.mult, ALU.mult),
                         reads=([xr, s_, gqk] if hh in (0, nh - 1) else []), writes=([xr] if hh in (0, nh - 1) else []))
                if not rope:
                    P.op("gpsimd", CP(q_[:, 0:nh, :], xr[:, 0:nh, :]), reads=[xr], writes=[q_])
                else:
                    xe = xr[:, 0:nh, 0:128:2]
                    xo = xr[:, 0:nh, 1:128:2]
                    cb = cos[:, ltile:ltile + 1, :].to_broadcast([128, nh, 64])
                    sb_ = sin[:, ltile:ltile + 1, :].to_broadcast([128, nh, 64])
                    t1, t2, t3, t4 = [r_[:, 0:nh, :] for r_ in rt]
                    P.op("vector", TTo(t1, xe, cb, ALU.mult), reads=[xr, cos], writes=[rt[0]])
                    P.op("gpsimd", TTo(t3, xe, sb_, ALU.mult), reads=[xr, sin], writes=[rt[2]])
                    P.op("vector", TTo(t2, xo, sb_, ALU.mult), reads=[xr, sin], writes=[rt[1]])
                    P.op("gpsimd", TTo(t4, xo, cb, ALU.mult), reads=[xr, cos], writes=[rt[3]])
                    P.op("vector", TTo(q_[:, 0:nh, 0:128:2], t1, t2, ALU.subtract), reads=[rt[0], rt[1]], writes=[q_])
                    P.op("gpsimd", TTo(q_[:, 0:nh, 1:128:2], t3, t4, ALU.add), reads=[rt[2], rt[3]], writes=[q_])
                pt = self.bank("atq", [6, 7])
                ptb = pt[:].bitcast(BF16).rearrange("p (k t) -> p k t", k=8)
                for hh in range(2):
                    P.op("tensor", TR(ptb[:, hh, :], q_[:, hh, :], C["identb"][:]), reads=[q_, C["identb"]], writes=[pt], inc=(hh == 1))
                P.op("vector", CP(kTs[:, :, ti * 128:(ti + 1) * 128], ptb[:, 0:2, :]), reads=[pt], writes=[kTs])
                if full:
                    pt = self.bank("atq", [6, 7])
                    ptb = pt[:].bitcast(BF16).rearrange("p (k t) -> p k t", k=8)
                    for hh in range(8):
                        P.op("tensor", TR(ptb[:, hh, :], q_[:, 2 + hh, :], C["identb"][:]), reads=[q_, C["identb"]], writes=[pt], inc=(hh == 7))
                    P.op("scalar", ACT(qTs[:, :, ti * 128:(ti + 1) * 128], ptb[:, :, :], AF.Copy), reads=[pt], writes=[qTs])
            for kv in range(2):
                P.dma("sync", S["KT"].t[b, kv, :, g0:g0 + ntok], kTs[:, kv, 0:ntok], reads=[kTs], writes=[S["KT"]])
            if full:
                for hh in range(8):
                    P.dma("sync", S["QT"].t[b, hh, :, g0:g0 + ntok], qTs[:, hh, 0:ntok], reads=[qTs], writes=[S["QT"]])

        def featmajor(si, between):
            b, seg, st_i, ntok = sts[si]
            full = (seg == "lat") or (not self.last)
            g0 = st_i * ntok + (0 if seg == "ctx" else TC)
            h = hT[si % 2]
            if not full:
                for f in between:
                    f()
                return
            for ct in range(36):
                ps = self.bank("amm", [2, 3, 4, 5])
                c0 = 2048 + ct * 128
                for k in range(8):
                    P.op("tensor", MM(ps[:, 0:ntok], W[:, k, c0:c0 + 128], h[:, k, 0:ntok], k == 0, k == 7), reads=[h, W], writes=[ps], inc=(k == 7))
                f_ = fst[cnt["f"] % 3]
                cnt["f"] += 1
                if ct < 12:
                    dst = S[["AXT", "BGT", "CGT"][ct // 4]]
                    row0 = (ct % 4) * 128
                    P.op("scalar", ACT(f_[:, 0:ntok], ps[:, 0:ntok], AF.Copy), reads=[ps], writes=[f_])
                else:
                    dst = S["GT"]
                    row0 = (ct - 12) * 128
                    P.op("scalar", ACT(f_[:, 0:ntok], ps[:, 0:ntok], AF.Sigmoid), reads=[ps], writes=[f_])
                P.dma("scalar", dst.t[b, row0:row0 + 128, g0:g0 + ntok], f_[:, 0:ntok], reads=[f_], writes=[dst])
                if ct % 9 == 8 and between:
                    between.pop(0)()
            for f in between:
                f()

        for ti in range(sts[0][3] // 128):
            ln_tile_emit(0, ti)
        for si in range(len(sts)):
            tokmajor(si)
            between = []
            if si + 1 < len(sts):
                between = [(lambda si=si, ti=ti: ln_tile_emit(si + 1, ti)) for ti in range(sts[si + 1][3] // 128)]
            featmajor(si, between)
        P.pop()

    def phaseB(self, bg=None):
        P, I, S, C, l = self.P, self.I, self.S, self.C, self.l
        P.push()
        KTs = P.sb("KTs", [128, 2, TT], BF16)
        Vs = P.sb("Vs", [128, 18, 256], BF16)
        qT = [P.sb("qTb%d" % i, [128, 512], BF16) for i in range(2)]
        pT = [P.sb("pT%d" % i, [128, 512], BF16) for i in range(4)]
        rec = [P.sb("rec%d" % i, [128, 512], F32) for i in range(2)]
        oT = [P.sb("oT%d" % i, [128, 512], BF16) for i in range(2)]
        n = 0
        pbufs = self.prep_bufs() if self.prep_todo else None
        niter = NBC * 8 * (4 if self.last else 5)
        per_it = -(-len(self.prep_todo) // niter) if self.prep_todo else 0
        for b in range(NBC):
            for kv in range(2):
                P.dma("sync", KTs[:, kv, :], S["KT"].t[b, kv], reads=[S["KT"]], writes=[KTs])
            P.dma("sync", Vs[:], S["V"].t[b].rearrange("(t p) c -> p t c", p=128), reads=[S["V"]], writes=[Vs])
            blocks = [("lat", qb) for qb in range(4)] + ([] if self.last else [("ctx", 0)])
            for h in range(8):
                kv = h // 4
                for seg, qb in blocks:
                    if seg == "lat":
                        q0, nq, kts = TC + qb * 512, 512, list(range(18))
                    else:
                        q0, nq, kts = 0, 256, [0, 1]
                    q_ = qT[n % 2]
                    P.dma("sync", q_[:, 0:nq], S["QT"].t[b, h, :, q0:q0 + nq], reads=[S["QT"]], writes=[q_])
                    po = self.bank("bo", [4, 5])
                    pz = self.bank("bs", [6, 7])
                    def qk(j):
                        kt = kts[j]
                        ps = self.bank("bqk", [0, 1, 2])
                        P.op("tensor", MM(ps[:, 0:nq], KTs[:, kv, kt * 128:(kt + 1) * 128], q_[:, 0:nq], True, True), reads=[KTs, q_], writes=[ps])
                        P.op("scalar", ACT(pT[j % 4][:, 0:nq], ps[:, 0:nq], AF.Exp, scale=ATTN_SCALE), reads=[ps], writes=[pT[j % 4]])

                    LOOK = 2
                    for j in range(min(LOOK, len(kts))):
                        qk(j)
                    for j, kt in enumerate(kts):
                        if j + LOOK < len(kts):
                            qk(j + LOOK)
                        p_ = pT[j % 4]
                        last = (j == len(kts) - 1)
                        P.op("tensor", MM(po[:, 0:nq], Vs[:, kt, kv * 128:(kv + 1) * 128], p_[:, 0:nq], j == 0, last), reads=[Vs, p_], writes=[po], inc=last)
                        P.op("tensor", MM(pz[:, 0:nq], C["onesb"][:], p_[:, 0:nq], j == 0, last), reads=[C["onesb"], p_], writes=[pz], inc=last)
                    r_, o_ = rec[n % 2], oT[n % 2]
                    P.op("vector", lambda e, r_=r_, pz=pz, nq=nq: e.reciprocal(out=r_[:, 0:nq], in_=pz[:, 0:nq]), reads=[pz], writes=[r_])
                    P.op("vector", TTo(o_[:, 0:nq], po[:, 0:nq], r_[:, 0:nq], ALU.mult), reads=[po, r_], writes=[o_])
                    P.dma("gpsimd", S["AOT"].t[b, h * 128:(h + 1) * 128, q0:q0 + nq], o_[:, 0:nq], reads=[o_], writes=[S["AOT"]])
                    n += 1
                    if pbufs is not None:
                        self.prep_emit(pbufs, per_it, ["gpsimd", "vector"])
                    if bg is not None:
                        for _ in range(2 if (self.last and n % 2 == 0) else 1):
                            next(bg, None)
        if pbufs is not None:
            self.prep_emit(pbufs, len(self.prep_todo), ["gpsimd", "vector"])
        if bg is not None:
            for _ in bg:
                pass
        P.pop()


    def ssm_consts(self, Bst, CL, Toep, A8, B8):
        P, I, S, C, l = self.P, self.I, self.S, self.C, self.l
        V_ = "vector"

        def T(name, shape=(128, 32), dt=F32):
            return P.sb(name, list(shape), dt)

        lre, lim, ldt = T("lre"), T("lim"), T("ldt")
        for d in range(2):
            hs = slice(64 * d, 64 * d + 64)
            P.dma("sync", lre[hs, :], I["ssm_lam_re"].t[l, d].rearrange("g p -> p g"), reads=[I["ssm_lam_re"]], writes=[lre], allow_slow_non_contiguous=True)
            P.dma("sync", lim[hs, :], I["ssm_lam_im"].t[l, d].rearrange("g p -> p g"), reads=[I["ssm_lam_im"]], writes=[lim], allow_slow_non_contiguous=True)
            P.dma("sync", ldt[hs, :], I["ssm_log_dt"].t[l, d, :].partition_broadcast(64), reads=[I["ssm_log_dt"]], writes=[ldt])
        dt, tmp, mag, th, kf, kf2, r, m, sn, cs, r2 = [T(n) for n in ["dt", "tmp", "mag", "th", "kf", "kf2", "r", "m", "sn", "cs", "r2"]]
        P.op("scalar", ACT(dt[:], ldt[:], AF.Exp), reads=[ldt], writes=[dt])
        P.op(V_, TTo(tmp[:], lre[:], dt[:], ALU.mult), reads=[lre, dt], writes=[tmp])
        P.op("scalar", ACT(mag[:], tmp[:], AF.Exp), reads=[tmp], writes=[mag])
        P.op(V_, TTo(th[:], lim[:], dt[:], ALU.mult), reads=[lim, dt], writes=[th])
        P.op(V_, TS(r[:], th[:], 1.0 / 16, None, ALU.mult), reads=[th], writes=[r])
        P.op("scalar", ACT(sn[:], r[:], AF.Sin), reads=[r], writes=[sn])
        P.op(V_, TS(r2[:], r[:], PI / 2, None, ALU.add), reads=[r], writes=[r2])
        P.op("scalar", ACT(cs[:], r2[:], AF.Sin), reads=[r2], writes=[cs])
        for _ in range(4):
            P.op(V_, TTo(m[:], sn[:], cs[:], ALU.mult), reads=[sn, cs], writes=[m])
            P.op(V_, TTo(kf[:], sn[:], sn[:], ALU.mult), reads=[sn], writes=[kf])
            P.op(V_, TS(sn[:], m[:], 2.0, None, ALU.mult), reads=[m], writes=[sn])
            P.op(V_, TS(cs[:], kf[:], -2.0, 1.0, ALU.mult, ALU.add), reads=[kf], writes=[cs])
        lbr, lbi = T("lbr"), T("lbi")
        P.op(V_, TTo(lbr[:], mag[:], cs[:], ALU.mult), reads=[mag, cs], writes=[lbr])
        P.op(V_, TTo(lbi[:], mag[:], sn[:], ALU.mult), reads=[mag, sn], writes=[lbi])
        nr, den, t1, t2, fr, fi = [T(n) for n in ["nr", "den", "t1", "t2", "fr", "fi"]]
        P.op(V_, TS(nr[:], lbr[:], -1.0, None, ALU.add), reads=[lbr], writes=[nr])
        P.op(V_, TTo(t1[:], lre[:], lre[:], ALU.mult), reads=[lre], writes=[t1])
        P.op(V_, TTo(t2[:], lim[:], lim[:], ALU.mult), reads=[lim], writes=[t2])
        P.op(V_, TTo(den[:], t1[:], t2[:], ALU.add), reads=[t1, t2], writes=[den])
        P.op(V_, lambda e: e.reciprocal(out=den[:], in_=den[:]), reads=[den], writes=[den])
        P.op(V_, TTo(t1[:], nr[:], lre[:], ALU.mult), reads=[nr, lre], writes=[t1])
        P.op(V_, TTo(t2[:], lbi[:], lim[:], ALU.mult), reads=[lbi, lim], writes=[t2])
        P.op(V_, TTo(fr[:], t1[:], t2[:], ALU.add), reads=[t1, t2], writes=[fr])
        P.op(V_, TTo(fr[:], fr[:], den[:], ALU.mult), reads=[fr, den], writes=[fr])
        P.op(V_, TTo(t1[:], lbi[:], lre[:], ALU.mult), reads=[lbi, lre], writes=[t1])
        P.op(V_, TTo(t2[:], nr[:], lim[:], ALU.mult), reads=[nr, lim], writes=[t2])
        P.op(V_, TTo(fi[:], t1[:], t2[:], ALU.subtract), reads=[t1, t2], writes=[fi])
        P.op(V_, TTo(fi[:], fi[:], den[:], ALU.mult), reads=[fi, den], writes=[fi])
        ir, ii = T("ir"), T("ii")
        P.op(V_, TTo(t1[:], lbr[:], lbr[:], ALU.mult), reads=[lbr], writes=[t1])
        P.op(V_, TTo(t2[:], lbi[:], lbi[:], ALU.mult), reads=[lbi], writes=[t2])
        P.op(V_, TTo(den[:], t1[:], t2[:], ALU.add), reads=[t1, t2], writes=[den])
        P.op(V_, lambda e: e.reciprocal(out=den[:], in_=den[:]), reads=[den], writes=[den])
        P.op(V_, TTo(ir[:], lbr[:], den[:], ALU.mult), reads=[lbr, den], writes=[ir])
        P.op(V_, STT(ii[:], lbi[:], -1.0, den[:], ALU.mult, ALU.mult), reads=[lbi, den], writes=[ii])
        yield
        LPr, LPi = T("LPr", (128, 9, 32)), T("LPi", (128, 9, 32))
        LNr, LNi = T("LNr", (128, 8, 32)), T("LNi", (128, 8, 32))
        for (Xr, Xi, br, bi, n) in [(LPr, LPi, lbr, lbi, 9), (LNr, LNi, ir, ii, 8)]:
            P.op(V_, lambda e, Xr=Xr: e.memset(Xr[:, 0, :], 1.0), writes=[Xr])
            P.op(V_, lambda e, Xi=Xi: e.memset(Xi[:, 0, :], 0.0), writes=[Xi])
            for k in range(1, n):
                P.op(V_, TTo(t1[:], Xr[:, k - 1, :], br[:], ALU.mult), reads=[Xr, br], writes=[t1])
                P.op(V_, TTo(t2[:], Xi[:, k - 1, :], bi[:], ALU.mult), reads=[Xi, bi], writes=[t2])
                P.op(V_, TTo(Xr[:, k, :], t1[:], t2[:], ALU.subtract), reads=[t1, t2], writes=[Xr])
                P.op(V_, TTo(t1[:], Xr[:, k - 1, :], bi[:], ALU.mult), reads=[Xr, bi], writes=[t1])
                P.op(V_, TTo(t2[:], Xi[:, k - 1, :], br[:], ALU.mult), reads=[Xi, br], writes=[t2])
                P.op(V_, TTo(Xi[:, k, :], t1[:], t2[:], ALU.add), reads=[t1, t2], writes=[Xi])
        P.op(V_, CP(A8[:], LPr[:, 8, :]), reads=[LPr], writes=[A8])
        P.op(V_, CP(B8[:], LPi[:, 8, :]), reads=[LPi], writes=[B8])
        yield
        G3 = (128, 32, 16)
        Bre, Bim, Bbr, Bbi, CTr, CTi = [T(n, G3) for n in ["Bre", "Bim", "Bbr", "Bbi", "CTr", "CTi"]]
        for d in range(2):
            hs = slice(64 * d, 64 * d + 64)
            P.dma("sync", Bre[hs], I["ssm_b_re"].t[l, d].rearrange("g p h -> p g h"), reads=[I["ssm_b_re"]], writes=[Bre], allow_slow_non_contiguous=True)
            P.dma("sync", Bim[hs], I["ssm_b_im"].t[l, d].rearrange("g p h -> p g h"), reads=[I["ssm_b_im"]], writes=[Bim], allow_slow_non_contiguous=True)
        craw = [T("craw%d" % i, (128, 2, 64)) for i in range(2)]
        n = 0
        for (src, dst) in [("ssm_c_re", CTr), ("ssm_c_im", CTi)]:
            for gb in range(4):
                cr_ = craw[n % 2]
                n += 1
                for d in range(2):
                    P.dma("sync", cr_[:, d, :], I[src].t[l, d, gb * 8:(gb + 1) * 8].rearrange("g h p -> (g h) p"), reads=[I[src]], writes=[cr_])
                ps = self.bank("c0", [3])
                P.op("tensor", TR(ps[:, 0:128], cr_[:].rearrange("q d p -> q (d p)"), C["identf"][:]), reads=[cr_, C["identf"]], writes=[ps])
                P.op(V_, CP(dst[:, gb * 8:(gb + 1) * 8, :].rearrange("q g h -> q (g h)"), ps[:, 0:128]), reads=[ps], writes=[dst])

        yield

        def bc(x_ap):
            return x_ap.unsqueeze(2).to_broadcast([128, 32, 16])

        u1, u2, u3, u4 = [T(n, G3) for n in ["u1", "u2", "u3", "u4"]]
        Rr = [T("Rr%d" % i, G3) for i in range(2)]
        Ri = [T("Ri%d" % i, G3) for i in range(2)]
        fcnt = [0]

        def cprod(lr, li, Xr, Xi, lbufs, neg_im=False):
            i = fcnt[0]
            fcnt[0] += 1
            rr, ri = Rr[i % 2], Ri[i % 2]
            P.op(V_, TTo(u1[:], Xr[:], bc(lr), ALU.mult), reads=[Xr] + lbufs, writes=[u1])
            P.op("gpsimd", TTo(u2[:], Xi[:], bc(li), ALU.mult), reads=[Xi] + lbufs, writes=[u2])
            P.op(V_, TTo(u3[:], Xi[:], bc(lr), ALU.mult), reads=[Xi] + lbufs, writes=[u3])
            P.op("gpsimd", TTo(u4[:], Xr[:], bc(li), ALU.mult), reads=[Xr] + lbufs, writes=[u4])
            P.op(V_, TTo(rr[:], u1[:], u2[:], ALU.subtract), reads=[u1, u2], writes=[rr])
            if neg_im:
                P.op("gpsimd", TS(u3[:], u3[:], -1.0, None, ALU.mult), reads=[u3], writes=[u3])
                P.op("gpsimd", TTo(ri[:], u3[:], u4[:], ALU.subtract), reads=[u3, u4], writes=[ri])
            else:
                P.op("gpsimd", TTo(ri[:], u3[:], u4[:], ALU.add), reads=[u3, u4], writes=[ri])
            return rr, ri

        def STT_pool(out, a, b_):
            def f(e):
                e.tensor_scalar(out=a, in0=a, scalar1=-1.0, scalar2=None, op0=ALU.mult)
                return e.tensor_tensor(out=out, in0=a, in1=b_, op=ALU.subtract)
            return f

        rr, ri = cprod(fr[:], fi[:], Bre, Bim, [fr, fi])
        P.op(V_, CP(Bbr[:], rr[:]), reads=[rr], writes=[Bbr])
        P.op(V_, CP(Bbi[:], ri[:]), reads=[ri], writes=[Bbi])

        G4 = (128, 32, 8, 16)
        BSr, BSi, Xr_, Xi_, PCr, PCi = [T(n, G4, BF16) for n in ["BSr", "BSi", "Xr_", "Xi_", "PCr", "PCi"]]
        F_, R_ = slice(0, 64), slice(64, 128)
        for Pc in (PCr, PCi):
            P.op("gpsimd", lambda e, Pc=Pc: e.memset(Pc[R_], 0.0), writes=[Pc])
        CLv = CL[:].rearrange("q g r (i h) -> q g r i h", i=8)
        cpe = ["gpsimd", "vector"]
        cc = [0]

        def cpy(dst_ap, dst_buf, src_ap, src_buf):
            e = cpe[cc[0] % 2]
            cc[0] += 1
            if e == "scalar":
                P.op(e, ACT(dst_ap, src_ap, AF.Copy), reads=[src_buf], writes=[dst_buf])
            else:
                P.op(e, CP(dst_ap, src_ap), reads=[src_buf], writes=[dst_buf])

        for k in range(8):
            rr, ri = cprod(LPr[:, k, :], LPi[:, k, :], Bbr, Bbi, [LPr, LPi])
            cpy(BSr[F_, :, 7 - k, :], BSr, rr[F_], rr)
            cpy(BSi[F_, :, 7 - k, :], BSi, ri[F_], ri)
            cpy(BSr[R_, :, k, :], BSr, rr[R_], rr)
            cpy(BSi[R_, :, k, :], BSi, ri[R_], ri)
            yield
        for k in range(8):
            rr, ri = cprod(LNr[:, k, :], LNi[:, k, :], Bbr, Bbi, [LNr, LNi])
            cpy(Xr_[F_, :, k, :], Xr_, rr[F_], rr)
            cpy(Xi_[F_, :, k, :], Xi_, ri[F_], ri)
            yield
            rr, ri = cprod(LNr[:, k, :], LNi[:, k, :], CTr, CTi, [LNr, LNi], neg_im=True)
            cpy(Xr_[R_, :, k, :], Xr_, rr[R_], rr)
            cpy(Xi_[R_, :, k, :], Xi_, ri[R_], ri)
            yield
        for k in range(9):
            rr, ri = cprod(LPr[:, k, :], LPi[:, k, :], CTr, CTi, [LPr, LPi], neg_im=True)
            if k <= 7:
                cpy(PCr[F_, :, k, :], PCr, rr[F_], rr)
                cpy(PCi[F_, :, k, :], PCi, ri[F_], ri)
            if k >= 1:
                cpy(CLv[F_, :, 0, k - 1, :], CL, rr[F_], rr)
                cpy(CLv[F_, :, 1, k - 1, :], CL, ri[F_], ri)
                cpy(CLv[R_, :, 0, 8 - k, :], CL, rr[R_], rr)
                cpy(CLv[R_, :, 1, 8 - k, :], CL, ri[R_], ri)
            yield
        yield
        for g4 in range(8):
            ps = self.bank("c0", [3])
            pb = ps[:].bitcast(BF16).rearrange("p (k t) -> p k t", k=8)
            for gg in range(4):
                g = g4 * 4 + gg
                for ri_, Bs in enumerate((BSr, BSi)):
                    P.op("tensor", TR(pb[:, gg * 2 + ri_, :], Bs[:, g].rearrange("q j h -> q (j h)"), C["identb"][:]), reads=[Bs, C["identb"]], writes=[ps],
                         inc=(gg == 3 and ri_ == 1))
            P.op(V_, CP(Bst[:, g4 * 4:g4 * 4 + 4].rearrange("q g r c -> q (g r) c"), pb[:, :, :]), reads=[ps], writes=[Bst])
            yield
        yield
        for Bs in (BSr, BSi):
            P.op("gpsimd", lambda e, Bs=Bs: e.memset(Bs[F_], 0.0), reads=[Bs], writes=[Bs])
        maskf, maskr, Dcol = T("maskf", (128, 128)), T("maskr", (128, 128)), T("Dcol", (128, 32))
        P.dma("sync", maskf[:], I["k_maskf"].t, reads=[I["k_maskf"]], writes=[maskf])
        P.dma("sync", maskr[:], I["k_maskr"].t, reads=[I["k_maskr"]], writes=[maskr])
        for j in range(8):
            P.dma("sync", Dcol[j * 16:(j + 1) * 16, :], I["ssm_d"].t[l, :].rearrange("(g h) -> h g", h=16), reads=[I["ssm_d"]], writes=[Dcol],
                  allow_slow_non_contiguous=True)
        tf = [T("tf%d" % i, (128, 128)) for i in range(2)]
        tr = [T("tr%d" % i, (128, 128)) for i in range(2)]
        for g in range(32):
            pf = self.bank("c0", [3])
            pr = self.bank("c1", [3])
            fl = lambda X: X[:, g].rearrange("q j h -> q (j h)")
            P.op("tensor", MM(pf[:, 0:128], fl(Xr_), fl(PCr), True, False), reads=[Xr_, PCr], writes=[pf], inc=False)
            P.op("tensor", MM(pf[:, 0:128], fl(Xi_), fl(PCi), False, True), reads=[Xi_, PCi], writes=[pf])
            P.op("tensor", MM(pr[:, 128:256], fl(BSr), fl(Xr_), True, False), reads=[BSr, Xr_], writes=[pr], inc=False)
            P.op("tensor", MM(pr[:, 128:256], fl(BSi), fl(Xi_), False, True), reads=[BSi, Xi_], writes=[pr])
            a_, b_ = tf[g % 2], tr[g % 2]
            P.op(V_, TTo(a_[:], pf[:, 0:128], maskf[:], ALU.mult), reads=[pf, maskf], writes=[a_])
            P.op(V_, TTo(b_[:], pr[:, 128:256], maskr[:], ALU.mult), reads=[pr, maskr], writes=[b_])
            P.op("gpsimd", TTo(a_[:], a_[:], b_[:], ALU.add), reads=[a_, b_], writes=[a_])
            P.op(V_, STT(Toep[:, g, :], C["identf"][:], Dcol[:, g:g + 1], a_[:], ALU.mult, ALU.add), reads=[C["identf"], Dcol, a_], writes=[Toep])
            yield

    def ssm_batch(self, b, Bst, CL, Toep, A8, B8):
        P, I, S, C, l = self.P, self.I, self.S, self.C, self.l
        P.push()
        Ust = P.sb("Ust", [128, 32, NCH], BF16)
        ZSr = P.sb("ZSr", [128, 32, NCH], BF16)
        ZSi = P.sb("ZSi", [128, 32, NCH], BF16)
        Sr = P.sb("Sr", [128, 32, 290], BF16)
        Si = P.sb("Si", [128, 32, 290], BF16)
        F_, R_ = slice(0, 64), slice(64, 128)
        P.push()
        utm = [P.sb("utm%d" % i, [96, 4096], BF16) for i in range(2)]
        utm2 = [P.sb("utm2%d" % i, [96, 4096], BF16) for i in range(2)]
        for tile in range(3):
            u_ = utm[tile % 2]
            P.dma("sync", u_[0:96, :], S["U"].t[b, tile * 768:(tile + 1) * 768, :].rearrange("(c j) ch -> c (j ch)", j=8), reads=[S["U"]], writes=[u_])
            u2 = utm2[tile % 2]
            P.op("gpsimd", CP(u2[0:96, :].rearrange("c (g j h) -> c g j h", g=32, j=8), u_[0:96, :].rearrange("c (j g h) -> c g j h", j=8, g=32)),
                 reads=[u_], writes=[u2])
            uv = u2[0:96, :].rearrange("c (g x) -> c g x", g=32)
            for g8 in range(4):
                ps = self.bank("cs", [4, 5])
                pb = ps[:].bitcast(BF16).rearrange("p (k t) -> p k t", k=8)
                for gg in range(8):
                    P.op("tensor", TR(pb[:, gg, 0:96], uv[:, g8 * 8 + gg, :], C["identb"][0:96, 0:96]), reads=[u2, C["identb"]], writes=[ps], inc=(gg == 7))
                P.op("vector" if g8 % 2 else "scalar",
                     (CP if g8 % 2 else (lambda o, i_: ACT(o, i_, AF.Copy)))(Ust[:, g8 * 8:(g8 + 1) * 8, tile * 96:(tile + 1) * 96], pb[:, :, 0:96]),
                     reads=[ps], writes=[Ust])
        P.pop()
        self.check_stop("c1a")
        for g in range(32):
            for ri_, Z in enumerate((ZSr, ZSi)):
                ps = self.bank("cz", [0, 1, 2, 3])
                P.op("tensor", MM(ps[:, 0:NCH], Bst[:, g, ri_, :], Ust[:, g, :], True, True), reads=[Bst, Ust], writes=[ps])
                if ri_ == 0:
                    P.op("vector", CP(Z[:, g, :], ps[:, 0:NCH]), reads=[ps], writes=[Z])
                else:
                    P.op("scalar", ACT(Z[:, g, :], ps[:, 0:NCH], AF.Copy), reads=[ps], writes=[Z])
        self.check_stop("c1b")
        NR = 4
        Rg = [P.sb("Rg%d" % i, [128, 32], F32) for i in range(NR)]
        Ig = [P.sb("Ig%d" % i, [128, 32], F32) for i in range(NR)]
        tt = [[P.sb("sc%d_%d" % (i, j), [128, 32], F32) for j in range(6)] for i in range(2)]
        P.op("vector", lambda e: e.memset(Rg[0][:], 0.0), writes=[Rg[0]])
        P.op("vector", lambda e: e.memset(Ig[0][:], 0.0), writes=[Ig[0]])
        for Sx in (Sr, Si):
            P.op("gpsimd", lambda e, Sx=Sx: e.memset(Sx[F_, :, 1:2], 0.0), writes=[Sx])
            P.op("gpsimd", lambda e, Sx=Sx: e.memset(Sx[R_, :, 32:33], 0.0), writes=[Sx])
        order_r = list(range(31, -1, -1)) + list(range(287, 31, -1))
        V_ = "vector"
        for n in range(NCH):
            cf, cr = n, order_r[n]
            Rp, Ip, Rn, In = Rg[n % NR], Ig[n % NR], Rg[(n + 1) % NR], Ig[(n + 1) % NR]
            t1, t2, t3, t4, t5, t6 = tt[n % 2]
            P.op(V_, TTo(t1[:], A8[:], Rp[:], ALU.mult), reads=[A8, Rp], writes=[t1])
            P.op(V_, TTo(t2[:], B8[:], Ip[:], ALU.mult), reads=[B8, Ip], writes=[t2])
            P.op(V_, TTo(t4[:], B8[:], Rp[:], ALU.mult), reads=[B8, Rp], writes=[t4])
            P.op(V_, TTo(t5[:], A8[:], Ip[:], ALU.mult), reads=[A8, Ip], writes=[t5])
            P.op(V_, TTo(t3[:], t1[:], t2[:], ALU.subtract), reads=[t1, t2], writes=[t3])
            P.op(V_, TTo(t6[:], t4[:], t5[:], ALU.add), reads=[t4, t5], writes=[t6])
            P.op(V_, TTo(Rn[F_], t3[F_], ZSr[F_, :, cf], ALU.add), reads=[t3, ZSr], writes=[Rn])
            P.op(V_, TTo(Rn[R_], t3[R_], ZSr[R_, :, cr], ALU.add), reads=[t3, ZSr], writes=[Rn])
            P.op(V_, TTo(In[F_], t6[F_], ZSi[F_, :, cf], ALU.add), reads=[t6, ZSi], writes=[In])
            P.op(V_, TTo(In[R_], t6[R_], ZSi[R_, :, cr], ALU.add), reads=[t6, ZSi], writes=[In])
            sf = cf + 2
            for Sx, Xn in ((Sr, Rn), (Si, In)):
                P.op("gpsimd", CP(Sx[F_, :, sf], Xn[F_]), reads=[Xn], writes=[Sx])
                if cr != 32:
                    P.op("gpsimd", CP(Sx[R_, :, cr], Xn[R_]), reads=[Xn], writes=[Sx])
                if cr == 0:
                    P.op("gpsimd", CP(Sx[R_, :, 288], Xn[R_]), reads=[Xn], writes=[Sx])
        self.check_stop("c1c")
        P.push()
        Wg = P.sb("Wg", [128, 4, 2 * D], BF16)
        self.wload(Wg, I["w_glu"].t[l].rearrange("(t p) n -> p t n", p=128), 4, 2 * D, eng="gpsimd")
        ygl = P.sb("ygl", [128, 8, 512], BF16)
        yT = P.sb("yT", [128, 4, 1024], BF16)
        yTv = yT[:].rearrange("p t (c i) -> p t c i", i=8)
        sig = [P.sb("sig%d" % i, [128, 512], F32) for i in range(2)]
        gst = [P.sb("gst%d" % i, [128, 1024], BF16) for i in range(2)]
        ydb = P.sb("ydb", [128, 8, 512], F32) if "YS" in self.dbg else None
        blocks = ([] if self.last else [(0, 32, 1)]) + [(32, 128, 2), (160, 128, 2)]
        nsg = 0
        for (c0, M, roff) in blocks:
            for g4 in range(8):
                ps = self.bank("cr", [0, 1, 2, 3])
                for gg in range(4):
                    g = g4 * 4 + gg
                    o = ps[0:M, gg * 128:(gg + 1) * 128]
                    P.op("tensor", MM(o, Ust[:, g, c0:c0 + M], Toep[:, g, :], True, False), reads=[Ust, Toep], writes=[ps], inc=False)
                    P.op("tensor", MM(o, Sr[:, g, c0 + 1:c0 + 1 + M], CL[:, g, 0, :], False, False), reads=[Sr, CL], writes=[ps], inc=False)
                    P.op("tensor", MM(o, Si[:, g, c0 + 1:c0 + 1 + M], CL[:, g, 1, :], False, True), reads=[Si, CL], writes=[ps], inc=(gg == 3))
                pv = ps[0:M, :].rearrange("c (g i h) -> c i g h", g=4, i=8)
                yo = ygl[0:M, :, g4 * 64:(g4 + 1) * 64].rearrange("c i (g h) -> c i g h", g=4)
                P.op("scalar", ACT(yo, pv, AF.Gelu_apprx_tanh), reads=[ps], writes=[ygl])
                if ydb is not None:
                    P.op("vector", CP(ydb[0:M, :, g4 * 64:(g4 + 1) * 64].rearrange("c i (g h) -> c i g h", g=4), pv), reads=[ps], writes=[ydb])
            self.check_stop("c2a")
            if ydb is not None:
                P.dma("gpsimd", S["YS"].t[b, c0 * 8:(c0 + M) * 8, :].rearrange("(c i) ch -> c i ch", i=8), ydb[0:M], reads=[ydb], writes=[S["YS"]])
            self.check_stop("c2b")
            for i in range(8):
                ps = self.bank("cs", [4, 5])
                pb = ps[:].bitcast(BF16).rearrange("p (k t) -> p k t", k=8)
                for t in range(4):
                    P.op("tensor", TR(pb[:, t, 0:M], ygl[0:M, i, t * 128:(t + 1) * 128], C["identb"][0:M, 0:M]), reads=[ygl, C["identb"]], writes=[ps], inc=(t == 3))
                P.op("vector", CP(yTv[:, :, 0:M, i], pb[:, 0:4, 0:M]), reads=[ps], writes=[yT])
            self.check_stop("c2c")
            ntok = M * 8
            for ct in range(8):
                g_ = gst[ct % 2]
                for n0 in range(0, ntok, 512):
                    nn = min(512, ntok - n0)
                    pa = self.bank("cg", [6, 7, 0, 1, 2, 3])
                    pg = self.bank("cg", [6, 7, 0, 1, 2, 3])
                    for t in range(4):
                        P.op("tensor", MM(pa[:, 0:nn], Wg[:, t, ct * 128:(ct + 1) * 128], yT[:, t, n0:n0 + nn], t == 0, t == 3), reads=[Wg, yT], writes=[pa], inc=(t == 3))
                    for t in range(4):
                        P.op("tensor", MM(pg[:, 0:nn], Wg[:, t, D + ct * 128:D + (ct + 1) * 128], yT[:, t, n0:n0 + nn], t == 0, t == 3), reads=[Wg, yT], writes=[pg], inc=(t == 3))
                    sg = sig[nsg % 2]
                    nsg += 1
                    P.op("scalar", ACT(sg[:, 0:nn], pg[:, 0:nn], AF.Sigmoid), reads=[pg], writes=[sg])
                    P.op("vector", TTo(g_[:, n0:n0 + nn], pa[:, 0:nn], sg[:, 0:nn], ALU.mult), reads=[pa, sg], writes=[g_])
                P.dma("gpsimd", S["SSMT"].t[b, ct * 128:(ct + 1) * 128, c0 * 8:c0 * 8 + ntok], g_[:, 0:ntok], reads=[g_], writes=[S["SSMT"]])
            self.check_stop("c2d")
            if c0 == 32:
                self.check_stop("c2e")
        P.pop()
        P.pop()


    def postnorm(self, pss, xt, gbc, lng, lnb, upd, yv, yn, sm, dst_buf, dst_ap):
        P = self.P
        for half in range(2):
            P.op("vector", TTo(upd[:, half * 512:(half + 1) * 512], pss[half][:, 0:512], gbc[:, half * 512:(half + 1) * 512], ALU.mult),
                 reads=[pss[half], gbc], writes=[upd])
        P.op("vector", STT(yv[:], xt[:], ALPHA, upd[:], ALU.mult, ALU.add), reads=[xt, upd], writes=[yv])
        self.ln_tile(yv, yn[:], yn, sm)
        P.op("gpsimd", TTo(yn[:], yn[:], lng[:], ALU.mult), reads=[yn, lng], writes=[yn])
        P.op("gpsimd", TTo(yn[:], yn[:], lnb[:], ALU.add), reads=[yn, lnb], writes=[yn])
        P.dma("gpsimd", dst_ap, yn[:], reads=[yn], writes=[dst_buf])

    def segs(self):
        out = []
        for b in range(NBC):
            if not self.last:
                out.append((b, "ctx", 2, 0, TC))
            out.append((b, "lat", b, TC, TL))
        return out

    def phaseD(self):
        P, I, S, C, l = self.P, self.I, self.S, self.C, self.l
        P.push()
        Wco = P.sb("Wco", [128, 4, D], BF16)
        Wao = P.sb("Wao", [128, 8, D], BF16)
        Wo = P.sb("Wo", [128, 8, D], BF16)
        for Wt, nm, nk in ((Wco, "w_conv_out", 4), (Wao, "w_attn_out", 8), (Wo, "w_o", 8)):
            self.wload(Wt, I[nm].t[l].rearrange("(t p) n -> p t n", p=128), nk, D, eng="gpsimd")
        cw = P.sb("cw", [128, 4, 3], F32)
        for k in range(3):
            P.dma("sync", cw[:, :, k], I["conv_w"].t[l, k, :].rearrange("(t p) -> p t", p=128), reads=[I["conv_w"]], writes=[cw], allow_slow_non_contiguous=True)
        lng = P.sb("lng", [128, D], F32)
        lnb = P.sb("lnb", [128, D], F32)
        P.dma("sync", lng[:], I["ln1_g"].t[l, :].partition_broadcast(128), reads=[I["ln1_g"]], writes=[lng])
        P.dma("sync", lnb[:], I["ln1_b"].t[l, :].partition_broadcast(128), reads=[I["ln1_b"]], writes=[lnb])
        gbc = P.sb("gbc", [128, D], F32)
        axh = P.sb("axh", [128, 4, 514], BF16)
        cgh = P.sb("cgh", [128, 4, 514], BF16)
        bgs = P.sb("bgs", [128, 4, 512], BF16)
        prod = P.sb("prod", [128, 4, 514], F32)
        acc = P.sb("acc", [128, 4, 512], F32)
        convT = P.sb("convT", [128, 4, 512], BF16)
        aoT = P.sb("aoT", [128, 8, 512], BF16)
        gts = [P.sb("gts%d" % i, [128, 3, 512], BF16) for i in range(2)]
        ssmT = [P.sb("ssmT%d" % i, [128, 512], BF16) for i in range(2)]
        m1 = [P.sb("m1_%d" % i, [128, 512], F32) for i in range(2)]
        m2 = [P.sb("m2_%d" % i, [128, 512], F32) for i in range(2)]
        m3 = [P.sb("m3_%d" % i, [128, 512], F32) for i in range(2)]
        mgT = P.sb("mgT", [128, 8, 512], BF16)
        xt = [P.sb("dxt%d" % i, [128, D], F32) for i in range(2)]
        upd = [P.sb("dupd%d" % i, [128, D], F32) for i in range(2)]
        yv = [P.sb("dyv%d" % i, [128, D], F32) for i in range(2)]
        yn = [P.sb("dyn%d" % i, [128, D], F32) for i in range(2)]
        sm = self.ln_small("d")
        nx = 0
        nd = 0
        for (b, seg, r, gofs, slen) in self.segs():
            P.dma("sync", gbc[:], S["modr%d" % l].t[r, 2 * D:3 * D].partition_broadcast(128), reads=[S["modr%d" % l]], writes=[gbc])
            rbuf, rap = self.res_in(b, seg)
            n = min(512, slen)
            for t0 in range(0, slen, n):
                g0 = gofs + t0
                lo = 1 if t0 == 0 else 0
                hi = n + 1 if t0 + n == slen else n + 2
                for (hb, nm) in ((axh, "AXT"), (cgh, "CGT")):
                    if lo == 1:
                        P.op("gpsimd", lambda e, hb=hb: e.memset(hb[:, :, 0:1], 0.0), writes=[hb])
                    if hi == n + 1:
                        P.op("gpsimd", lambda e, hb=hb, n=n: e.memset(hb[:, :, n + 1:n + 2], 0.0), writes=[hb])
                    P.dma("sync", hb[:, :, lo:hi], S[nm].t[b].rearrange("(t p) c -> p t c", p=128)[:, :, g0 - 1 + lo:g0 - 1 + hi], reads=[S[nm]], writes=[hb])
                P.dma("sync", bgs[:, :, 0:n], S["BGT"].t[b].rearrange("(t p) c -> p t c", p=128)[:, :, g0:g0 + n], reads=[S["BGT"]], writes=[bgs])
                P.dma("sync", aoT[:, :, 0:n], S["AOT"].t[b].rearrange("(t p) c -> p t c", p=128)[:, :, g0:g0 + n], reads=[S["AOT"]], writes=[aoT])
                P.op("gpsimd", TTo(prod[:, :, 0:n + 2], cgh[:, :, 0:n + 2], axh[:, :, 0:n + 2], ALU.mult), reads=[cgh, axh], writes=[prod])
                for t in range(4):
                    P.op("vector", TS(acc[:, t, 0:n], prod[:, t, 0:n], cw[:, t, 0:1], None, ALU.mult), reads=[prod, cw], writes=[acc])
                    P.op("vector", STT(acc[:, t, 0:n], prod[:, t, 1:n + 1], cw[:, t, 1:2], acc[:, t, 0:n], ALU.mult, ALU.add), reads=[prod, cw, acc], writes=[acc])
                    P.op("vector", STT(acc[:, t, 0:n], prod[:, t, 2:n + 2], cw[:, t, 2:3], acc[:, t, 0:n], ALU.mult, ALU.add), reads=[prod, cw, acc], writes=[acc])
                P.op("gpsimd", TTo(convT[:, :, 0:n], acc[:, :, 0:n], bgs[:, :, 0:n], ALU.mult), reads=[acc, bgs], writes=[convT])
                for dt in range(8):
                    gt_, sm_ = gts[nd % 2], ssmT[nd % 2]
                    a1, a2, a3 = m1[nd % 2], m2[nd % 2], m3[nd % 2]
                    nd += 1
                    for s3 in range(3):
                        P.dma("sync", gt_[:, s3, 0:n], S["GT"].t[b, s3 * D + dt * 128:s3 * D + (dt + 1) * 128, g0:g0 + n], reads=[S["GT"]], writes=[gt_])
                    P.dma("sync", sm_[:, 0:n], S["SSMT"].t[b, dt * 128:(dt + 1) * 128, g0:g0 + n], reads=[S["SSMT"]], writes=[sm_])
                    pc = self.bank("dc", [0, 1])
                    pa = self.bank("da", [2, 3])
                    for t in range(4):
                        P.op("tensor", MM(pc[:, 0:n], Wco[:, t, dt * 128:(dt + 1) * 128], convT[:, t, 0:n], t == 0, t == 3), reads=[Wco, convT], writes=[pc], inc=(t == 3))
                    for k in range(8):
                        P.op("tensor", MM(pa[:, 0:n], Wao[:, k, dt * 128:(dt + 1) * 128], aoT[:, k, 0:n], k == 0, k == 7), reads=[Wao, aoT], writes=[pa], inc=(k == 7))
                    P.op("vector", TTo(a1[:, 0:n], pc[:, 0:n], gt_[:, 0, 0:n], ALU.mult), reads=[pc, gt_], writes=[a1])
                    P.op("vector", TTo(a2[:, 0:n], pa[:, 0:n], gt_[:, 2, 0:n], ALU.mult), reads=[pa, gt_], writes=[a2])
                    P.op("gpsimd", TTo(a3[:, 0:n], sm_[:, 0:n], gt_[:, 1, 0:n], ALU.mult), reads=[sm_, gt_], writes=[a3])
                    P.op("gpsimd", TTo(a1[:, 0:n], a1[:, 0:n], a2[:, 0:n], ALU.add), reads=[a1, a2], writes=[a1])
                    P.op("gpsimd", TTo(mgT[:, dt, 0:n], a1[:, 0:n], a3[:, 0:n], ALU.add), reads=[a1, a3], writes=[mgT])
                for ti in range(n // 128):
                    i = nx
                    nx += 1
                    x_ = xt[i % 2]
                    P.dma("sync", x_[:], rap[t0 + ti * 128:t0 + (ti + 1) * 128, :], reads=[rbuf], writes=[x_])
                    pss = []
                    for half in range(2):
                        ps = self.bank("do", [4, 5, 6, 7])
                        for k in range(8):
                            P.op("tensor", MM(ps[:, 0:512], mgT[:, k, ti * 128:(ti + 1) * 128], Wo[:, k, half * 512:(half + 1) * 512], k == 0, k == 7),
                                 reads=[mgT, Wo], writes=[ps], inc=(k == 7))
                        pss.append(ps)
                    row = g0 + ti * 128
                    self.postnorm(pss, x_, gbc, lng, lnb, upd[i % 2], yv[i % 2], yn[i % 2], sm[i % 2], S["resA"], S["resA"].t[b, row:row + 128, :])
        P.pop()

    def phaseF(self):
        P, I, S, C, l = self.P, self.I, self.S, self.C, self.l
        P.push()
        lng = P.sb("lng2", [128, D], F32)
        lnb = P.sb("lnb2", [128, D], F32)
        P.dma("sync", lng[:], I["ln2_g"].t[l, :].partition_broadcast(128), reads=[I["ln2_g"]], writes=[lng])
        P.dma("sync", lnb[:], I["ln2_b"].t[l, :].partition_broadcast(128), reads=[I["ln2_b"]], writes=[lnb])
        cwf = P.sb("cwf", [128, 22, 3], F32)
        cbf = P.sb("cbf", [128, 22], F32)
        for k in range(3):
            P.dma("sync", cwf[:, :, k], I["ffn_conv_w"].t[l, k, :].rearrange("(t p) -> p t", p=128), reads=[I["ffn_conv_w"]], writes=[cwf], allow_slow_non_contiguous=True)
        P.dma("sync", cbf[:], I["ffn_conv_b"].t[l, :].rearrange("(t p) -> p t", p=128), reads=[I["ffn_conv_b"]], writes=[cbf], allow_slow_non_contiguous=True)
        hff = P.sb("hff", [128, 22, TL], BF16)
        gbc = P.sb("gbc2", [128, D], F32)
        wup = I["ffn_w_up"].t[l].rearrange("(k p) n -> p k n", p=128)
        for (b, seg, r, gofs, slen) in self.segs():
            ntile = slen // 128
            P.push()
            hT2 = P.sb("hT2", [128, 8, slen], BF16)
            P.push()
            xt = [P.sb("fxt%d" % i, [128, D], F32) for i in range(2)]
            xn = [P.sb("fxn%d" % i, [128, D], BF16) for i in range(2)]
            sm = self.ln_small("f")
            for ti in range(ntile):
                x_, n_ = xt[ti % 2], xn[ti % 2]
                row = gofs + ti * 128
                P.dma("sync", x_[:], S["resA"].t[b, row:row + 128, :], reads=[S["resA"]], writes=[x_])
                self.ln_tile(x_, n_[:], n_, sm[ti % 2])
                ps = self.bank("ftr", [0, 1])
                pb = ps[:].bitcast(BF16).rearrange("p (k t) -> p k t", k=8)
                for k in range(8):
                    P.op("tensor", TR(pb[:, k, :], n_[:, k * 128:(k + 1) * 128], C["identb"][:]), reads=[n_, C["identb"]], writes=[ps], inc=(k == 7))
                for k in range(8):
                    o = hT2[:, k, ti * 128:(ti + 1) * 128]
                    rd = [ps, self.modT] if k in (0, 7) else []
                    wr = [hT2] if k in (0, 7) else []
                    if ti % 2 == 0:
                        P.op("vector", TS(o, pb[:, k, :], self.modT[:, 4, k, r:r + 1], self.modT[:, 3, k, r:r + 1], ALU.mult, ALU.add), reads=rd, writes=wr)
                    else:
                        P.op("scalar", ACT(o, pb[:, k, :], AF.Identity, bias=self.modT[:, 3, k, r:r + 1], scale=self.modT[:, 4, k, r:r + 1]), reads=rd, writes=wr)
            P.pop()
            wu = [P.sb("wu%d" % i, [128, 8, 128], BF16) for i in range(2)]
            wv = [P.sb("wv%d" % i, [128, 8, 128], BF16) for i in range(2)]
            wuf = [P.sb("wuf%d" % i, [128, 8, 128], F32) for i in range(2)]
            wvf = [P.sb("wvf%d" % i, [128, 8, 128], F32) for i in range(2)]
            ucp = [P.sb("ucp%d" % i, [128, slen + 2], F32) for i in range(2)]
            acc = P.sb("facc", [128, slen], F32)
            ge = [P.sb("fge%d" % i, [128, slen], BF16) for i in range(2)]
            for u_ in ucp:
                P.op("gpsimd", lambda e, u_=u_: e.memset(u_[:, 0:1], 0.0), writes=[u_])
                P.op("gpsimd", lambda e, u_=u_, slen=slen: e.memset(u_[:, slen + 1:slen + 2], 0.0), writes=[u_])
            nbs = [(n0, min(512, slen - n0)) for n0 in range(0, slen, 512)]
            for j in range(22):
                wu_, wv_, u_, g_ = wu[j % 2], wv[j % 2], ucp[j % 2], ge[j % 2]
                P.dma("sync", wuf[j % 2][:], wup[:, :, j * 128:(j + 1) * 128], writes=[wuf[j % 2]])
                P.dma("sync", wvf[j % 2][:], wup[:, :, DFF + j * 128:DFF + (j + 1) * 128], writes=[wvf[j % 2]])
                P.op("gpsimd", CP(wu_[:], wuf[j % 2][:]), reads=[wuf[j % 2]], writes=[wu_])
                P.op("gpsimd", CP(wv_[:], wvf[j % 2][:]), reads=[wvf[j % 2]], writes=[wv_])
                for (n0, nn) in nbs:
                    ps = self.bank("fu", [0, 1, 2, 3])
                    for k in range(8):
                        P.op("tensor", MM(ps[:, 0:nn], wu_[:, k, :], hT2[:, k, n0:n0 + nn], k == 0, k == 7), reads=[wu_, hT2], writes=[ps], inc=(k == 7))
                    P.op("scalar", ACT(u_[:, 1 + n0:1 + n0 + nn], ps[:, 0:nn], AF.Copy), reads=[ps], writes=[u_])
                P.op("vector", TS(acc[:], u_[:, 0:slen], cwf[:, j, 0:1], None, ALU.mult), reads=[u_, cwf], writes=[acc])
                P.op("vector", STT(acc[:], u_[:, 1:slen + 1], cwf[:, j, 1:2], acc[:], ALU.mult, ALU.add), reads=[u_, cwf, acc], writes=[acc])
                P.op("vector", STT(acc[:], u_[:, 2:slen + 2], cwf[:, j, 2:3], acc[:], ALU.mult, ALU.add), reads=[u_, cwf, acc], writes=[acc])
                P.op("scalar", ACT(g_[:], acc[:], AF.Gelu_apprx_tanh, bias=cbf[:, j:j + 1], scale=1.0), reads=[acc, cbf], writes=[g_])
                for (n0, nn) in nbs:
                    ps = self.bank("fv", [4, 5, 6, 7])
                    for k in range(8):
                        P.op("tensor", MM(ps[:, 0:nn], wv_[:, k, :], hT2[:, k, n0:n0 + nn], k == 0, k == 7), reads=[wv_, hT2], writes=[ps], inc=(k == 7))
                    P.op("vector", TTo(hff[:, j, n0:n0 + nn], ps[:, 0:nn], g_[:, n0:n0 + nn], ALU.mult), reads=[ps, g_], writes=[hff])
            P.pop()
            P.push()
            Wd = P.sb("Wd", [128, 22, D], BF16)
            P.dma("sync", Wd[:], S["wb_dn%d" % l].t.rearrange("(t p) n -> p t n", p=128), reads=[S["wb_dn%d" % l]], writes=[Wd])
            P.dma("sync", gbc[:], S["modr%d" % l].t[r, 5 * D:6 * D].partition_broadcast(128), reads=[S["modr%d" % l]], writes=[gbc])
            xt = [P.sb("gxt%d" % i, [128, D], F32) for i in range(2)]
            upd = [P.sb("gupd%d" % i, [128, D], F32) for i in range(2)]
            yv = [P.sb("gyv%d" % i, [128, D], F32) for i in range(2)]
            yn = [P.sb("gyn%d" % i, [128, D], F32) for i in range(2)]
            sm = self.ln_small("g")
            for ti in range(ntile):
                x_ = xt[ti % 2]
                row = gofs + ti * 128
                P.dma("sync", x_[:], S["resA"].t[b, row:row + 128, :], reads=[S["resA"]], writes=[x_])
                pss = []
                for half in range(2):
                    ps = self.bank("fd", [0, 1, 2, 3])
                    for j in range(22):
                        P.op("tensor", MM(ps[:, 0:512], hff[:, j, ti * 128:(ti + 1) * 128], Wd[:, j, half * 512:(half + 1) * 512], j == 0, j == 21),
                             reads=[hff, Wd], writes=[ps], inc=(j == 21))
                    pss.append(ps)
                if self.last:
                    dbuf, dap = self.out, self.out.t[b, ti * 128:(ti + 1) * 128, :]
                else:
                    dbuf, dap = S["resB"], S["resB"].t[b, row:row + 128, :]
                self.postnorm(pss, x_, gbc, lng, lnb, upd[ti % 2], yv[ti % 2], yn[ti % 2], sm[ti % 2], dbuf, dap)
            P.pop()
        P.pop()


def _rope_tables():
    half = 64
    inv_freq = (1.0 / (np.float32(10000.0) ** (np.arange(0, half, 2, dtype=np.float32) / np.float32(half)))).astype(np.float32)
    rows = TL // 64
    row = np.repeat(np.arange(rows, dtype=np.float32), 64)
    col = np.tile(np.arange(64, dtype=np.float32), rows)
    ang = np.concatenate([row[:, None] * inv_freq, col[:, None] * inv_freq], -1).astype(np.float32)
    cos = np.cos(ang).astype(np.float32).reshape(16, 128, 64).transpose(1, 0, 2)
    sin = np.sin(ang).astype(np.float32).reshape(16, 128, 64).transpose(1, 0, 2)
    return np.ascontiguousarray(cos), np.ascontiguousarray(sin)


def _host_consts():
    cos, sin = _rope_tables()
    jj = np.arange(128) // 16
    maskf = (jj[None, :] >= jj[:, None]).astype(np.float32)
    maskr = (jj[None, :] <= jj[:, None]).astype(np.float32)
    return {
        "k_identf": np.eye(128, dtype=np.float32),
        "k_identb": np.eye(128, dtype=np.float32).astype(ml_dtypes.bfloat16),
        "k_onesb": np.ones((128, 128), dtype=np.float32).astype(ml_dtypes.bfloat16),
        "k_cos": cos, "k_sin": sin, "k_maskf": maskf, "k_maskr": maskr,
    }


_NC_CACHE = {}


def _get_nc():
    if "nc" not in _NC_CACHE:
        _NC_CACHE["nc"] = K().build()
    return _NC_CACHE["nc"]


def make_in_maps(inputs, cores):
    consts = _host_consts()
    maps = []
    for i in cores:
        m = {}
        for k, v in inputs.items():
            v = np.asarray(v)
            if k in ("x", "c", "ctx"):
                m[k] = np.ascontiguousarray(v[NBC * i:NBC * (i + 1)])
            elif k == "c_ctx":
                m[k] = np.ascontiguousarray(v.reshape(1, D))
            else:
                m[k] = v
        m.update(consts)
        maps.append(m)
    return maps


def kernel(**inputs):
    nc = _get_nc()
    maps = make_in_maps(inputs, range(8))
    res = run_bass_kernel_spmd(nc, maps, core_ids=list(range(8)))
    return np.concatenate([np.asarray(r["out"]) for r in res.results], axis=0).astype(np.float32)
```
